# Optimizing a Trainium2 kernel written in Bass

```python
import math
import jax, jax.numpy as jnp
from jax import lax
import numpy as np

D_MODEL = 1024
BATCH = 8
SEQ = 2048
DEPTH = 4

CHUNK = 64
N_MEM = 256
Q_BLOCK = 128
NORM_EPS = 1e-6
NEG_INF = -1e30

A_HEADS = 4
A_DK = 128
A_DV = 128
A_QK = A_HEADS * A_DK
A_V = A_HEADS * A_DV
SHORT_CONV = 4
B_WIDTH = 512
B_BLOCKS = 8
B_BLOCK_DIM = B_WIDTH // B_BLOCKS
B_CONV = 4
LRU_C = 8.0
C_HEADS = 4
C_DK = 64
C_DV = 2 * C_DK
D_HEADS = 4
Q_LORA = 256
KV_LORA = 128
D_NOPE = 64
D_ROPE = 32
D_DV = 128
ROPE_THETA = 10000.0
X_HEADS = 4
X_DH = 128
FF_HIDDEN = -(-(8 * D_MODEL) // (3 * 256)) * 256

EV_COLS = 2 * A_QK + 2 * A_V + 2 * A_HEADS + 2 * B_WIDTH
EV_OUT = A_V + B_WIDTH
OD_COLS = 4 * C_HEADS * C_DK + C_HEADS * C_DV + Q_LORA + KV_LORA + D_ROPE
OD_OUT = C_HEADS * C_DV + D_HEADS * D_DV
N_EVEN = (DEPTH + 1) // 2
N_ODD = DEPTH // 2

kernel_name = 'hybrid_deltanet_rglru_diffattn_mla_trunk'


def _split(x, sizes):
    return jnp.split(x, np.cumsum(sizes)[:-1].tolist(), axis=-1)


def _rmsnorm(x, g):
    xf = x.astype(jnp.float32)
    y = xf * lax.rsqrt(jnp.mean(xf * xf, axis=-1, keepdims=True) + NORM_EPS)
    return (y * g.astype(jnp.float32)).astype(x.dtype)


def _l2norm(x):
    xf = x.astype(jnp.float32)
    return xf * lax.rsqrt(jnp.sum(xf * xf, axis=-1, keepdims=True) + NORM_EPS)


def _causal_dwconv(x, w):
    k, s = w.shape[0], x.shape[1]
    xp = jnp.pad(x, ((0, 0), (k - 1, 0), (0, 0)))
    y = xp[:, 0:s] * w[0]
    for j in range(1, k):
        y = y + xp[:, j:j + s] * w[j]
    return y


def _unit_lower_inverse(m):
    eye = jnp.eye(m.shape[-1], dtype=m.dtype)
    n = -m
    acc = eye + n
    p = n
    for _ in range(CHUNK.bit_length() - 2):
        p = p @ p
        acc = acc @ (eye + p)
    return acc


def _gated_delta_rule(q, k, v, g, beta):
    f32 = jnp.float32
    b, s, h, dk = q.shape
    dv = v.shape[-1]
    nc = s // CHUNK

    def chunks(t):
        t = t.astype(f32).reshape((b, nc, CHUNK, h) + t.shape[3:])
        return jnp.moveaxis(t, 3, 1)

    q = chunks(q) * (dk ** -0.5)
    k = chunks(k)
    v = chunks(v)
    g = chunks(g)
    beta = chunks(beta)
    cum = jnp.cumsum(g, axis=-1)
    incl = jnp.tril(jnp.ones((CHUNK, CHUNK), dtype=bool))
    strict = jnp.tril(jnp.ones((CHUNK, CHUNK), dtype=bool), -1)
    rel = cum[..., :, None] - cum[..., None, :]
    decay = jnp.where(incl, jnp.exp(jnp.where(incl, rel, 0.0)), 0.0)
    kb = k * beta[..., None]
    m = jnp.where(strict, jnp.einsum('bhnid,bhnjd->bhnij', kb, k) * decay, 0.0)
    t_inv = _unit_lower_inverse(m)
    w = jnp.einsum('bhnij,bhnjd->bhnid', t_inv, kb * jnp.exp(cum)[..., None])
    u = jnp.einsum('bhnij,bhnje->bhnie', t_inv, v * beta[..., None])
    qk = jnp.where(incl, jnp.einsum('bhnid,bhnjd->bhnij', q, k) * decay, 0.0)
    q_dec = q * jnp.exp(cum)[..., None]
    k_dec = k * jnp.exp(cum[..., -1:] - cum)[..., None]
    last = jnp.exp(cum[..., -1])

    def step(state, xs):
        w_c, u_c, q_c, qk_c, k_c, last_c = xs
        v_new = u_c - jnp.einsum('bhcd,bhde->bhce', w_c, state)
        o_c = (jnp.einsum('bhcd,bhde->bhce', q_c, state)
               + jnp.einsum('bhij,bhje->bhie', qk_c, v_new))
        state = state * last_c[..., None, None] + jnp.einsum('bhcd,bhce->bhde', k_c, v_new)
        return state, o_c

    xs = tuple(jnp.moveaxis(t, 2, 0) for t in (w, u, q_dec, qk, k_dec, last))
    _, o = lax.scan(step, jnp.zeros((b, h, dk, dv), f32), xs)
    return jnp.transpose(o, (1, 0, 3, 2, 4)).reshape(b, s, h, dv)


def _linear_recurrence_combine(left, right):
    a_l, b_l = left
    a_r, b_r = right
    return a_l * a_r, a_r * b_l + b_r


def _alibi_slopes(n_heads):
    return 2.0 ** (-8.0 * jnp.arange(1, n_heads + 1, dtype=jnp.float32) / n_heads)


def _chunk_causal_mask(start, seq):
    t = start + jnp.arange(Q_BLOCK, dtype=jnp.int32)
    s = jnp.arange(seq, dtype=jnp.int32)
    return (s[None, :] // CHUNK) <= (t[:, None] // CHUNK)


def _sweep_query_blocks(block_fn, seq):
    starts = jnp.arange(seq // Q_BLOCK, dtype=jnp.int32) * Q_BLOCK
    out = lax.map(block_fn, starts)
    nqb, b, h, qb, dv = out.shape
    return jnp.transpose(out, (1, 0, 3, 2, 4)).reshape(b, nqb * qb, h * dv)


def _diff_attention(q, k, v, lam, positions):
    b, h2, s, dk = q.shape
    slopes = jnp.repeat(_alibi_slopes(C_HEADS), 2)
    scale = dk ** -0.5

    def block(start):
        qb = lax.dynamic_slice_in_dim(q, start, Q_BLOCK, axis=2)
        pq = lax.dynamic_slice_in_dim(positions, start, Q_BLOCK, axis=1)
        dist = jnp.abs(pq[:, :, None] - positions[:, None, :]).astype(jnp.float32)
        sc = jnp.einsum('bhqd,bhkd->bhqk', qb, k).astype(jnp.float32) * scale
        sc = sc - slopes[None, :, None, None] * dist[:, None]
        sc = jnp.where(_chunk_causal_mask(start, s), sc, NEG_INF)
        p = jax.nn.softmax(sc, axis=-1).reshape(b, C_HEADS, 2, Q_BLOCK, s)
        p = (p[:, :, 0] - lam * p[:, :, 1]).astype(v.dtype)
        return jnp.einsum('bhqk,bhkd->bhqd', p, v)

    return _sweep_query_blocks(block, s)


def _mla_attention(q, k, v):
    s, dk = q.shape[2], q.shape[3]
    scale = dk ** -0.5

    def block(start):
        qb = lax.dynamic_slice_in_dim(q, start, Q_BLOCK, axis=2)
        sc = jnp.einsum('bhqd,bhkd->bhqk', qb, k).astype(jnp.float32) * scale
        sc = jnp.where(_chunk_causal_mask(start, s), sc, NEG_INF)
        p = jax.nn.softmax(sc, axis=-1).astype(v.dtype)
        return jnp.einsum('bhqk,bhkd->bhqd', p, v)

    return _sweep_query_blocks(block, s)


def _rope(x, positions):
    half = x.shape[-1] // 2
    inv_freq = ROPE_THETA ** (-jnp.arange(half, dtype=jnp.float32) / half)
    ang = positions.astype(jnp.float32)[:, :, None] * inv_freq
    cos = jnp.cos(ang)[:, :, None, :]
    sin = jnp.sin(ang)[:, :, None, :]
    x1 = x[..., :half].astype(jnp.float32)
    x2 = x[..., half:].astype(jnp.float32)
    return jnp.concatenate([x1 * cos - x2 * sin, x2 * cos + x1 * sin], axis=-1).astype(x.dtype)


def _even_mixer(h, w_in, conv_qkv, a_log, dt_bias, o_norm, conv_b_w, conv_b_b,
                gate_a_w, gate_a_b, gate_x_w, gate_x_b, lru_l, w_out):
    f32 = jnp.float32
    b, s, _ = h.shape
    qkv, z, beta_raw, decay_raw, xb, gb = _split(
        h @ w_in, [2 * A_QK + A_V, A_V, A_HEADS, A_HEADS, B_WIDTH, B_WIDTH])
    qkv = jax.nn.silu(_causal_dwconv(qkv, conv_qkv))
    q, k, v = _split(qkv, [A_QK, A_QK, A_V])
    q = _l2norm(q.reshape(b, s, A_HEADS, A_DK))
    k = _l2norm(k.reshape(b, s, A_HEADS, A_DK))
    v = v.reshape(b, s, A_HEADS, A_DV)
    beta = jax.nn.sigmoid(beta_raw.astype(f32))
    g = -jnp.exp(a_log.astype(f32)) * jax.nn.softplus(decay_raw.astype(f32) + dt_bias.astype(f32))
    o = _gated_delta_rule(q, k, v, g, beta)
    o = _rmsnorm(o, o_norm) * jax.nn.silu(z.reshape(b, s, A_HEADS, A_DV).astype(f32))
    y_a = o.reshape(b, s, A_V).astype(h.dtype)
    xb = _causal_dwconv(xb, conv_b_w) + conv_b_b
    xblk = xb.reshape(b, s, B_BLOCKS, B_BLOCK_DIM)
    r = jax.nn.sigmoid((jnp.einsum('bsnd,nde->bsne', xblk, gate_a_w).reshape(b, s, B_WIDTH)
                        + gate_a_b).astype(f32))
    i = jax.nn.sigmoid((jnp.einsum('bsnd,nde->bsne', xblk, gate_x_w).reshape(b, s, B_WIDTH)
                        + gate_x_b).astype(f32))
    log_a = -LRU_C * r * jax.nn.softplus(-lru_l.astype(f32))
    a = jnp.exp(log_a)
    u = jnp.sqrt(-jnp.expm1(2.0 * log_a)) * (i * xb.astype(f32))
    _, hs = lax.associative_scan(_linear_recurrence_combine, (a, u), axis=1)
    y_b = (jax.nn.gelu(gb.astype(f32)) * hs).astype(h.dtype)
    return jnp.concatenate([y_a, y_b], axis=-1) @ w_out


def _odd_mixer(h, positions, layer_idx, w_in, c_q_norm, c_k_norm, lam_q1, lam_k1, lam_q2, lam_k2,
               c_sub_norm, q_lat_norm, w_uq, kv_lat_norm, w_ukv, d_q_norm, d_k_norm, w_out):
    f32 = jnp.float32
    b, s, _ = h.shape
    qc, kc, vc, q_lat, kv_lat, k_rope = _split(
        h @ w_in, [2 * C_HEADS * C_DK, 2 * C_HEADS * C_DK, C_HEADS * C_DV, Q_LORA, KV_LORA, D_ROPE])
    qc = jnp.transpose(_rmsnorm(qc.reshape(b, s, 2 * C_HEADS, C_DK), c_q_norm), (0, 2, 1, 3))
    kc = jnp.transpose(_rmsnorm(kc.reshape(b, s, 2 * C_HEADS, C_DK), c_k_norm), (0, 2, 1, 3))
    vc = jnp.transpose(vc.reshape(b, s, C_HEADS, C_DV), (0, 2, 1, 3))
    lam_init = 0.8 - 0.6 * math.exp(-0.3 * layer_idx)
    lam = (jnp.exp(jnp.sum(lam_q1.astype(f32) * lam_k1.astype(f32)))
           - jnp.exp(jnp.sum(lam_q2.astype(f32) * lam_k2.astype(f32))) + lam_init)
    yc = _diff_attention(qc, kc, vc, lam, positions)
    yc = (_rmsnorm(yc.reshape(b, s, C_HEADS, C_DV), c_sub_norm) * (1.0 - lam_init)).reshape(b, s, C_HEADS * C_DV)
    qd = (_rmsnorm(q_lat, q_lat_norm) @ w_uq).reshape(b, s, D_HEADS, D_NOPE + D_ROPE)
    kvd = (_rmsnorm(kv_lat, kv_lat_norm) @ w_ukv).reshape(b, s, D_HEADS, D_NOPE + D_DV)
    k_nope, vd = _split(kvd, [D_NOPE, D_DV])
    kd = jnp.concatenate([k_nope, jnp.broadcast_to(k_rope[:, :, None, :], (b, s, D_HEADS, D_ROPE))], axis=-1)
    qd = _rmsnorm(qd, d_q_norm)
    kd = _rmsnorm(kd, d_k_norm)
    qd = jnp.concatenate([qd[..., :D_NOPE], _rope(qd[..., D_NOPE:], positions)], axis=-1)
    kd = jnp.concatenate([kd[..., :D_NOPE], _rope(kd[..., D_NOPE:], positions)], axis=-1)
    yd = _mla_attention(jnp.transpose(qd, (0, 2, 1, 3)), jnp.transpose(kd, (0, 2, 1, 3)),
                        jnp.transpose(vd, (0, 2, 1, 3)))
    return jnp.concatenate([yc, yd], axis=-1) @ w_out


def _memory_cross_attention(h, m, wq, wkv, q_norm, k_norm, wo):
    b, s, _ = h.shape
    nm = m.shape[1]
    q = _rmsnorm((h @ wq).reshape(b, s, X_HEADS, X_DH), q_norm)
    kv = (m @ wkv).reshape(b, nm, 2, X_HEADS, X_DH)
    k = _rmsnorm(kv[:, :, 0], k_norm)
    v = kv[:, :, 1]
    sc = jnp.einsum('bshd,bmhd->bhsm', q, k).astype(jnp.float32) * (X_DH ** -0.5)
    p = jax.nn.softmax(sc, axis=-1).astype(v.dtype)
    o = jnp.einsum('bhsm,bmhd->bshd', p, v).reshape(b, s, X_HEADS * X_DH)
    return o @ wo


def _swiglu(h, w_in, w_out):
    gate, up = jnp.split(h @ w_in, 2, axis=-1)
    return (jax.nn.silu(gate) * up) @ w_out


def setup_inputs(seed: int = 0) -> dict:
    key = jax.random.key(seed)
    ks = iter(jax.random.split(key, 64))
    f32 = jnp.float32
    out_gain = 0.5

    def w(shape, fan_in, gain=1.0):
        return jax.random.normal(next(ks), shape, f32) * (gain * fan_in ** -0.5)

    def ones_noise(shape):
        return 1.0 + 0.02 * jax.random.normal(next(ks), shape, f32)

    def small(shape, scale):
        return scale * jax.random.normal(next(ks), shape, f32)

    ne, no = N_EVEN, N_ODD
    x = jax.random.normal(next(ks), (BATCH, SEQ, D_MODEL), f32)
    mem = jax.random.normal(next(ks), (BATCH, N_MEM, D_MODEL), f32)
    offsets = jax.random.randint(next(ks), (BATCH, 1), 0, 64, dtype=jnp.int32) * CHUNK
    positions = (offsets + jnp.arange(SEQ, dtype=jnp.int32)[None, :]).astype(jnp.int32)

    norm_mix = ones_noise((DEPTH, D_MODEL))
    norm_x = ones_noise((DEPTH, D_MODEL))
    norm_mem = ones_noise((DEPTH, D_MODEL))
    x_wq = w((DEPTH, D_MODEL, X_HEADS * X_DH), D_MODEL)
    x_wkv = w((DEPTH, D_MODEL, 2 * X_HEADS * X_DH), D_MODEL)
    x_q_norm = ones_noise((DEPTH, X_DH))
    x_k_norm = ones_noise((DEPTH, X_DH))
    x_wo = w((DEPTH, X_HEADS * X_DH, D_MODEL), X_HEADS * X_DH, out_gain)
    norm_ffn = ones_noise((DEPTH, D_MODEL))
    ffn_w_in = w((DEPTH, D_MODEL, 2 * FF_HIDDEN), D_MODEL)
    ffn_w_out = w((DEPTH, FF_HIDDEN, D_MODEL), FF_HIDDEN, out_gain)

    ev_w_in = w((ne, D_MODEL, EV_COLS), D_MODEL)
    ev_conv_qkv = w((ne, SHORT_CONV, 2 * A_QK + A_V), SHORT_CONV)
    ev_a_log = jnp.log(jax.random.uniform(next(ks), (ne, A_HEADS), f32, 1.0, 16.0))
    dt = jnp.exp(jax.random.uniform(next(ks), (ne, A_HEADS), f32, math.log(1e-3), math.log(1e-1)))
    ev_dt_bias = dt + jnp.log(-jnp.expm1(-dt))
    ev_o_norm = ones_noise((ne, A_DV))
    ev_conv_b_w = w((ne, B_CONV, B_WIDTH), B_CONV)
    ev_conv_b_b = small((ne, B_WIDTH), 0.02)
    ev_gate_a_w = w((ne, B_BLOCKS, B_BLOCK_DIM, B_BLOCK_DIM), B_BLOCK_DIM)
    ev_gate_a_b = small((ne, B_WIDTH), 0.1)
    ev_gate_x_w = w((ne, B_BLOCKS, B_BLOCK_DIM, B_BLOCK_DIM), B_BLOCK_DIM)
    ev_gate_x_b = small((ne, B_WIDTH), 0.1)
    a0 = jax.random.uniform(next(ks), (ne, B_WIDTH), f32, 0.9, 0.999)
    a1 = a0 ** (1.0 / LRU_C)
    ev_lru_l = jnp.log(a1) - jnp.log1p(-a1)
    ev_w_out = w((ne, EV_OUT, D_MODEL), EV_OUT, out_gain)

    od_w_in = w((no, D_MODEL, OD_COLS), D_MODEL)
    od_c_q_norm = ones_noise((no, C_DK))
    od_c_k_norm = ones_noise((no, C_DK))
    od_lam_q1 = small((no, C_DK), 0.1)
    od_lam_k1 = small((no, C_DK), 0.1)
    od_lam_q2 = small((no, C_DK), 0.1)
    od_lam_k2 = small((no, C_DK), 0.1)
    od_c_sub_norm = ones_noise((no, C_DV))
    od_q_lat_norm = ones_noise((no, Q_LORA))
    od_w_uq = w((no, Q_LORA, D_HEADS * (D_NOPE + D_ROPE)), Q_LORA)
    od_kv_lat_norm = ones_noise((no, KV_LORA))
    od_w_ukv = w((no, KV_LORA, D_HEADS * (D_NOPE + D_DV)), KV_LORA)
    od_d_q_norm = ones_noise((no, D_NOPE + D_ROPE))
    od_d_k_norm = ones_noise((no, D_NOPE + D_ROPE))
    od_w_out = w((no, OD_OUT, D_MODEL), OD_OUT, out_gain)

    return {
        'x': x, 'mem': mem, 'positions': positions,
        'norm_mix': norm_mix, 'norm_x': norm_x, 'norm_mem': norm_mem,
        'x_wq': x_wq, 'x_wkv': x_wkv, 'x_q_norm': x_q_norm, 'x_k_norm': x_k_norm, 'x_wo': x_wo,
        'norm_ffn': norm_ffn, 'ffn_w_in': ffn_w_in, 'ffn_w_out': ffn_w_out,
        'ev_w_in': ev_w_in, 'ev_conv_qkv': ev_conv_qkv, 'ev_a_log': ev_a_log, 'ev_dt_bias': ev_dt_bias,
        'ev_o_norm': ev_o_norm, 'ev_conv_b_w': ev_conv_b_w, 'ev_conv_b_b': ev_conv_b_b,
        'ev_gate_a_w': ev_gate_a_w, 'ev_gate_a_b': ev_gate_a_b, 'ev_gate_x_w': ev_gate_x_w,
        'ev_gate_x_b': ev_gate_x_b, 'ev_lru_l': ev_lru_l, 'ev_w_out': ev_w_out,
        'od_w_in': od_w_in, 'od_c_q_norm': od_c_q_norm, 'od_c_k_norm': od_c_k_norm,
        'od_lam_q1': od_lam_q1, 'od_lam_k1': od_lam_k1, 'od_lam_q2': od_lam_q2, 'od_lam_k2': od_lam_k2,
        'od_c_sub_norm': od_c_sub_norm, 'od_q_lat_norm': od_q_lat_norm, 'od_w_uq': od_w_uq,
        'od_kv_lat_norm': od_kv_lat_norm, 'od_w_ukv': od_w_ukv, 'od_d_q_norm': od_d_q_norm,
        'od_d_k_norm': od_d_k_norm, 'od_w_out': od_w_out,
    }


def reference(x, mem, positions,
              norm_mix, norm_x, norm_mem, x_wq, x_wkv, x_q_norm, x_k_norm, x_wo,
              norm_ffn, ffn_w_in, ffn_w_out,
              ev_w_in, ev_conv_qkv, ev_a_log, ev_dt_bias, ev_o_norm, ev_conv_b_w, ev_conv_b_b,
              ev_gate_a_w, ev_gate_a_b, ev_gate_x_w, ev_gate_x_b, ev_lru_l, ev_w_out,
              od_w_in, od_c_q_norm, od_c_k_norm, od_lam_q1, od_lam_k1, od_lam_q2, od_lam_k2,
              od_c_sub_norm, od_q_lat_norm, od_w_uq, od_kv_lat_norm, od_w_ukv, od_d_q_norm,
              od_d_k_norm, od_w_out):
    for l in range(DEPTH):
        h = _rmsnorm(x, norm_mix[l])
        if l % 2 == 0:
            e = l // 2
            y = _even_mixer(h, ev_w_in[e], ev_conv_qkv[e], ev_a_log[e], ev_dt_bias[e], ev_o_norm[e],
                            ev_conv_b_w[e], ev_conv_b_b[e], ev_gate_a_w[e], ev_gate_a_b[e],
                            ev_gate_x_w[e], ev_gate_x_b[e], ev_lru_l[e], ev_w_out[e])
        else:
            o = l // 2
            y = _odd_mixer(h, positions, l, od_w_in[o], od_c_q_norm[o], od_c_k_norm[o],
                           od_lam_q1[o], od_lam_k1[o], od_lam_q2[o], od_lam_k2[o], od_c_sub_norm[o],
                           od_q_lat_norm[o], od_w_uq[o], od_kv_lat_norm[o], od_w_ukv[o],
                           od_d_q_norm[o], od_d_k_norm[o], od_w_out[o])
        x = x + y
        x = x + _memory_cross_attention(_rmsnorm(x, norm_x[l]), _rmsnorm(mem, norm_mem[l]),
                                        x_wq[l], x_wkv[l], x_q_norm[l], x_k_norm[l], x_wo[l])
        x = x + _swiglu(_rmsnorm(x, norm_ffn[l]), ffn_w_in[l], ffn_w_out[l])
    return x
```

```python
from contextlib import ExitStack
import math
import numpy as np
import concourse.bass as bass
import concourse.mybir as mybir
from concourse.bass_utils import run_bass_kernel_spmd

F32 = mybir.dt.float32
BF16 = mybir.dt.bfloat16
I32 = mybir.dt.int32
AF = mybir.ActivationFunctionType
ALU = mybir.AluOpType
AX = mybir.AxisListType

EPOCH = 30000
STRICT_SAME_ENGINE = True
NSLOT = 8
ENGS = ("pe", "act", "dve", "pool", "sp")


class Prog:
    def __init__(self, nc):
        self.nc = nc
        self.ops = {e: [] for e in ENGS}
        self.ncomp = {e: 0 for e in ENGS}
        self.ndma = {e: 0 for e in ENGS}
        self.last_w = {}
        self.readers = {}
        self.waited = {e: {} for e in ENGS}
        self.semkeys = set()
        self.sems = {}
        self.last_tok = {}
        self.gdep = None

    def add(self, eng, fn, r=(), w=(), dma=False, nofence=False):
        nowait_only = fn is None
        raw = {}
        oth = {}
        if eng != "pe" and not nowait_only:
            locks = [("pslock", k[1]) for k in r if isinstance(k, tuple) and len(k) == 2 and k[0] == "ps"]
            if locks:
                w = list(w) + locks

        def put(d, tok):
            sk, v, e2, d2 = tok
            if d.get(sk, (0,))[0] < v:
                d[sk] = (v, e2, d2)

        for k in r:
            t = self.last_w.get(k)
            if t is not None:
                put(raw, t)
        for k in w:
            t = self.last_w.get(k)
            if t is not None:
                put(oth, t)
            for sk, (v, e2, d2) in self.readers.get(k, {}).items():
                put(oth, (sk, v, e2, d2))
        if self.gdep is not None and not nofence:
            put(raw, self.gdep)
        if nowait_only:
            semkey, val = None, 0
        elif dma:
            i = self.ndma[eng]
            self.ndma[eng] += 1
            slot, rnd = i % NSLOT, i // NSLOT
            semkey = ("d", eng, slot)
            val = 16 * (rnd + 1)
            if rnd > 0:
                put(raw, (semkey, 16 * rnd, eng, True))
        else:
            i = self.ncomp[eng]
            self.ncomp[eng] += 1
            semkey = ("c", eng, i // EPOCH)
            val = i % EPOCH + 1
        tok = (semkey, val, eng, dma)
        waits = []
        wd = self.waited[eng]
        for d, is_raw in ((raw, True), (oth, False)):
            for sk, (v, e2, d2) in d.items():
                if not d2 and e2 == eng:
                    if eng == "pe" or (not is_raw and not STRICT_SAME_ENGINE):
                        continue
                if wd.get(sk, 0) >= v:
                    continue
                wd[sk] = v
                waits.append((sk, v))
        if nowait_only:
            self.ops[eng].append((None, waits, None, False))
            return None
        self.semkeys.add(semkey)
        self.last_tok[semkey] = tok
        for k in w:
            self.last_w[k] = tok
            self.readers[k] = {}
        for k in r:
            d = self.readers.setdefault(k, {})
            if d.get(semkey, (0,))[0] < val:
                d[semkey] = (val, eng, dma)
        self.ops[eng].append((fn, waits, semkey, dma))
        return tok

    def op(self, eng, name, *args, r=(), w=(), **kw):
        return self.add(eng, lambda e: getattr(e, name)(*args, **kw), r, w)

    def mm(self, out, lhsT, rhs, start=True, stop=True, r=(), w=(), **kw):
        return self.add("pe", lambda e: e.matmul(out, lhsT, rhs, start=start, stop=stop, **kw), r, w)

    def tr(self, out, in_, ident, r=(), w=()):
        return self.add("pe", lambda e: e.transpose(out, in_, ident), r, w)

    def act(self, out, in_, func, r=(), w=(), **kw):
        return self.add("act", lambda e: e.activation(out, in_, func, **kw), r, w)

    def ts(self, eng, out, in0, s1, s2, op0, op1=None, r=(), w=()):
        if op1 is None:
            return self.add(eng, lambda e: e.tensor_scalar(out, in0, s1, None, op0), r, w)
        return self.add(eng, lambda e: e.tensor_scalar(out, in0, s1, s2, op0, op1), r, w)

    def tt(self, eng, out, in0, in1, op, r=(), w=()):
        return self.add(eng, lambda e: e.tensor_tensor(out, in0, in1, op), r, w)

    def stt(self, eng, out, in0, scalar, in1, op0, op1, r=(), w=()):
        return self.add(eng, lambda e: e.scalar_tensor_tensor(out, in0, scalar, in1, op0, op1), r, w)

    def copy(self, eng, out, in_, r=(), w=()):
        if eng == "act":
            return self.add(eng, lambda e: e.copy(out, in_), r, w)
        return self.add(eng, lambda e: e.tensor_copy(out, in_), r, w)

    def dma(self, eng, out, in_, r=(), w=(), nofence=False, **kw):
        return self.add(eng, lambda e: e.dma_start(out, in_, **kw), r, w, dma=True, nofence=nofence)

    def fence(self):
        keys = []
        for sk, tok in list(self.last_tok.items()):
            k = ("_fence", sk)
            self.last_w[k] = tok
            self.readers[k] = {}
            keys.append(k)
        self.gdep = None
        tok = self.add("sp", lambda e: e.nop(), r=keys, w=[])
        self.gdep = tok

    def finish(self, eng, keys):
        self.add(eng, None, r=keys, w=())

    def emit(self):
        nc = self.nc
        with ExitStack() as st:
            for sk in sorted(self.semkeys, key=str):
                self.sems[sk] = st.enter_context(nc.semaphore("s_%s_%s_%d" % sk))
            with nc.Block() as block:
                def mk(name):
                    def body(e):
                        for fn, waits, semkey, dma in self.ops[name]:
                            for sk, v in waits:
                                e.wait_ge(self.sems[sk], v)
                            if fn is None:
                                continue
                            ins = fn(e)
                            ins.then_inc(self.sems[semkey], 16 if dma else 1)
                    return body
                block.tensor(mk("pe"))
                block.scalar(mk("act"))
                block.vector(mk("dve"))
                block.gpsimd(mk("pool"))
                block.sync(mk("sp"))


T = 2048
D = 1024
TB = 512
NTB = 4
NT = 16
DC = 8
FF = 2816
FC = 22
NMEM = 256
EPS = 1e-6
SB_BASE = 16512
SB_END = 229344
DEPTH = 4

WEIGHTS = [
    ("norm_mix", [4, 1024]), ("norm_x", [4, 1024]), ("norm_mem", [4, 1024]),
    ("x_wq", [4, 1024, 512]), ("x_wkv", [4, 1024, 1024]), ("x_q_norm", [4, 128]), ("x_k_norm", [4, 128]),
    ("x_wo", [4, 512, 1024]), ("norm_ffn", [4, 1024]), ("ffn_w_in", [4, 1024, 5632]),
    ("ffn_w_out", [4, 2816, 1024]),
    ("ev_w_in", [2, 1024, 3080]), ("ev_conv_qkv", [2, 4, 1536]), ("ev_a_log", [2, 4]), ("ev_dt_bias", [2, 4]),
    ("ev_o_norm", [2, 128]), ("ev_conv_b_w", [2, 4, 512]), ("ev_conv_b_b", [2, 512]),
    ("ev_gate_a_w", [2, 8, 64, 64]), ("ev_gate_a_b", [2, 512]), ("ev_gate_x_w", [2, 8, 64, 64]),
    ("ev_gate_x_b", [2, 512]), ("ev_lru_l", [2, 512]), ("ev_w_out", [2, 1024, 1024]),
    ("od_w_in", [2, 1024, 1952]), ("od_c_q_norm", [2, 64]), ("od_c_k_norm", [2, 64]),
    ("od_lam_q1", [2, 64]), ("od_lam_k1", [2, 64]), ("od_lam_q2", [2, 64]), ("od_lam_k2", [2, 64]),
    ("od_c_sub_norm", [2, 128]), ("od_q_lat_norm", [2, 256]), ("od_w_uq", [2, 256, 384]),
    ("od_kv_lat_norm", [2, 128]), ("od_w_ukv", [2, 128, 768]), ("od_d_q_norm", [2, 96]),
    ("od_d_k_norm", [2, 96]), ("od_w_out", [2, 1024, 1024]),
]


def host_consts():
    c = {}
    c["c_ident"] = np.eye(128, dtype=np.float32)
    rt = np.zeros((128, 128), np.float32)
    for i in range(16):
        rt[80 + i, 64 + i] = -1.0
        rt[64 + i, 80 + i] = 1.0
    c["c_rotT"] = rt
    fr = np.zeros((128, 2), np.float32)
    for i in range(16):
        f = 10000.0 ** (-(i / 16.0))
        fr[64 + i, 0] = fr[80 + i, 0] = np.float32(f)
    fr[:, 1] = fr[:, 0] / np.float32(2 * np.pi)
    c["c_freq"] = fr
    bo = np.zeros((128, 128), np.float32)
    bo[0:64, 0:64] = 1.0
    bo[64:128, 64:128] = 1.0
    c["c_blockones"] = bo
    ii = np.arange(128)
    same = (ii[:, None] // 64) == (ii[None, :] // 64)
    c["c_maskSL"] = (same & (ii[None, :] < ii[:, None])).astype(np.float32)
    c["c_maskIU"] = (same & (ii[None, :] >= ii[:, None])).astype(np.float32)
    c["c_lsel"] = (ii[:, None] == (ii[None, :] // 64) * 64 + 63).astype(np.float32)
    return c


class Arena:
    def __init__(self, nc, base, end):
        self.nc, self.p, self.end, self.n = nc, base, end, 0

    def alloc(self, name, shape, dtype):
        esz = 4 if dtype in (F32, I32) else 2
        nbytes = int(np.prod(shape[1:])) * esz
        off = (self.p + 31) // 32 * 32
        self.p = off + nbytes
        assert self.p <= self.end, ("SBUF overflow", name, self.p, self.end)
        self.n += 1
        return self.nc.alloc_sbuf_tensor_at("%s_%d" % (name, self.n), list(shape), dtype, offset=off).ap()

    def mark(self):
        return self.p

    def reset(self, m):
        self.p = m


class Builder:
    def __init__(self, nc, cfg):
        self.nc = nc
        self.cfg = cfg
        self.P = Prog(nc)
        self.d = {}
        P = self.P
        d = self.d
        d["x"] = nc.dram_tensor("x", [T, D], F32, kind="ExternalInput").ap()
        d["mem"] = nc.dram_tensor("mem", [NMEM, D], F32, kind="ExternalInput").ap()
        d["positions"] = nc.dram_tensor("positions", [1, T], I32, kind="ExternalInput").ap()
        for name, shp in WEIGHTS:
            d[name] = nc.dram_tensor(name, shp, F32, kind="ExternalInput").ap()
        for name, arr in host_consts().items():
            d[name] = nc.dram_tensor(name, list(arr.shape), F32, kind="ExternalInput").ap()
        d["y"] = nc.dram_tensor("y", [T, D], F32, kind="ExternalOutput").ap()
        self.A = Arena(nc, SB_BASE, SB_END)
        A = self.A
        self.ps = [nc.alloc_psum_tensor("psb%d" % i, [128, 512], F32).ap() for i in range(8)]
        self.bi = 0
        self.rot_banks = list(range(8))
        self.misc_banks = [6, 7]
        self.score_banks = [2, 3, 4, 5]
        self.mi = 0
        self.si = 0
        self.rots = {}
        self.xT = A.alloc("xT", [128, DC, T], F32)
        self.identF = A.alloc("identF", [128, 128], F32)
        self.identB = A.alloc("identB", [128, 128], BF16)
        self.onesB = A.alloc("onesB", [128, 128], BF16)
        self.onesF = A.alloc("onesF", [128, 128], F32)
        self.colsA = A.alloc("colsA", [128, 128], F32)
        self.colsB = A.alloc("colsB", [128, 128], F32)
        self.colsC = A.alloc("colsC", [128, 128], F32)
        self.blockB = A.alloc("blockB", [128, 128], BF16)
        self.rotTB = A.alloc("rotTB", [128, 128], BF16)
        self.freq = A.alloc("freq", [128, 2], F32)
        self.posk = A.alloc("posk", [128, NT], F32)
        self.negposk = A.alloc("negposk", [128, NT], F32)
        self.pkf = A.alloc("pkf", [NT, 128], F32)
        self.colsD = A.alloc("colsD", [128, 256], F32)
        self.maskSL = A.alloc("maskSL", [128, 128], F32)
        self.maskIU = A.alloc("maskIU", [128, 128], F32)
        self.lsel = A.alloc("lsel", [128, 128], F32)
        self.memTn = A.alloc("memTn", [128, DC, NMEM], F32)
        self.kTx = A.alloc("kTx", [128, 4, NMEM], BF16)
        self.vx = A.alloc("vx", [128, 2, 512], BF16)
        self.phase_base = A.mark()

    def bank(self):
        i = self.rot_banks[self.bi % len(self.rot_banks)]
        self.bi += 1
        return self.ps[i], ("ps", i)

    def mkrot(self, name, n, shape, dtype):
        self.rots[name] = [[self.A.alloc(name, shape, dtype) for _ in range(n)], 0]

    def rot(self, name):
        lst, i = self.rots[name]
        self.rots[name][1] = (i + 1) % len(lst)
        return lst[i], (name, i)

    def gcol(self, which, l, c):
        j = which * 32 + l * 8 + c
        return self.colsA[:, j:j + 1]

    def setup(self):
        P, d, A = self.P, self.d, self.A
        P.dma("sp", self.identF, d["c_ident"], w=["identF"])
        P.copy("dve", self.identB, self.identF, r=["identF"], w=["identB"])
        P.op("dve", "memset", self.onesB, 1.0, w=["onesB"])
        P.op("dve", "memset", self.onesF, 1.0, w=["onesF"])
        m = A.mark()
        stg = A.alloc("stg", [128, 128], F32)
        for i, nm in enumerate(["norm_mix", "norm_x", "norm_ffn", "norm_mem"]):
            P.dma("sp", stg[i * 32:(i + 1) * 32, :], d[nm].rearrange("l (c p) -> (l c) p", p=128), w=[("stg", i)])
        pb, pk = self.bank()
        P.tr(pb[:, 0:128], stg, self.identF, r=[("stg", i) for i in range(4)] + ["identF"], w=[pk])
        P.copy("dve", self.colsA, pb[:, 0:128], r=[pk], w=["colsA"])
        stg2 = A.alloc("stg2", [128, 128], F32)
        P.op("dve", "memset", stg2, 0.0, w=["stg2"])
        P.dma("sp", stg2[0:4, :], d["x_q_norm"], r=[], w=["stg2"])
        P.dma("sp", stg2[4:8, :], d["x_k_norm"], r=["stg2"], w=["stg2b"])
        pb, pk = self.bank()
        P.tr(pb[:, 0:128], stg2, self.identF, r=["stg2", "stg2b", "identF"], w=[pk])
        P.copy("dve", self.colsB, pb[:, 0:128], r=[pk], w=["colsB"])
        stg3 = A.alloc("stg3", [128, 128], F32)
        P.op("dve", "memset", stg3, 0.0, w=["stg3z"])
        k3 = []
        def ld3(row, c0, src):
            k = ("stg3", len(k3))
            k3.append(k)
            P.dma("sp", stg3[row:row + 1, c0:c0 + src.shape[1]], src, r=["stg3z"], w=[k])
        for o in range(2):
            for half in range(2):
                ld3(o * 8 + 0, half * 64, d["od_c_q_norm"][o:o + 1, :])
                ld3(o * 8 + 1, half * 64, d["od_c_k_norm"][o:o + 1, :])
            ld3(o * 8 + 2, 0, d["od_c_sub_norm"][o:o + 1, :])
            ld3(o * 8 + 3, 0, d["od_q_lat_norm"][o:o + 1, 0:128])
            ld3(o * 8 + 4, 0, d["od_q_lat_norm"][o:o + 1, 128:256])
            ld3(o * 8 + 5, 0, d["od_kv_lat_norm"][o:o + 1, :])
            ld3(o * 8 + 6, 0, d["od_d_q_norm"][o:o + 1, :])
            ld3(o * 8 + 7, 0, d["od_d_k_norm"][o:o + 1, :])
        pb, pk = self.bank()
        P.tr(pb[:, 0:128], stg3, self.identF, r=k3 + ["identF"], w=[pk])
        P.copy("dve", self.colsC, pb[:, 0:128], r=[pk], w=["colsC"])
        P.dma("sp", self.maskSL, d["c_maskSL"], w=["maskSL"])
        P.dma("sp", self.maskIU, d["c_maskIU"], w=["maskIU"])
        P.dma("sp", self.lsel, d["c_lsel"], w=["lsel"])
        for e in range(2):
            st4 = A.alloc("stg4", [128, 128], F32)
            P.op("dve", "memset", st4, 0.0, w=[("st4z", e)])
            k4 = []
            def ld4(r0, src):
                k = ("stg4", e, len(k4))
                k4.append(k)
                P.dma("sp", st4[r0:r0 + src.shape[0], :], src, r=[("st4z", e)], w=[k])
            ld4(0, d["ev_conv_qkv"][e].rearrange("j (c p) -> (j c) p", p=128))
            ld4(48, d["ev_conv_b_w"][e].rearrange("j (c p) -> (j c) p", p=128))
            ld4(64, d["ev_conv_b_b"][e:e + 1, :].rearrange("o (c p) -> (o c) p", p=128))
            ld4(68, d["ev_gate_a_b"][e:e + 1, :].rearrange("o (c p) -> (o c) p", p=128))
            ld4(72, d["ev_gate_x_b"][e:e + 1, :].rearrange("o (c p) -> (o c) p", p=128))
            ld4(76, d["ev_lru_l"][e:e + 1, :].rearrange("o (c p) -> (o c) p", p=128))
            ld4(80, d["ev_o_norm"][e:e + 1, :])
            pb, pk = self.bank()
            P.tr(pb[:, 0:128], st4, self.identF, r=k4 + ["identF"], w=[pk])
            P.copy("dve", self.colsD[:, e * 128:(e + 1) * 128], pb[:, 0:128], r=[pk], w=[("colsD", e)])
        cst = A.alloc("cst", [128, 128], F32)
        P.dma("sp", cst, d["c_blockones"], w=["cst"])
        P.copy("dve", self.blockB, cst, r=["cst"], w=["blockB"])
        cst2 = A.alloc("cst2", [128, 128], F32)
        P.dma("sp", cst2, d["c_rotT"], w=["cst2"])
        P.copy("dve", self.rotTB, cst2, r=["cst2"], w=["rotTB"])
        P.dma("sp", self.freq, d["c_freq"], w=["freq"])
        pk_i = A.alloc("pk_i", [NT, 128], I32)
        pk_f = self.pkf
        P.dma("sp", pk_i, d["positions"].rearrange("o (t p) -> (o t) p", p=128), w=["pk_i"])
        P.copy("dve", pk_f, pk_i, r=["pk_i"], w=["pk_f"])
        pb, pk = self.bank()
        P.tr(pb[:, 0:NT], pk_f, self.identF[0:NT, 0:NT], r=["pk_f", "identF"], w=[pk])
        P.copy("dve", self.posk, pb[:, 0:NT], r=[pk], w=["posk"])
        P.ts("dve", self.negposk, self.posk, -1.0, None, ALU.mult, r=["posk"], w=["negposk"])
        xin = [A.alloc("xin", [128, D], F32) for _ in range(2)]
        for tt in range(NT):
            xb = xin[tt % 2]
            xk = ("xin", tt % 2)
            P.dma("sp", xb, d["x"][tt * 128:(tt + 1) * 128, :], w=[xk])
            for hb in range(2):
                pb, pk = self.bank()
                for q in range(4):
                    c = hb * 4 + q
                    P.tr(pb[:, q * 128:(q + 1) * 128], xb[:, c * 128:(c + 1) * 128], self.identF,
                         r=[xk, "identF"], w=[pk])
                eng = "dve" if hb == 0 else "act"
                P.copy(eng, self.xT[:, hb * 4:(hb + 1) * 4, tt * 128:(tt + 1) * 128],
                       pb.rearrange("p (a b) -> p a b", a=4),
                       r=[pk], w=[("xT", c, tt // 4) for c in range(hb * 4, hb * 4 + 4)])
        mm_ = [A.alloc("memin", [128, D], F32) for _ in range(2)]
        msq = A.alloc("msq", [128, D], F32)
        mss = A.alloc("mss", [128, 2], F32)
        for mt in range(2):
            P.dma("sp", mm_[mt], d["mem"][mt * 128:(mt + 1) * 128, :], w=[("memin", mt)])
            P.act(msq, mm_[mt], AF.Square, r=[("memin", mt)], w=["msq"], accum_out=mss[:, mt:mt + 1])
            P.act(mss[:, mt:mt + 1], mss[:, mt:mt + 1], AF.Sqrt, r=["msq"], w=[("mss", mt)], scale=1.0 / D, bias=EPS)
            P.op("dve", "reciprocal", mss[:, mt:mt + 1], mss[:, mt:mt + 1], r=[("mss", mt)], w=[("mss", mt)])
            P.ts("dve", mm_[mt], mm_[mt], mss[:, mt:mt + 1], None, ALU.mult, r=[("memin", mt), ("mss", mt)],
                 w=[("memin", mt)])
            for hb in range(2):
                pb, pk = self.bank()
                for q in range(4):
                    c = hb * 4 + q
                    P.tr(pb[:, q * 128:(q + 1) * 128], mm_[mt][:, c * 128:(c + 1) * 128], self.identF,
                         r=[("memin", mt), "identF"], w=[pk])
                P.copy("dve", self.memTn[:, hb * 4:(hb + 1) * 4, mt * 128:(mt + 1) * 128],
                       pb.rearrange("p (a b) -> p a b", a=4), r=[pk], w=["memTn"])
        P.fence()
        A.reset(m)

    def norm_block(self, tb, which, l, hT_out, hkey):
        P = self.P
        sl = slice(tb * TB, (tb + 1) * TB)
        sq, sqk = self.rot("sq")
        P.act(sq, self.xT[:, :, sl], AF.Square, r=[("xT", c, tb) for c in range(DC)], w=[sqk])
        ss, ssk = self.bank()
        for c in range(DC):
            P.mm(ss, self.onesB, sq[:, c, :], start=(c == 0), stop=(c == DC - 1), r=[sqk, "onesB"], w=[ssk])
        rs, rsk = self.rot("rs")
        P.act(rs, ss, AF.Ln, r=[ssk], w=[rsk], scale=1.0 / D, bias=EPS)
        P.act(rs, rs, AF.Exp, r=[rsk], w=[rsk], scale=-0.5)
        for c in range(DC):
            P.stt("dve", hT_out[:, c, :], self.xT[:, c, sl], self.gcol(which, l, c), rs, ALU.mult, ALU.mult,
                  r=[("xT", c, tb), rsk, "colsA"], w=[(hkey, c)])

    def pnorm(self, src, srck, npart, ones_l, inv_n, gain, out, outk):
        P = self.P
        srcks = srck if isinstance(srck, list) else [srck]
        sq, sqk = self.rot("sq1")
        P.act(sq[0:npart, :], src, AF.Square, r=srcks, w=[sqk])
        ss, ssk = self.bank()
        P.mm(ss[0:npart, :], ones_l, sq[0:npart, :], r=[sqk, "onesB"], w=[ssk])
        rs, rsk = self.rot("rs")
        P.act(rs[0:npart, :], ss[0:npart, :], AF.Ln, r=[ssk], w=[rsk], scale=inv_n, bias=EPS)
        P.act(rs[0:npart, :], rs[0:npart, :], AF.Exp, r=[rsk], w=[rsk], scale=-0.5)
        if gain is None:
            P.tt("dve", out, src, rs[0:npart, :], ALU.mult, r=srcks + [rsk], w=[outk])
        elif isinstance(gain, float):
            P.stt("dve", out, src, gain, rs[0:npart, :], ALU.mult, ALU.mult, r=srcks + [rsk], w=[outk])
        else:
            P.stt("dve", out, src, gain, rs[0:npart, :], ALU.mult, ALU.mult, r=srcks + [rsk], w=[outk])

    def pnorm_pipe(self, blocks):
        P = self.P
        prev = None

        def part_b(stt_):
            (src, srck, npart, ones_l, inv_n, gain, out, outk), sq, sqk = stt_
            srcks = srck if isinstance(srck, list) else [srck]
            ss, ssk = self.bank()
            P.mm(ss[0:npart, :], ones_l, sq[0:npart, :], r=[sqk], w=[ssk])
            rs, rsk = self.rot("rs")
            P.act(rs[0:npart, :], ss[0:npart, :], AF.Ln, r=[ssk], w=[rsk], scale=inv_n, bias=EPS)
            P.act(rs[0:npart, :], rs[0:npart, :], AF.Exp, r=[rsk], w=[rsk], scale=-0.5)
            if gain is None:
                P.tt("dve", out, src, rs[0:npart, :], ALU.mult, r=srcks + [rsk], w=[outk])
            else:
                P.stt("dve", out, src, gain, rs[0:npart, :], ALU.mult, ALU.mult, r=srcks + [rsk], w=[outk])

        for b in blocks:
            src, srck, npart = b[0], b[1], b[2]
            srcks = srck if isinstance(srck, list) else [srck]
            sq, sqk = self.rot("sq1")
            P.act(sq[0:npart, :], src, AF.Square, r=srcks, w=[sqk])
            if prev is not None:
                part_b(prev)
            prev = (b, sq, sqk)
        if prev is not None:
            part_b(prev)

    def run_chains(self, makers, K):
        pending = list(makers)
        active = []
        free = list(range(K))
        while pending or active:
            while pending and free:
                j = free.pop(0)
                active.append((pending.pop(0)(j), j))
            nxt = []
            for g, j in active:
                try:
                    next(g)
                    nxt.append((g, j))
                except StopIteration:
                    free.append(j)
            active = nxt

    def recip_act(self, out, outk, src, srck):
        P = self.P
        P.act(out, src, AF.Ln, r=[srck], w=[outk])
        P.act(out, out, AF.Exp, r=[outk], w=[outk], scale=-1.0)

    def mbank(self):
        i = self.misc_banks[self.mi % len(self.misc_banks)]
        self.mi += 1
        return self.ps[i], ("ps", i)

    def sbank(self):
        i = self.score_banks[self.si % len(self.score_banks)]
        self.si += 1
        return self.ps[i], ("ps", i)

    def ffn(self, l):
        P, d, A = self.P, self.d, self.A
        m = A.mark()
        self.rots = {}
        hT = A.alloc("ffn_hT", [128, DC, 1024], BF16)
        act = A.alloc("ffn_act", [128, FC, 1024], BF16)
        self.mkrot("win", 2, [128, DC, 1024], BF16)
        self.mkrot("wout", 2, [128, FC, 128], BF16)
        self.mkrot("sq", 1, [128, DC, TB], BF16)
        self.mkrot("rs", 2, [128, TB], F32)
        self.mkrot("sg", 2, [128, TB], F32)
        w_in_d = d["ffn_w_in"][l].rearrange("(c p) n -> p c n", p=128)
        w_out_d = d["ffn_w_out"][l].rearrange("(f p) n -> p f n", p=128)
        SLW = 512
        nsl = (FF + SLW - 1) // SLW
        for half in range(2):
            if half == 0:
                for j in range(2):
                    self.norm_block(j, 2, l, hT[:, :, j * TB:(j + 1) * TB], ("ffn_hT", j))
            for s in range(nsl):
                c0 = s * SLW
                ncol = min(SLW, FF - c0)
                wb, wk = self.rot("win")
                P.dma("pool", wb[:, :, 0:ncol], w_in_d[:, :, c0:c0 + ncol], w=[(wk, "g")])
                P.dma("pool", wb[:, :, SLW:SLW + ncol], w_in_d[:, :, FF + c0:FF + c0 + ncol], w=[(wk, "u")])
                for fi in range(ncol // 128):
                    f = (c0 // 128) + fi
                    for j in range(2):
                        gps, gk = self.bank()
                        ups, uk = self.bank()
                        for c in range(DC):
                            P.mm(gps, wb[:, c, fi * 128:(fi + 1) * 128], hT[:, c, j * TB:(j + 1) * TB],
                                 start=(c == 0), stop=(c == DC - 1), r=[(wk, "g"), (("ffn_hT", j), c)], w=[gk])
                        for c in range(DC):
                            P.mm(ups, wb[:, c, SLW + fi * 128:SLW + (fi + 1) * 128], hT[:, c, j * TB:(j + 1) * TB],
                                 start=(c == 0), stop=(c == DC - 1), r=[(wk, "u"), (("ffn_hT", j), c)], w=[uk])
                        sg, sgk = self.rot("sg")
                        P.act(sg, gps, AF.Silu, r=[gk], w=[sgk])
                        P.tt("dve", act[:, f, j * TB:(j + 1) * TB], sg, ups, ALU.mult, r=[sgk, uk], w=[("act", f, j)])
            if half == 0:
                for j in range(2):
                    self.norm_block(2 + j, 2, l, hT[:, :, j * TB:(j + 1) * TB], ("ffn_hT", j))
            for dc in range(DC):
                wo, wok = self.rot("wout")
                P.dma("pool", wo, w_out_d[:, :, dc * 128:(dc + 1) * 128], w=[wok])
                for j in range(2):
                    tb = half * 2 + j
                    sl = slice(tb * TB, (tb + 1) * TB)
                    yps, yk = self.bank()
                    for f in range(FC):
                        P.mm(yps, wo[:, f, :], act[:, f, j * TB:(j + 1) * TB], start=(f == 0), stop=(f == FC - 1),
                             r=[wok, ("act", f, j)], w=[yk])
                    P.tt("dve", self.xT[:, dc, sl], self.xT[:, dc, sl], yps, ALU.add, r=[("xT", dc, tb), yk],
                         w=[("xT", dc, tb)])
        P.fence()
        A.reset(m)

    def out_proj_block(self, wo, wok, oT, ok, nk, tb):
        P = self.P
        sl = slice(tb * TB, (tb + 1) * TB)
        for dc in range(DC):
            yps, yk = self.bank()
            for h in range(nk):
                P.mm(yps, wo[:, h, dc * 128:(dc + 1) * 128], oT[:, h, :], start=(h == 0), stop=(h == nk - 1),
                     r=[wok, (ok, h)], w=[yk])
            P.tt("dve", self.xT[:, dc, sl], self.xT[:, dc, sl], yps, ALU.add, r=[("xT", dc, tb), yk],
                 w=[("xT", dc, tb)])

    def xattn(self, l):
        P, d, A = self.P, self.d, self.A
        m = A.mark()
        self.rots = {}
        wq = A.alloc("x_wq", [128, DC, 512], BF16)
        wo = A.alloc("x_wo", [128, 4, D], BF16)
        wkv = A.alloc("x_wkv", [128, DC, D], BF16)
        mh = A.alloc("x_mh", [128, DC, NMEM], BF16)
        self.mkrot("hTb", 2, [128, DC, TB], BF16)
        self.mkrot("sq", 1, [128, DC, TB], BF16)
        self.mkrot("sq1", 3, [128, TB], BF16)
        self.mkrot("rs", 3, [128, TB], F32)
        self.mkrot("qh", 8, [128, TB], BF16)
        self.mkrot("pT", 4, [128, TB], BF16)
        self.mkrot("oTb", 2, [128, 4, TB], BF16)
        self.mkrot("kraw", 2, [128, NMEM], F32)
        P.dma("pool", wq, d["x_wq"][l].rearrange("(c p) n -> p c n", p=128), w=["x_wq"])
        for hh in range(2):
            P.dma("pool", wkv[:, :, hh * 512:(hh + 1) * 512],
                  d["x_wkv"][l].rearrange("(c p) n -> p c n", p=128)[:, :, hh * 512:(hh + 1) * 512], w=[("x_wkv", hh)])
        P.dma("pool", wo, d["x_wo"][l].rearrange("(h p) n -> p h n", p=128), w=["x_wo"])
        prep_state = {}

        def prepN(tb):
            hT, hk = self.rot("hTb")
            self.norm_block(tb, 1, l, hT, hk)
            prep_state[tb] = {"hT": (hT, hk), "q": {}}

        prepN(0)
        for c in range(DC):
            P.ts("dve", mh[:, c, :], self.memTn[:, c, :], self.gcol(3, l, c), None, ALU.mult,
                 r=["memTn", "colsA"], w=[("x_mh", c)])
        mhk = [("x_mh", c) for c in range(DC)]
        for h in range(4):
            kp, kk = self.bank()
            for c in range(DC):
                P.mm(kp[:, 0:NMEM], wkv[:, c, h * 128:(h + 1) * 128], mh[:, c, :], start=(c == 0), stop=(c == DC - 1),
                     r=[("x_wkv", 0), ("x_mh", c)], w=[kk])
            sq, sqk = self.rot("sq1")
            P.act(sq[:, 0:NMEM], kp[:, 0:NMEM], AF.Square, r=[kk], w=[sqk])
            ss, ssk = self.bank()
            P.mm(ss[:, 0:NMEM], self.onesB, sq[:, 0:NMEM], r=[sqk, "onesB"], w=[ssk])
            rs, rsk = self.rot("rs")
            P.act(rs[:, 0:NMEM], ss[:, 0:NMEM], AF.Ln, r=[ssk], w=[rsk], scale=1.0 / 128, bias=EPS)
            P.act(rs[:, 0:NMEM], rs[:, 0:NMEM], AF.Exp, r=[rsk], w=[rsk], scale=-0.5)
            P.stt("dve", self.kTx[:, h, :], kp[:, 0:NMEM], self.colsB[:, 4 + l:5 + l], rs[:, 0:NMEM], ALU.mult, ALU.mult,
                  r=[kk, rsk, "colsB"], w=[("kTx", h)])
        for mt in range(2):
            vp, vk = self.bank()
            for c in range(DC):
                P.mm(vp, mh[:, c, mt * 128:(mt + 1) * 128], wkv[:, c, 512:1024], start=(c == 0), stop=(c == DC - 1),
                     r=[("x_wkv", 1), ("x_mh", c)], w=[vk])
            P.copy("act", self.vx[:, mt, :], vp, r=[vk], w=[("vx", mt)])
        self.rot_banks = [3, 6, 7]
        self.bi = 0
        self.score_banks = [4, 5]
        self.si = 0
        def prepA(tb, h):
            hT, hk = prep_state[tb]["hT"]
            qp, qk = self.bank()
            for c in range(DC):
                P.mm(qp, wq[:, c, h * 128:(h + 1) * 128], hT[:, c, :], start=(c == 0), stop=(c == DC - 1),
                     r=["x_wq", (hk, c)], w=[qk])
            sq, sqk = self.rot("sq1")
            P.act(sq, qp, AF.Square, r=[qk], w=[sqk])
            prep_state[tb]["q"][h] = (qp, qk, sq, sqk)

        def prepB(tb, h):
            qp, qk, sq, sqk = prep_state[tb]["q"][h]
            ss, ssk = self.bank()
            P.mm(ss, self.onesB, sq, r=[sqk], w=[ssk])
            rs, rsk = self.rot("rs")
            P.act(rs, ss, AF.Ln, r=[ssk], w=[rsk], scale=1.0 / 128, bias=EPS)
            P.act(rs, rs, AF.Exp, r=[rsk], w=[rsk], scale=-0.5)
            qh, qhk = self.rot("qh")
            P.stt("dve", qh, qp, self.colsB[:, l:l + 1], rs, ALU.mult, ALU.mult, r=[qk, rsk], w=[qhk])
            prep_state[tb]["q"][h] = (qh, qhk)

        for h in range(4):
            prepA(0, h)
            prepB(0, h)
        for tb in range(NTB):
            oT, ok = self.rot("oTb")
            if tb + 1 < NTB:
                prepN(tb + 1)
            for h in range(4):
                if tb + 1 < NTB:
                    prepA(tb + 1, h)
                qh, qhk = prep_state[tb]["q"][h]
                sps = []
                for mt in range(2):
                    sp_, spk = self.sbank()
                    P.mm(sp_, self.kTx[:, h, mt * 128:(mt + 1) * 128], qh, r=[("kTx", h), qhk], w=[spk])
                    sps.append((sp_, spk))
                if tb + 1 < NTB:
                    prepB(tb + 1, h)
                op_, opk = self.ps[h % 2], ("ps", h % 2)
                lp, lpk = self.ps[2], ("ps", 2)
                for mt in range(2):
                    sp_, spk = sps[mt]
                    pT, pTk = self.rot("pT")
                    P.act(pT, sp_, AF.Exp, r=[spk], w=[pTk], scale=128 ** -0.5)
                    P.mm(op_, self.vx[:, mt, h * 128:(h + 1) * 128], pT, start=(mt == 0), stop=(mt == 1),
                         r=[("vx", mt), pTk], w=[opk])
                    P.mm(lp, self.onesB, pT, start=(mt == 0), stop=(mt == 1), r=[pTk], w=[lpk])
                rs, rsk = self.rot("rs")
                self.recip_act(rs, rsk, lp, lpk)
                P.tt("dve", oT[:, h, :], op_, rs, ALU.mult, r=[opk, rsk], w=[(ok, h)])
            self.out_proj_block(wo, "x_wo", oT, ok, 4, tb)
        self.rot_banks = list(range(8))
        self.bi = 0
        self.score_banks = [2, 3, 4, 5]
        self.si = 0
        P.fence()
        A.reset(m)

    def store(self):
        P, d, A = self.P, self.d, self.A
        m = A.mark()
        yo = [A.alloc("yout", [128, D], F32) for _ in range(2)]
        keys = []
        for tt in range(NT):
            yb = yo[tt % 2]
            for hb in range(2):
                pb, pk = self.bank()
                for q in range(4):
                    c = hb * 4 + q
                    P.tr(pb[:, q * 128:(q + 1) * 128], self.xT[:, c, tt * 128:(tt + 1) * 128], self.identF,
                         r=[("xT", c, tt // 4), "identF"], w=[pk])
                eng = "dve" if hb == 0 else "act"
                P.copy(eng, yb[:, hb * 512:(hb + 1) * 512], pb, r=[pk], w=[("yout", tt % 2, hb)])
            P.dma("sp", d["y"][tt * 128:(tt + 1) * 128, :], yb, r=[("yout", tt % 2, 0), ("yout", tt % 2, 1)],
                  w=[("y", tt)])
            keys.append(("y", tt))
        P.finish("sp", keys)
        A.reset(m)

    def even_mixer(self, e, l):
        P, d, A = self.P, self.d, self.A
        m0 = A.mark()
        self.rots = {}
        self.rot_banks = list(range(8))
        cD = lambda j: self.colsD[:, e * 128 + j:e * 128 + j + 1]
        w_in_d = d["ev_w_in"][e].rearrange("(c p) n -> p c n", p=128)
        hT = A.alloc("e_hT", [128, DC, T], BF16)
        m1 = A.mark()
        self.mkrot("sq", 1, [128, DC, TB], BF16)
        self.mkrot("rs", 2, [128, TB], F32)
        for tb in range(NTB):
            self.norm_block(tb, 0, l, hT[:, :, tb * TB:(tb + 1) * TB], ("e_hT", tb))
        P.fence()
        A.reset(m1)
        self.rots = {}

        def proj(sbw, sk, tb, dst_ps, ppk):
            sl = slice(tb * TB, (tb + 1) * TB)
            for c in range(DC):
                P.mm(dst_ps, sbw[:, c, :], hT[:, c, sl], start=(c == 0), stop=(c == DC - 1),
                     r=[sk, (("e_hT", tb), c)], w=[ppk])

        if not self.cfg.get("skip_e1"):
            self._even_e1(e, l, hT, w_in_d, cD, proj, m1)
        if not self.cfg.get("skip_e2"):
            self._even_e2(e, l, hT, w_in_d, cD, proj, m1)
        P.fence()
        A.reset(m0)

    def _even_e1(self, e, l, hT, w_in_d, cD, proj, m1):
        P, d, A = self.P, self.d, self.A
        ybT = A.alloc("ybT", [128, 4, T], BF16)
        wo = A.alloc("e_wo", [128, 4, D], BF16)
        xraw = A.alloc("xraw", [128, 3 + T], F32)
        xc = A.alloc("xc", [128, T], F32)
        av = A.alloc("av", [128, T], F32)
        hs = A.alloc("hs", [128, T], F32)
        rfull = A.alloc("rfull", [128, T], F32)
        ifull = A.alloc("ifull", [128, T], F32)
        xcb = A.alloc("xcb", [128, T], BF16)
        gts = [A.alloc("gate", [128, 128], BF16) for _ in range(8)]
        c1 = A.alloc("c1", [128, 4], F32)
        self.mkrot("slab", 2, [128, DC, 256], BF16)
        self.mkrot("gg", 2, [128, TB], F32)
        slabs = []
        for cc in range(2):
            sb, sk = self.rot("slab")
            P.dma("pool", sb[:, :, 0:128], w_in_d[:, :, 2056 + cc * 128:2056 + (cc + 1) * 128], w=[(sk, 0)])
            P.dma("pool", sb[:, :, 128:256], w_in_d[:, :, 2568 + cc * 128:2568 + (cc + 1) * 128], w=[(sk, 1)])
            slabs.append((sb, sk))
        lcols = self.colsD[:, e * 128 + 76:e * 128 + 80]
        P.act(c1, lcols, AF.Exp, w=["c1"], scale=-1.0)
        P.act(c1, c1, AF.Ln, r=["c1"], w=["c1"], bias=1.0)
        P.ts("dve", c1, c1, -8.0, None, ALU.mult, r=["c1"], w=["c1"])
        P.op("dve", "memset", xraw[:, 0:3], 0.0, w=["xraw_pad"])
        for gi, g in enumerate(gts):
            P.op("dve", "memset", g, 0.0, w=[("gz", gi)])
        for cc in range(4):
            for which, nm in enumerate(["ev_gate_a_w", "ev_gate_x_w"]):
                gi = which * 4 + cc
                P.dma("pool", gts[gi][0:64, 0:64], d[nm][e, 2 * cc], r=[("gz", gi)], w=[("gate", gi, 0)])
                P.dma("pool", gts[gi][64:128, 64:128], d[nm][e, 2 * cc + 1], r=[("gz", gi)], w=[("gate", gi, 1)])
        P.dma("pool", wo, d["ev_w_out"][e].rearrange("(h p) n -> p h n", p=128)[:, 4:8, :], w=["e_wo"])
        allT = list(range(NTB))
        for cc in range(4):
            if cc < 2:
                sb, sk = slabs[cc]
            else:
                sb, sk = self.rot("slab")
                P.dma("pool", sb[:, :, 0:128], w_in_d[:, :, 2056 + cc * 128:2056 + (cc + 1) * 128], w=[(sk, 0)])
                P.dma("pool", sb[:, :, 128:256], w_in_d[:, :, 2568 + cc * 128:2568 + (cc + 1) * 128], w=[(sk, 1)])
            for tb in range(NTB):
                pp, ppk = self.bank()
                proj(sb[:, :, 0:128], (sk, 0), tb, pp, ppk)
                P.copy("act", xraw[:, 3 + tb * TB:3 + (tb + 1) * TB], pp, r=[ppk], w=[("xraw", tb)])
            xrk = [("xraw", tb) for tb in range(NTB)] + ["xraw_pad"]
            P.ts("dve", xc, xraw[:, 3:3 + T], cD(48 + 3 * 4 + cc), cD(64 + cc), ALU.mult, ALU.add, r=xrk, w=["xc"])
            for j in range(3):
                P.stt("dve", xc, xraw[:, j:j + T], cD(48 + j * 4 + cc), xc, ALU.mult, ALU.add, r=xrk + ["xc"], w=["xc"])
            P.copy("act", xcb, xc, r=["xc"], w=["xcb"])
            for tb in range(NTB):
                sl = slice(tb * TB, (tb + 1) * TB)
                rp, rpk = self.bank()
                P.mm(rp, gts[cc], xcb[:, sl], r=["xcb", ("gate", cc, 0), ("gate", cc, 1)], w=[rpk])
                ip, ipk = self.bank()
                P.mm(ip, gts[4 + cc], xcb[:, sl], r=["xcb", ("gate", 4 + cc, 0), ("gate", 4 + cc, 1)], w=[ipk])
                P.act(rfull[:, sl], rp, AF.Sigmoid, r=[rpk], w=[("rfull", tb)], bias=cD(68 + cc))
                P.act(ifull[:, sl], ip, AF.Sigmoid, r=[ipk], w=[("ifull", tb)], bias=cD(72 + cc))
            rk_all = [("rfull", tb) for tb in allT]
            ik_all = [("ifull", tb) for tb in allT]
            P.act(av, rfull, AF.Exp, r=rk_all + ["c1"], w=["av"], scale=c1[:, cc:cc + 1])
            P.tt("dve", rfull, av, av, ALU.mult, r=["av"], w=rk_all)
            P.act(rfull, rfull, AF.Sqrt, r=rk_all, w=rk_all, scale=-1.0, bias=1.0)
            P.tt("dve", ifull, ifull, xc, ALU.mult, r=ik_all + ["xc"], w=ik_all)
            P.tt("dve", xc, rfull, ifull, ALU.mult, r=rk_all + ik_all, w=["xc"])
            P.op("dve", "tensor_tensor_scan", hs, av, xc, 0.0, ALU.mult, ALU.add, r=["av", "xc"], w=["hs"])
            for tb in range(NTB):
                sl = slice(tb * TB, (tb + 1) * TB)
                gp, gpk = self.bank()
                proj(sb[:, :, 128:256], (sk, 1), tb, gp, gpk)
                gg, ggk = self.rot("gg")
                P.act(gg, gp, AF.Gelu_apprx_tanh, r=[gpk], w=[ggk])
                P.tt("dve", ybT[:, cc, sl], gg, hs[:, sl], ALU.mult, r=[ggk, "hs"], w=[(("ybT", tb), cc)])
        for tb in range(NTB):
            self.out_proj_block(wo, "e_wo", ybT[:, :, tb * TB:(tb + 1) * TB], ("ybT", tb), 4, tb)
        P.fence()
        A.reset(m1)

    def _even_e2(self, e, l, hT, w_in_d, cD, proj, m1):
        P, d, A = self.P, self.d, self.A
        self.rots = {}
        yaT = A.alloc("yaT", [128, 4, T], BF16)
        woa = A.alloc("e_woa", [128, 4, D], BF16)
        P.dma("pool", woa, d["ev_w_out"][e].rearrange("(h p) n -> p h n", p=128)[:, 0:4, :], w=["e_woa"])
        scn = ["beta", "g", "cum", "cl", "kbe", "kdec", "negb", "tmp"]
        sc = {nm: A.alloc("sc_" + nm, [128, NT, 4], F32) for nm in scn}
        scf = {nm: sc[nm].rearrange("p t h -> p (t h)") for nm in scn}
        lastc = A.alloc("lastc", [128, 4, 32], F32)
        bd = A.alloc("bdrow", [1, 8], F32)
        bdb = A.alloc("bdb", [128, 8], F32)
        slab8 = A.alloc("slab8", [128, DC, 8], BF16)
        P.dma("sp", bd[0:1, 0:4], d["ev_dt_bias"][e:e + 1, :], w=[("bd", 0)])
        P.dma("sp", bd[0:1, 4:8], d["ev_a_log"][e:e + 1, :], w=[("bd", 1)])
        pb, pk = self.bank()
        P.mm(pb[:, 0:8], self.onesF[0:1, :], bd, r=[("bd", 0), ("bd", 1)], w=[pk])
        P.copy("dve", bdb, pb[:, 0:8], r=[pk], w=["bdb"])
        P.act(bdb[:, 4:8], bdb[:, 4:8], AF.Exp, r=["bdb"], w=["bdb2"])
        P.ts("dve", bdb[:, 4:8], bdb[:, 4:8], -1.0, None, ALU.mult, r=["bdb2"], w=["bdb2"])
        P.dma("pool", slab8, w_in_d[:, :, 2048:2056], w=["slab8"])
        for tt in range(NT):
            pp, ppk = self.bank()
            for c in range(DC):
                P.mm(pp[:, 0:8], hT[:, c, tt * 128:(tt + 1) * 128], slab8[:, c, :], start=(c == 0), stop=(c == DC - 1),
                     r=["slab8", (("e_hT", tt // 4), c)], w=[ppk])
            P.act(sc["beta"][:, tt, :], pp[:, 0:4], AF.Sigmoid, r=[ppk], w=[("sc_beta", tt)])
            P.tt("dve", sc["tmp"][:, tt, :], pp[:, 4:8], bdb[:, 0:4], ALU.add, r=[ppk, "bdb"], w=[("sc_tmp", tt)])
        alltmp = [("sc_tmp", tt) for tt in range(NT)]
        P.act(scf["tmp"], scf["tmp"], AF.Exp, r=alltmp, w=["sc_tmp2"])
        P.act(scf["tmp"], scf["tmp"], AF.Ln, r=["sc_tmp2"], w=["sc_tmp2"], bias=1.0)
        for tt in range(NT):
            P.tt("dve", sc["g"][:, tt, :], sc["tmp"][:, tt, :], bdb[:, 4:8], ALU.mult, r=["sc_tmp2", "bdb2"], w=[("sc_g", tt)])
        pb, pk = self.bank()
        P.mm(pb[:, 0:64], self.maskIU, scf["g"], r=[("sc_g", tt) for tt in range(NT)], w=[pk])
        P.copy("dve", scf["cum"], pb[:, 0:64], r=[pk], w=["sc_cum"])
        pb, pk = self.bank()
        P.mm(pb[:, 0:64], self.lsel, scf["cum"], r=["sc_cum"], w=[pk])
        P.copy("dve", scf["cl"], pb[:, 0:64], r=[pk], w=["sc_cl"])
        allb = [("sc_beta", tt) for tt in range(NT)]
        P.act(scf["kbe"], scf["cum"], AF.Exp, r=["sc_cum"], w=["sc_kbe"])
        P.tt("dve", scf["kbe"], scf["kbe"], scf["beta"], ALU.mult, r=["sc_kbe"] + allb, w=["sc_kbe"])
        P.tt("dve", scf["kdec"], scf["cl"], scf["cum"], ALU.subtract, r=["sc_cl", "sc_cum"], w=["sc_kdec"])
        P.act(scf["kdec"], scf["kdec"], AF.Exp, r=["sc_kdec"], w=["sc_kdec"])
        P.ts("dve", scf["negb"], scf["beta"], -1.0, None, ALU.mult, r=allb, w=["sc_negb"])
        P.fence()
        slabq = [A.alloc("slabq", [128, DC, 128], BF16) for _ in range(1)]

        def load_head_slabs(hh):
            cols = [hh * 128]
            for i_, c_ in enumerate(cols):
                P.dma("pool", slabq[i_], w_in_d[:, :, c_:c_ + 128], w=[("slabq", i_)], nofence=True)

        load_head_slabs(0)
        mh = A.mark()
        for h in range(4):
            A.reset(mh)
            self.rots = {}
            self.rot_banks = list(range(8))
            qT = A.alloc("qT", [128, T], BF16)
            kT = A.alloc("kT", [128, T], BF16)
            vT = A.alloc("vT", [128, T], BF16)
            zs = A.alloc("zs", [128, T], BF16)
            mA = A.mark()
            raws = [A.alloc("raw", [128, 3 + T], F32) for _ in range(3)]
            cvs = [A.alloc("cv", [128, T], F32) for _ in range(2)]
            self.mkrot("sq1", 2, [128, TB], BF16)
            self.mkrot("rs", 1, [128, TB], F32)
            self.mkrot("slabv", 2, [128, DC, 128], BF16)
            kinds = [(h * 128, qT, "qT"), (512 + h * 128, kT, "kT"), (1024 + h * 128, vT, "vT")]
            for kind, (col0, dst, dn) in enumerate(kinds):
                raw = raws[kind]
                P.op("dve", "memset", raw[:, 0:3], 0.0, w=[("raw_pad", kind)])
                if kind == 0:
                    sb, sk = slabq[0], ("slabq", 0)
                else:
                    sb, sk = self.rot("slabv")
                    P.dma("pool", sb, w_in_d[:, :, col0:col0 + 128], w=[sk])
                for tb in range(NTB):
                    pp, ppk = self.bank()
                    proj(sb, sk, tb, pp, ppk)
                    P.copy("act", raw[:, 3 + tb * TB:3 + (tb + 1) * TB], pp, r=[ppk], w=[("raw", kind, tb)])
            sb, sk = self.rot("slabv")
            P.dma("pool", sb, w_in_d[:, :, 1536 + h * 128:1536 + (h + 1) * 128], w=[sk])
            zps = []
            for tb in range(NTB):
                pp, ppk = self.bank()
                proj(sb, sk, tb, pp, ppk)
                zps.append((pp, ppk))

            def conv(kind, cv, cvk, eng):
                raw = raws[kind]
                cc = kind * 4 + h
                rk_ = [("raw", kind, tb) for tb in range(NTB)] + [("raw_pad", kind)]
                P.ts(eng, cv, raw[:, 3:3 + T], cD(3 * 12 + cc), None, ALU.mult, r=rk_, w=[cvk])
                for j in range(3):
                    P.stt(eng, cv, raw[:, j:j + T], cD(j * 12 + cc), cv, ALU.mult, ALU.add, r=rk_ + [cvk], w=[cvk])

            conv(0, cvs[0], "cv0", "dve")
            conv(1, cvs[1], "cv1", "dve")
            silq = raws[0][:, 3:3 + T]
            silk = raws[1][:, 3:3 + T]
            P.act(silq, cvs[0], AF.Silu, r=["cv0"], w=["silq"] + [("raw", 0, tb) for tb in range(NTB)])
            P.act(silk, cvs[1], AF.Silu, r=["cv1"], w=["silk"] + [("raw", 1, tb) for tb in range(NTB)])
            conv(2, cvs[0], "cv0", "dve")
            for tb in range(NTB):
                pp, ppk = zps[tb]
                P.act(zs[:, tb * TB:(tb + 1) * TB], pp, AF.Silu, r=[ppk], w=[("zs", tb)])
            P.act(vT, cvs[0], AF.Silu, r=["cv0"], w=["vT"])
            blocks = []
            for sil_, silkey, dst, dn, gsc in ((silq, "silq", qT, "qT", 128 ** -0.5), (silk, "silk", kT, "kT", None)):
                for tb in range(NTB):
                    sl = slice(tb * TB, (tb + 1) * TB)
                    blocks.append((sil_[:, sl], silkey, 128, self.onesB, 1.0, gsc, dst[:, sl], (dn, tb)))
            self.pnorm_pipe(blocks)
            P.fence()
            A.reset(mA)
            self.rots = {}
            if h + 1 < 4:
                load_head_slabs(h + 1)
            kdec = A.alloc("ktm_dec", [128, NT, 128], BF16)
            utm = A.alloc("u_tm", [128, NT, 128], BF16)
            wT = A.alloc("wT", [128, T], BF16)
            qdecT = A.alloc("qdecT", [128, T], BF16)
            qkT = A.alloc("qkT", [128, NT, 128], BF16)
            S = A.alloc("S", [128, 128], F32)
            Sb = A.alloc("Sb", [128, 128], BF16)
            KCH = 5
            slots = []
            for j in range(KCH):
                sd = {}
                for nm in ("dg", "t1"):
                    sd[nm] = A.alloc("sl_" + nm, [128, 128], F32)
                sd["t2"] = sd["dg"]
                for nm in ("Dt", "Dm", "E", "p0", "p1", "pT0", "pT1", "kbe", "vtb", "AT0", "AT1"):
                    sd[nm] = A.alloc("sl_" + nm, [128, 128], BF16)
                slots.append(sd)
            self.mkrot("vn", 2, [128, 128], BF16)
            self.mkrot("ot", 2, [128, TB], F32)
            self.mkrot("sq1", 1, [128, TB], BF16)
            self.mkrot("rs", 1, [128, TB], F32)
            iF, iB = self.identF, self.identB
            self.rot_banks = [1, 2, 3, 4, 5, 6, 7]
            self.bi = 0

            def tile_chain(tt, j, h=h):
                sd = slots[j]
                K_ = lambda nm: ("sl", j, nm)
                tsl = slice(tt * 128, (tt + 1) * 128)
                tb = tt // 4
                cumc = sc["cum"][:, tt, h:h + 1]
                kp, kpk = self.bank()
                P.mm(kp[:, 0:128], kT[:, tsl], iB, r=[("kT", tb)], w=[kpk])
                P.ts("dve", sd["kbe"], kp[:, 0:128], sc["kbe"][:, tt, h:h + 1], None, ALU.mult, r=[kpk], w=[K_("kbe")])
                P.act(kdec[:, tt, :], kp[:, 0:128], AF.Copy, r=[kpk], w=[("kdec", tt)], scale=sc["kdec"][:, tt, h:h + 1])
                vp, vpk = self.bank()
                P.mm(vp[:, 0:128], vT[:, tsl], iB, r=["vT"], w=[vpk])
                P.act(sd["vtb"], vp[:, 0:128], AF.Copy, r=[vpk], w=[K_("vtb")], scale=sc["beta"][:, tt, h:h + 1])
                P.ts("pool", sd["dg"], iF, cumc, None, ALU.mult, w=[K_("dg")])
                yield
                bc, bck = self.bank()
                P.mm(bc[:, 0:128], self.onesF, sd["dg"], r=[K_("dg")], w=[bck])
                P.ts("dve", sd["t1"], bc[:, 0:128], cumc, 0.0, ALU.subtract, ALU.min, r=[bck], w=[K_("t1")])
                P.act(sd["Dt"], sd["t1"], AF.Exp, r=[K_("t1")], w=[K_("Dt")])
                P.ts("dve", sd["t2"], bc[:, 0:128], cumc, 0.0, ALU.subtract, ALU.max, r=[bck], w=[K_("t2"), K_("dg")])
                P.act(sd["Dm"], sd["t2"], AF.Exp, r=[K_("t2")], w=[K_("Dm")], scale=-1.0)
                P.act(sd["E"], bc[:, 0:128], AF.Exp, r=[bck], w=[K_("E")])
                P.act(lastc[:, h, 2 * tt:2 * tt + 2], bc[:, 63:128:64], AF.Exp, r=[bck], w=[("lastc", tt)])
                P.tt("pool", qdecT[:, tsl], qT[:, tsl], sd["E"], ALU.mult, r=[("qT", tb), K_("E")], w=[("qdecT", tt)])
                P.tt("pool", sd["Dm"], sd["Dm"], self.maskSL, ALU.mult, r=[K_("Dm")], w=[K_("Dm")])
                P.tt("pool", sd["Dt"], sd["Dt"], self.maskIU, ALU.mult, r=[K_("Dt")], w=[K_("Dt")])
                yield
                KK, KKk = self.bank()
                P.mm(KK[:, 0:128], kT[:, tsl], kT[:, tsl], r=[("kT", tb)], w=[KKk])
                QK, QKk = self.bank()
                P.mm(QK[:, 0:128], kT[:, tsl], qT[:, tsl], r=[("kT", tb), ("qT", tb)], w=[QKk])
                p, pk_ = sd["p0"], K_("p0")
                P.stt("dve", p, KK[:, 0:128], sc["negb"][:, tt, h:h + 1], sd["Dm"], ALU.mult, ALU.mult,
                      r=[KKk, K_("Dm")], w=[pk_])
                P.tt("dve", qkT[:, tt, :], QK[:, 0:128], sd["Dt"], ALU.mult, r=[QKk, K_("Dt")], w=[("qkT", tt)])
                yield
                tp, tpk = self.bank()
                P.mm(tp[:, 0:128], p, iB, r=[pk_], w=[tpk])
                pT, pTk = sd["pT0"], K_("pT0")
                P.copy("act", pT, tp[:, 0:128], r=[tpk], w=[pTk])
                AT, ATk = sd["AT0"], K_("AT0")
                P.tt("dve", AT, tp[:, 0:128], iF, ALU.add, r=[tpk], w=[ATk])
                yield
                def a_update(pcur, pcurk, AT, ATk, s_):
                    an, ank = self.bank()
                    P.mm(an[:, 0:128], iB, AT, start=True, stop=False, r=[ATk], w=[ank])
                    P.mm(an[:, 0:128], pcur, AT, start=False, stop=True, r=[pcurk, ATk], w=[ank])
                    nx_ = "1" if (s_ % 2 == 0) else "0"
                    ATn, ATnk = sd["AT" + nx_], K_("AT" + nx_)
                    P.copy("dve" if s_ % 2 else "act", ATn, an[:, 0:128], r=[ank], w=[ATnk])
                    return ATn, ATnk

                for s_ in range(5):
                    nx = "1" if (s_ % 2 == 0) else "0"
                    p2, p2k = self.bank()
                    P.mm(p2[:, 0:128], pT, p, r=[pTk, pk_], w=[p2k])
                    pn, pnk = sd["p" + nx], K_("p" + nx)
                    P.copy("act", pn, p2[:, 0:128], r=[p2k], w=[pnk])
                    if s_ < 4:
                        p2T, p2Tk = self.bank()
                        P.mm(p2T[:, 0:128], p, pT, r=[pTk, pk_], w=[p2Tk])
                        pTn, pTnk = sd["pT" + nx], K_("pT" + nx)
                        P.copy("act" if s_ % 2 else "dve", pTn, p2T[:, 0:128], r=[p2Tk], w=[pTnk])
                    if s_ > 0:
                        AT, ATk = a_update(p, pk_, AT, ATk, s_ - 1)
                    yield
                    if s_ < 4:
                        p, pk_, pT, pTk = pn, pnk, pTn, pTnk
                    else:
                        p, pk_ = pn, pnk
                AT, ATk = a_update(p, pk_, AT, ATk, 4)
                yield
                wp, wpk = self.bank()
                P.mm(wp[:, 0:128], sd["kbe"], AT, r=[K_("kbe"), ATk], w=[wpk])
                P.copy("act", wT[:, tsl], wp[:, 0:128], r=[wpk], w=[("wT", tt)])
                up, upk = self.bank()
                P.mm(up[:, 0:128], AT, sd["vtb"], r=[K_("vtb"), ATk], w=[upk])
                P.copy("dve", utm[:, tt, :], up[:, 0:128], r=[upk], w=[("utm", tt)])

            def scan_chain(h=h):
                P.op("dve", "memset", S, 0.0, w=["S"])
                P.op("dve", "memset", Sb, 0.0, w=["Sb"])
                ob, obk = self.ps[0], ("ps", 0)
                for ck in range(32):
                    tt, half = ck // 2, ck % 2
                    yield ("need", tt)
                    hp = slice(half * 64, half * 64 + 64)
                    csl = slice(ck * 64, (ck + 1) * 64)
                    col = (ck % 8) * 64
                    a_ps, ak = self.bank()
                    P.mm(a_ps[hp, 0:128], wT[:, csl], Sb, r=[("wT", tt), "Sb"], w=[ak])
                    P.mm(ob[:, col:col + 64], Sb, qdecT[:, csl], start=True, stop=False, r=["Sb", ("qdecT", tt)], w=[obk])
                    vn, vnk = self.rot("vn")
                    P.tt("dve", vn[hp, :], utm[hp, tt, :], a_ps[hp, 0:128], ALU.subtract, r=[("utm", tt), ak], w=[vnk])
                    yield None
                    s_ps, spk = self.bank()
                    P.mm(s_ps[:, 0:128], kdec[hp, tt, :], vn[hp, :], r=[("kdec", tt), vnk], w=[spk])
                    P.mm(ob[:, col:col + 64], vn[hp, :], qkT[hp, tt, half * 64:half * 64 + 64], start=False, stop=True,
                         r=[vnk, ("qkT", tt)], w=[obk])
                    P.stt("dve", Sb, S, lastc[:, h, ck:ck + 1], s_ps[:, 0:128], ALU.mult, ALU.add,
                          r=[spk, "S", ("lastc", tt)], w=["Sb"])
                    P.stt("dve", S, S, lastc[:, h, ck:ck + 1], s_ps[:, 0:128], ALU.mult, ALU.add,
                          r=[spk, "S", ("lastc", tt)], w=["S"])
                    if ck % 8 == 7:
                        tb = ck // 8
                        sl = slice(tb * TB, (tb + 1) * TB)
                        ot, otk = self.rot("ot")
                        P.copy("act", ot, ob, r=[obk], w=[otk])
                        on, onk = self.rot("ot")
                        self.pnorm(ot, otk, 128, self.onesB, 1.0 / 128, cD(80), on, onk)
                        P.tt("dve", yaT[:, h, sl], on, zs[:, sl], ALU.mult, r=[onk, ("zs", tb)], w=[(("yaT", tb), h)])
                    yield None

            pending = list(range(NT))
            active = []
            free_slots = list(range(KCH))
            done_tiles = set()
            scan = scan_chain()
            scan_wait = next(scan)
            scan_done = False
            while pending or active or not scan_done:
                while pending and free_slots:
                    tt = pending.pop(0)
                    j = free_slots.pop(0)
                    active.append((tile_chain(tt, j), tt, j))
                nxt = []
                for g, tt, j in active:
                    try:
                        next(g)
                        nxt.append((g, tt, j))
                    except StopIteration:
                        done_tiles.add(tt)
                        free_slots.append(j)
                active = nxt
                for _ in range(3):
                    if not scan_done:
                        if scan_wait is None or scan_wait[1] in done_tiles:
                            try:
                                scan_wait = next(scan)
                            except StopIteration:
                                scan_done = True
            self.rot_banks = list(range(8))
            self.bi = 0
            P.fence()
        for tb in range(NTB):
            self.out_proj_block(woa, "e_woa", yaT[:, :, tb * TB:(tb + 1) * TB], ("yaT", tb), 4, tb)
        P.fence()
        A.reset(m1)

    def build_posfb(self, qb):
        P = self.P
        pf, pfk = self.rot("posfb")
        pb, pk = self.bank()
        for j in range(4):
            tt = qb * 4 + j
            tm, tmk = self.rot("pkm")
            P.ts("dve", tm, self.pkf, self.identF[0:NT, tt:tt + 1], None, ALU.mult, w=[tmk])
            P.mm(pb[:, j * 128:(j + 1) * 128], self.onesF[0:NT, :], tm, r=[tmk], w=[pk])
        P.copy("act", pf, pb, r=[pk], w=[pfk])
        return pf, pfk

    def rope_tables(self, tb):
        P = self.P
        pf, pfk = self.build_posfb(tb)
        R = slice(64, 96)
        f0 = self.freq[R, 0:1]
        out = {}
        for nm, shift in (("sin", 0.0), ("cos", math.pi / 2)):
            ang, ak = self.rot("ang")
            P.ts("dve", ang[R, :], pf[R, :], f0, shift, ALU.mult, ALU.add, r=[pfk], w=[ak])
            ki, kik = self.rot("angi")
            P.ts("dve", ki[R, :], ang[R, :], 1.0 / (2 * math.pi), None, ALU.mult, r=[ak], w=[kik])
            kf, kfk = self.rot("ang")
            P.copy("dve", kf[R, :], ki[R, :], r=[kik], w=[kfk])
            P.stt("dve", ang[R, :], kf[R, :], -2 * math.pi, ang[R, :], ALU.mult, ALU.add, r=[kfk, ak], w=[ak])
            P.ts("dve", ang[R, :], ang[R, :], math.pi, -math.pi, ALU.min, ALU.max, r=[ak], w=[ak])
            tab, tk = self.rot(nm)
            P.act(tab[R, :], ang[R, :], AF.Sin, r=[ak], w=[tk])
            out[nm] = (tab, tk)
        return out

    def odd_mixer(self, o, l):
        P, d, A = self.P, self.d, self.A
        m0 = A.mark()
        self.rots = {}
        cC = lambda j: self.colsC[:, o * 8 + j:o * 8 + j + 1]
        SC_C = 64 ** -0.5
        SC_D = 96 ** -0.5
        slopes = [2.0 ** (-8.0 * (i + 1) / 4) for i in range(4)]
        lam_init = 0.8 - 0.6 * math.exp(-0.3 * l)
        qlatT = A.alloc("qlatT", [128, 2, T], BF16)
        kvlatT = A.alloc("kvlatT", [128, T], BF16)
        kropeT = A.alloc("kropeT", [128, T], BF16)
        neglam = A.alloc("neglam", [128, 2], F32)
        gsub = A.alloc("gsub", [128, 1], F32)
        lamt = A.alloc("lamt", [1, 4, 64], F32)
        lamp = A.alloc("lamp", [1, 2, 64], F32)
        ls = A.alloc("lams", [1, 8], F32)
        m1 = A.mark()
        qcT = A.alloc("qcT", [128, 4, T], BF16)
        kcT = A.alloc("kcT", [128, 4, T], BF16)
        vc = A.alloc("vc", [128, NT, 512], BF16)
        m2 = A.mark()
        for i, nm in enumerate(["od_lam_q1", "od_lam_k1", "od_lam_q2", "od_lam_k2"]):
            P.dma("sp", lamt[0:1, i, :], d[nm][o:o + 1, :], w=[("lamt", i)])
        P.tt("dve", lamp[0:1, 0, :], lamt[0:1, 0, :], lamt[0:1, 1, :], ALU.mult, r=[("lamt", 0), ("lamt", 1)], w=["lp0"])
        P.tt("dve", lamp[0:1, 1, :], lamt[0:1, 2, :], lamt[0:1, 3, :], ALU.mult, r=[("lamt", 2), ("lamt", 3)], w=["lp1"])
        P.op("dve", "memset", ls, 0.0, w=["ls"])
        P.op("dve", "reduce_sum", ls[0:1, 0:1], lamp[0:1, 0, :], AX.X, r=["lp0", "ls"], w=["ls0"])
        P.op("dve", "reduce_sum", ls[0:1, 1:2], lamp[0:1, 1, :], AX.X, r=["lp1", "ls"], w=["ls1"])
        P.act(ls[0:1, 0:2], ls[0:1, 0:2], AF.Exp, r=["ls0", "ls1"], w=["lse"])
        P.tt("dve", ls[0:1, 2:3], ls[0:1, 1:2], ls[0:1, 0:1], ALU.subtract, r=["lse"], w=["ls2"])
        P.ts("dve", ls[0:1, 4:5], ls[0:1, 2:3], -lam_init, None, ALU.add, r=["ls2"], w=["ls3"])
        pb, pk = self.bank()
        P.mm(pb[:, 0:2], self.onesF[0:1, :], ls[0:1, 4:6], r=["ls3"], w=[pk])
        P.copy("dve", neglam, pb[:, 0:2], r=[pk], w=["neglam"])
        P.ts("dve", gsub, cC(2), 1.0 - lam_init, None, ALU.mult, w=["gsub"])
        hT = A.alloc("o_hT", [128, DC, T], BF16)
        self.mkrot("slab", 1, [128, DC, 512], BF16)
        self.mkrot("sq", 1, [128, DC, TB], BF16)
        self.rots["slab"][0].append(self.rots["sq"][0][0])
        self.mkrot("sq1", 2, [128, TB], BF16)
        self.mkrot("rs", 2, [128, TB], F32)
        w_in_d = d["od_w_in"][o].rearrange("(c p) n -> p c n", p=128)
        for tb in range(NTB):
            self.norm_block(tb, 0, l, hT[:, :, tb * TB:(tb + 1) * TB], ("o_hT", tb))

        def load_slab(c0, n):
            sb, sk = self.rot("slab")
            wk = [sk, ("sq", 0)] if sk == ("slab", 1) else [sk]
            P.dma("pool", sb[:, :, 0:n], w_in_d[:, :, c0:c0 + n], w=wk)
            return sb, sk

        def pn_a(src, srck, ones_l, inv_n, gain, out, outk):
            sq, sqk = self.rot("sq1")
            P.act(sq, src, AF.Square, r=[srck], w=[sqk])
            return (src, srck, ones_l, inv_n, gain, out, outk, sq, sqk)

        def pn_b(stt_):
            src, srck, ones_l, inv_n, gain, out, outk, sq, sqk = stt_
            ss, ssk = self.bank()
            P.mm(ss, ones_l, sq, r=[sqk], w=[ssk])
            rs, rsk = self.rot("rs")
            P.act(rs, ss, AF.Ln, r=[ssk], w=[rsk], scale=inv_n, bias=EPS)
            P.act(rs, rs, AF.Exp, r=[rsk], w=[rsk], scale=-0.5)
            P.stt("dve", out, src, gain, rs, ALU.mult, ALU.mult, r=[srck, rsk], w=[outk])

        slab_q = load_slab(0, 512)
        slab_k = load_slab(512, 512)
        prev = None
        for dst, dname, gj, (sb, sk) in ((qcT, "qcT", 0, slab_q), (kcT, "kcT", 1, slab_k)):
            for ch in range(4):
                for tb in range(NTB):
                    sl = slice(tb * TB, (tb + 1) * TB)
                    pp, ppk = self.bank()
                    for c in range(DC):
                        P.mm(pp, sb[:, c, ch * 128:(ch + 1) * 128], hT[:, c, sl], start=(c == 0), stop=(c == DC - 1),
                             r=[sk, (("o_hT", tb), c)], w=[ppk])
                    cur = pn_a(pp, ppk, self.blockB, 1.0 / 64, cC(gj), dst[:, ch, sl], (dname, ch, tb))
                    if prev is not None:
                        pn_b(prev)
                    prev = cur
        sb, sk = load_slab(1024, 512)
        pn_b(prev)
        for tt in range(NT):
            pp, ppk = self.bank()
            for c in range(DC):
                P.mm(pp, hT[:, c, tt * 128:(tt + 1) * 128], sb[:, c, :], start=(c == 0), stop=(c == DC - 1),
                     r=[sk, (("o_hT", tt // 4), c)], w=[ppk])
            P.copy("act", vc[:, tt, :], pp, r=[ppk], w=[("vc", tt)])
        sb, sk = load_slab(1536, 416)
        for tb in range(NTB):
            sl = slice(tb * TB, (tb + 1) * TB)
            qq = [self.bank(), self.bank()]
            for ci, (pp, ppk) in enumerate(qq):
                for c in range(DC):
                    P.mm(pp, sb[:, c, ci * 128:(ci + 1) * 128], hT[:, c, sl], start=(c == 0), stop=(c == DC - 1),
                         r=[sk, (("o_hT", tb), c)], w=[ppk])
            ss, ssk = self.bank()
            for ci, (pp, ppk) in enumerate(qq):
                sq, sqk = self.rot("sq1")
                P.act(sq, pp, AF.Square, r=[ppk], w=[sqk])
                P.mm(ss, self.onesB, sq, start=(ci == 0), stop=(ci == 1), r=[sqk], w=[ssk])
            rs, rsk = self.rot("rs")
            P.act(rs, ss, AF.Ln, r=[ssk], w=[rsk], scale=1.0 / 256, bias=EPS)
            P.act(rs, rs, AF.Exp, r=[rsk], w=[rsk], scale=-0.5)
            for ci, (pp, ppk) in enumerate(qq):
                P.stt("dve", qlatT[:, ci, sl], pp, cC(3 + ci), rs, ALU.mult, ALU.mult, r=[ppk, rsk],
                      w=[("qlatT", ci, tb)])
            pp, ppk = self.bank()
            for c in range(DC):
                P.mm(pp, sb[:, c, 256:384], hT[:, c, sl], start=(c == 0), stop=(c == DC - 1),
                     r=[sk, (("o_hT", tb), c)], w=[ppk])
            self.pnorm(pp, ppk, 128, self.onesB, 1.0 / 128, cC(5), kvlatT[:, sl], ("kvlatT", tb))
            pp, ppk = self.bank()
            for c in range(DC):
                P.mm(pp[64:96, :], sb[:, c, 384:416], hT[:, c, sl], start=(c == 0), stop=(c == DC - 1),
                     r=[sk, (("o_hT", tb), c)], w=[ppk])
            P.copy("act", kropeT[64:96, sl], pp[64:96, :], r=[ppk], w=[("kropeT", tb)])
        P.fence()
        A.reset(m2)
        self.rots = {}
        wo = A.alloc("o_wo", [128, 4, D], BF16)
        P.dma("pool", wo, d["od_w_out"][o].rearrange("(h p) n -> p h n", p=128)[:, 0:4, :], w=["o_wo"])
        self.mkrot("posfb", 2, [128, TB], F32)
        self.mkrot("pkm", 2, [NT, 128], F32)
        self.mkrot("dist", 3, [128, TB], F32)
        self.mkrot("tS", 2, [128, TB], F32)
        self.mkrot("pT", 4, [128, TB], BF16)
        self.mkrot("yT", 2, [128, 4, TB], BF16)
        self.mkrot("rs", 3, [128, TB], F32)
        self.mkrot("sq1", 2, [128, TB], BF16)
        self.mkrot("ya", 2, [128, TB], F32)
        self.mkrot("yb", 1, [128, TB], F32)
        self.mkrot("lsum", 4, [128, TB], F32)
        self.mkrot("lsb", 2, [128, TB], BF16)
        self.score_banks = [4, 5, 6, 7]
        self.si = 0
        self.rot_banks = [6, 7]
        self.bi = 0
        items = [(qb, h, kt) for qb in range(NTB) for h in range(4) for kt in range(4 * qb + 4)]
        st = {}
        deferred = []

        def defer(n, fn):
            deferred.append([n, fn])

        def tick(flush=False):
            while deferred and (flush or deferred[0][0] <= 0):
                deferred.pop(0)[1]()
            for dd_ in deferred:
                dd_[0] -= 1

        def stageA(it):
            qb, h, kt = it
            if h == 0 and kt == 0:
                st[("pf", qb)] = self.build_posfb(qb)
                st[("yT", qb)] = self.rot("yT")
            pf, pfk = st[("pf", qb)]
            j = kt - 4 * qb
            c0 = max(j, 0) * 128
            dist, dkk = self.rot("dist")
            P.act(dist[:, c0:], pf[:, c0:], AF.Abs, r=[pfk], w=[dkk], bias=self.negposk[:, kt:kt + 1])
            sps = []
            for mi in range(2):
                hs = slice(mi * 64, (mi + 1) * 64)
                sp_, spk = self.sbank()
                P.mm(sp_[:, c0:], kcT[hs, h, kt * 128:(kt + 1) * 128], qcT[hs, h, qb * TB + c0:(qb + 1) * TB],
                     r=[("kcT", h, kt // 4), ("qcT", h, qb)], w=[spk])
                sps.append((sp_, spk))
            st[it] = (dist, dkk, sps)

        def stageB(it):
            qb, h, kt = it
            yT, yk = st[("yT", qb)]
            nkt = 4 * qb + 4
            j = kt - 4 * qb
            c0 = max(j, 0) * 128
            dist, dkk, sps = st.pop(it)
            cur_banks = [spk[1] for _, spk in sps]
            if kt == 0:
                st[("ls", qb, h)] = [self.rot("lsum"), self.rot("lsum")]
            lss = st[("ls", qb, h)]
            pts = []
            for mi in range(2):
                sp_, spk = sps[mi]
                tS, tSk = self.rot("tS")
                P.stt("dve", tS[:, c0:], dist[:, c0:], -slopes[h] / SC_C, sp_[:, c0:], ALU.mult, ALU.add,
                      r=[dkk, spk], w=[tSk])
                pT, pTk = self.rot("pT")
                P.act(pT[:, c0:], tS[:, c0:], AF.Exp, r=[tSk], w=[pTk], scale=SC_C)
                pts.append((pT, pTk))
            for mi in range(2):
                pT, pTk = pts[mi]
                if j >= 0:
                    P.op("dve", "memset", pT[64:128, c0:c0 + 64], 0.0, w=[pTk])
                ab = 2 * (h % 2) + mi
                Ob, Ok = self.ps[ab], ("ps", ab)
                P.mm(Ob[:, c0:], vc[:, kt, h * 128:(h + 1) * 128], pT[:, c0:], start=(kt == 0),
                     stop=(kt == nkt - 1), r=[("vc", kt), pTk], w=[Ok])
                ls, lsk = lss[mi]
                if kt == 0:
                    P.copy("pool", ls, pT, r=[pTk], w=[lsk])
                else:
                    P.tt("pool", ls[:, c0:], ls[:, c0:], pT[:, c0:], ALU.add, r=[pTk, lsk], w=[lsk])
                if kt == nkt - 1:
                    lb, lbk = self.rot("lsb")
                    P.copy("dve", lb, ls, r=[lsk], w=[lbk])
                    lss[mi] = (lb, lbk)
            self.rot_banks = cur_banks
            self.bi = 0
            tick()
            if kt == nkt - 1:
                del st[("ls", qb, h)]
                ya, yak = self.rot("ya")
                yb, ybk = self.rot("yb")
                sq, sqk = self.rot("sq1")

                def step1(lss=lss, ya=ya, yak=yak, yb=yb, ybk=ybk, sq=sq, sqk=sqk, h=h):
                    for mi, (y_, y_k) in enumerate(((ya, yak), (yb, ybk))):
                        ls, lsk = lss[mi]
                        ab = 2 * (h % 2) + mi
                        lp, lpk = self.bank()
                        P.mm(lp, self.onesB, ls, r=[lsk], w=[lpk])
                        r0, r0k = self.rot("rs")
                        self.recip_act(r0, r0k, lp, lpk)
                        P.tt("dve", y_, self.ps[ab], r0, ALU.mult, r=[("ps", ab), r0k], w=[y_k])
                    P.stt("dve", ya, yb, neglam[:, 0:1], ya, ALU.mult, ALU.add, r=[ybk, yak], w=[yak])
                    P.act(sq, ya, AF.Square, r=[yak], w=[sqk])

                def step2(h=h, ya=ya, yak=yak, sq=sq, sqk=sqk, yT=yT, yk=yk):
                    ss, ssk = self.bank()
                    P.mm(ss, self.onesB, sq, r=[sqk], w=[ssk])
                    rs, rsk = self.rot("rs")
                    P.act(rs, ss, AF.Ln, r=[ssk], w=[rsk], scale=1.0 / 128, bias=EPS)
                    P.act(rs, rs, AF.Exp, r=[rsk], w=[rsk], scale=-0.5)
                    P.stt("dve", yT[:, h, :], ya, gsub, rs, ALU.mult, ALU.mult, r=[yak, rsk], w=[(yk, h)])

                step1()
                defer(1, step2)
                if h == 3:
                    defer(3, lambda qb=qb, yT=yT, yk=yk: self.out_proj_block(wo, "o_wo", yT, yk, 4, qb))

        LOOK2 = 1
        for i in range(min(LOOK2, len(items))):
            stageA(items[i])
        for i in range(len(items)):
            if i + LOOK2 < len(items):
                stageA(items[i + LOOK2])
            stageB(items[i])
        tick(flush=True)
        self.rot_banks = list(range(8))
        self.bi = 0
        self.score_banks = [2, 3, 4, 5]
        self.si = 0
        P.fence()
        A.reset(m1)
        self.rots = {}
        qdT = A.alloc("qdT", [128, 4, T], BF16)
        kdT = A.alloc("kdT", [128, 4, T], BF16)
        vd = A.alloc("vd", [128, NT, 512], BF16)
        wuq = A.alloc("wuq", [128, 2, 384], BF16)
        wukv = A.alloc("wukv", [128, 768], BF16)
        P.dma("pool", wuq, d["od_w_uq"][o].rearrange("(c p) n -> p c n", p=128), w=["wuq"])
        P.dma("pool", wukv, d["od_w_ukv"][o], w=["wukv"])
        m3 = A.mark()
        self.mkrot("posfb", 2, [128, TB], F32)
        self.mkrot("pkm", 2, [NT, 128], F32)
        self.mkrot("ang", 3, [128, TB], F32)
        self.mkrot("angi", 1, [128, TB], I32)
        self.mkrot("sin", 2, [128, TB], F32)
        self.mkrot("cos", 2, [128, TB], F32)
        KO3 = 3
        oslots = []
        for j in range(KO3):
            sd = {"sq": A.alloc("o3_sq", [128, TB], BF16)}
            for nm in ("rs", "kraw", "rtmp", "rtmp2"):
                sd[nm] = A.alloc("o3_" + nm, [128, TB], F32)
            oslots.append(sd)
        self.rot_banks = list(range(8))
        self.bi = 0
        ones96 = self.onesB[0:96, 0:96]

        def qk_chain(tb, h, isk, cosb, cosk, sinb, sink, j):
            sd = oslots[j]
            K_ = lambda nm: ("o3s", j, nm)
            sl = slice(tb * TB, (tb + 1) * TB)
            if not isk:
                dst, dkey, gcol = qdT[:, h, sl], ("qdT", h, tb), cC(6)[0:96, :]
                qp, qpk = self.bank()
                for c in range(2):
                    P.mm(qp[0:96, :], wuq[:, c, h * 96:(h + 1) * 96], qlatT[:, c, sl], start=(c == 0), stop=(c == 1),
                         r=["wuq", ("qlatT", c, tb)], w=[qpk])
                src, srck = qp[0:96, :], [qpk]
            else:
                dst, dkey, gcol = kdT[:, h, sl], ("kdT", h, tb), cC(7)[0:96, :]
                kp, kpk = self.bank()
                P.mm(kp[0:64, :], wukv[:, h * 192:h * 192 + 64], kvlatT[:, sl], r=["wukv", ("kvlatT", tb)], w=[kpk])
                kr = sd["kraw"]
                P.copy("act", kr[0:64, :], kp[0:64, :], r=[kpk], w=[K_("kraw0")])
                P.copy("act", kr[64:96, :], kropeT[64:96, sl], r=[("kropeT", tb)], w=[K_("kraw1")])
                src, srck = kr[0:96, :], [K_("kraw0"), K_("kraw1")]
            P.act(sd["sq"][0:96, :], src, AF.Square, r=srck, w=[K_("sq")])
            yield
            ss, ssk = self.bank()
            P.mm(ss[0:96, :], ones96, sd["sq"][0:96, :], r=[K_("sq")], w=[ssk])
            rs = sd["rs"]
            P.act(rs[0:96, :], ss[0:96, :], AF.Ln, r=[ssk], w=[K_("rs")], scale=1.0 / 96, bias=EPS)
            P.act(rs[0:96, :], rs[0:96, :], AF.Exp, r=[K_("rs")], w=[K_("rs")], scale=-0.5)
            P.stt("dve", dst[0:96, :], src, gcol, rs[0:96, :], ALU.mult, ALU.mult, r=srck + [K_("rs")], w=[dkey])
            yield
            rp, rpk = self.bank()
            P.mm(rp[0:96, :], self.rotTB[0:96, 0:96], dst[0:96, :], r=[dkey], w=[rpk])
            t1, t2 = sd["rtmp"], sd["rtmp2"]
            P.tt("dve", t1[64:96, :], dst[64:96, :], cosb[64:96, :], ALU.mult, r=[dkey, cosk], w=[K_("t1")])
            P.tt("dve", t2[64:96, :], rp[64:96, :], sinb[64:96, :], ALU.mult, r=[rpk, sink], w=[K_("t2")])
            yield
            P.tt("dve", dst[64:96, :], t1[64:96, :], t2[64:96, :], ALU.add, r=[K_("t1"), K_("t2")], w=[dkey])

        for tb in range(NTB):
            tabs = self.rope_tables(tb)
            cosb, cosk = tabs["cos"]
            sinb, sink = tabs["sin"]
            makers = []
            for h in range(4):
                for isk in (False, True):
                    makers.append(lambda j, tb=tb, h=h, isk=isk, cosb=cosb, cosk=cosk, sinb=sinb, sink=sink:
                                  qk_chain(tb, h, isk, cosb, cosk, sinb, sink, j))
            self.run_chains(makers, KO3)
        wv = wukv.rearrange("p (h e) -> p h e", h=4)[:, :, 64:192]
        for tt in range(NT):
            pp, ppk = self.bank()
            P.mm(pp.rearrange("p (h e) -> p h e", h=4), kvlatT[:, tt * 128:(tt + 1) * 128], wv,
                 r=["wukv", ("kvlatT", tt // 4)], w=[ppk])
            P.copy("act", vd[:, tt, :], pp, r=[ppk], w=[("vd", tt)])
        P.fence()
        A.reset(m3)
        self.rots = {}
        wo2 = A.alloc("o_wo2", [128, 4, D], BF16)
        P.dma("pool", wo2, d["od_w_out"][o].rearrange("(h p) n -> p h n", p=128)[:, 4:8, :], w=["o_wo2"])
        self.mkrot("rs", 3, [128, TB], F32)
        self.mkrot("pT", 6, [128, TB], BF16)
        self.mkrot("yT", 2, [128, 4, TB], BF16)
        self.rot_banks = [6, 7]
        self.bi = 0
        self.mkrot("lsum", 2, [128, TB], F32)
        for qb in range(NTB):
            yT, yk = self.rot("yT")
            items = [(h, kt) for h in range(4) for kt in range(4 * qb + 4)]
            st = {}

            def stageA(it, qb=qb):
                h, kt = it
                c0 = max(kt - 4 * qb, 0) * 128
                sp_, spk = self.sbank()
                P.mm(sp_[:, c0:], kdT[0:96, h, kt * 128:(kt + 1) * 128], qdT[0:96, h, qb * TB + c0:(qb + 1) * TB],
                     r=[("kdT", h, kt // 4), ("qdT", h, qb)], w=[spk])
                st[it] = (sp_, spk)

            def stageB(it, qb=qb, yT=yT, yk=yk):
                h, kt = it
                nkt = 4 * qb + 4
                j = kt - 4 * qb
                c0 = max(j, 0) * 128
                sp_, spk = st.pop(it)
                if kt == 0:
                    st[("ls", h)] = self.rot("lsum")
                ls, lsk = st[("ls", h)]
                pT, pTk = self.rot("pT")
                P.act(pT[:, c0:], sp_[:, c0:], AF.Exp, r=[spk], w=[pTk], scale=SC_D)
                if j >= 0:
                    P.op("dve", "memset", pT[64:128, c0:c0 + 64], 0.0, w=[pTk])
                Ob, Ok = self.ps[h % 2], ("ps", h % 2)
                P.mm(Ob[:, c0:], vd[:, kt, h * 128:(h + 1) * 128], pT[:, c0:], start=(kt == 0), stop=(kt == nkt - 1),
                     r=[("vd", kt), pTk], w=[Ok])
                le = "dve" if kt % 3 else "pool"
                if kt == 0:
                    P.copy(le, ls, pT, r=[pTk], w=[lsk])
                else:
                    P.tt(le, ls[:, c0:], ls[:, c0:], pT[:, c0:], ALU.add, r=[pTk, lsk], w=[lsk])
                if kt == nkt - 1:
                    lp, lpk = self.bank()
                    P.mm(lp, self.onesF, ls, r=[lsk], w=[lpk])
                    r0, r0k = self.rot("rs")
                    self.recip_act(r0, r0k, lp, lpk)
                    P.tt("dve", yT[:, h, :], Ob, r0, ALU.mult, r=[Ok, r0k], w=[(yk, h)])
                    del st[("ls", h)]

            LOOK = 3
            for i in range(min(LOOK, len(items))):
                stageA(items[i])
            for i in range(len(items)):
                if i + LOOK < len(items):
                    stageA(items[i + LOOK])
                stageB(items[i])
            self.out_proj_block(wo2, "o_wo2", yT, yk, 4, qb)
        self.rot_banks = list(range(8))
        self.bi = 0
        P.fence()
        A.reset(m0)

    def build(self):
        cfg = self.cfg
        self.setup()
        for l in range(cfg.get("layers", DEPTH)):
            if cfg.get("mixer", True):
                if l % 2 == 0:
                    if not cfg.get("skip_even"):
                        self.even_mixer(l // 2, l)
                elif not cfg.get("skip_odd"):
                    self.odd_mixer(l // 2, l)
            if cfg.get("xattn", True):
                self.xattn(l)
            if cfg.get("ffn", True):
                self.ffn(l)
        self.store()
        self.P.emit()


def build_nc(cfg=None):
    nc = bass.Bass("TRN2", target_bir_lowering=False)
    b = Builder(nc, cfg or {})
    b.build()
    return nc, b


def make_in_maps(inputs, n):
    consts = host_consts()
    maps = []
    for i in range(n):
        mp = {
            "x": np.ascontiguousarray(inputs["x"][i]),
            "mem": np.ascontiguousarray(inputs["mem"][i]),
            "positions": np.ascontiguousarray(inputs["positions"][i:i + 1]),
        }
        for name, _ in WEIGHTS:
            mp[name] = np.ascontiguousarray(inputs[name])
        mp.update(consts)
        maps.append(mp)
    return maps


def kernel(**inputs):
    inputs = {k: np.asarray(v) for k, v in inputs.items()}
    n = inputs["x"].shape[0]
    nc, _ = build_nc({})
    in_maps = make_in_maps(inputs, n)
    res = run_bass_kernel_spmd(nc, in_maps, core_ids=list(range(n)))
    return np.stack([np.asarray(r["y"]) for r in res.results], axis=0).astype(np.float32)
```

```python
from contextlib import ExitStack
import math
import numpy as np
import concourse.bass as bass
import concourse.mybir as mybir
from concourse.bass_utils import run_bass_kernel_spmd

F32 = mybir.dt.float32
BF16 = mybir.dt.bfloat16
I32 = mybir.dt.int32
AF = mybir.ActivationFunctionType
ALU = mybir.AluOpType
AX = mybir.AxisListType

EPOCH = 30000
STRICT_SAME_ENGINE = True
NSLOT = 8
ENGS = ("pe", "act", "dve", "pool", "sp")


class Prog:
    def __init__(self, nc):
        self.nc = nc
        self.ops = {e: [] for e in ENGS}
        self.ncomp = {e: 0 for e in ENGS}
        self.ndma = {e: 0 for e in ENGS}
        self.last_w = {}
        self.readers = {}
        self.waited = {e: {} for e in ENGS}
        self.semkeys = set()
        self.sems = {}
        self.last_tok = {}
        self.gdep = None

    def add(self, eng, fn, r=(), w=(), dma=False, nofence=False):
        nowait_only = fn is None
        raw = {}
        oth = {}
        if eng != "pe" and not nowait_only:
            locks = [("pslock", k[1]) for k in r if isinstance(k, tuple) and len(k) == 2 and k[0] == "ps"]
            if locks:
                w = list(w) + locks

        def put(d, tok):
            sk, v, e2, d2 = tok
            if d.get(sk, (0,))[0] < v:
                d[sk] = (v, e2, d2)

        for k in r:
            t = self.last_w.get(k)
            if t is not None:
                put(raw, t)
        for k in w:
            t = self.last_w.get(k)
            if t is not None:
                put(oth, t)
            for sk, (v, e2, d2) in self.readers.get(k, {}).items():
                put(oth, (sk, v, e2, d2))
        if self.gdep is not None and not nofence:
            put(raw, self.gdep)
        if nowait_only:
            semkey, val = None, 0
        elif dma:
            i = self.ndma[eng]
            self.ndma[eng] += 1
            slot, rnd = i % NSLOT, i // NSLOT
            semkey = ("d", eng, slot)
            val = 16 * (rnd + 1)
            if rnd > 0:
                put(raw, (semkey, 16 * rnd, eng, True))
        else:
            i = self.ncomp[eng]
            self.ncomp[eng] += 1
            semkey = ("c", eng, i // EPOCH)
            val = i % EPOCH + 1
        tok = (semkey, val, eng, dma)
        waits = []
        wd = self.waited[eng]
        for d, is_raw in ((raw, True), (oth, False)):
            for sk, (v, e2, d2) in d.items():
                if not d2 and e2 == eng:
                    if eng == "pe" or (not is_raw and not STRICT_SAME_ENGINE):
                        continue
                if wd.get(sk, 0) >= v:
                    continue
                wd[sk] = v
                waits.append((sk, v))
        if nowait_only:
            self.ops[eng].append((None, waits, None, False))
            return None
        self.semkeys.add(semkey)
        self.last_tok[semkey] = tok
        for k in w:
            self.last_w[k] = tok
            self.readers[k] = {}
        for k in r:
            d = self.readers.setdefault(k, {})
            if d.get(semkey, (0,))[0] < val:
                d[semkey] = (val, eng, dma)
        self.ops[eng].append((fn, waits, semkey, dma))
        return tok

    def op(self, eng, name, *args, r=(), w=(), **kw):
        return self.add(eng, lambda e: getattr(e, name)(*args, **kw), r, w)

    def mm(self, out, lhsT, rhs, start=True, stop=True, r=(), w=(), **kw):
        return self.add("pe", lambda e: e.matmul(out, lhsT, rhs, start=start, stop=stop, **kw), r, w)

    def tr(self, out, in_, ident, r=(), w=()):
        return self.add("pe", lambda e: e.transpose(out, in_, ident), r, w)

    def act(self, out, in_, func, r=(), w=(), **kw):
        return self.add("act", lambda e: e.activation(out, in_, func, **kw), r, w)

    def ts(self, eng, out, in0, s1, s2, op0, op1=None, r=(), w=()):
        if op1 is None:
            return self.add(eng, lambda e: e.tensor_scalar(out, in0, s1, None, op0), r, w)
        return self.add(eng, lambda e: e.tensor_scalar(out, in0, s1, s2, op0, op1), r, w)

    def tt(self, eng, out, in0, in1, op, r=(), w=()):
        return self.add(eng, lambda e: e.tensor_tensor(out, in0, in1, op), r, w)

    def stt(self, eng, out, in0, scalar, in1, op0, op1, r=(), w=()):
        return self.add(eng, lambda e: e.scalar_tensor_tensor(out, in0, scalar, in1, op0, op1), r, w)

    def copy(self, eng, out, in_, r=(), w=()):
        if eng == "act":
            return self.add(eng, lambda e: e.copy(out, in_), r, w)
        return self.add(eng, lambda e: e.tensor_copy(out, in_), r, w)

    def dma(self, eng, out, in_, r=(), w=(), nofence=False, **kw):
        return self.add(eng, lambda e: e.dma_start(out, in_, **kw), r, w, dma=True, nofence=nofence)

    def fence(self):
        keys = []
        for sk, tok in list(self.last_tok.items()):
            k = ("_fence", sk)
            self.last_w[k] = tok
            self.readers[k] = {}
            keys.append(k)
        self.gdep = None
        tok = self.add("sp", lambda e: e.nop(), r=keys, w=[])
        self.gdep = tok

    def finish(self, eng, keys):
        self.add(eng, None, r=keys, w=())

    def emit(self):
        nc = self.nc
        with ExitStack() as st:
            for sk in sorted(self.semkeys, key=str):
                self.sems[sk] = st.enter_context(nc.semaphore("s_%s_%s_%d" % sk))
            with nc.Block() as block:
                def mk(name):
                    def body(e):
                        for fn, waits, semkey, dma in self.ops[name]:
                            for sk, v in waits:
                                e.wait_ge(self.sems[sk], v)
                            if fn is None:
                                continue
                            ins = fn(e)
                            ins.then_inc(self.sems[semkey], 16 if dma else 1)
                    return body
                block.tensor(mk("pe"))
                block.scalar(mk("act"))
                block.vector(mk("dve"))
                block.gpsimd(mk("pool"))
                block.sync(mk("sp"))


T = 2048
D = 1024
TB = 512
NTB = 4
NT = 16
DC = 8
FF = 2816
FC = 22
NMEM = 256
EPS = 1e-6
SB_BASE = 16512
SB_END = 229344
DEPTH = 4

WEIGHTS = [
    ("norm_mix", [4, 1024]), ("norm_x", [4, 1024]), ("norm_mem", [4, 1024]),
    ("x_wq", [4, 1024, 512]), ("x_wkv", [4, 1024, 1024]), ("x_q_norm", [4, 128]), ("x_k_norm", [4, 128]),
    ("x_wo", [4, 512, 1024]), ("norm_ffn", [4, 1024]), ("ffn_w_in", [4, 1024, 5632]),
    ("ffn_w_out", [4, 2816, 1024]),
    ("ev_w_in", [2, 1024, 3080]), ("ev_conv_qkv", [2, 4, 1536]), ("ev_a_log", [2, 4]), ("ev_dt_bias", [2, 4]),
    ("ev_o_norm", [2, 128]), ("ev_conv_b_w", [2, 4, 512]), ("ev_conv_b_b", [2, 512]),
    ("ev_gate_a_w", [2, 8, 64, 64]), ("ev_gate_a_b", [2, 512]), ("ev_gate_x_w", [2, 8, 64, 64]),
    ("ev_gate_x_b", [2, 512]), ("ev_lru_l", [2, 512]), ("ev_w_out", [2, 1024, 1024]),
    ("od_w_in", [2, 1024, 1952]), ("od_c_q_norm", [2, 64]), ("od_c_k_norm", [2, 64]),
    ("od_lam_q1", [2, 64]), ("od_lam_k1", [2, 64]), ("od_lam_q2", [2, 64]), ("od_lam_k2", [2, 64]),
    ("od_c_sub_norm", [2, 128]), ("od_q_lat_norm", [2, 256]), ("od_w_uq", [2, 256, 384]),
    ("od_kv_lat_norm", [2, 128]), ("od_w_ukv", [2, 128, 768]), ("od_d_q_norm", [2, 96]),
    ("od_d_k_norm", [2, 96]), ("od_w_out", [2, 1024, 1024]),
]


def host_consts():
    c = {}
    c["c_ident"] = np.eye(128, dtype=np.float32)
    rt = np.zeros((128, 128), np.float32)
    for i in range(16):
        rt[80 + i, 64 + i] = -1.0
        rt[64 + i, 80 + i] = 1.0
    c["c_rotT"] = rt
    fr = np.zeros((128, 2), np.float32)
    for i in range(16):
        f = 10000.0 ** (-(i / 16.0))
        fr[64 + i, 0] = fr[80 + i, 0] = np.float32(f)
    fr[:, 1] = fr[:, 0] / np.float32(2 * np.pi)
    c["c_freq"] = fr
    bo = np.zeros((128, 128), np.float32)
    bo[0:64, 0:64] = 1.0
    bo[64:128, 64:128] = 1.0
    c["c_blockones"] = bo
    ii = np.arange(128)
    same = (ii[:, None] // 64) == (ii[None, :] // 64)
    c["c_maskSL"] = (same & (ii[None, :] < ii[:, None])).astype(np.float32)
    c["c_maskIU"] = (same & (ii[None, :] >= ii[:, None])).astype(np.float32)
    c["c_lsel"] = (ii[:, None] == (ii[None, :] // 64) * 64 + 63).astype(np.float32)
    return c


class Arena:
    def __init__(self, nc, base, end):
        self.nc, self.p, self.end, self.n = nc, base, end, 0

    def alloc(self, name, shape, dtype):
        esz = 4 if dtype in (F32, I32) else 2
        nbytes = int(np.prod(shape[1:])) * esz
        off = (self.p + 31) // 32 * 32
        self.p = off + nbytes
        assert self.p <= self.end, ("SBUF overflow", name, self.p, self.end)
        self.n += 1
        return self.nc.alloc_sbuf_tensor_at("%s_%d" % (name, self.n), list(shape), dtype, offset=off).ap()

    def mark(self):
        return self.p

    def reset(self, m):
        self.p = m


class Builder:
    def __init__(self, nc, cfg):
        self.nc = nc
        self.cfg = cfg
        self.P = Prog(nc)
        self.d = {}
        P = self.P
        d = self.d
        d["x"] = nc.dram_tensor("x", [T, D], F32, kind="ExternalInput").ap()
        d["mem"] = nc.dram_tensor("mem", [NMEM, D], F32, kind="ExternalInput").ap()
        d["positions"] = nc.dram_tensor("positions", [1, T], I32, kind="ExternalInput").ap()
        for name, shp in WEIGHTS:
            d[name] = nc.dram_tensor(name, shp, F32, kind="ExternalInput").ap()
        for name, arr in host_consts().items():
            d[name] = nc.dram_tensor(name, list(arr.shape), F32, kind="ExternalInput").ap()
        d["y"] = nc.dram_tensor("y", [T, D], F32, kind="ExternalOutput").ap()
        self.A = Arena(nc, SB_BASE, SB_END)
        A = self.A
        self.ps = [nc.alloc_psum_tensor("psb%d" % i, [128, 512], F32).ap() for i in range(8)]
        self.bi = 0
        self.rot_banks = list(range(8))
        self.misc_banks = [6, 7]
        self.score_banks = [2, 3, 4, 5]
        self.mi = 0
        self.si = 0
        self.rots = {}
        self.xT = A.alloc("xT", [128, DC, T], F32)
        self.identF = A.alloc("identF", [128, 128], F32)
        self.identB = A.alloc("identB", [128, 128], BF16)
        self.onesB = A.alloc("onesB", [128, 128], BF16)
        self.onesF = A.alloc("onesF", [128, 128], F32)
        self.colsA = A.alloc("colsA", [128, 128], F32)
        self.colsB = A.alloc("colsB", [128, 128], F32)
        self.colsC = A.alloc("colsC", [128, 128], F32)
        self.blockB = A.alloc("blockB", [128, 128], BF16)
        self.rotTB = A.alloc("rotTB", [128, 128], BF16)
        self.freq = A.alloc("freq", [128, 2], F32)
        self.posk = A.alloc("posk", [128, NT], F32)
        self.negposk = A.alloc("negposk", [128, NT], F32)
        self.pkf = A.alloc("pkf", [NT, 128], F32)
        self.colsD = A.alloc("colsD", [128, 256], F32)
        self.maskSL = A.alloc("maskSL", [128, 128], F32)
        self.maskIU = A.alloc("maskIU", [128, 128], F32)
        self.lsel = A.alloc("lsel", [128, 128], F32)
        self.memTn = A.alloc("memTn", [128, DC, NMEM], F32)
        self.kTx = A.alloc("kTx", [128, 4, NMEM], BF16)
        self.vx = A.alloc("vx", [128, 2, 512], BF16)
        self.phase_base = A.mark()

    def bank(self):
        i = self.rot_banks[self.bi % len(self.rot_banks)]
        self.bi += 1
        return self.ps[i], ("ps", i)

    def mkrot(self, name, n, shape, dtype):
        self.rots[name] = [[self.A.alloc(name, shape, dtype) for _ in range(n)], 0]

    def rot(self, name):
        lst, i = self.rots[name]
        self.rots[name][1] = (i + 1) % len(lst)
        return lst[i], (name, i)

    def gcol(self, which, l, c):
        j = which * 32 + l * 8 + c
        return self.colsA[:, j:j + 1]

    def setup(self):
        P, d, A = self.P, self.d, self.A
        P.dma("sp", self.identF, d["c_ident"], w=["identF"])
        P.copy("dve", self.identB, self.identF, r=["identF"], w=["identB"])
        P.op("dve", "memset", self.onesB, 1.0, w=["onesB"])
        P.op("dve", "memset", self.onesF, 1.0, w=["onesF"])
        m = A.mark()
        stg = A.alloc("stg", [128, 128], F32)
        for i, nm in enumerate(["norm_mix", "norm_x", "norm_ffn", "norm_mem"]):
            P.dma("sp", stg[i * 32:(i + 1) * 32, :], d[nm].rearrange("l (c p) -> (l c) p", p=128), w=[("stg", i)])
        pb, pk = self.bank()
        P.tr(pb[:, 0:128], stg, self.identF, r=[("stg", i) for i in range(4)] + ["identF"], w=[pk])
        P.copy("dve", self.colsA, pb[:, 0:128], r=[pk], w=["colsA"])
        stg2 = A.alloc("stg2", [128, 128], F32)
        P.op("dve", "memset", stg2, 0.0, w=["stg2"])
        P.dma("sp", stg2[0:4, :], d["x_q_norm"], r=[], w=["stg2"])
        P.dma("sp", stg2[4:8, :], d["x_k_norm"], r=["stg2"], w=["stg2b"])
        pb, pk = self.bank()
        P.tr(pb[:, 0:128], stg2, self.identF, r=["stg2", "stg2b", "identF"], w=[pk])
        P.copy("dve", self.colsB, pb[:, 0:128], r=[pk], w=["colsB"])
        stg3 = A.alloc("stg3", [128, 128], F32)
        P.op("dve", "memset", stg3, 0.0, w=["stg3z"])
        k3 = []
        def ld3(row, c0, src):
            k = ("stg3", len(k3))
            k3.append(k)
            P.dma("sp", stg3[row:row + 1, c0:c0 + src.shape[1]], src, r=["stg3z"], w=[k])
        for o in range(2):
            for half in range(2):
                ld3(o * 8 + 0, half * 64, d["od_c_q_norm"][o:o + 1, :])
                ld3(o * 8 + 1, half * 64, d["od_c_k_norm"][o:o + 1, :])
            ld3(o * 8 + 2, 0, d["od_c_sub_norm"][o:o + 1, :])
            ld3(o * 8 + 3, 0, d["od_q_lat_norm"][o:o + 1, 0:128])
            ld3(o * 8 + 4, 0, d["od_q_lat_norm"][o:o + 1, 128:256])
            ld3(o * 8 + 5, 0, d["od_kv_lat_norm"][o:o + 1, :])
            ld3(o * 8 + 6, 0, d["od_d_q_norm"][o:o + 1, :])
            ld3(o * 8 + 7, 0, d["od_d_k_norm"][o:o + 1, :])
        pb, pk = self.bank()
        P.tr(pb[:, 0:128], stg3, self.identF, r=k3 + ["identF"], w=[pk])
        P.copy("dve", self.colsC, pb[:, 0:128], r=[pk], w=["colsC"])
        P.dma("sp", self.maskSL, d["c_maskSL"], w=["maskSL"])
        P.dma("sp", self.maskIU, d["c_maskIU"], w=["maskIU"])
        P.dma("sp", self.lsel, d["c_lsel"], w=["lsel"])
        for e in range(2):
            st4 = A.alloc("stg4", [128, 128], F32)
            P.op("dve", "memset", st4, 0.0, w=[("st4z", e)])
            k4 = []
            def ld4(r0, src):
                k = ("stg4", e, len(k4))
                k4.append(k)
                P.dma("sp", st4[r0:r0 + src.shape[0], :], src, r=[("st4z", e)], w=[k])
            ld4(0, d["ev_conv_qkv"][e].rearrange("j (c p) -> (j c) p", p=128))
            ld4(48, d["ev_conv_b_w"][e].rearrange("j (c p) -> (j c) p", p=128))
            ld4(64, d["ev_conv_b_b"][e:e + 1, :].rearrange("o (c p) -> (o c) p", p=128))
            ld4(68, d["ev_gate_a_b"][e:e + 1, :].rearrange("o (c p) -> (o c) p", p=128))
            ld4(72, d["ev_gate_x_b"][e:e + 1, :].rearrange("o (c p) -> (o c) p", p=128))
            ld4(76, d["ev_lru_l"][e:e + 1, :].rearrange("o (c p) -> (o c) p", p=128))
            ld4(80, d["ev_o_norm"][e:e + 1, :])
            pb, pk = self.bank()
            P.tr(pb[:, 0:128], st4, self.identF, r=k4 + ["identF"], w=[pk])
            P.copy("dve", self.colsD[:, e * 128:(e + 1) * 128], pb[:, 0:128], r=[pk], w=[("colsD", e)])
        cst = A.alloc("cst", [128, 128], F32)
        P.dma("sp", cst, d["c_blockones"], w=["cst"])
        P.copy("dve", self.blockB, cst, r=["cst"], w=["blockB"])
        cst2 = A.alloc("cst2", [128, 128], F32)
        P.dma("sp", cst2, d["c_rotT"], w=["cst2"])
        P.copy("dve", self.rotTB, cst2, r=["cst2"], w=["rotTB"])
        P.dma("sp", self.freq, d["c_freq"], w=["freq"])
        pk_i = A.alloc("pk_i", [NT, 128], I32)
        pk_f = self.pkf
        P.dma("sp", pk_i, d["positions"].rearrange("o (t p) -> (o t) p", p=128), w=["pk_i"])
        P.copy("dve", pk_f, pk_i, r=["pk_i"], w=["pk_f"])
        pb, pk = self.bank()
        P.tr(pb[:, 0:NT], pk_f, self.identF[0:NT, 0:NT], r=["pk_f", "identF"], w=[pk])
        P.copy("dve", self.posk, pb[:, 0:NT], r=[pk], w=["posk"])
        P.ts("dve", self.negposk, self.posk, -1.0, None, ALU.mult, r=["posk"], w=["negposk"])
        xin = [A.alloc("xin", [128, D], F32) for _ in range(2)]
        for tt in range(NT):
            xb = xin[tt % 2]
            xk = ("xin", tt % 2)
            P.dma("sp", xb, d["x"][tt * 128:(tt + 1) * 128, :], w=[xk])
            for hb in range(2):
                pb, pk = self.bank()
                for q in range(4):
                    c = hb * 4 + q
                    P.tr(pb[:, q * 128:(q + 1) * 128], xb[:, c * 128:(c + 1) * 128], self.identF,
                         r=[xk, "identF"], w=[pk])
                eng = "dve" if hb == 0 else "act"
                P.copy(eng, self.xT[:, hb * 4:(hb + 1) * 4, tt * 128:(tt + 1) * 128],
                       pb.rearrange("p (a b) -> p a b", a=4),
                       r=[pk], w=[("xT", c, tt // 4) for c in range(hb * 4, hb * 4 + 4)])
        mm_ = [A.alloc("memin", [128, D], F32) for _ in range(2)]
        msq = A.alloc("msq", [128, D], F32)
        mss = A.alloc("mss", [128, 2], F32)
        for mt in range(2):
            P.dma("sp", mm_[mt], d["mem"][mt * 128:(mt + 1) * 128, :], w=[("memin", mt)])
            P.act(msq, mm_[mt], AF.Square, r=[("memin", mt)], w=["msq"], accum_out=mss[:, mt:mt + 1])
            P.act(mss[:, mt:mt + 1], mss[:, mt:mt + 1], AF.Sqrt, r=["msq"], w=[("mss", mt)], scale=1.0 / D, bias=EPS)
            P.op("dve", "reciprocal", mss[:, mt:mt + 1], mss[:, mt:mt + 1], r=[("mss", mt)], w=[("mss", mt)])
            P.ts("dve", mm_[mt], mm_[mt], mss[:, mt:mt + 1], None, ALU.mult, r=[("memin", mt), ("mss", mt)],
                 w=[("memin", mt)])
            for hb in range(2):
                pb, pk = self.bank()
                for q in range(4):
                    c = hb * 4 + q
                    P.tr(pb[:, q * 128:(q + 1) * 128], mm_[mt][:, c * 128:(c + 1) * 128], self.identF,
                         r=[("memin", mt), "identF"], w=[pk])
                P.copy("dve", self.memTn[:, hb * 4:(hb + 1) * 4, mt * 128:(mt + 1) * 128],
                       pb.rearrange("p (a b) -> p a b", a=4), r=[pk], w=["memTn"])
        P.fence()
        A.reset(m)

    def norm_block(self, tb, which, l, hT_out, hkey):
        P = self.P
        sl = slice(tb * TB, (tb + 1) * TB)
        sq, sqk = self.rot("sq")
        P.act(sq, self.xT[:, :, sl], AF.Square, r=[("xT", c, tb) for c in range(DC)], w=[sqk])
        ss, ssk = self.bank()
        for c in range(DC):
            P.mm(ss, self.onesB, sq[:, c, :], start=(c == 0), stop=(c == DC - 1), r=[sqk, "onesB"], w=[ssk])
        rs, rsk = self.rot("rs")
        P.act(rs, ss, AF.Ln, r=[ssk], w=[rsk], scale=1.0 / D, bias=EPS)
        P.act(rs, rs, AF.Exp, r=[rsk], w=[rsk], scale=-0.5)
        for c in range(DC):
            P.stt("dve", hT_out[:, c, :], self.xT[:, c, sl], self.gcol(which, l, c), rs, ALU.mult, ALU.mult,
                  r=[("xT", c, tb), rsk, "colsA"], w=[(hkey, c)])

    def pnorm(self, src, srck, npart, ones_l, inv_n, gain, out, outk):
        P = self.P
        srcks = srck if isinstance(srck, list) else [srck]
        sq, sqk = self.rot("sq1")
        P.act(sq[0:npart, :], src, AF.Square, r=srcks, w=[sqk])
        ss, ssk = self.bank()
        P.mm(ss[0:npart, :], ones_l, sq[0:npart, :], r=[sqk, "onesB"], w=[ssk])
        rs, rsk = self.rot("rs")
        P.act(rs[0:npart, :], ss[0:npart, :], AF.Ln, r=[ssk], w=[rsk], scale=inv_n, bias=EPS)
        P.act(rs[0:npart, :], rs[0:npart, :], AF.Exp, r=[rsk], w=[rsk], scale=-0.5)
        if gain is None:
            P.tt("dve", out, src, rs[0:npart, :], ALU.mult, r=srcks + [rsk], w=[outk])
        elif isinstance(gain, float):
            P.stt("dve", out, src, gain, rs[0:npart, :], ALU.mult, ALU.mult, r=srcks + [rsk], w=[outk])
        else:
            P.stt("dve", out, src, gain, rs[0:npart, :], ALU.mult, ALU.mult, r=srcks + [rsk], w=[outk])

    def pnorm_pipe(self, blocks):
        P = self.P
        prev = None

        def part_b(stt_):
            (src, srck, npart, ones_l, inv_n, gain, out, outk), sq, sqk = stt_
            srcks = srck if isinstance(srck, list) else [srck]
            ss, ssk = self.bank()
            P.mm(ss[0:npart, :], ones_l, sq[0:npart, :], r=[sqk], w=[ssk])
            rs, rsk = self.rot("rs")
            P.act(rs[0:npart, :], ss[0:npart, :], AF.Ln, r=[ssk], w=[rsk], scale=inv_n, bias=EPS)
            P.act(rs[0:npart, :], rs[0:npart, :], AF.Exp, r=[rsk], w=[rsk], scale=-0.5)
            if gain is None:
                P.tt("dve", out, src, rs[0:npart, :], ALU.mult, r=srcks + [rsk], w=[outk])
            else:
                P.stt("dve", out, src, gain, rs[0:npart, :], ALU.mult, ALU.mult, r=srcks + [rsk], w=[outk])

        for b in blocks:
            src, srck, npart = b[0], b[1], b[2]
            srcks = srck if isinstance(srck, list) else [srck]
            sq, sqk = self.rot("sq1")
            P.act(sq[0:npart, :], src, AF.Square, r=srcks, w=[sqk])
            if prev is not None:
                part_b(prev)
            prev = (b, sq, sqk)
        if prev is not None:
            part_b(prev)

    def run_chains(self, makers, K):
        pending = list(makers)
        active = []
        free = list(range(K))
        while pending or active:
            while pending and free:
                j = free.pop(0)
                active.append((pending.pop(0)(j), j))
            nxt = []
            for g, j in active:
                try:
                    next(g)
                    nxt.append((g, j))
                except StopIteration:
                    free.append(j)
            active = nxt

    def recip_act(self, out, outk, src, srck):
        P = self.P
        P.act(out, src, AF.Ln, r=[srck], w=[outk])
        P.act(out, out, AF.Exp, r=[outk], w=[outk], scale=-1.0)

    def mbank(self):
        i = self.misc_banks[self.mi % len(self.misc_banks)]
        self.mi += 1
        return self.ps[i], ("ps", i)

    def sbank(self):
        i = self.score_banks[self.si % len(self.score_banks)]
        self.si += 1
        return self.ps[i], ("ps", i)

    def ffn(self, l):
        P, d, A = self.P, self.d, self.A
        m = A.mark()
        self.rots = {}
        hT = A.alloc("ffn_hT", [128, DC, 1024], BF16)
        act = A.alloc("ffn_act", [128, FC, 1024], BF16)
        self.mkrot("win", 2, [128, DC, 1024], BF16)
        self.mkrot("wout", 2, [128, FC, 128], BF16)
        self.mkrot("sq", 1, [128, DC, TB], BF16)
        self.mkrot("rs", 2, [128, TB], F32)
        self.mkrot("sg", 2, [128, TB], F32)
        w_in_d = d["ffn_w_in"][l].rearrange("(c p) n -> p c n", p=128)
        w_out_d = d["ffn_w_out"][l].rearrange("(f p) n -> p f n", p=128)
        SLW = 512
        nsl = (FF + SLW - 1) // SLW
        for half in range(2):
            if half == 0:
                for j in range(2):
                    self.norm_block(j, 2, l, hT[:, :, j * TB:(j + 1) * TB], ("ffn_hT", j))
            for s in range(nsl):
                c0 = s * SLW
                ncol = min(SLW, FF - c0)
                wb, wk = self.rot("win")
                P.dma("pool", wb[:, :, 0:ncol], w_in_d[:, :, c0:c0 + ncol], w=[(wk, "g")])
                P.dma("pool", wb[:, :, SLW:SLW + ncol], w_in_d[:, :, FF + c0:FF + c0 + ncol], w=[(wk, "u")])
                for fi in range(ncol // 128):
                    f = (c0 // 128) + fi
                    for j in range(2):
                        gps, gk = self.bank()
                        ups, uk = self.bank()
                        for c in range(DC):
                            P.mm(gps, wb[:, c, fi * 128:(fi + 1) * 128], hT[:, c, j * TB:(j + 1) * TB],
                                 start=(c == 0), stop=(c == DC - 1), r=[(wk, "g"), (("ffn_hT", j), c)], w=[gk])
                        for c in range(DC):
                            P.mm(ups, wb[:, c, SLW + fi * 128:SLW + (fi + 1) * 128], hT[:, c, j * TB:(j + 1) * TB],
                                 start=(c == 0), stop=(c == DC - 1), r=[(wk, "u"), (("ffn_hT", j), c)], w=[uk])
                        sg, sgk = self.rot("sg")
                        P.act(sg, gps, AF.Silu, r=[gk], w=[sgk])
                        P.tt("dve", act[:, f, j * TB:(j + 1) * TB], sg, ups, ALU.mult, r=[sgk, uk], w=[("act", f, j)])
            if half == 0:
                for j in range(2):
                    self.norm_block(2 + j, 2, l, hT[:, :, j * TB:(j + 1) * TB], ("ffn_hT", j))
            for dc in range(DC):
                wo, wok = self.rot("wout")
                P.dma("pool", wo, w_out_d[:, :, dc * 128:(dc + 1) * 128], w=[wok])
                for j in range(2):
                    tb = half * 2 + j
                    sl = slice(tb * TB, (tb + 1) * TB)
                    yps, yk = self.bank()
                    for f in range(FC):
                        P.mm(yps, wo[:, f, :], act[:, f, j * TB:(j + 1) * TB], start=(f == 0), stop=(f == FC - 1),
                             r=[wok, ("act", f, j)], w=[yk])
                    P.tt("dve", self.xT[:, dc, sl], self.xT[:, dc, sl], yps, ALU.add, r=[("xT", dc, tb), yk],
                         w=[("xT", dc, tb)])
        P.fence()
        A.reset(m)

    def out_proj_block(self, wo, wok, oT, ok, nk, tb):
        P = self.P
        sl = slice(tb * TB, (tb + 1) * TB)
        for dc in range(DC):
            yps, yk = self.bank()
            for h in range(nk):
                P.mm(yps, wo[:, h, dc * 128:(dc + 1) * 128], oT[:, h, :], start=(h == 0), stop=(h == nk - 1),
                     r=[wok, (ok, h)], w=[yk])
            P.tt("dve", self.xT[:, dc, sl], self.xT[:, dc, sl], yps, ALU.add, r=[("xT", dc, tb), yk],
                 w=[("xT", dc, tb)])

    def xattn(self, l):
        P, d, A = self.P, self.d, self.A
        m = A.mark()
        self.rots = {}
        wq = A.alloc("x_wq", [128, DC, 512], BF16)
        wo = A.alloc("x_wo", [128, 4, D], BF16)
        wkv = A.alloc("x_wkv", [128, DC, D], BF16)
        mh = A.alloc("x_mh", [128, DC, NMEM], BF16)
        self.mkrot("hTb", 2, [128, DC, TB], BF16)
        self.mkrot("sq", 1, [128, DC, TB], BF16)
        self.mkrot("sq1", 3, [128, TB], BF16)
        self.mkrot("rs", 3, [128, TB], F32)
        self.mkrot("qh", 8, [128, TB], BF16)
        self.mkrot("pT", 4, [128, TB], BF16)
        self.mkrot("oTb", 2, [128, 4, TB], BF16)
        self.mkrot("kraw", 2, [128, NMEM], F32)
        P.dma("pool", wq, d["x_wq"][l].rearrange("(c p) n -> p c n", p=128), w=["x_wq"])
        for hh in range(2):
            P.dma("pool", wkv[:, :, hh * 512:(hh + 1) * 512],
                  d["x_wkv"][l].rearrange("(c p) n -> p c n", p=128)[:, :, hh * 512:(hh + 1) * 512], w=[("x_wkv", hh)])
        P.dma("pool", wo, d["x_wo"][l].rearrange("(h p) n -> p h n", p=128), w=["x_wo"])
        prep_state = {}

        def prepN(tb):
            hT, hk = self.rot("hTb")
            self.norm_block(tb, 1, l, hT, hk)
            prep_state[tb] = {"hT": (hT, hk), "q": {}}

        prepN(0)
        for c in range(DC):
            P.ts("dve", mh[:, c, :], self.memTn[:, c, :], self.gcol(3, l, c), None, ALU.mult,
                 r=["memTn", "colsA"], w=[("x_mh", c)])
        mhk = [("x_mh", c) for c in range(DC)]
        for h in range(4):
            kp, kk = self.bank()
            for c in range(DC):
                P.mm(kp[:, 0:NMEM], wkv[:, c, h * 128:(h + 1) * 128], mh[:, c, :], start=(c == 0), stop=(c == DC - 1),
                     r=[("x_wkv", 0), ("x_mh", c)], w=[kk])
            sq, sqk = self.rot("sq1")
            P.act(sq[:, 0:NMEM], kp[:, 0:NMEM], AF.Square, r=[kk], w=[sqk])
            ss, ssk = self.bank()
            P.mm(ss[:, 0:NMEM], self.onesB, sq[:, 0:NMEM], r=[sqk, "onesB"], w=[ssk])
            rs, rsk = self.rot("rs")
            P.act(rs[:, 0:NMEM], ss[:, 0:NMEM], AF.Ln, r=[ssk], w=[rsk], scale=1.0 / 128, bias=EPS)
            P.act(rs[:, 0:NMEM], rs[:, 0:NMEM], AF.Exp, r=[rsk], w=[rsk], scale=-0.5)
            P.stt("dve", self.kTx[:, h, :], kp[:, 0:NMEM], self.colsB[:, 4 + l:5 + l], rs[:, 0:NMEM], ALU.mult, ALU.mult,
                  r=[kk, rsk, "colsB"], w=[("kTx", h)])
        for mt in range(2):
            vp, vk = self.bank()
            for c in range(DC):
                P.mm(vp, mh[:, c, mt * 128:(mt + 1) * 128], wkv[:, c, 512:1024], start=(c == 0), stop=(c == DC - 1),
                     r=[("x_wkv", 1), ("x_mh", c)], w=[vk])
            P.copy("act", self.vx[:, mt, :], vp, r=[vk], w=[("vx", mt)])
        self.rot_banks = [3, 6, 7]
        self.bi = 0
        self.score_banks = [4, 5]
        self.si = 0
        def prepA(tb, h):
            hT, hk = prep_state[tb]["hT"]
            qp, qk = self.bank()
            for c in range(DC):
                P.mm(qp, wq[:, c, h * 128:(h + 1) * 128], hT[:, c, :], start=(c == 0), stop=(c == DC - 1),
                     r=["x_wq", (hk, c)], w=[qk])
            sq, sqk = self.rot("sq1")
            P.act(sq, qp, AF.Square, r=[qk], w=[sqk])
            prep_state[tb]["q"][h] = (qp, qk, sq, sqk)

        def prepB(tb, h):
            qp, qk, sq, sqk = prep_state[tb]["q"][h]
            ss, ssk = self.bank()
            P.mm(ss, self.onesB, sq, r=[sqk], w=[ssk])
            rs, rsk = self.rot("rs")
            P.act(rs, ss, AF.Ln, r=[ssk], w=[rsk], scale=1.0 / 128, bias=EPS)
            P.act(rs, rs, AF.Exp, r=[rsk], w=[rsk], scale=-0.5)
            qh, qhk = self.rot("qh")
            P.stt("dve", qh, qp, self.colsB[:, l:l + 1], rs, ALU.mult, ALU.mult, r=[qk, rsk], w=[qhk])
            prep_state[tb]["q"][h] = (qh, qhk)

        for h in range(4):
            prepA(0, h)
            prepB(0, h)
        for tb in range(NTB):
            oT, ok = self.rot("oTb")
            if tb + 1 < NTB:
                prepN(tb + 1)
            for h in range(4):
                if tb + 1 < NTB:
                    prepA(tb + 1, h)
                qh, qhk = prep_state[tb]["q"][h]
                sps = []
                for mt in range(2):
                    sp_, spk = self.sbank()
                    P.mm(sp_, self.kTx[:, h, mt * 128:(mt + 1) * 128], qh, r=[("kTx", h), qhk], w=[spk])
                    sps.append((sp_, spk))
                if tb + 1 < NTB:
                    prepB(tb + 1, h)
                op_, opk = self.ps[h % 2], ("ps", h % 2)
                lp, lpk = self.ps[2], ("ps", 2)
                for mt in range(2):
                    sp_, spk = sps[mt]
                    pT, pTk = self.rot("pT")
                    P.act(pT, sp_, AF.Exp, r=[spk], w=[pTk], scale=128 ** -0.5)
                    P.mm(op_, self.vx[:, mt, h * 128:(h + 1) * 128], pT, start=(mt == 0), stop=(mt == 1),
                         r=[("vx", mt), pTk], w=[opk])
                    P.mm(lp, self.onesB, pT, start=(mt == 0), stop=(mt == 1), r=[pTk], w=[lpk])
                rs, rsk = self.rot("rs")
                self.recip_act(rs, rsk, lp, lpk)
                P.tt("dve", oT[:, h, :], op_, rs, ALU.mult, r=[opk, rsk], w=[(ok, h)])
            self.out_proj_block(wo, "x_wo", oT, ok, 4, tb)
        self.rot_banks = list(range(8))
        self.bi = 0
        self.score_banks = [2, 3, 4, 5]
        self.si = 0
        P.fence()
        A.reset(m)

    def store(self):
        P, d, A = self.P, self.d, self.A
        m = A.mark()
        yo = [A.alloc("yout", [128, D], F32) for _ in range(2)]
        keys = []
        for tt in range(NT):
            yb = yo[tt % 2]
            for hb in range(2):
                pb, pk = self.bank()
                for q in range(4):
                    c = hb * 4 + q
                    P.tr(pb[:, q * 128:(q + 1) * 128], self.xT[:, c, tt * 128:(tt + 1) * 128], self.identF,
                         r=[("xT", c, tt // 4), "identF"], w=[pk])
                eng = "dve" if hb == 0 else "act"
                P.copy(eng, yb[:, hb * 512:(hb + 1) * 512], pb, r=[pk], w=[("yout", tt % 2, hb)])
            P.dma("sp", d["y"][tt * 128:(tt + 1) * 128, :], yb, r=[("yout", tt % 2, 0), ("yout", tt % 2, 1)],
                  w=[("y", tt)])
            keys.append(("y", tt))
        P.finish("sp", keys)
        A.reset(m)

    def even_mixer(self, e, l):
        P, d, A = self.P, self.d, self.A
        m0 = A.mark()
        self.rots = {}
        self.rot_banks = list(range(8))
        cD = lambda j: self.colsD[:, e * 128 + j:e * 128 + j + 1]
        w_in_d = d["ev_w_in"][e].rearrange("(c p) n -> p c n", p=128)
        hT = A.alloc("e_hT", [128, DC, T], BF16)
        m1 = A.mark()
        self.mkrot("sq", 1, [128, DC, TB], BF16)
        self.mkrot("rs", 2, [128, TB], F32)
        for tb in range(NTB):
            self.norm_block(tb, 0, l, hT[:, :, tb * TB:(tb + 1) * TB], ("e_hT", tb))
        P.fence()
        A.reset(m1)
        self.rots = {}

        def proj(sbw, sk, tb, dst_ps, ppk):
            sl = slice(tb * TB, (tb + 1) * TB)
            for c in range(DC):
                P.mm(dst_ps, sbw[:, c, :], hT[:, c, sl], start=(c == 0), stop=(c == DC - 1),
                     r=[sk, (("e_hT", tb), c)], w=[ppk])

        if not self.cfg.get("skip_e1"):
            self._even_e1(e, l, hT, w_in_d, cD, proj, m1)
        if not self.cfg.get("skip_e2"):
            self._even_e2(e, l, hT, w_in_d, cD, proj, m1)
        P.fence()
        A.reset(m0)

    def _even_e1(self, e, l, hT, w_in_d, cD, proj, m1):
        P, d, A = self.P, self.d, self.A
        ybT = A.alloc("ybT", [128, 4, T], BF16)
        wo = A.alloc("e_wo", [128, 4, D], BF16)
        xraw = A.alloc("xraw", [128, 3 + T], F32)
        xc = A.alloc("xc", [128, T], F32)
        av = A.alloc("av", [128, T], F32)
        hs = A.alloc("hs", [128, T], F32)
        rfull = A.alloc("rfull", [128, T], F32)
        ifull = A.alloc("ifull", [128, T], F32)
        xcb = A.alloc("xcb", [128, T], BF16)
        gts = [A.alloc("gate", [128, 128], BF16) for _ in range(8)]
        c1 = A.alloc("c1", [128, 4], F32)
        self.mkrot("slab", 2, [128, DC, 256], BF16)
        self.mkrot("gg", 2, [128, TB], F32)
        slabs = []
        for cc in range(2):
            sb, sk = self.rot("slab")
            P.dma("pool", sb[:, :, 0:128], w_in_d[:, :, 2056 + cc * 128:2056 + (cc + 1) * 128], w=[(sk, 0)])
            P.dma("pool", sb[:, :, 128:256], w_in_d[:, :, 2568 + cc * 128:2568 + (cc + 1) * 128], w=[(sk, 1)])
            slabs.append((sb, sk))
        lcols = self.colsD[:, e * 128 + 76:e * 128 + 80]
        P.act(c1, lcols, AF.Exp, w=["c1"], scale=-1.0)
        P.act(c1, c1, AF.Ln, r=["c1"], w=["c1"], bias=1.0)
        P.ts("dve", c1, c1, -8.0, None, ALU.mult, r=["c1"], w=["c1"])
        P.op("dve", "memset", xraw[:, 0:3], 0.0, w=["xraw_pad"])
        for gi, g in enumerate(gts):
            P.op("dve", "memset", g, 0.0, w=[("gz", gi)])
        for cc in range(4):
            for which, nm in enumerate(["ev_gate_a_w", "ev_gate_x_w"]):
                gi = which * 4 + cc
                P.dma("pool", gts[gi][0:64, 0:64], d[nm][e, 2 * cc], r=[("gz", gi)], w=[("gate", gi, 0)])
                P.dma("pool", gts[gi][64:128, 64:128], d[nm][e, 2 * cc + 1], r=[("gz", gi)], w=[("gate", gi, 1)])
        P.dma("pool", wo, d["ev_w_out"][e].rearrange("(h p) n -> p h n", p=128)[:, 4:8, :], w=["e_wo"])
        allT = list(range(NTB))
        for cc in range(4):
            if cc < 2:
                sb, sk = slabs[cc]
            else:
                sb, sk = self.rot("slab")
                P.dma("pool", sb[:, :, 0:128], w_in_d[:, :, 2056 + cc * 128:2056 + (cc + 1) * 128], w=[(sk, 0)])
                P.dma("pool", sb[:, :, 128:256], w_in_d[:, :, 2568 + cc * 128:2568 + (cc + 1) * 128], w=[(sk, 1)])
            for tb in range(NTB):
                pp, ppk = self.bank()
                proj(sb[:, :, 0:128], (sk, 0), tb, pp, ppk)
                P.copy("act", xraw[:, 3 + tb * TB:3 + (tb + 1) * TB], pp, r=[ppk], w=[("xraw", tb)])
            xrk = [("xraw", tb) for tb in range(NTB)] + ["xraw_pad"]
            P.ts("dve", xc, xraw[:, 3:3 + T], cD(48 + 3 * 4 + cc), cD(64 + cc), ALU.mult, ALU.add, r=xrk, w=["xc"])
            for j in range(3):
                P.stt("dve", xc, xraw[:, j:j + T], cD(48 + j * 4 + cc), xc, ALU.mult, ALU.add, r=xrk + ["xc"], w=["xc"])
            P.copy("act", xcb, xc, r=["xc"], w=["xcb"])
            for tb in range(NTB):
                sl = slice(tb * TB, (tb + 1) * TB)
                rp, rpk = self.bank()
                P.mm(rp, gts[cc], xcb[:, sl], r=["xcb", ("gate", cc, 0), ("gate", cc, 1)], w=[rpk])
                ip, ipk = self.bank()
                P.mm(ip, gts[4 + cc], xcb[:, sl], r=["xcb", ("gate", 4 + cc, 0), ("gate", 4 + cc, 1)], w=[ipk])
                P.act(rfull[:, sl], rp, AF.Sigmoid, r=[rpk], w=[("rfull", tb)], bias=cD(68 + cc))
                P.act(ifull[:, sl], ip, AF.Sigmoid, r=[ipk], w=[("ifull", tb)], bias=cD(72 + cc))
            rk_all = [("rfull", tb) for tb in allT]
            ik_all = [("ifull", tb) for tb in allT]
            P.act(av, rfull, AF.Exp, r=rk_all + ["c1"], w=["av"], scale=c1[:, cc:cc + 1])
            P.tt("dve", rfull, av, av, ALU.mult, r=["av"], w=rk_all)
            P.act(rfull, rfull, AF.Sqrt, r=rk_all, w=rk_all, scale=-1.0, bias=1.0)
            P.tt("dve", ifull, ifull, xc, ALU.mult, r=ik_all + ["xc"], w=ik_all)
            P.tt("dve", xc, rfull, ifull, ALU.mult, r=rk_all + ik_all, w=["xc"])
            P.op("dve", "tensor_tensor_scan", hs, av, xc, 0.0, ALU.mult, ALU.add, r=["av", "xc"], w=["hs"])
            for tb in range(NTB):
                sl = slice(tb * TB, (tb + 1) * TB)
                gp, gpk = self.bank()
                proj(sb[:, :, 128:256], (sk, 1), tb, gp, gpk)
                gg, ggk = self.rot("gg")
                P.act(gg, gp, AF.Gelu_apprx_tanh, r=[gpk], w=[ggk])
                P.tt("dve", ybT[:, cc, sl], gg, hs[:, sl], ALU.mult, r=[ggk, "hs"], w=[(("ybT", tb), cc)])
        for tb in range(NTB):
            self.out_proj_block(wo, "e_wo", ybT[:, :, tb * TB:(tb + 1) * TB], ("ybT", tb), 4, tb)
        P.fence()
        A.reset(m1)

    def _even_e2(self, e, l, hT, w_in_d, cD, proj, m1):
        P, d, A = self.P, self.d, self.A
        self.rots = {}
        yaT = A.alloc("yaT", [128, 4, T], BF16)
        woa = A.alloc("e_woa", [128, 4, D], BF16)
        P.dma("pool", woa, d["ev_w_out"][e].rearrange("(h p) n -> p h n", p=128)[:, 0:4, :], w=["e_woa"])
        scn = ["beta", "g", "cum", "cl", "kbe", "kdec", "negb", "tmp"]
        sc = {nm: A.alloc("sc_" + nm, [128, NT, 4], F32) for nm in scn}
        scf = {nm: sc[nm].rearrange("p t h -> p (t h)") for nm in scn}
        lastc = A.alloc("lastc", [128, 4, 32], F32)
        bd = A.alloc("bdrow", [1, 8], F32)
        bdb = A.alloc("bdb", [128, 8], F32)
        slab8 = A.alloc("slab8", [128, DC, 8], BF16)
        P.dma("sp", bd[0:1, 0:4], d["ev_dt_bias"][e:e + 1, :], w=[("bd", 0)])
        P.dma("sp", bd[0:1, 4:8], d["ev_a_log"][e:e + 1, :], w=[("bd", 1)])
        pb, pk = self.bank()
        P.mm(pb[:, 0:8], self.onesF[0:1, :], bd, r=[("bd", 0), ("bd", 1)], w=[pk])
        P.copy("dve", bdb, pb[:, 0:8], r=[pk], w=["bdb"])
        P.act(bdb[:, 4:8], bdb[:, 4:8], AF.Exp, r=["bdb"], w=["bdb2"])
        P.ts("dve", bdb[:, 4:8], bdb[:, 4:8], -1.0, None, ALU.mult, r=["bdb2"], w=["bdb2"])
        P.dma("pool", slab8, w_in_d[:, :, 2048:2056], w=["slab8"])
        for tt in range(NT):
            pp, ppk = self.bank()
            for c in range(DC):
                P.mm(pp[:, 0:8], hT[:, c, tt * 128:(tt + 1) * 128], slab8[:, c, :], start=(c == 0), stop=(c == DC - 1),
                     r=["slab8", (("e_hT", tt // 4), c)], w=[ppk])
            P.act(sc["beta"][:, tt, :], pp[:, 0:4], AF.Sigmoid, r=[ppk], w=[("sc_beta", tt)])
            P.tt("dve", sc["tmp"][:, tt, :], pp[:, 4:8], bdb[:, 0:4], ALU.add, r=[ppk, "bdb"], w=[("sc_tmp", tt)])
        alltmp = [("sc_tmp", tt) for tt in range(NT)]
        P.act(scf["tmp"], scf["tmp"], AF.Exp, r=alltmp, w=["sc_tmp2"])
        P.act(scf["tmp"], scf["tmp"], AF.Ln, r=["sc_tmp2"], w=["sc_tmp2"], bias=1.0)
        for tt in range(NT):
            P.tt("dve", sc["g"][:, tt, :], sc["tmp"][:, tt, :], bdb[:, 4:8], ALU.mult, r=["sc_tmp2", "bdb2"], w=[("sc_g", tt)])
        pb, pk = self.bank()
        P.mm(pb[:, 0:64], self.maskIU, scf["g"], r=[("sc_g", tt) for tt in range(NT)], w=[pk])
        P.copy("dve", scf["cum"], pb[:, 0:64], r=[pk], w=["sc_cum"])
        pb, pk = self.bank()
        P.mm(pb[:, 0:64], self.lsel, scf["cum"], r=["sc_cum"], w=[pk])
        P.copy("dve", scf["cl"], pb[:, 0:64], r=[pk], w=["sc_cl"])
        allb = [("sc_beta", tt) for tt in range(NT)]
        P.act(scf["kbe"], scf["cum"], AF.Exp, r=["sc_cum"], w=["sc_kbe"])
        P.tt("dve", scf["kbe"], scf["kbe"], scf["beta"], ALU.mult, r=["sc_kbe"] + allb, w=["sc_kbe"])
        P.tt("dve", scf["kdec"], scf["cl"], scf["cum"], ALU.subtract, r=["sc_cl", "sc_cum"], w=["sc_kdec"])
        P.act(scf["kdec"], scf["kdec"], AF.Exp, r=["sc_kdec"], w=["sc_kdec"])
        P.ts("dve", scf["negb"], scf["beta"], -1.0, None, ALU.mult, r=allb, w=["sc_negb"])
        P.fence()
        slabq = [A.alloc("slabq", [128, DC, 128], BF16) for _ in range(1)]

        def load_head_slabs(hh):
            cols = [hh * 128]
            for i_, c_ in enumerate(cols):
                P.dma("pool", slabq[i_], w_in_d[:, :, c_:c_ + 128], w=[("slabq", i_)], nofence=True)

        load_head_slabs(0)
        mh = A.mark()
        for h in range(4):
            A.reset(mh)
            self.rots = {}
            self.rot_banks = list(range(8))
            qT = A.alloc("qT", [128, T], BF16)
            kT = A.alloc("kT", [128, T], BF16)
            vT = A.alloc("vT", [128, T], BF16)
            zs = A.alloc("zs", [128, T], BF16)
            mA = A.mark()
            raws = [A.alloc("raw", [128, 3 + T], F32) for _ in range(3)]
            cvs = [A.alloc("cv", [128, T], F32) for _ in range(2)]
            self.mkrot("sq1", 2, [128, TB], BF16)
            self.mkrot("rs", 1, [128, TB], F32)
            self.mkrot("slabv", 2, [128, DC, 128], BF16)
            kinds = [(h * 128, qT, "qT"), (512 + h * 128, kT, "kT"), (1024 + h * 128, vT, "vT")]
            for kind, (col0, dst, dn) in enumerate(kinds):
                raw = raws[kind]
                P.op("dve", "memset", raw[:, 0:3], 0.0, w=[("raw_pad", kind)])
                if kind == 0:
                    sb, sk = slabq[0], ("slabq", 0)
                else:
                    sb, sk = self.rot("slabv")
                    P.dma("pool", sb, w_in_d[:, :, col0:col0 + 128], w=[sk])
                for tb in range(NTB):
                    pp, ppk = self.bank()
                    proj(sb, sk, tb, pp, ppk)
                    P.copy("act", raw[:, 3 + tb * TB:3 + (tb + 1) * TB], pp, r=[ppk], w=[("raw", kind, tb)])
            sb, sk = self.rot("slabv")
            P.dma("pool", sb, w_in_d[:, :, 1536 + h * 128:1536 + (h + 1) * 128], w=[sk])
            zps = []
            for tb in range(NTB):
                pp, ppk = self.bank()
                proj(sb, sk, tb, pp, ppk)
                zps.append((pp, ppk))

            def conv(kind, cv, cvk, eng):
                raw = raws[kind]
                cc = kind * 4 + h
                rk_ = [("raw", kind, tb) for tb in range(NTB)] + [("raw_pad", kind)]
                P.ts(eng, cv, raw[:, 3:3 + T], cD(3 * 12 + cc), None, ALU.mult, r=rk_, w=[cvk])
                for j in range(3):
                    P.stt(eng, cv, raw[:, j:j + T], cD(j * 12 + cc), cv, ALU.mult, ALU.add, r=rk_ + [cvk], w=[cvk])

            conv(0, cvs[0], "cv0", "dve")
            conv(1, cvs[1], "cv1", "dve")
            silq = raws[0][:, 3:3 + T]
            silk = raws[1][:, 3:3 + T]
            P.act(silq, cvs[0], AF.Silu, r=["cv0"], w=["silq"] + [("raw", 0, tb) for tb in range(NTB)])
            P.act(silk, cvs[1], AF.Silu, r=["cv1"], w=["silk"] + [("raw", 1, tb) for tb in range(NTB)])
            conv(2, cvs[0], "cv0", "dve")
            for tb in range(NTB):
                pp, ppk = zps[tb]
                P.act(zs[:, tb * TB:(tb + 1) * TB], pp, AF.Silu, r=[ppk], w=[("zs", tb)])
            P.act(vT, cvs[0], AF.Silu, r=["cv0"], w=["vT"])
            blocks = []
            for sil_, silkey, dst, dn, gsc in ((silq, "silq", qT, "qT", 128 ** -0.5), (silk, "silk", kT, "kT", None)):
                for tb in range(NTB):
                    sl = slice(tb * TB, (tb + 1) * TB)
                    blocks.append((sil_[:, sl], silkey, 128, self.onesB, 1.0, gsc, dst[:, sl], (dn, tb)))
            self.pnorm_pipe(blocks)
            P.fence()
            A.reset(mA)
            self.rots = {}
            if h + 1 < 4:
                load_head_slabs(h + 1)
            kdec = A.alloc("ktm_dec", [128, NT, 128], BF16)
            utm = A.alloc("u_tm", [128, NT, 128], BF16)
            wT = A.alloc("wT", [128, T], BF16)
            qdecT = A.alloc("qdecT", [128, T], BF16)
            qkT = A.alloc("qkT", [128, NT, 128], BF16)
            S = A.alloc("S", [128, 128], F32)
            Sb = A.alloc("Sb", [128, 128], BF16)
            KCH = 5
            slots = []
            for j in range(KCH):
                sd = {}
                for nm in ("dg", "t1"):
                    sd[nm] = A.alloc("sl_" + nm, [128, 128], F32)
                sd["t2"] = sd["dg"]
                for nm in ("Dt", "Dm", "E", "p0", "p1", "pT0", "pT1", "kbe", "vtb", "AT0", "AT1"):
                    sd[nm] = A.alloc("sl_" + nm, [128, 128], BF16)
                slots.append(sd)
            self.mkrot("vn", 2, [128, 128], BF16)
            self.mkrot("ot", 2, [128, TB], F32)
            self.mkrot("sq1", 1, [128, TB], BF16)
            self.mkrot("rs", 1, [128, TB], F32)
            iF, iB = self.identF, self.identB
            self.rot_banks = [1, 2, 3, 4, 5, 6, 7]
            self.bi = 0

            def tile_chain(tt, j, h=h):
                sd = slots[j]
                K_ = lambda nm: ("sl", j, nm)
                tsl = slice(tt * 128, (tt + 1) * 128)
                tb = tt // 4
                cumc = sc["cum"][:, tt, h:h + 1]
                kp, kpk = self.bank()
                P.mm(kp[:, 0:128], kT[:, tsl], iB, r=[("kT", tb)], w=[kpk])
                P.ts("dve", sd["kbe"], kp[:, 0:128], sc["kbe"][:, tt, h:h + 1], None, ALU.mult, r=[kpk], w=[K_("kbe")])
                P.act(kdec[:, tt, :], kp[:, 0:128], AF.Copy, r=[kpk], w=[("kdec", tt)], scale=sc["kdec"][:, tt, h:h + 1])
                vp, vpk = self.bank()
                P.mm(vp[:, 0:128], vT[:, tsl], iB, r=["vT"], w=[vpk])
                P.act(sd["vtb"], vp[:, 0:128], AF.Copy, r=[vpk], w=[K_("vtb")], scale=sc["beta"][:, tt, h:h + 1])
                P.ts("pool", sd["dg"], iF, cumc, None, ALU.mult, w=[K_("dg")])
                yield
                bc, bck = self.bank()
                P.mm(bc[:, 0:128], self.onesF, sd["dg"], r=[K_("dg")], w=[bck])
                P.ts("dve", sd["t1"], bc[:, 0:128], cumc, 0.0, ALU.subtract, ALU.min, r=[bck], w=[K_("t1")])
                P.act(sd["Dt"], sd["t1"], AF.Exp, r=[K_("t1")], w=[K_("Dt")])
                P.ts("dve", sd["t2"], bc[:, 0:128], cumc, 0.0, ALU.subtract, ALU.max, r=[bck], w=[K_("t2"), K_("dg")])
                P.act(sd["Dm"], sd["t2"], AF.Exp, r=[K_("t2")], w=[K_("Dm")], scale=-1.0)
                P.act(sd["E"], bc[:, 0:128], AF.Exp, r=[bck], w=[K_("E")])
                P.act(lastc[:, h, 2 * tt:2 * tt + 2], bc[:, 63:128:64], AF.Exp, r=[bck], w=[("lastc", tt)])
                P.tt("pool", qdecT[:, tsl], qT[:, tsl], sd["E"], ALU.mult, r=[("qT", tb), K_("E")], w=[("qdecT", tt)])
                P.tt("pool", sd["Dm"], sd["Dm"], self.maskSL, ALU.mult, r=[K_("Dm")], w=[K_("Dm")])
                P.tt("pool", sd["Dt"], sd["Dt"], self.maskIU, ALU.mult, r=[K_("Dt")], w=[K_("Dt")])
                yield
                KK, KKk = self.bank()
                P.mm(KK[:, 0:128], kT[:, tsl], kT[:, tsl], r=[("kT", tb)], w=[KKk])
                QK, QKk = self.bank()
                P.mm(QK[:, 0:128], kT[:, tsl], qT[:, tsl], r=[("kT", tb), ("qT", tb)], w=[QKk])
                p, pk_ = sd["p0"], K_("p0")
                P.stt("dve", p, KK[:, 0:128], sc["negb"][:, tt, h:h + 1], sd["Dm"], ALU.mult, ALU.mult,
                      r=[KKk, K_("Dm")], w=[pk_])
                P.tt("dve", qkT[:, tt, :], QK[:, 0:128], sd["Dt"], ALU.mult, r=[QKk, K_("Dt")], w=[("qkT", tt)])
                yield
                tp, tpk = self.bank()
                P.mm(tp[:, 0:128], p, iB, r=[pk_], w=[tpk])
                pT, pTk = sd["pT0"], K_("pT0")
                P.copy("act", pT, tp[:, 0:128], r=[tpk], w=[pTk])
                AT, ATk = sd["AT0"], K_("AT0")
                P.tt("dve", AT, tp[:, 0:128], iF, ALU.add, r=[tpk], w=[ATk])
                yield
                def a_update(pcur, pcurk, AT, ATk, s_):
                    an, ank = self.bank()
                    P.mm(an[:, 0:128], iB, AT, start=True, stop=False, r=[ATk], w=[ank])
                    P.mm(an[:, 0:128], pcur, AT, start=False, stop=True, r=[pcurk, ATk], w=[ank])
                    nx_ = "1" if (s_ % 2 == 0) else "0"
                    ATn, ATnk = sd["AT" + nx_], K_("AT" + nx_)
                    P.copy("dve" if s_ % 2 else "act", ATn, an[:, 0:128], r=[ank], w=[ATnk])
                    return ATn, ATnk

                for s_ in range(5):
                    nx = "1" if (s_ % 2 == 0) else "0"
                    p2, p2k = self.bank()
                    P.mm(p2[:, 0:128], pT, p, r=[pTk, pk_], w=[p2k])
                    pn, pnk = sd["p" + nx], K_("p" + nx)
                    P.copy("act", pn, p2[:, 0:128], r=[p2k], w=[pnk])
                    if s_ < 4:
                        p2T, p2Tk = self.bank()
                        P.mm(p2T[:, 0:128], p, pT, r=[pTk, pk_], w=[p2Tk])
                        pTn, pTnk = sd["pT" + nx], K_("pT" + nx)
                        P.copy("act" if s_ % 2 else "dve", pTn, p2T[:, 0:128], r=[p2Tk], w=[pTnk])
                    if s_ > 0:
                        AT, ATk = a_update(p, pk_, AT, ATk, s_ - 1)
                    yield
                    if s_ < 4:
                        p, pk_, pT, pTk = pn, pnk, pTn, pTnk
                    else:
                        p, pk_ = pn, pnk
                AT, ATk = a_update(p, pk_, AT, ATk, 4)
                yield
                wp, wpk = self.bank()
                P.mm(wp[:, 0:128], sd["kbe"], AT, r=[K_("kbe"), ATk], w=[wpk])
                P.copy("act", wT[:, tsl], wp[:, 0:128], r=[wpk], w=[("wT", tt)])
                up, upk = self.bank()
                P.mm(up[:, 0:128], AT, sd["vtb"], r=[K_("vtb"), ATk], w=[upk])
                P.copy("dve", utm[:, tt, :], up[:, 0:128], r=[upk], w=[("utm", tt)])

            def scan_chain(h=h):
                P.op("dve", "memset", S, 0.0, w=["S"])
                P.op("dve", "memset", Sb, 0.0, w=["Sb"])
                ob, obk = self.ps[0], ("ps", 0)
                for ck in range(32):
                    tt, half = ck // 2, ck % 2
                    yield ("need", tt)
                    hp = slice(half * 64, half * 64 + 64)
                    csl = slice(ck * 64, (ck + 1) * 64)
                    col = (ck % 8) * 64
                    a_ps, ak = self.bank()
                    P.mm(a_ps[hp, 0:128], wT[:, csl], Sb, r=[("wT", tt), "Sb"], w=[ak])
                    P.mm(ob[:, col:col + 64], Sb, qdecT[:, csl], start=True, stop=False, r=["Sb", ("qdecT", tt)], w=[obk])
                    vn, vnk = self.rot("vn")
                    P.tt("dve", vn[hp, :], utm[hp, tt, :], a_ps[hp, 0:128], ALU.subtract, r=[("utm", tt), ak], w=[vnk])
                    yield None
                    s_ps, spk = self.bank()
                    P.mm(s_ps[:, 0:128], kdec[hp, tt, :], vn[hp, :], r=[("kdec", tt), vnk], w=[spk])
                    P.mm(ob[:, col:col + 64], vn[hp, :], qkT[hp, tt, half * 64:half * 64 + 64], start=False, stop=True,
                         r=[vnk, ("qkT", tt)], w=[obk])
                    P.stt("dve", Sb, S, lastc[:, h, ck:ck + 1], s_ps[:, 0:128], ALU.mult, ALU.add,
                          r=[spk, "S", ("lastc", tt)], w=["Sb"])
                    P.stt("dve", S, S, lastc[:, h, ck:ck + 1], s_ps[:, 0:128], ALU.mult, ALU.add,
                          r=[spk, "S", ("lastc", tt)], w=["S"])
                    if ck % 8 == 7:
                        tb = ck // 8
                        sl = slice(tb * TB, (tb + 1) * TB)
                        ot, otk = self.rot("ot")
                        P.copy("act", ot, ob, r=[obk], w=[otk])
                        on, onk = self.rot("ot")
                        self.pnorm(ot, otk, 128, self.onesB, 1.0 / 128, cD(80), on, onk)
                        P.tt("dve", yaT[:, h, sl], on, zs[:, sl], ALU.mult, r=[onk, ("zs", tb)], w=[(("yaT", tb), h)])
                    yield None

            pending = list(range(NT))
            active = []
            free_slots = list(range(KCH))
            done_tiles = set()
            scan = scan_chain()
            scan_wait = next(scan)
            scan_done = False
            while pending or active or not scan_done:
                while pending and free_slots:
                    tt = pending.pop(0)
                    j = free_slots.pop(0)
                    active.append((tile_chain(tt, j), tt, j))
                nxt = []
                for g, tt, j in active:
                    try:
                        next(g)
                        nxt.append((g, tt, j))
                    except StopIteration:
                        done_tiles.add(tt)
                        free_slots.append(j)
                active = nxt
                for _ in range(3):
                    if not scan_done:
                        if scan_wait is None or scan_wait[1] in done_tiles:
                            try:
                                scan_wait = next(scan)
                            except StopIteration:
                                scan_done = True
            self.rot_banks = list(range(8))
            self.bi = 0
            P.fence()
        for tb in range(NTB):
            self.out_proj_block(woa, "e_woa", yaT[:, :, tb * TB:(tb + 1) * TB], ("yaT", tb), 4, tb)
        P.fence()
        A.reset(m1)

    def build_posfb(self, qb):
        P = self.P
        pf, pfk = self.rot("posfb")
        pb, pk = self.bank()
        for j in range(4):
            tt = qb * 4 + j
            tm, tmk = self.rot("pkm")
            P.ts("dve", tm, self.pkf, self.identF[0:NT, tt:tt + 1], None, ALU.mult, w=[tmk])
            P.mm(pb[:, j * 128:(j + 1) * 128], self.onesF[0:NT, :], tm, r=[tmk], w=[pk])
        P.copy("act", pf, pb, r=[pk], w=[pfk])
        return pf, pfk

    def rope_tables(self, tb):
        P = self.P
        pf, pfk = self.build_posfb(tb)
        R = slice(64, 96)
        f0 = self.freq[R, 0:1]
        out = {}
        for nm, shift in (("sin", 0.0), ("cos", math.pi / 2)):
            ang, ak = self.rot("ang")
            P.ts("dve", ang[R, :], pf[R, :], f0, shift, ALU.mult, ALU.add, r=[pfk], w=[ak])
            ki, kik = self.rot("angi")
            P.ts("dve", ki[R, :], ang[R, :], 1.0 / (2 * math.pi), None, ALU.mult, r=[ak], w=[kik])
            kf, kfk = self.rot("ang")
            P.copy("dve", kf[R, :], ki[R, :], r=[kik], w=[kfk])
            P.stt("dve", ang[R, :], kf[R, :], -2 * math.pi, ang[R, :], ALU.mult, ALU.add, r=[kfk, ak], w=[ak])
            P.ts("dve", ang[R, :], ang[R, :], math.pi, -math.pi, ALU.min, ALU.max, r=[ak], w=[ak])
            tab, tk = self.rot(nm)
            P.act(tab[R, :], ang[R, :], AF.Sin, r=[ak], w=[tk])
            out[nm] = (tab, tk)
        return out

    def odd_mixer(self, o, l):
        P, d, A = self.P, self.d, self.A
        m0 = A.mark()
        self.rots = {}
        cC = lambda j: self.colsC[:, o * 8 + j:o * 8 + j + 1]
        SC_C = 64 ** -0.5
        SC_D = 96 ** -0.5
        slopes = [2.0 ** (-8.0 * (i + 1) / 4) for i in range(4)]
        lam_init = 0.8 - 0.6 * math.exp(-0.3 * l)
        qlatT = A.alloc("qlatT", [128, 2, T], BF16)
        kvlatT = A.alloc("kvlatT", [128, T], BF16)
        kropeT = A.alloc("kropeT", [128, T], BF16)
        neglam = A.alloc("neglam", [128, 2], F32)
        gsub = A.alloc("gsub", [128, 1], F32)
        lamt = A.alloc("lamt", [1, 4, 64], F32)
        lamp = A.alloc("lamp", [1, 2, 64], F32)
        ls = A.alloc("lams", [1, 8], F32)
        m1 = A.mark()
        qcT = A.alloc("qcT", [128, 4, T], BF16)
        kcT = A.alloc("kcT", [128, 4, T], BF16)
        vc = A.alloc("vc", [128, NT, 512], BF16)
        m2 = A.mark()
        for i, nm in enumerate(["od_lam_q1", "od_lam_k1", "od_lam_q2", "od_lam_k2"]):
            P.dma("sp", lamt[0:1, i, :], d[nm][o:o + 1, :], w=[("lamt", i)])
        P.tt("dve", lamp[0:1, 0, :], lamt[0:1, 0, :], lamt[0:1, 1, :], ALU.mult, r=[("lamt", 0), ("lamt", 1)], w=["lp0"])
        P.tt("dve", lamp[0:1, 1, :], lamt[0:1, 2, :], lamt[0:1, 3, :], ALU.mult, r=[("lamt", 2), ("lamt", 3)], w=["lp1"])
        P.op("dve", "memset", ls, 0.0, w=["ls"])
        P.op("dve", "reduce_sum", ls[0:1, 0:1], lamp[0:1, 0, :], AX.X, r=["lp0", "ls"], w=["ls0"])
        P.op("dve", "reduce_sum", ls[0:1, 1:2], lamp[0:1, 1, :], AX.X, r=["lp1", "ls"], w=["ls1"])
        P.act(ls[0:1, 0:2], ls[0:1, 0:2], AF.Exp, r=["ls0", "ls1"], w=["lse"])
        P.tt("dve", ls[0:1, 2:3], ls[0:1, 1:2], ls[0:1, 0:1], ALU.subtract, r=["lse"], w=["ls2"])
        P.ts("dve", ls[0:1, 4:5], ls[0:1, 2:3], -lam_init, None, ALU.add, r=["ls2"], w=["ls3"])
        pb, pk = self.bank()
        P.mm(pb[:, 0:2], self.onesF[0:1, :], ls[0:1, 4:6], r=["ls3"], w=[pk])
        P.copy("dve", neglam, pb[:, 0:2], r=[pk], w=["neglam"])
        P.ts("dve", gsub, cC(2), 1.0 - lam_init, None, ALU.mult, w=["gsub"])
        hT = A.alloc("o_hT", [128, DC, T], BF16)
        self.mkrot("slab", 1, [128, DC, 512], BF16)
        self.mkrot("sq", 1, [128, DC, TB], BF16)
        self.rots["slab"][0].append(self.rots["sq"][0][0])
        self.mkrot("sq1", 2, [128, TB], BF16)
        self.mkrot("rs", 2, [128, TB], F32)
        w_in_d = d["od_w_in"][o].rearrange("(c p) n -> p c n", p=128)
        for tb in range(NTB):
            self.norm_block(tb, 0, l, hT[:, :, tb * TB:(tb + 1) * TB], ("o_hT", tb))

        def load_slab(c0, n):
            sb, sk = self.rot("slab")
            wk = [sk, ("sq", 0)] if sk == ("slab", 1) else [sk]
            P.dma("pool", sb[:, :, 0:n], w_in_d[:, :, c0:c0 + n], w=wk)
            return sb, sk

        def pn_a(src, srck, ones_l, inv_n, gain, out, outk):
            sq, sqk = self.rot("sq1")
            P.act(sq, src, AF.Square, r=[srck], w=[sqk])
            return (src, srck, ones_l, inv_n, gain, out, outk, sq, sqk)

        def pn_b(stt_):
            src, srck, ones_l, inv_n, gain, out, outk, sq, sqk = stt_
            ss, ssk = self.bank()
            P.mm(ss, ones_l, sq, r=[sqk], w=[ssk])
            rs, rsk = self.rot("rs")
            P.act(rs, ss, AF.Ln, r=[ssk], w=[rsk], scale=inv_n, bias=EPS)
            P.act(rs, rs, AF.Exp, r=[rsk], w=[rsk], scale=-0.5)
            P.stt("dve", out, src, gain, rs, ALU.mult, ALU.mult, r=[srck, rsk], w=[outk])

        slab_q = load_slab(0, 512)
        slab_k = load_slab(512, 512)
        prev = None
        for dst, dname, gj, (sb, sk) in ((qcT, "qcT", 0, slab_q), (kcT, "kcT", 1, slab_k)):
            for ch in range(4):
                for tb in range(NTB):
                    sl = slice(tb * TB, (tb + 1) * TB)
                    pp, ppk = self.bank()
                    for c in range(DC):
                        P.mm(pp, sb[:, c, ch * 128:(ch + 1) * 128], hT[:, c, sl], start=(c == 0), stop=(c == DC - 1),
                             r=[sk, (("o_hT", tb), c)], w=[ppk])
                    cur = pn_a(pp, ppk, self.blockB, 1.0 / 64, cC(gj), dst[:, ch, sl], (dname, ch, tb))
                    if prev is not None:
                        pn_b(prev)
                    prev = cur
        sb, sk = load_slab(1024, 512)
        pn_b(prev)
        for tt in range(NT):
            pp, ppk = self.bank()
            for c in range(DC):
                P.mm(pp, hT[:, c, tt * 128:(tt + 1) * 128], sb[:, c, :], start=(c == 0), stop=(c == DC - 1),
                     r=[sk, (("o_hT", tt // 4), c)], w=[ppk])
            P.copy("act", vc[:, tt, :], pp, r=[ppk], w=[("vc", tt)])
        sb, sk = load_slab(1536, 416)
        for tb in range(NTB):
            sl = slice(tb * TB, (tb + 1) * TB)
            qq = [self.bank(), self.bank()]
            for ci, (pp, ppk) in enumerate(qq):
                for c in range(DC):
                    P.mm(pp, sb[:, c, ci * 128:(ci + 1) * 128], hT[:, c, sl], start=(c == 0), stop=(c == DC - 1),
                         r=[sk, (("o_hT", tb), c)], w=[ppk])
            ss, ssk = self.bank()
            for ci, (pp, ppk) in enumerate(qq):
                sq, sqk = self.rot("sq1")
                P.act(sq, pp, AF.Square, r=[ppk], w=[sqk])
                P.mm(ss, self.onesB, sq, start=(ci == 0), stop=(ci == 1), r=[sqk], w=[ssk])
            rs, rsk = self.rot("rs")
            P.act(rs, ss, AF.Ln, r=[ssk], w=[rsk], scale=1.0 / 256, bias=EPS)
            P.act(rs, rs, AF.Exp, r=[rsk], w=[rsk], scale=-0.5)
            for ci, (pp, ppk) in enumerate(qq):
                P.stt("dve", qlatT[:, ci, sl], pp, cC(3 + ci), rs, ALU.mult, ALU.mult, r=[ppk, rsk],
                      w=[("qlatT", ci, tb)])
            pp, ppk = self.bank()
            for c in range(DC):
                P.mm(pp, sb[:, c, 256:384], hT[:, c, sl], start=(c == 0), stop=(c == DC - 1),
                     r=[sk, (("o_hT", tb), c)], w=[ppk])
            self.pnorm(pp, ppk, 128, self.onesB, 1.0 / 128, cC(5), kvlatT[:, sl], ("kvlatT", tb))
            pp, ppk = self.bank()
            for c in range(DC):
                P.mm(pp[64:96, :], sb[:, c, 384:416], hT[:, c, sl], start=(c == 0), stop=(c == DC - 1),
                     r=[sk, (("o_hT", tb), c)], w=[ppk])
            P.copy("act", kropeT[64:96, sl], pp[64:96, :], r=[ppk], w=[("kropeT", tb)])
        P.fence()
        A.reset(m2)
        self.rots = {}
        wo = A.alloc("o_wo", [128, 4, D], BF16)
        P.dma("pool", wo, d["od_w_out"][o].rearrange("(h p) n -> p h n", p=128)[:, 0:4, :], w=["o_wo"])
        self.mkrot("posfb", 2, [128, TB], F32)
        self.mkrot("pkm", 2, [NT, 128], F32)
        self.mkrot("dist", 3, [128, TB], F32)
        self.mkrot("tS", 2, [128, TB], F32)
        self.mkrot("pT", 4, [128, TB], BF16)
        self.mkrot("yT", 2, [128, 4, TB], BF16)
        self.mkrot("rs", 3, [128, TB], F32)
        self.mkrot("sq1", 2, [128, TB], BF16)
        self.mkrot("ya", 2, [128, TB], F32)
        self.mkrot("yb", 1, [128, TB], F32)
        self.mkrot("lsum", 4, [128, TB], F32)
        self.mkrot("lsb", 2, [128, TB], BF16)
        self.score_banks = [4, 5, 6, 7]
        self.si = 0
        self.rot_banks = [6, 7]
        self.bi = 0
        items = [(qb, h, kt) for qb in range(NTB) for h in range(4) for kt in range(4 * qb + 4)]
        st = {}
        deferred = []

        def defer(n, fn):
            deferred.append([n, fn])

        def tick(flush=False):
            while deferred and (flush or deferred[0][0] <= 0):
                deferred.pop(0)[1]()
            for dd_ in deferred:
                dd_[0] -= 1

        def stageA(it):
            qb, h, kt = it
            if h == 0 and kt == 0:
                st[("pf", qb)] = self.build_posfb(qb)
                st[("yT", qb)] = self.rot("yT")
            pf, pfk = st[("pf", qb)]
            j = kt - 4 * qb
            c0 = max(j, 0) * 128
            dist, dkk = self.rot("dist")
            P.act(dist[:, c0:], pf[:, c0:], AF.Abs, r=[pfk], w=[dkk], bias=self.negposk[:, kt:kt + 1])
            sps = []
            for mi in range(2):
                hs = slice(mi * 64, (mi + 1) * 64)
                sp_, spk = self.sbank()
                P.mm(sp_[:, c0:], kcT[hs, h, kt * 128:(kt + 1) * 128], qcT[hs, h, qb * TB + c0:(qb + 1) * TB],
                     r=[("kcT", h, kt // 4), ("qcT", h, qb)], w=[spk])
                sps.append((sp_, spk))
            st[it] = (dist, dkk, sps)

        def stageB(it):
            qb, h, kt = it
            yT, yk = st[("yT", qb)]
            nkt = 4 * qb + 4
            j = kt - 4 * qb
            c0 = max(j, 0) * 128
            dist, dkk, sps = st.pop(it)
            cur_banks = [spk[1] for _, spk in sps]
            if kt == 0:
                st[("ls", qb, h)] = [self.rot("lsum"), self.rot("lsum")]
            lss = st[("ls", qb, h)]
            pts = []
            for mi in range(2):
                sp_, spk = sps[mi]
                tS, tSk = self.rot("tS")
                P.stt("dve", tS[:, c0:], dist[:, c0:], -slopes[h] / SC_C, sp_[:, c0:], ALU.mult, ALU.add,
                      r=[dkk, spk], w=[tSk])
                pT, pTk = self.rot("pT")
                P.act(pT[:, c0:], tS[:, c0:], AF.Exp, r=[tSk], w=[pTk], scale=SC_C)
                pts.append((pT, pTk))
            for mi in range(2):
                pT, pTk = pts[mi]
                if j >= 0:
                    P.op("dve", "memset", pT[64:128, c0:c0 + 64], 0.0, w=[pTk])
                ab = mi
                Ob, Ok = self.ps[ab], ("ps", ab)
                P.mm(Ob[:, c0:], vc[:, kt, h * 128:(h + 1) * 128], pT[:, c0:], start=(kt == 0),
                     stop=(kt == nkt - 1), r=[("vc", kt), pTk], w=[Ok])
                lb_, lbk_ = self.ps[2 + mi], ("ps", 2 + mi)
                P.mm(lb_[:, c0:], self.onesB, pT[:, c0:], start=(kt == 0), stop=(kt == nkt - 1), r=[pTk], w=[lbk_])
            self.rot_banks = cur_banks
            self.bi = 0
            tick()
            if kt == nkt - 1:
                del st[("ls", qb, h)]
                ya, yak = self.rot("ya")
                yb, ybk = self.rot("yb")
                sq, sqk = self.rot("sq1")

                def step1(lss=lss, ya=ya, yak=yak, yb=yb, ybk=ybk, sq=sq, sqk=sqk, h=h):
                    for mi, (y_, y_k) in enumerate(((ya, yak), (yb, ybk))):
                        ab = mi
                        r0, r0k = self.rot("rs")
                        self.recip_act(r0, r0k, self.ps[2 + mi], ("ps", 2 + mi))
                        P.tt("dve", y_, self.ps[ab], r0, ALU.mult, r=[("ps", ab), r0k], w=[y_k])
                    P.stt("dve", ya, yb, neglam[:, 0:1], ya, ALU.mult, ALU.add, r=[ybk, yak], w=[yak])
                    P.act(sq, ya, AF.Square, r=[yak], w=[sqk])

                def step2(h=h, ya=ya, yak=yak, sq=sq, sqk=sqk, yT=yT, yk=yk):
                    ss, ssk = self.bank()
                    P.mm(ss, self.onesB, sq, r=[sqk], w=[ssk])
                    rs, rsk = self.rot("rs")
                    P.act(rs, ss, AF.Ln, r=[ssk], w=[rsk], scale=1.0 / 128, bias=EPS)
                    P.act(rs, rs, AF.Exp, r=[rsk], w=[rsk], scale=-0.5)
                    P.stt("dve", yT[:, h, :], ya, gsub, rs, ALU.mult, ALU.mult, r=[yak, rsk], w=[(yk, h)])

                step1()
                defer(1, step2)
                if h == 3:
                    defer(3, lambda qb=qb, yT=yT, yk=yk: self.out_proj_block(wo, "o_wo", yT, yk, 4, qb))

        LOOK2 = 1
        for i in range(min(LOOK2, len(items))):
            stageA(items[i])
        for i in range(len(items)):
            if i + LOOK2 < len(items):
                stageA(items[i + LOOK2])
            stageB(items[i])
        tick(flush=True)
        self.rot_banks = list(range(8))
        self.bi = 0
        self.score_banks = [2, 3, 4, 5]
        self.si = 0
        P.fence()
        A.reset(m1)
        self.rots = {}
        qdT = A.alloc("qdT", [128, 4, T], BF16)
        kdT = A.alloc("kdT", [128, 4, T], BF16)
        vd = A.alloc("vd", [128, NT, 512], BF16)
        wuq = A.alloc("wuq", [128, 2, 384], BF16)
        wukv = A.alloc("wukv", [128, 768], BF16)
        P.dma("pool", wuq, d["od_w_uq"][o].rearrange("(c p) n -> p c n", p=128), w=["wuq"])
        P.dma("pool", wukv, d["od_w_ukv"][o], w=["wukv"])
        m3 = A.mark()
        self.mkrot("posfb", 2, [128, TB], F32)
        self.mkrot("pkm", 2, [NT, 128], F32)
        self.mkrot("ang", 3, [128, TB], F32)
        self.mkrot("angi", 1, [128, TB], I32)
        self.mkrot("sin", 2, [128, TB], F32)
        self.mkrot("cos", 2, [128, TB], F32)
        KO3 = 3
        oslots = []
        for j in range(KO3):
            sd = {"sq": A.alloc("o3_sq", [128, TB], BF16)}
            for nm in ("rs", "kraw", "rtmp", "rtmp2"):
                sd[nm] = A.alloc("o3_" + nm, [128, TB], F32)
            oslots.append(sd)
        self.rot_banks = list(range(8))
        self.bi = 0
        ones96 = self.onesB[0:96, 0:96]

        def qk_chain(tb, h, isk, cosb, cosk, sinb, sink, j):
            sd = oslots[j]
            K_ = lambda nm: ("o3s", j, nm)
            sl = slice(tb * TB, (tb + 1) * TB)
            if not isk:
                dst, dkey, gcol = qdT[:, h, sl], ("qdT", h, tb), cC(6)[0:96, :]
                qp, qpk = self.bank()
                for c in range(2):
                    P.mm(qp[0:96, :], wuq[:, c, h * 96:(h + 1) * 96], qlatT[:, c, sl], start=(c == 0), stop=(c == 1),
                         r=["wuq", ("qlatT", c, tb)], w=[qpk])
                src, srck = qp[0:96, :], [qpk]
            else:
                dst, dkey, gcol = kdT[:, h, sl], ("kdT", h, tb), cC(7)[0:96, :]
                kp, kpk = self.bank()
                P.mm(kp[0:64, :], wukv[:, h * 192:h * 192 + 64], kvlatT[:, sl], r=["wukv", ("kvlatT", tb)], w=[kpk])
                kr = sd["kraw"]
                P.copy("act", kr[0:64, :], kp[0:64, :], r=[kpk], w=[K_("kraw0")])
                P.copy("act", kr[64:96, :], kropeT[64:96, sl], r=[("kropeT", tb)], w=[K_("kraw1")])
                src, srck = kr[0:96, :], [K_("kraw0"), K_("kraw1")]
            P.act(sd["sq"][0:96, :], src, AF.Square, r=srck, w=[K_("sq")])
            yield
            ss, ssk = self.bank()
            P.mm(ss[0:96, :], ones96, sd["sq"][0:96, :], r=[K_("sq")], w=[ssk])
            rs = sd["rs"]
            P.act(rs[0:96, :], ss[0:96, :], AF.Ln, r=[ssk], w=[K_("rs")], scale=1.0 / 96, bias=EPS)
            P.act(rs[0:96, :], rs[0:96, :], AF.Exp, r=[K_("rs")], w=[K_("rs")], scale=-0.5)
            P.stt("dve", dst[0:96, :], src, gcol, rs[0:96, :], ALU.mult, ALU.mult, r=srck + [K_("rs")], w=[dkey])
            yield
            rp, rpk = self.bank()
            P.mm(rp[0:96, :], self.rotTB[0:96, 0:96], dst[0:96, :], r=[dkey], w=[rpk])
            t1, t2 = sd["rtmp"], sd["rtmp2"]
            P.tt("dve", t1[64:96, :], dst[64:96, :], cosb[64:96, :], ALU.mult, r=[dkey, cosk], w=[K_("t1")])
            P.tt("dve", t2[64:96, :], rp[64:96, :], sinb[64:96, :], ALU.mult, r=[rpk, sink], w=[K_("t2")])
            yield
            P.tt("dve", dst[64:96, :], t1[64:96, :], t2[64:96, :], ALU.add, r=[K_("t1"), K_("t2")], w=[dkey])

        for tb in range(NTB):
            tabs = self.rope_tables(tb)
            cosb, cosk = tabs["cos"]
            sinb, sink = tabs["sin"]
            makers = []
            for h in range(4):
                for isk in (False, True):
                    makers.append(lambda j, tb=tb, h=h, isk=isk, cosb=cosb, cosk=cosk, sinb=sinb, sink=sink:
                                  qk_chain(tb, h, isk, cosb, cosk, sinb, sink, j))
            self.run_chains(makers, KO3)
        wv = wukv.rearrange("p (h e) -> p h e", h=4)[:, :, 64:192]
        for tt in range(NT):
            pp, ppk = self.bank()
            P.mm(pp.rearrange("p (h e) -> p h e", h=4), kvlatT[:, tt * 128:(tt + 1) * 128], wv,
                 r=["wukv", ("kvlatT", tt // 4)], w=[ppk])
            P.copy("act", vd[:, tt, :], pp, r=[ppk], w=[("vd", tt)])
        P.fence()
        A.reset(m3)
        self.rots = {}
        wo2 = A.alloc("o_wo2", [128, 4, D], BF16)
        P.dma("pool", wo2, d["od_w_out"][o].rearrange("(h p) n -> p h n", p=128)[:, 4:8, :], w=["o_wo2"])
        self.mkrot("rs", 3, [128, TB], F32)
        self.mkrot("pT", 6, [128, TB], BF16)
        self.mkrot("yT", 2, [128, 4, TB], BF16)
        self.rot_banks = [6, 7]
        self.bi = 0
        if self.cfg.get("pe_l", True):
            self.score_banks = [4, 5, 6, 7]
            self.si = 0
        self.mkrot("lsum", 2, [128, TB], F32)
        for qb in range(NTB):
            yT, yk = self.rot("yT")
            items = [(h, kt) for h in range(4) for kt in range(4 * qb + 4)]
            st = {}

            def stageA(it, qb=qb):
                h, kt = it
                c0 = max(kt - 4 * qb, 0) * 128
                sp_, spk = self.sbank()
                P.mm(sp_[:, c0:], kdT[0:96, h, kt * 128:(kt + 1) * 128], qdT[0:96, h, qb * TB + c0:(qb + 1) * TB],
                     r=[("kdT", h, kt // 4), ("qdT", h, qb)], w=[spk])
                st[it] = (sp_, spk)

            def stageB(it, qb=qb, yT=yT, yk=yk):
                h, kt = it
                nkt = 4 * qb + 4
                j = kt - 4 * qb
                c0 = max(j, 0) * 128
                sp_, spk = st.pop(it)
                if kt == 0:
                    st[("ls", h)] = self.rot("lsum")
                ls, lsk = st[("ls", h)]
                pT, pTk = self.rot("pT")
                P.act(pT[:, c0:], sp_[:, c0:], AF.Exp, r=[spk], w=[pTk], scale=SC_D)
                if j >= 0:
                    P.op("dve", "memset", pT[64:128, c0:c0 + 64], 0.0, w=[pTk])
                Ob, Ok = self.ps[h % 2], ("ps", h % 2)
                P.mm(Ob[:, c0:], vd[:, kt, h * 128:(h + 1) * 128], pT[:, c0:], start=(kt == 0), stop=(kt == nkt - 1),
                     r=[("vd", kt), pTk], w=[Ok])
                if self.cfg.get("pe_l", True):
                    lb_, lbk_ = self.ps[2 + h % 2], ("ps", 2 + h % 2)
                    P.mm(lb_[:, c0:], self.onesB, pT[:, c0:], start=(kt == 0), stop=(kt == nkt - 1), r=[pTk], w=[lbk_])
                    if kt == nkt - 1:
                        r0, r0k = self.rot("rs")
                        self.recip_act(r0, r0k, lb_, lbk_)
                        P.tt("dve", yT[:, h, :], Ob, r0, ALU.mult, r=[Ok, r0k], w=[(yk, h)])
                        del st[("ls", h)]
                else:
                    le = "dve" if kt % 3 else "pool"
                    if kt == 0:
                        P.copy(le, ls, pT, r=[pTk], w=[lsk])
                    else:
                        P.tt(le, ls[:, c0:], ls[:, c0:], pT[:, c0:], ALU.add, r=[pTk, lsk], w=[lsk])
                    if kt == nkt - 1:
                        lp, lpk = self.bank()
                        P.mm(lp, self.onesF, ls, r=[lsk], w=[lpk])
                        r0, r0k = self.rot("rs")
                        self.recip_act(r0, r0k, lp, lpk)
                        P.tt("dve", yT[:, h, :], Ob, r0, ALU.mult, r=[Ok, r0k], w=[(yk, h)])
                        del st[("ls", h)]

            LOOK = 3
            for i in range(min(LOOK, len(items))):
                stageA(items[i])
            for i in range(len(items)):
                if i + LOOK < len(items):
                    stageA(items[i + LOOK])
                stageB(items[i])
            self.out_proj_block(wo2, "o_wo2", yT, yk, 4, qb)
        self.rot_banks = list(range(8))
        self.bi = 0
        P.fence()
        A.reset(m0)

    def build(self):
        cfg = self.cfg
        self.setup()
        for l in range(cfg.get("layers", DEPTH)):
            if cfg.get("mixer", True):
                if l % 2 == 0:
                    if not cfg.get("skip_even"):
                        self.even_mixer(l // 2, l)
                elif not cfg.get("skip_odd"):
                    self.odd_mixer(l // 2, l)
            if cfg.get("xattn", True):
                self.xattn(l)
            if cfg.get("ffn", True):
                self.ffn(l)
        self.store()
        self.P.emit()


def build_nc(cfg=None):
    nc = bass.Bass("TRN2", target_bir_lowering=False)
    b = Builder(nc, cfg or {})
    b.build()
    return nc, b


def make_in_maps(inputs, n):
    consts = host_consts()
    maps = []
    for i in range(n):
        mp = {
            "x": np.ascontiguousarray(inputs["x"][i]),
            "mem": np.ascontiguousarray(inputs["mem"][i]),
            "positions": np.ascontiguousarray(inputs["positions"][i:i + 1]),
        }
        for name, _ in WEIGHTS:
            mp[name] = np.ascontiguousarray(inputs[name])
        mp.update(consts)
        maps.append(mp)
    return maps


def kernel(**inputs):
    inputs = {k: np.asarray(v) for k, v in inputs.items()}
    n = inputs["x"].shape[0]
    nc, _ = build_nc({})
    in_maps = make_in_maps(inputs, n)
    res = run_bass_kernel_spmd(nc, in_maps, core_ids=list(range(n)))
    return np.stack([np.asarray(r["y"]) for r in res.results], axis=0).astype(np.float32)
```

```python
from contextlib import ExitStack
import math
import numpy as np
import concourse.bass as bass
import concourse.mybir as mybir
from concourse.bass_utils import run_bass_kernel_spmd

F32 = mybir.dt.float32
BF16 = mybir.dt.bfloat16
I32 = mybir.dt.int32
AF = mybir.ActivationFunctionType
ALU = mybir.AluOpType
AX = mybir.AxisListType

EPOCH = 30000
STRICT_SAME_ENGINE = True
NSLOT = 8
ENGS = ("pe", "act", "dve", "pool", "sp")


class Prog:
    def __init__(self, nc):
        self.nc = nc
        self.ops = {e: [] for e in ENGS}
        self.ncomp = {e: 0 for e in ENGS}
        self.ndma = {e: 0 for e in ENGS}
        self.last_w = {}
        self.readers = {}
        self.waited = {e: {} for e in ENGS}
        self.semkeys = set()
        self.sems = {}
        self.last_tok = {}
        self.gdep = None

    def add(self, eng, fn, r=(), w=(), dma=False, nofence=False):
        nowait_only = fn is None
        raw = {}
        oth = {}
        if eng != "pe" and not nowait_only:
            locks = [("pslock", k[1]) for k in r if isinstance(k, tuple) and len(k) == 2 and k[0] == "ps"]
            if locks:
                w = list(w) + locks

        def put(d, tok):
            sk, v, e2, d2 = tok
            if d.get(sk, (0,))[0] < v:
                d[sk] = (v, e2, d2)

        for k in r:
            t = self.last_w.get(k)
            if t is not None:
                put(raw, t)
        for k in w:
            t = self.last_w.get(k)
            if t is not None:
                put(oth, t)
            for sk, (v, e2, d2) in self.readers.get(k, {}).items():
                put(oth, (sk, v, e2, d2))
        if self.gdep is not None and not nofence:
            put(raw, self.gdep)
        if nowait_only:
            semkey, val = None, 0
        elif dma:
            i = self.ndma[eng]
            self.ndma[eng] += 1
            slot, rnd = i % NSLOT, i // NSLOT
            semkey = ("d", eng, slot)
            val = 16 * (rnd + 1)
            if rnd > 0:
                put(raw, (semkey, 16 * rnd, eng, True))
        else:
            i = self.ncomp[eng]
            self.ncomp[eng] += 1
            semkey = ("c", eng, i // EPOCH)
            val = i % EPOCH + 1
        tok = (semkey, val, eng, dma)
        waits = []
        wd = self.waited[eng]
        for d, is_raw in ((raw, True), (oth, False)):
            for sk, (v, e2, d2) in d.items():
                if not d2 and e2 == eng:
                    if eng == "pe" or (not is_raw and not STRICT_SAME_ENGINE):
                        continue
                if wd.get(sk, 0) >= v:
                    continue
                wd[sk] = v
                waits.append((sk, v))
        if nowait_only:
            self.ops[eng].append((None, waits, None, False))
            return None
        self.semkeys.add(semkey)
        self.last_tok[semkey] = tok
        for k in w:
            self.last_w[k] = tok
            self.readers[k] = {}
        for k in r:
            d = self.readers.setdefault(k, {})
            if d.get(semkey, (0,))[0] < val:
                d[semkey] = (val, eng, dma)
        self.ops[eng].append((fn, waits, semkey, dma))
        return tok

    def op(self, eng, name, *args, r=(), w=(), **kw):
        return self.add(eng, lambda e: getattr(e, name)(*args, **kw), r, w)

    def mm(self, out, lhsT, rhs, start=True, stop=True, r=(), w=(), **kw):
        return self.add("pe", lambda e: e.matmul(out, lhsT, rhs, start=start, stop=stop, **kw), r, w)

    def tr(self, out, in_, ident, r=(), w=()):
        return self.add("pe", lambda e: e.transpose(out, in_, ident), r, w)

    def act(self, out, in_, func, r=(), w=(), **kw):
        return self.add("act", lambda e: e.activation(out, in_, func, **kw), r, w)

    def ts(self, eng, out, in0, s1, s2, op0, op1=None, r=(), w=()):
        if op1 is None:
            return self.add(eng, lambda e: e.tensor_scalar(out, in0, s1, None, op0), r, w)
        return self.add(eng, lambda e: e.tensor_scalar(out, in0, s1, s2, op0, op1), r, w)

    def tt(self, eng, out, in0, in1, op, r=(), w=()):
        return self.add(eng, lambda e: e.tensor_tensor(out, in0, in1, op), r, w)

    def stt(self, eng, out, in0, scalar, in1, op0, op1, r=(), w=()):
        return self.add(eng, lambda e: e.scalar_tensor_tensor(out, in0, scalar, in1, op0, op1), r, w)

    def copy(self, eng, out, in_, r=(), w=()):
        if eng == "act":
            return self.add(eng, lambda e: e.copy(out, in_), r, w)
        return self.add(eng, lambda e: e.tensor_copy(out, in_), r, w)

    def dma(self, eng, out, in_, r=(), w=(), nofence=False, **kw):
        return self.add(eng, lambda e: e.dma_start(out, in_, **kw), r, w, dma=True, nofence=nofence)

    def fence(self):
        keys = []
        for sk, tok in list(self.last_tok.items()):
            k = ("_fence", sk)
            self.last_w[k] = tok
            self.readers[k] = {}
            keys.append(k)
        self.gdep = None
        tok = self.add("sp", lambda e: e.nop(), r=keys, w=[])
        self.gdep = tok

    def finish(self, eng, keys):
        self.add(eng, None, r=keys, w=())

    def emit(self):
        nc = self.nc
        with ExitStack() as st:
            for sk in sorted(self.semkeys, key=str):
                self.sems[sk] = st.enter_context(nc.semaphore("s_%s_%s_%d" % sk))
            with nc.Block() as block:
                def mk(name):
                    def body(e):
                        for fn, waits, semkey, dma in self.ops[name]:
                            for sk, v in waits:
                                e.wait_ge(self.sems[sk], v)
                            if fn is None:
                                continue
                            ins = fn(e)
                            ins.then_inc(self.sems[semkey], 16 if dma else 1)
                    return body
                block.tensor(mk("pe"))
                block.scalar(mk("act"))
                block.vector(mk("dve"))
                block.gpsimd(mk("pool"))
                block.sync(mk("sp"))


T = 2048
D = 1024
TB = 512
NTB = 4
NT = 16
DC = 8
FF = 2816
FC = 22
NMEM = 256
EPS = 1e-6
SB_BASE = 16512
SB_END = 229344
DEPTH = 4

WEIGHTS = [
    ("norm_mix", [4, 1024]), ("norm_x", [4, 1024]), ("norm_mem", [4, 1024]),
    ("x_wq", [4, 1024, 512]), ("x_wkv", [4, 1024, 1024]), ("x_q_norm", [4, 128]), ("x_k_norm", [4, 128]),
    ("x_wo", [4, 512, 1024]), ("norm_ffn", [4, 1024]), ("ffn_w_in", [4, 1024, 5632]),
    ("ffn_w_out", [4, 2816, 1024]),
    ("ev_w_in", [2, 1024, 3080]), ("ev_conv_qkv", [2, 4, 1536]), ("ev_a_log", [2, 4]), ("ev_dt_bias", [2, 4]),
    ("ev_o_norm", [2, 128]), ("ev_conv_b_w", [2, 4, 512]), ("ev_conv_b_b", [2, 512]),
    ("ev_gate_a_w", [2, 8, 64, 64]), ("ev_gate_a_b", [2, 512]), ("ev_gate_x_w", [2, 8, 64, 64]),
    ("ev_gate_x_b", [2, 512]), ("ev_lru_l", [2, 512]), ("ev_w_out", [2, 1024, 1024]),
    ("od_w_in", [2, 1024, 1952]), ("od_c_q_norm", [2, 64]), ("od_c_k_norm", [2, 64]),
    ("od_lam_q1", [2, 64]), ("od_lam_k1", [2, 64]), ("od_lam_q2", [2, 64]), ("od_lam_k2", [2, 64]),
    ("od_c_sub_norm", [2, 128]), ("od_q_lat_norm", [2, 256]), ("od_w_uq", [2, 256, 384]),
    ("od_kv_lat_norm", [2, 128]), ("od_w_ukv", [2, 128, 768]), ("od_d_q_norm", [2, 96]),
    ("od_d_k_norm", [2, 96]), ("od_w_out", [2, 1024, 1024]),
]


def host_consts():
    c = {}
    c["c_ident"] = np.eye(128, dtype=np.float32)
    rt = np.zeros((128, 128), np.float32)
    for i in range(16):
        rt[80 + i, 64 + i] = -1.0
        rt[64 + i, 80 + i] = 1.0
    c["c_rotT"] = rt
    fr = np.zeros((128, 2), np.float32)
    for i in range(16):
        f = 10000.0 ** (-(i / 16.0))
        fr[64 + i, 0] = fr[80 + i, 0] = np.float32(f)
    fr[:, 1] = fr[:, 0] / np.float32(2 * np.pi)
    c["c_freq"] = fr
    bo = np.zeros((128, 128), np.float32)
    bo[0:64, 0:64] = 1.0
    bo[64:128, 64:128] = 1.0
    c["c_blockones"] = bo
    ii = np.arange(128)
    same = (ii[:, None] // 64) == (ii[None, :] // 64)
    c["c_maskSL"] = (same & (ii[None, :] < ii[:, None])).astype(np.float32)
    c["c_maskIU"] = (same & (ii[None, :] >= ii[:, None])).astype(np.float32)
    c["c_lsel"] = (ii[:, None] == (ii[None, :] // 64) * 64 + 63).astype(np.float32)
    return c


class Arena:
    def __init__(self, nc, base, end):
        self.nc, self.p, self.end, self.n = nc, base, end, 0

    def alloc(self, name, shape, dtype):
        esz = 4 if dtype in (F32, I32) else 2
        nbytes = int(np.prod(shape[1:])) * esz
        off = (self.p + 31) // 32 * 32
        self.p = off + nbytes
        assert self.p <= self.end, ("SBUF overflow", name, self.p, self.end)
        self.n += 1
        return self.nc.alloc_sbuf_tensor_at("%s_%d" % (name, self.n), list(shape), dtype, offset=off).ap()

    def mark(self):
        return self.p

    def reset(self, m):
        self.p = m


class Builder:
    def __init__(self, nc, cfg):
        self.nc = nc
        self.cfg = cfg
        self.P = Prog(nc)
        self.d = {}
        P = self.P
        d = self.d
        d["x"] = nc.dram_tensor("x", [T, D], F32, kind="ExternalInput").ap()
        d["mem"] = nc.dram_tensor("mem", [NMEM, D], F32, kind="ExternalInput").ap()
        d["positions"] = nc.dram_tensor("positions", [1, T], I32, kind="ExternalInput").ap()
        for name, shp in WEIGHTS:
            d[name] = nc.dram_tensor(name, shp, F32, kind="ExternalInput").ap()
        for name, arr in host_consts().items():
            d[name] = nc.dram_tensor(name, list(arr.shape), F32, kind="ExternalInput").ap()
        d["y"] = nc.dram_tensor("y", [T, D], F32, kind="ExternalOutput").ap()
        self.A = Arena(nc, SB_BASE, SB_END)
        A = self.A
        self.ps = [nc.alloc_psum_tensor("psb%d" % i, [128, 512], F32).ap() for i in range(8)]
        self.bi = 0
        self.rot_banks = list(range(8))
        self.misc_banks = [6, 7]
        self.score_banks = [2, 3, 4, 5]
        self.mi = 0
        self.si = 0
        self.rots = {}
        self.xT = A.alloc("xT", [128, DC, T], F32)
        self.identF = A.alloc("identF", [128, 128], F32)
        self.identB = A.alloc("identB", [128, 128], BF16)
        self.onesB = A.alloc("onesB", [128, 128], BF16)
        self.onesF = A.alloc("onesF", [128, 128], F32)
        self.colsA = A.alloc("colsA", [128, 128], F32)
        self.colsB = A.alloc("colsB", [128, 128], F32)
        self.colsC = A.alloc("colsC", [128, 128], F32)
        self.blockB = A.alloc("blockB", [128, 128], BF16)
        self.rotTB = A.alloc("rotTB", [128, 128], BF16)
        self.freq = A.alloc("freq", [128, 2], F32)
        self.posk = A.alloc("posk", [128, NT], F32)
        self.negposk = A.alloc("negposk", [128, NT], F32)
        self.pkf = A.alloc("pkf", [NT, 128], F32)
        self.colsD = A.alloc("colsD", [128, 256], F32)
        self.maskSL = A.alloc("maskSL", [128, 128], F32)
        self.maskIU = A.alloc("maskIU", [128, 128], F32)
        self.lsel = A.alloc("lsel", [128, 128], F32)
        self.memTn = A.alloc("memTn", [128, DC, NMEM], F32)
        self.kTx = A.alloc("kTx", [128, 4, NMEM], BF16)
        self.vx = A.alloc("vx", [128, 2, 512], BF16)
        self.phase_base = A.mark()

    def bank(self):
        i = self.rot_banks[self.bi % len(self.rot_banks)]
        self.bi += 1
        return self.ps[i], ("ps", i)

    def mkrot(self, name, n, shape, dtype):
        self.rots[name] = [[self.A.alloc(name, shape, dtype) for _ in range(n)], 0]

    def rot(self, name):
        lst, i = self.rots[name]
        self.rots[name][1] = (i + 1) % len(lst)
        return lst[i], (name, i)

    def gcol(self, which, l, c):
        j = which * 32 + l * 8 + c
        return self.colsA[:, j:j + 1]

    def setup(self):
        P, d, A = self.P, self.d, self.A
        P.dma("sp", self.identF, d["c_ident"], w=["identF"])
        P.copy("dve", self.identB, self.identF, r=["identF"], w=["identB"])
        P.op("dve", "memset", self.onesB, 1.0, w=["onesB"])
        P.op("dve", "memset", self.onesF, 1.0, w=["onesF"])
        m = A.mark()
        stg = A.alloc("stg", [128, 128], F32)
        for i, nm in enumerate(["norm_mix", "norm_x", "norm_ffn", "norm_mem"]):
            P.dma("sp", stg[i * 32:(i + 1) * 32, :], d[nm].rearrange("l (c p) -> (l c) p", p=128), w=[("stg", i)])
        pb, pk = self.bank()
        P.tr(pb[:, 0:128], stg, self.identF, r=[("stg", i) for i in range(4)] + ["identF"], w=[pk])
        P.copy("dve", self.colsA, pb[:, 0:128], r=[pk], w=["colsA"])
        stg2 = A.alloc("stg2", [128, 128], F32)
        P.op("dve", "memset", stg2, 0.0, w=["stg2"])
        P.dma("sp", stg2[0:4, :], d["x_q_norm"], r=[], w=["stg2"])
        P.dma("sp", stg2[4:8, :], d["x_k_norm"], r=["stg2"], w=["stg2b"])
        pb, pk = self.bank()
        P.tr(pb[:, 0:128], stg2, self.identF, r=["stg2", "stg2b", "identF"], w=[pk])
        P.copy("dve", self.colsB, pb[:, 0:128], r=[pk], w=["colsB"])
        stg3 = A.alloc("stg3", [128, 128], F32)
        P.op("dve", "memset", stg3, 0.0, w=["stg3z"])
        k3 = []
        def ld3(row, c0, src):
            k = ("stg3", len(k3))
            k3.append(k)
            P.dma("sp", stg3[row:row + 1, c0:c0 + src.shape[1]], src, r=["stg3z"], w=[k])
        for o in range(2):
            for half in range(2):
                ld3(o * 8 + 0, half * 64, d["od_c_q_norm"][o:o + 1, :])
                ld3(o * 8 + 1, half * 64, d["od_c_k_norm"][o:o + 1, :])
            ld3(o * 8 + 2, 0, d["od_c_sub_norm"][o:o + 1, :])
            ld3(o * 8 + 3, 0, d["od_q_lat_norm"][o:o + 1, 0:128])
            ld3(o * 8 + 4, 0, d["od_q_lat_norm"][o:o + 1, 128:256])
            ld3(o * 8 + 5, 0, d["od_kv_lat_norm"][o:o + 1, :])
            ld3(o * 8 + 6, 0, d["od_d_q_norm"][o:o + 1, :])
            ld3(o * 8 + 7, 0, d["od_d_k_norm"][o:o + 1, :])
        pb, pk = self.bank()
        P.tr(pb[:, 0:128], stg3, self.identF, r=k3 + ["identF"], w=[pk])
        P.copy("dve", self.colsC, pb[:, 0:128], r=[pk], w=["colsC"])
        P.dma("sp", self.maskSL, d["c_maskSL"], w=["maskSL"])
        P.dma("sp", self.maskIU, d["c_maskIU"], w=["maskIU"])
        P.dma("sp", self.lsel, d["c_lsel"], w=["lsel"])
        for e in range(2):
            st4 = A.alloc("stg4", [128, 128], F32)
            P.op("dve", "memset", st4, 0.0, w=[("st4z", e)])
            k4 = []
            def ld4(r0, src):
                k = ("stg4", e, len(k4))
                k4.append(k)
                P.dma("sp", st4[r0:r0 + src.shape[0], :], src, r=[("st4z", e)], w=[k])
            ld4(0, d["ev_conv_qkv"][e].rearrange("j (c p) -> (j c) p", p=128))
            ld4(48, d["ev_conv_b_w"][e].rearrange("j (c p) -> (j c) p", p=128))
            ld4(64, d["ev_conv_b_b"][e:e + 1, :].rearrange("o (c p) -> (o c) p", p=128))
            ld4(68, d["ev_gate_a_b"][e:e + 1, :].rearrange("o (c p) -> (o c) p", p=128))
            ld4(72, d["ev_gate_x_b"][e:e + 1, :].rearrange("o (c p) -> (o c) p", p=128))
            ld4(76, d["ev_lru_l"][e:e + 1, :].rearrange("o (c p) -> (o c) p", p=128))
            ld4(80, d["ev_o_norm"][e:e + 1, :])
            pb, pk = self.bank()
            P.tr(pb[:, 0:128], st4, self.identF, r=k4 + ["identF"], w=[pk])
            P.copy("dve", self.colsD[:, e * 128:(e + 1) * 128], pb[:, 0:128], r=[pk], w=[("colsD", e)])
        cst = A.alloc("cst", [128, 128], F32)
        P.dma("sp", cst, d["c_blockones"], w=["cst"])
        P.copy("dve", self.blockB, cst, r=["cst"], w=["blockB"])
        cst2 = A.alloc("cst2", [128, 128], F32)
        P.dma("sp", cst2, d["c_rotT"], w=["cst2"])
        P.copy("dve", self.rotTB, cst2, r=["cst2"], w=["rotTB"])
        P.dma("sp", self.freq, d["c_freq"], w=["freq"])
        pk_i = A.alloc("pk_i", [NT, 128], I32)
        pk_f = self.pkf
        P.dma("sp", pk_i, d["positions"].rearrange("o (t p) -> (o t) p", p=128), w=["pk_i"])
        P.copy("dve", pk_f, pk_i, r=["pk_i"], w=["pk_f"])
        pb, pk = self.bank()
        P.tr(pb[:, 0:NT], pk_f, self.identF[0:NT, 0:NT], r=["pk_f", "identF"], w=[pk])
        P.copy("dve", self.posk, pb[:, 0:NT], r=[pk], w=["posk"])
        P.ts("dve", self.negposk, self.posk, -1.0, None, ALU.mult, r=["posk"], w=["negposk"])
        xin = [A.alloc("xin", [128, D], F32) for _ in range(2)]
        for tt in range(NT):
            xb = xin[tt % 2]
            xk = ("xin", tt % 2)
            P.dma("sp", xb, d["x"][tt * 128:(tt + 1) * 128, :], w=[xk])
            for hb in range(2):
                pb, pk = self.bank()
                for q in range(4):
                    c = hb * 4 + q
                    P.tr(pb[:, q * 128:(q + 1) * 128], xb[:, c * 128:(c + 1) * 128], self.identF,
                         r=[xk, "identF"], w=[pk])
                eng = "dve" if hb == 0 else "act"
                P.copy(eng, self.xT[:, hb * 4:(hb + 1) * 4, tt * 128:(tt + 1) * 128],
                       pb.rearrange("p (a b) -> p a b", a=4),
                       r=[pk], w=[("xT", c, tt // 4) for c in range(hb * 4, hb * 4 + 4)])
        mm_ = [A.alloc("memin", [128, D], F32) for _ in range(2)]
        msq = A.alloc("msq", [128, D], F32)
        mss = A.alloc("mss", [128, 2], F32)
        for mt in range(2):
            P.dma("sp", mm_[mt], d["mem"][mt * 128:(mt + 1) * 128, :], w=[("memin", mt)])
            P.act(msq, mm_[mt], AF.Square, r=[("memin", mt)], w=["msq"], accum_out=mss[:, mt:mt + 1])
            P.act(mss[:, mt:mt + 1], mss[:, mt:mt + 1], AF.Sqrt, r=["msq"], w=[("mss", mt)], scale=1.0 / D, bias=EPS)
            P.op("dve", "reciprocal", mss[:, mt:mt + 1], mss[:, mt:mt + 1], r=[("mss", mt)], w=[("mss", mt)])
            P.ts("dve", mm_[mt], mm_[mt], mss[:, mt:mt + 1], None, ALU.mult, r=[("memin", mt), ("mss", mt)],
                 w=[("memin", mt)])
            for hb in range(2):
                pb, pk = self.bank()
                for q in range(4):
                    c = hb * 4 + q
                    P.tr(pb[:, q * 128:(q + 1) * 128], mm_[mt][:, c * 128:(c + 1) * 128], self.identF,
                         r=[("memin", mt), "identF"], w=[pk])
                P.copy("dve", self.memTn[:, hb * 4:(hb + 1) * 4, mt * 128:(mt + 1) * 128],
                       pb.rearrange("p (a b) -> p a b", a=4), r=[pk], w=["memTn"])
        P.fence()
        A.reset(m)

    def norm_block(self, tb, which, l, hT_out, hkey):
        P = self.P
        sl = slice(tb * TB, (tb + 1) * TB)
        sq, sqk = self.rot("sq")
        P.act(sq, self.xT[:, :, sl], AF.Square, r=[("xT", c, tb) for c in range(DC)], w=[sqk])
        ss, ssk = self.bank()
        for c in range(DC):
            P.mm(ss, self.onesB, sq[:, c, :], start=(c == 0), stop=(c == DC - 1), r=[sqk, "onesB"], w=[ssk])
        rs, rsk = self.rot("rs")
        P.act(rs, ss, AF.Ln, r=[ssk], w=[rsk], scale=1.0 / D, bias=EPS)
        P.act(rs, rs, AF.Exp, r=[rsk], w=[rsk], scale=-0.5)
        for c in range(DC):
            P.stt("dve", hT_out[:, c, :], self.xT[:, c, sl], self.gcol(which, l, c), rs, ALU.mult, ALU.mult,
                  r=[("xT", c, tb), rsk, "colsA"], w=[(hkey, c)])

    def pnorm(self, src, srck, npart, ones_l, inv_n, gain, out, outk):
        P = self.P
        srcks = srck if isinstance(srck, list) else [srck]
        sq, sqk = self.rot("sq1")
        P.act(sq[0:npart, :], src, AF.Square, r=srcks, w=[sqk])
        ss, ssk = self.bank()
        P.mm(ss[0:npart, :], ones_l, sq[0:npart, :], r=[sqk, "onesB"], w=[ssk])
        rs, rsk = self.rot("rs")
        P.act(rs[0:npart, :], ss[0:npart, :], AF.Ln, r=[ssk], w=[rsk], scale=inv_n, bias=EPS)
        P.act(rs[0:npart, :], rs[0:npart, :], AF.Exp, r=[rsk], w=[rsk], scale=-0.5)
        if gain is None:
            P.tt("dve", out, src, rs[0:npart, :], ALU.mult, r=srcks + [rsk], w=[outk])
        elif isinstance(gain, float):
            P.stt("dve", out, src, gain, rs[0:npart, :], ALU.mult, ALU.mult, r=srcks + [rsk], w=[outk])
        else:
            P.stt("dve", out, src, gain, rs[0:npart, :], ALU.mult, ALU.mult, r=srcks + [rsk], w=[outk])

    def pnorm_pipe(self, blocks):
        P = self.P
        prev = None

        def part_b(stt_):
            (src, srck, npart, ones_l, inv_n, gain, out, outk), sq, sqk = stt_
            srcks = srck if isinstance(srck, list) else [srck]
            ss, ssk = self.bank()
            P.mm(ss[0:npart, :], ones_l, sq[0:npart, :], r=[sqk], w=[ssk])
            rs, rsk = self.rot("rs")
            P.act(rs[0:npart, :], ss[0:npart, :], AF.Ln, r=[ssk], w=[rsk], scale=inv_n, bias=EPS)
            P.act(rs[0:npart, :], rs[0:npart, :], AF.Exp, r=[rsk], w=[rsk], scale=-0.5)
            if gain is None:
                P.tt("dve", out, src, rs[0:npart, :], ALU.mult, r=srcks + [rsk], w=[outk])
            else:
                P.stt("dve", out, src, gain, rs[0:npart, :], ALU.mult, ALU.mult, r=srcks + [rsk], w=[outk])

        for b in blocks:
            src, srck, npart = b[0], b[1], b[2]
            srcks = srck if isinstance(srck, list) else [srck]
            sq, sqk = self.rot("sq1")
            P.act(sq[0:npart, :], src, AF.Square, r=srcks, w=[sqk])
            if prev is not None:
                part_b(prev)
            prev = (b, sq, sqk)
        if prev is not None:
            part_b(prev)

    def run_chains(self, makers, K):
        pending = list(makers)
        active = []
        free = list(range(K))
        while pending or active:
            while pending and free:
                j = free.pop(0)
                active.append((pending.pop(0)(j), j))
            nxt = []
            for g, j in active:
                try:
                    next(g)
                    nxt.append((g, j))
                except StopIteration:
                    free.append(j)
            active = nxt

    def recip_act(self, out, outk, src, srck):
        P = self.P
        P.act(out, src, AF.Ln, r=[srck], w=[outk])
        P.act(out, out, AF.Exp, r=[outk], w=[outk], scale=-1.0)

    def mbank(self):
        i = self.misc_banks[self.mi % len(self.misc_banks)]
        self.mi += 1
        return self.ps[i], ("ps", i)

    def sbank(self):
        i = self.score_banks[self.si % len(self.score_banks)]
        self.si += 1
        return self.ps[i], ("ps", i)

    def ffn(self, l):
        P, d, A = self.P, self.d, self.A
        m = A.mark()
        self.rots = {}
        hT = A.alloc("ffn_hT", [128, DC, 1024], BF16)
        act = A.alloc("ffn_act", [128, FC, 1024], BF16)
        self.mkrot("win", 2, [128, DC, 1024], BF16)
        self.mkrot("wout", 2, [128, FC, 128], BF16)
        self.mkrot("sq", 1, [128, DC, TB], BF16)
        self.mkrot("rs", 2, [128, TB], F32)
        self.mkrot("sg", 2, [128, TB], F32)
        w_in_d = d["ffn_w_in"][l].rearrange("(c p) n -> p c n", p=128)
        w_out_d = d["ffn_w_out"][l].rearrange("(f p) n -> p f n", p=128)
        SLW = 512
        nsl = (FF + SLW - 1) // SLW
        for half in range(2):
            if half == 0:
                for j in range(2):
                    self.norm_block(j, 2, l, hT[:, :, j * TB:(j + 1) * TB], ("ffn_hT", j))
            for s in range(nsl):
                c0 = s * SLW
                ncol = min(SLW, FF - c0)
                wb, wk = self.rot("win")
                P.dma("pool", wb[:, :, 0:ncol], w_in_d[:, :, c0:c0 + ncol], w=[(wk, "g")])
                P.dma("pool", wb[:, :, SLW:SLW + ncol], w_in_d[:, :, FF + c0:FF + c0 + ncol], w=[(wk, "u")])
                for fi in range(ncol // 128):
                    f = (c0 // 128) + fi
                    for j in range(2):
                        gps, gk = self.bank()
                        ups, uk = self.bank()
                        for c in range(DC):
                            P.mm(gps, wb[:, c, fi * 128:(fi + 1) * 128], hT[:, c, j * TB:(j + 1) * TB],
                                 start=(c == 0), stop=(c == DC - 1), r=[(wk, "g"), (("ffn_hT", j), c)], w=[gk])
                        for c in range(DC):
                            P.mm(ups, wb[:, c, SLW + fi * 128:SLW + (fi + 1) * 128], hT[:, c, j * TB:(j + 1) * TB],
                                 start=(c == 0), stop=(c == DC - 1), r=[(wk, "u"), (("ffn_hT", j), c)], w=[uk])
                        sg, sgk = self.rot("sg")
                        P.act(sg, gps, AF.Silu, r=[gk], w=[sgk])
                        P.tt("dve", act[:, f, j * TB:(j + 1) * TB], sg, ups, ALU.mult, r=[sgk, uk], w=[("act", f, j)])
            if half == 0:
                for j in range(2):
                    self.norm_block(2 + j, 2, l, hT[:, :, j * TB:(j + 1) * TB], ("ffn_hT", j))
            for dc in range(DC):
                wo, wok = self.rot("wout")
                P.dma("pool", wo, w_out_d[:, :, dc * 128:(dc + 1) * 128], w=[wok])
                for j in range(2):
                    tb = half * 2 + j
                    sl = slice(tb * TB, (tb + 1) * TB)
                    yps, yk = self.bank()
                    for f in range(FC):
                        P.mm(yps, wo[:, f, :], act[:, f, j * TB:(j + 1) * TB], start=(f == 0), stop=(f == FC - 1),
                             r=[wok, ("act", f, j)], w=[yk])
                    P.tt("dve", self.xT[:, dc, sl], self.xT[:, dc, sl], yps, ALU.add, r=[("xT", dc, tb), yk],
                         w=[("xT", dc, tb)])
        P.fence()
        A.reset(m)

    def out_proj_block(self, wo, wok, oT, ok, nk, tb):
        P = self.P
        sl = slice(tb * TB, (tb + 1) * TB)
        for dc in range(DC):
            yps, yk = self.bank()
            for h in range(nk):
                P.mm(yps, wo[:, h, dc * 128:(dc + 1) * 128], oT[:, h, :], start=(h == 0), stop=(h == nk - 1),
                     r=[wok, (ok, h)], w=[yk])
            P.tt("dve", self.xT[:, dc, sl], self.xT[:, dc, sl], yps, ALU.add, r=[("xT", dc, tb), yk],
                 w=[("xT", dc, tb)])

    def xattn(self, l):
        P, d, A = self.P, self.d, self.A
        m = A.mark()
        self.rots = {}
        wq = A.alloc("x_wq", [128, DC, 512], BF16)
        wo = A.alloc("x_wo", [128, 4, D], BF16)
        wkv = A.alloc("x_wkv", [128, DC, D], BF16)
        mh = A.alloc("x_mh", [128, DC, NMEM], BF16)
        hT = A.alloc("x_hT", [128, DC, T], BF16)
        qall = A.alloc("x_q", [128, 4, T], BF16)
        oall = A.alloc("x_o", [128, 4, T], BF16)
        self.mkrot("sq", 1, [128, DC, TB], BF16)
        self.mkrot("sq1", 3, [128, TB], BF16)
        self.mkrot("rs", 3, [128, TB], F32)
        self.mkrot("pT", 6, [128, TB], BF16)
        P.dma("pool", wq, d["x_wq"][l].rearrange("(c p) n -> p c n", p=128), w=["x_wq"])
        for hh in range(2):
            P.dma("pool", wkv[:, :, hh * 512:(hh + 1) * 512],
                  d["x_wkv"][l].rearrange("(c p) n -> p c n", p=128)[:, :, hh * 512:(hh + 1) * 512], w=[("x_wkv", hh)])
        P.dma("pool", wo, d["x_wo"][l].rearrange("(h p) n -> p h n", p=128), w=["x_wo"])
        self.rot_banks = list(range(8))
        self.bi = 0
        for tb in range(NTB):
            self.norm_block(tb, 1, l, hT[:, :, tb * TB:(tb + 1) * TB], ("x_hT", tb))
        prev = None

        def q_b(stt_):
            qp, qk, sq, sqk, h, tb = stt_
            ss, ssk = self.bank()
            P.mm(ss, self.onesB, sq, r=[sqk], w=[ssk])
            rs, rsk = self.rot("rs")
            P.act(rs, ss, AF.Ln, r=[ssk], w=[rsk], scale=1.0 / 128, bias=EPS)
            P.act(rs, rs, AF.Exp, r=[rsk], w=[rsk], scale=-0.5)
            P.stt("dve", qall[:, h, tb * TB:(tb + 1) * TB], qp, self.colsB[:, l:l + 1], rs, ALU.mult, ALU.mult,
                  r=[qk, rsk], w=[("x_q", h, tb)])

        for tb in range(NTB):
            sl = slice(tb * TB, (tb + 1) * TB)
            for h in range(4):
                qp, qk = self.bank()
                for c in range(DC):
                    P.mm(qp, wq[:, c, h * 128:(h + 1) * 128], hT[:, c, sl], start=(c == 0), stop=(c == DC - 1),
                         r=["x_wq", (("x_hT", tb), c)], w=[qk])
                sq, sqk = self.rot("sq1")
                P.act(sq, qp, AF.Square, r=[qk], w=[sqk])
                if prev is not None:
                    q_b(prev)
                prev = (qp, qk, sq, sqk, h, tb)
        for c in range(DC):
            P.ts("dve", mh[:, c, :], self.memTn[:, c, :], self.gcol(3, l, c), None, ALU.mult, w=[("x_mh", c)])
        q_b(prev)
        for h in range(4):
            kp, kk = self.bank()
            for c in range(DC):
                P.mm(kp[:, 0:NMEM], wkv[:, c, h * 128:(h + 1) * 128], mh[:, c, :], start=(c == 0), stop=(c == DC - 1),
                     r=[("x_wkv", 0), ("x_mh", c)], w=[kk])
            sq, sqk = self.rot("sq1")
            P.act(sq[:, 0:NMEM], kp[:, 0:NMEM], AF.Square, r=[kk], w=[sqk])
            ss, ssk = self.bank()
            P.mm(ss[:, 0:NMEM], self.onesB, sq[:, 0:NMEM], r=[sqk], w=[ssk])
            rs, rsk = self.rot("rs")
            P.act(rs[:, 0:NMEM], ss[:, 0:NMEM], AF.Ln, r=[ssk], w=[rsk], scale=1.0 / 128, bias=EPS)
            P.act(rs[:, 0:NMEM], rs[:, 0:NMEM], AF.Exp, r=[rsk], w=[rsk], scale=-0.5)
            P.stt("dve", self.kTx[:, h, :], kp[:, 0:NMEM], self.colsB[:, 4 + l:5 + l], rs[:, 0:NMEM], ALU.mult, ALU.mult,
                  r=[kk, rsk], w=[("kTx", h)])
        for mt in range(2):
            vp, vk = self.bank()
            for c in range(DC):
                P.mm(vp, mh[:, c, mt * 128:(mt + 1) * 128], wkv[:, c, 512:1024], start=(c == 0), stop=(c == DC - 1),
                     r=[("x_wkv", 1), ("x_mh", c)], w=[vk])
            P.copy("act", self.vx[:, mt, :], vp, r=[vk], w=[("vx", mt)])
        self.score_banks = [4, 5, 6, 7]
        self.si = 0
        items = [(tb, h, mt) for tb in range(NTB) for h in range(4) for mt in range(2)]
        st = {}

        def stageA(it):
            tb, h, mt = it
            sp_, spk = self.sbank()
            P.mm(sp_, self.kTx[:, h, mt * 128:(mt + 1) * 128], qall[:, h, tb * TB:(tb + 1) * TB],
                 r=[("kTx", h), ("x_q", h, tb)], w=[spk])
            st[it] = (sp_, spk)

        def stageB(it):
            tb, h, mt = it
            g = tb * 4 + h
            sp_, spk = st.pop(it)
            pT, pTk = self.rot("pT")
            P.act(pT, sp_, AF.Exp, r=[spk], w=[pTk], scale=128 ** -0.5)
            ob, obk = self.ps[g % 2], ("ps", g % 2)
            lb, lbk = self.ps[2 + g % 2], ("ps", 2 + g % 2)
            P.mm(ob, self.vx[:, mt, h * 128:(h + 1) * 128], pT, start=(mt == 0), stop=(mt == 1),
                 r=[("vx", mt), pTk], w=[obk])
            P.mm(lb, self.onesB, pT, start=(mt == 0), stop=(mt == 1), r=[pTk], w=[lbk])
            if mt == 1:
                rs, rsk = self.rot("rs")
                self.recip_act(rs, rsk, lb, lbk)
                P.tt("dve", oall[:, h, tb * TB:(tb + 1) * TB], ob, rs, ALU.mult, r=[obk, rsk], w=[(("x_o", tb), h)])

        LOOKX = 3
        for i in range(min(LOOKX, len(items))):
            stageA(items[i])
        for i in range(len(items)):
            if i + LOOKX < len(items):
                stageA(items[i + LOOKX])
            stageB(items[i])
        self.rot_banks = list(range(8))
        self.bi = 0
        for tb in range(NTB):
            self.out_proj_block(wo, "x_wo", oall[:, :, tb * TB:(tb + 1) * TB], ("x_o", tb), 4, tb)
        self.score_banks = [2, 3, 4, 5]
        self.si = 0
        P.fence()
        A.reset(m)

    def store(self):
        P, d, A = self.P, self.d, self.A
        m = A.mark()
        yo = [A.alloc("yout", [128, D], F32) for _ in range(2)]
        keys = []
        for tt in range(NT):
            yb = yo[tt % 2]
            for hb in range(2):
                pb, pk = self.bank()
                for q in range(4):
                    c = hb * 4 + q
                    P.tr(pb[:, q * 128:(q + 1) * 128], self.xT[:, c, tt * 128:(tt + 1) * 128], self.identF,
                         r=[("xT", c, tt // 4), "identF"], w=[pk])
                eng = "dve" if hb == 0 else "act"
                P.copy(eng, yb[:, hb * 512:(hb + 1) * 512], pb, r=[pk], w=[("yout", tt % 2, hb)])
            P.dma("sp", d["y"][tt * 128:(tt + 1) * 128, :], yb, r=[("yout", tt % 2, 0), ("yout", tt % 2, 1)],
                  w=[("y", tt)])
            keys.append(("y", tt))
        P.finish("sp", keys)
        A.reset(m)

    def even_mixer(self, e, l):
        P, d, A = self.P, self.d, self.A
        m0 = A.mark()
        self.rots = {}
        self.rot_banks = list(range(8))
        cD = lambda j: self.colsD[:, e * 128 + j:e * 128 + j + 1]
        w_in_d = d["ev_w_in"][e].rearrange("(c p) n -> p c n", p=128)
        hT = A.alloc("e_hT", [128, DC, T], BF16)
        m1 = A.mark()
        self.mkrot("sq", 1, [128, DC, TB], BF16)
        self.mkrot("rs", 2, [128, TB], F32)
        for tb in range(NTB):
            self.norm_block(tb, 0, l, hT[:, :, tb * TB:(tb + 1) * TB], ("e_hT", tb))
        P.fence()
        A.reset(m1)
        self.rots = {}

        def proj(sbw, sk, tb, dst_ps, ppk):
            sl = slice(tb * TB, (tb + 1) * TB)
            for c in range(DC):
                P.mm(dst_ps, sbw[:, c, :], hT[:, c, sl], start=(c == 0), stop=(c == DC - 1),
                     r=[sk, (("e_hT", tb), c)], w=[ppk])

        if not self.cfg.get("skip_e1"):
            self._even_e1(e, l, hT, w_in_d, cD, proj, m1)
        if not self.cfg.get("skip_e2"):
            self._even_e2(e, l, hT, w_in_d, cD, proj, m1)
        P.fence()
        A.reset(m0)

    def _even_e1(self, e, l, hT, w_in_d, cD, proj, m1):
        P, d, A = self.P, self.d, self.A
        ybT = A.alloc("ybT", [128, 4, T], BF16)
        wo = A.alloc("e_wo", [128, 4, D], BF16)
        xraw = A.alloc("xraw", [128, 3 + T], F32)
        xc = A.alloc("xc", [128, T], F32)
        av = A.alloc("av", [128, T], F32)
        hs = A.alloc("hs", [128, T], F32)
        rfull = A.alloc("rfull", [128, T], F32)
        ifull = A.alloc("ifull", [128, T], F32)
        xcb = A.alloc("xcb", [128, T], BF16)
        gts = [A.alloc("gate", [128, 128], BF16) for _ in range(8)]
        c1 = A.alloc("c1", [128, 4], F32)
        self.mkrot("slab", 2, [128, DC, 256], BF16)
        self.mkrot("gg", 2, [128, TB], F32)
        slabs = []
        for cc in range(2):
            sb, sk = self.rot("slab")
            P.dma("pool", sb[:, :, 0:128], w_in_d[:, :, 2056 + cc * 128:2056 + (cc + 1) * 128], w=[(sk, 0)])
            P.dma("pool", sb[:, :, 128:256], w_in_d[:, :, 2568 + cc * 128:2568 + (cc + 1) * 128], w=[(sk, 1)])
            slabs.append((sb, sk))
        lcols = self.colsD[:, e * 128 + 76:e * 128 + 80]
        P.act(c1, lcols, AF.Exp, w=["c1"], scale=-1.0)
        P.act(c1, c1, AF.Ln, r=["c1"], w=["c1"], bias=1.0)
        P.ts("dve", c1, c1, -8.0, None, ALU.mult, r=["c1"], w=["c1"])
        P.op("dve", "memset", xraw[:, 0:3], 0.0, w=["xraw_pad"])
        for gi, g in enumerate(gts):
            P.op("dve", "memset", g, 0.0, w=[("gz", gi)])
        for cc in range(4):
            for which, nm in enumerate(["ev_gate_a_w", "ev_gate_x_w"]):
                gi = which * 4 + cc
                P.dma("pool", gts[gi][0:64, 0:64], d[nm][e, 2 * cc], r=[("gz", gi)], w=[("gate", gi, 0)])
                P.dma("pool", gts[gi][64:128, 64:128], d[nm][e, 2 * cc + 1], r=[("gz", gi)], w=[("gate", gi, 1)])
        P.dma("pool", wo, d["ev_w_out"][e].rearrange("(h p) n -> p h n", p=128)[:, 4:8, :], w=["e_wo"])
        allT = list(range(NTB))
        for cc in range(4):
            if cc < 2:
                sb, sk = slabs[cc]
            else:
                sb, sk = self.rot("slab")
                P.dma("pool", sb[:, :, 0:128], w_in_d[:, :, 2056 + cc * 128:2056 + (cc + 1) * 128], w=[(sk, 0)])
                P.dma("pool", sb[:, :, 128:256], w_in_d[:, :, 2568 + cc * 128:2568 + (cc + 1) * 128], w=[(sk, 1)])
            for tb in range(NTB):
                pp, ppk = self.bank()
                proj(sb[:, :, 0:128], (sk, 0), tb, pp, ppk)
                P.copy("act", xraw[:, 3 + tb * TB:3 + (tb + 1) * TB], pp, r=[ppk], w=[("xraw", tb)])
            xrk = [("xraw", tb) for tb in range(NTB)] + ["xraw_pad"]
            P.ts("dve", xc, xraw[:, 3:3 + T], cD(48 + 3 * 4 + cc), cD(64 + cc), ALU.mult, ALU.add, r=xrk, w=["xc"])
            for j in range(3):
                P.stt("dve", xc, xraw[:, j:j + T], cD(48 + j * 4 + cc), xc, ALU.mult, ALU.add, r=xrk + ["xc"], w=["xc"])
            P.copy("act", xcb, xc, r=["xc"], w=["xcb"])
            for tb in range(NTB):
                sl = slice(tb * TB, (tb + 1) * TB)
                rp, rpk = self.bank()
                P.mm(rp, gts[cc], xcb[:, sl], r=["xcb", ("gate", cc, 0), ("gate", cc, 1)], w=[rpk])
                ip, ipk = self.bank()
                P.mm(ip, gts[4 + cc], xcb[:, sl], r=["xcb", ("gate", 4 + cc, 0), ("gate", 4 + cc, 1)], w=[ipk])
                P.act(rfull[:, sl], rp, AF.Sigmoid, r=[rpk], w=[("rfull", tb)], bias=cD(68 + cc))
                P.act(ifull[:, sl], ip, AF.Sigmoid, r=[ipk], w=[("ifull", tb)], bias=cD(72 + cc))
            rk_all = [("rfull", tb) for tb in allT]
            ik_all = [("ifull", tb) for tb in allT]
            P.act(av, rfull, AF.Exp, r=rk_all + ["c1"], w=["av"], scale=c1[:, cc:cc + 1])
            P.tt("dve", rfull, av, av, ALU.mult, r=["av"], w=rk_all)
            P.act(rfull, rfull, AF.Sqrt, r=rk_all, w=rk_all, scale=-1.0, bias=1.0)
            P.tt("dve", ifull, ifull, xc, ALU.mult, r=ik_all + ["xc"], w=ik_all)
            P.tt("dve", xc, rfull, ifull, ALU.mult, r=rk_all + ik_all, w=["xc"])
            P.op("dve", "tensor_tensor_scan", hs, av, xc, 0.0, ALU.mult, ALU.add, r=["av", "xc"], w=["hs"])
            for tb in range(NTB):
                sl = slice(tb * TB, (tb + 1) * TB)
                gp, gpk = self.bank()
                proj(sb[:, :, 128:256], (sk, 1), tb, gp, gpk)
                gg, ggk = self.rot("gg")
                P.act(gg, gp, AF.Gelu_apprx_tanh, r=[gpk], w=[ggk])
                P.tt("dve", ybT[:, cc, sl], gg, hs[:, sl], ALU.mult, r=[ggk, "hs"], w=[(("ybT", tb), cc)])
        for tb in range(NTB):
            self.out_proj_block(wo, "e_wo", ybT[:, :, tb * TB:(tb + 1) * TB], ("ybT", tb), 4, tb)
        P.fence()
        A.reset(m1)

    def _even_e2(self, e, l, hT, w_in_d, cD, proj, m1):
        P, d, A = self.P, self.d, self.A
        self.rots = {}
        yaT = A.alloc("yaT", [128, 4, T], BF16)
        woa = A.alloc("e_woa", [128, 4, D], BF16)
        P.dma("pool", woa, d["ev_w_out"][e].rearrange("(h p) n -> p h n", p=128)[:, 0:4, :], w=["e_woa"])
        scn = ["beta", "g", "cum", "cl", "kbe", "kdec", "negb", "tmp"]
        sc = {nm: A.alloc("sc_" + nm, [128, NT, 4], F32) for nm in scn}
        scf = {nm: sc[nm].rearrange("p t h -> p (t h)") for nm in scn}
        lastc = A.alloc("lastc", [128, 4, 32], F32)
        bd = A.alloc("bdrow", [1, 8], F32)
        bdb = A.alloc("bdb", [128, 8], F32)
        slab8 = A.alloc("slab8", [128, DC, 8], BF16)
        P.dma("sp", bd[0:1, 0:4], d["ev_dt_bias"][e:e + 1, :], w=[("bd", 0)])
        P.dma("sp", bd[0:1, 4:8], d["ev_a_log"][e:e + 1, :], w=[("bd", 1)])
        pb, pk = self.bank()
        P.mm(pb[:, 0:8], self.onesF[0:1, :], bd, r=[("bd", 0), ("bd", 1)], w=[pk])
        P.copy("dve", bdb, pb[:, 0:8], r=[pk], w=["bdb"])
        P.act(bdb[:, 4:8], bdb[:, 4:8], AF.Exp, r=["bdb"], w=["bdb2"])
        P.ts("dve", bdb[:, 4:8], bdb[:, 4:8], -1.0, None, ALU.mult, r=["bdb2"], w=["bdb2"])
        P.dma("pool", slab8, w_in_d[:, :, 2048:2056], w=["slab8"])
        for tt in range(NT):
            pp, ppk = self.bank()
            for c in range(DC):
                P.mm(pp[:, 0:8], hT[:, c, tt * 128:(tt + 1) * 128], slab8[:, c, :], start=(c == 0), stop=(c == DC - 1),
                     r=["slab8", (("e_hT", tt // 4), c)], w=[ppk])
            P.act(sc["beta"][:, tt, :], pp[:, 0:4], AF.Sigmoid, r=[ppk], w=[("sc_beta", tt)])
            P.tt("dve", sc["tmp"][:, tt, :], pp[:, 4:8], bdb[:, 0:4], ALU.add, r=[ppk, "bdb"], w=[("sc_tmp", tt)])
        alltmp = [("sc_tmp", tt) for tt in range(NT)]
        P.act(scf["tmp"], scf["tmp"], AF.Exp, r=alltmp, w=["sc_tmp2"])
        P.act(scf["tmp"], scf["tmp"], AF.Ln, r=["sc_tmp2"], w=["sc_tmp2"], bias=1.0)
        for tt in range(NT):
            P.tt("dve", sc["g"][:, tt, :], sc["tmp"][:, tt, :], bdb[:, 4:8], ALU.mult, r=["sc_tmp2", "bdb2"], w=[("sc_g", tt)])
        pb, pk = self.bank()
        P.mm(pb[:, 0:64], self.maskIU, scf["g"], r=[("sc_g", tt) for tt in range(NT)], w=[pk])
        P.copy("dve", scf["cum"], pb[:, 0:64], r=[pk], w=["sc_cum"])
        pb, pk = self.bank()
        P.mm(pb[:, 0:64], self.lsel, scf["cum"], r=["sc_cum"], w=[pk])
        P.copy("dve", scf["cl"], pb[:, 0:64], r=[pk], w=["sc_cl"])
        allb = [("sc_beta", tt) for tt in range(NT)]
        P.act(scf["kbe"], scf["cum"], AF.Exp, r=["sc_cum"], w=["sc_kbe"])
        P.tt("dve", scf["kbe"], scf["kbe"], scf["beta"], ALU.mult, r=["sc_kbe"] + allb, w=["sc_kbe"])
        P.tt("dve", scf["kdec"], scf["cl"], scf["cum"], ALU.subtract, r=["sc_cl", "sc_cum"], w=["sc_kdec"])
        P.act(scf["kdec"], scf["kdec"], AF.Exp, r=["sc_kdec"], w=["sc_kdec"])
        P.ts("dve", scf["negb"], scf["beta"], -1.0, None, ALU.mult, r=allb, w=["sc_negb"])
        P.fence()
        slabq = [A.alloc("slabq", [128, DC, 128], BF16) for _ in range(1)]

        def load_head_slabs(hh):
            cols = [hh * 128]
            for i_, c_ in enumerate(cols):
                P.dma("pool", slabq[i_], w_in_d[:, :, c_:c_ + 128], w=[("slabq", i_)], nofence=True)

        load_head_slabs(0)
        mh = A.mark()
        for h in range(4):
            A.reset(mh)
            self.rots = {}
            self.rot_banks = list(range(8))
            qT = A.alloc("qT", [128, T], BF16)
            kT = A.alloc("kT", [128, T], BF16)
            vT = A.alloc("vT", [128, T], BF16)
            zs = A.alloc("zs", [128, T], BF16)
            mA = A.mark()
            raws = [A.alloc("raw", [128, 3 + T], F32) for _ in range(3)]
            cvs = [A.alloc("cv", [128, T], F32) for _ in range(2)]
            self.mkrot("sq1", 2, [128, TB], BF16)
            self.mkrot("rs", 1, [128, TB], F32)
            self.mkrot("slabv", 2, [128, DC, 128], BF16)
            kinds = [(h * 128, qT, "qT"), (512 + h * 128, kT, "kT"), (1024 + h * 128, vT, "vT")]
            for kind, (col0, dst, dn) in enumerate(kinds):
                raw = raws[kind]
                P.op("dve", "memset", raw[:, 0:3], 0.0, w=[("raw_pad", kind)])
                if kind == 0:
                    sb, sk = slabq[0], ("slabq", 0)
                else:
                    sb, sk = self.rot("slabv")
                    P.dma("pool", sb, w_in_d[:, :, col0:col0 + 128], w=[sk])
                for tb in range(NTB):
                    pp, ppk = self.bank()
                    proj(sb, sk, tb, pp, ppk)
                    P.copy("act", raw[:, 3 + tb * TB:3 + (tb + 1) * TB], pp, r=[ppk], w=[("raw", kind, tb)])
            sb, sk = self.rot("slabv")
            P.dma("pool", sb, w_in_d[:, :, 1536 + h * 128:1536 + (h + 1) * 128], w=[sk])
            zps = []
            for tb in range(NTB):
                pp, ppk = self.bank()
                proj(sb, sk, tb, pp, ppk)
                zps.append((pp, ppk))

            def conv(kind, cv, cvk, eng):
                raw = raws[kind]
                cc = kind * 4 + h
                rk_ = [("raw", kind, tb) for tb in range(NTB)] + [("raw_pad", kind)]
                P.ts(eng, cv, raw[:, 3:3 + T], cD(3 * 12 + cc), None, ALU.mult, r=rk_, w=[cvk])
                for j in range(3):
                    P.stt(eng, cv, raw[:, j:j + T], cD(j * 12 + cc), cv, ALU.mult, ALU.add, r=rk_ + [cvk], w=[cvk])

            conv(0, cvs[0], "cv0", "dve")
            conv(1, cvs[1], "cv1", "dve")
            silq = raws[0][:, 3:3 + T]
            silk = raws[1][:, 3:3 + T]
            P.act(silq, cvs[0], AF.Silu, r=["cv0"], w=["silq"] + [("raw", 0, tb) for tb in range(NTB)])
            P.act(silk, cvs[1], AF.Silu, r=["cv1"], w=["silk"] + [("raw", 1, tb) for tb in range(NTB)])
            conv(2, cvs[0], "cv0", "dve")
            for tb in range(NTB):
                pp, ppk = zps[tb]
                P.act(zs[:, tb * TB:(tb + 1) * TB], pp, AF.Silu, r=[ppk], w=[("zs", tb)])
            P.act(vT, cvs[0], AF.Silu, r=["cv0"], w=["vT"])
            blocks = []
            for sil_, silkey, dst, dn, gsc in ((silq, "silq", qT, "qT", 128 ** -0.5), (silk, "silk", kT, "kT", None)):
                for tb in range(NTB):
                    sl = slice(tb * TB, (tb + 1) * TB)
                    blocks.append((sil_[:, sl], silkey, 128, self.onesB, 1.0, gsc, dst[:, sl], (dn, tb)))
            self.pnorm_pipe(blocks)
            P.fence()
            A.reset(mA)
            self.rots = {}
            if h + 1 < 4:
                load_head_slabs(h + 1)
            kdec = A.alloc("ktm_dec", [128, NT, 128], BF16)
            utm = A.alloc("u_tm", [128, NT, 128], BF16)
            wT = A.alloc("wT", [128, T], BF16)
            qdecT = A.alloc("qdecT", [128, T], BF16)
            qkT = A.alloc("qkT", [128, NT, 128], BF16)
            S = A.alloc("S", [128, 128], F32)
            Sb = A.alloc("Sb", [128, 128], BF16)
            KCH = 5
            slots = []
            for j in range(KCH):
                sd = {}
                for nm in ("dg", "t1"):
                    sd[nm] = A.alloc("sl_" + nm, [128, 128], F32)
                sd["t2"] = sd["dg"]
                for nm in ("Dt", "Dm", "E", "p0", "p1", "pT0", "pT1", "kbe", "vtb", "AT0", "AT1"):
                    sd[nm] = A.alloc("sl_" + nm, [128, 128], BF16)
                slots.append(sd)
            self.mkrot("vn", 2, [128, 128], BF16)
            self.mkrot("ot", 2, [128, TB], F32)
            self.mkrot("sq1", 1, [128, TB], BF16)
            self.mkrot("rs", 1, [128, TB], F32)
            iF, iB = self.identF, self.identB
            self.rot_banks = [1, 2, 3, 4, 5, 6, 7]
            self.bi = 0

            def tile_chain(tt, j, h=h):
                sd = slots[j]
                K_ = lambda nm: ("sl", j, nm)
                tsl = slice(tt * 128, (tt + 1) * 128)
                tb = tt // 4
                cumc = sc["cum"][:, tt, h:h + 1]
                kp, kpk = self.bank()
                P.mm(kp[:, 0:128], kT[:, tsl], iB, r=[("kT", tb)], w=[kpk])
                P.ts("dve", sd["kbe"], kp[:, 0:128], sc["kbe"][:, tt, h:h + 1], None, ALU.mult, r=[kpk], w=[K_("kbe")])
                P.act(kdec[:, tt, :], kp[:, 0:128], AF.Copy, r=[kpk], w=[("kdec", tt)], scale=sc["kdec"][:, tt, h:h + 1])
                vp, vpk = self.bank()
                P.mm(vp[:, 0:128], vT[:, tsl], iB, r=["vT"], w=[vpk])
                P.act(sd["vtb"], vp[:, 0:128], AF.Copy, r=[vpk], w=[K_("vtb")], scale=sc["beta"][:, tt, h:h + 1])
                P.ts("pool", sd["dg"], iF, cumc, None, ALU.mult, w=[K_("dg")])
                yield
                bc, bck = self.bank()
                P.mm(bc[:, 0:128], self.onesF, sd["dg"], r=[K_("dg")], w=[bck])
                P.ts("dve", sd["t1"], bc[:, 0:128], cumc, 0.0, ALU.subtract, ALU.min, r=[bck], w=[K_("t1")])
                P.act(sd["Dt"], sd["t1"], AF.Exp, r=[K_("t1")], w=[K_("Dt")])
                P.ts("dve", sd["t2"], bc[:, 0:128], cumc, 0.0, ALU.subtract, ALU.max, r=[bck], w=[K_("t2"), K_("dg")])
                P.act(sd["Dm"], sd["t2"], AF.Exp, r=[K_("t2")], w=[K_("Dm")], scale=-1.0)
                P.act(sd["E"], bc[:, 0:128], AF.Exp, r=[bck], w=[K_("E")])
                P.act(lastc[:, h, 2 * tt:2 * tt + 2], bc[:, 63:128:64], AF.Exp, r=[bck], w=[("lastc", tt)])
                P.tt("pool", qdecT[:, tsl], qT[:, tsl], sd["E"], ALU.mult, r=[("qT", tb), K_("E")], w=[("qdecT", tt)])
                P.tt("pool", sd["Dm"], sd["Dm"], self.maskSL, ALU.mult, r=[K_("Dm")], w=[K_("Dm")])
                P.tt("pool", sd["Dt"], sd["Dt"], self.maskIU, ALU.mult, r=[K_("Dt")], w=[K_("Dt")])
                yield
                KK, KKk = self.bank()
                P.mm(KK[:, 0:128], kT[:, tsl], kT[:, tsl], r=[("kT", tb)], w=[KKk])
                QK, QKk = self.bank()
                P.mm(QK[:, 0:128], kT[:, tsl], qT[:, tsl], r=[("kT", tb), ("qT", tb)], w=[QKk])
                p, pk_ = sd["p0"], K_("p0")
                P.stt("dve", p, KK[:, 0:128], sc["negb"][:, tt, h:h + 1], sd["Dm"], ALU.mult, ALU.mult,
                      r=[KKk, K_("Dm")], w=[pk_])
                P.tt("dve", qkT[:, tt, :], QK[:, 0:128], sd["Dt"], ALU.mult, r=[QKk, K_("Dt")], w=[("qkT", tt)])
                yield
                tp, tpk = self.bank()
                P.mm(tp[:, 0:128], p, iB, r=[pk_], w=[tpk])
                pT, pTk = sd["pT0"], K_("pT0")
                P.copy("act", pT, tp[:, 0:128], r=[tpk], w=[pTk])
                AT, ATk = sd["AT0"], K_("AT0")
                P.tt("dve", AT, tp[:, 0:128], iF, ALU.add, r=[tpk], w=[ATk])
                yield
                def a_update(pcur, pcurk, AT, ATk, s_):
                    an, ank = self.bank()
                    P.mm(an[:, 0:128], iB, AT, start=True, stop=False, r=[ATk], w=[ank])
                    P.mm(an[:, 0:128], pcur, AT, start=False, stop=True, r=[pcurk, ATk], w=[ank])
                    nx_ = "1" if (s_ % 2 == 0) else "0"
                    ATn, ATnk = sd["AT" + nx_], K_("AT" + nx_)
                    P.copy("dve" if s_ % 2 else "act", ATn, an[:, 0:128], r=[ank], w=[ATnk])
                    return ATn, ATnk

                for s_ in range(5):
                    nx = "1" if (s_ % 2 == 0) else "0"
                    p2, p2k = self.bank()
                    P.mm(p2[:, 0:128], pT, p, r=[pTk, pk_], w=[p2k])
                    pn, pnk = sd["p" + nx], K_("p" + nx)
                    P.copy("act", pn, p2[:, 0:128], r=[p2k], w=[pnk])
                    if s_ < 4:
                        p2T, p2Tk = self.bank()
                        P.mm(p2T[:, 0:128], p, pT, r=[pTk, pk_], w=[p2Tk])
                        pTn, pTnk = sd["pT" + nx], K_("pT" + nx)
                        P.copy("act" if s_ % 2 else "dve", pTn, p2T[:, 0:128], r=[p2Tk], w=[pTnk])
                    if s_ > 0:
                        AT, ATk = a_update(p, pk_, AT, ATk, s_ - 1)
                    yield
                    if s_ < 4:
                        p, pk_, pT, pTk = pn, pnk, pTn, pTnk
                    else:
                        p, pk_ = pn, pnk
                AT, ATk = a_update(p, pk_, AT, ATk, 4)
                yield
                wp, wpk = self.bank()
                P.mm(wp[:, 0:128], sd["kbe"], AT, r=[K_("kbe"), ATk], w=[wpk])
                P.copy("act", wT[:, tsl], wp[:, 0:128], r=[wpk], w=[("wT", tt)])
                up, upk = self.bank()
                P.mm(up[:, 0:128], AT, sd["vtb"], r=[K_("vtb"), ATk], w=[upk])
                P.copy("dve", utm[:, tt, :], up[:, 0:128], r=[upk], w=[("utm", tt)])

            def scan_chain(h=h):
                P.op("dve", "memset", S, 0.0, w=["S"])
                P.op("dve", "memset", Sb, 0.0, w=["Sb"])
                ob, obk = self.ps[0], ("ps", 0)
                for ck in range(32):
                    tt, half = ck // 2, ck % 2
                    yield ("need", tt)
                    hp = slice(half * 64, half * 64 + 64)
                    csl = slice(ck * 64, (ck + 1) * 64)
                    col = (ck % 8) * 64
                    a_ps, ak = self.bank()
                    P.mm(a_ps[hp, 0:128], wT[:, csl], Sb, r=[("wT", tt), "Sb"], w=[ak])
                    P.mm(ob[:, col:col + 64], Sb, qdecT[:, csl], start=True, stop=False, r=["Sb", ("qdecT", tt)], w=[obk])
                    vn, vnk = self.rot("vn")
                    P.tt("dve", vn[hp, :], utm[hp, tt, :], a_ps[hp, 0:128], ALU.subtract, r=[("utm", tt), ak], w=[vnk])
                    yield None
                    s_ps, spk = self.bank()
                    P.mm(s_ps[:, 0:128], kdec[hp, tt, :], vn[hp, :], r=[("kdec", tt), vnk], w=[spk])
                    P.mm(ob[:, col:col + 64], vn[hp, :], qkT[hp, tt, half * 64:half * 64 + 64], start=False, stop=True,
                         r=[vnk, ("qkT", tt)], w=[obk])
                    P.stt("dve", Sb, S, lastc[:, h, ck:ck + 1], s_ps[:, 0:128], ALU.mult, ALU.add,
                          r=[spk, "S", ("lastc", tt)], w=["Sb"])
                    P.stt("dve", S, S, lastc[:, h, ck:ck + 1], s_ps[:, 0:128], ALU.mult, ALU.add,
                          r=[spk, "S", ("lastc", tt)], w=["S"])
                    if ck % 8 == 7:
                        tb = ck // 8
                        sl = slice(tb * TB, (tb + 1) * TB)
                        ot, otk = self.rot("ot")
                        P.copy("act", ot, ob, r=[obk], w=[otk])
                        on, onk = self.rot("ot")
                        self.pnorm(ot, otk, 128, self.onesB, 1.0 / 128, cD(80), on, onk)
                        P.tt("dve", yaT[:, h, sl], on, zs[:, sl], ALU.mult, r=[onk, ("zs", tb)], w=[(("yaT", tb), h)])
                    yield None

            pending = list(range(NT))
            active = []
            free_slots = list(range(KCH))
            done_tiles = set()
            scan = scan_chain()
            scan_wait = next(scan)
            scan_done = False
            while pending or active or not scan_done:
                while pending and free_slots:
                    tt = pending.pop(0)
                    j = free_slots.pop(0)
                    active.append((tile_chain(tt, j), tt, j))
                nxt = []
                for g, tt, j in active:
                    try:
                        next(g)
                        nxt.append((g, tt, j))
                    except StopIteration:
                        done_tiles.add(tt)
                        free_slots.append(j)
                active = nxt
                for _ in range(3):
                    if not scan_done:
                        if scan_wait is None or scan_wait[1] in done_tiles:
                            try:
                                scan_wait = next(scan)
                            except StopIteration:
                                scan_done = True
            self.rot_banks = list(range(8))
            self.bi = 0
            P.fence()
        for tb in range(NTB):
            self.out_proj_block(woa, "e_woa", yaT[:, :, tb * TB:(tb + 1) * TB], ("yaT", tb), 4, tb)
        P.fence()
        A.reset(m1)

    def build_posfb(self, qb):
        P = self.P
        pf, pfk = self.rot("posfb")
        pb, pk = self.bank()
        for j in range(4):
            tt = qb * 4 + j
            tm, tmk = self.rot("pkm")
            P.ts("dve", tm, self.pkf, self.identF[0:NT, tt:tt + 1], None, ALU.mult, w=[tmk])
            P.mm(pb[:, j * 128:(j + 1) * 128], self.onesF[0:NT, :], tm, r=[tmk], w=[pk])
        P.copy("act", pf, pb, r=[pk], w=[pfk])
        return pf, pfk

    def rope_tables(self, tb):
        P = self.P
        pf, pfk = self.build_posfb(tb)
        R = slice(64, 96)
        f0 = self.freq[R, 0:1]
        out = {}
        for nm, shift in (("sin", 0.0), ("cos", math.pi / 2)):
            ang, ak = self.rot("ang")
            P.ts("dve", ang[R, :], pf[R, :], f0, shift, ALU.mult, ALU.add, r=[pfk], w=[ak])
            ki, kik = self.rot("angi")
            P.ts("dve", ki[R, :], ang[R, :], 1.0 / (2 * math.pi), None, ALU.mult, r=[ak], w=[kik])
            kf, kfk = self.rot("ang")
            P.copy("dve", kf[R, :], ki[R, :], r=[kik], w=[kfk])
            P.stt("dve", ang[R, :], kf[R, :], -2 * math.pi, ang[R, :], ALU.mult, ALU.add, r=[kfk, ak], w=[ak])
            P.ts("dve", ang[R, :], ang[R, :], math.pi, -math.pi, ALU.min, ALU.max, r=[ak], w=[ak])
            tab, tk = self.rot(nm)
            P.act(tab[R, :], ang[R, :], AF.Sin, r=[ak], w=[tk])
            out[nm] = (tab, tk)
        return out

    def odd_mixer(self, o, l):
        P, d, A = self.P, self.d, self.A
        m0 = A.mark()
        self.rots = {}
        cC = lambda j: self.colsC[:, o * 8 + j:o * 8 + j + 1]
        SC_C = 64 ** -0.5
        SC_D = 96 ** -0.5
        slopes = [2.0 ** (-8.0 * (i + 1) / 4) for i in range(4)]
        lam_init = 0.8 - 0.6 * math.exp(-0.3 * l)
        qlatT = A.alloc("qlatT", [128, 2, T], BF16)
        kvlatT = A.alloc("kvlatT", [128, T], BF16)
        kropeT = A.alloc("kropeT", [128, T], BF16)
        neglam = A.alloc("neglam", [128, 2], F32)
        gsub = A.alloc("gsub", [128, 1], F32)
        lamt = A.alloc("lamt", [1, 4, 64], F32)
        lamp = A.alloc("lamp", [1, 2, 64], F32)
        ls = A.alloc("lams", [1, 8], F32)
        m1 = A.mark()
        qcT = A.alloc("qcT", [128, 4, T], BF16)
        kcT = A.alloc("kcT", [128, 4, T], BF16)
        vc = A.alloc("vc", [128, NT, 512], BF16)
        m2 = A.mark()
        for i, nm in enumerate(["od_lam_q1", "od_lam_k1", "od_lam_q2", "od_lam_k2"]):
            P.dma("sp", lamt[0:1, i, :], d[nm][o:o + 1, :], w=[("lamt", i)])
        P.tt("dve", lamp[0:1, 0, :], lamt[0:1, 0, :], lamt[0:1, 1, :], ALU.mult, r=[("lamt", 0), ("lamt", 1)], w=["lp0"])
        P.tt("dve", lamp[0:1, 1, :], lamt[0:1, 2, :], lamt[0:1, 3, :], ALU.mult, r=[("lamt", 2), ("lamt", 3)], w=["lp1"])
        P.op("dve", "memset", ls, 0.0, w=["ls"])
        P.op("dve", "reduce_sum", ls[0:1, 0:1], lamp[0:1, 0, :], AX.X, r=["lp0", "ls"], w=["ls0"])
        P.op("dve", "reduce_sum", ls[0:1, 1:2], lamp[0:1, 1, :], AX.X, r=["lp1", "ls"], w=["ls1"])
        P.act(ls[0:1, 0:2], ls[0:1, 0:2], AF.Exp, r=["ls0", "ls1"], w=["lse"])
        P.tt("dve", ls[0:1, 2:3], ls[0:1, 1:2], ls[0:1, 0:1], ALU.subtract, r=["lse"], w=["ls2"])
        P.ts("dve", ls[0:1, 4:5], ls[0:1, 2:3], -lam_init, None, ALU.add, r=["ls2"], w=["ls3"])
        pb, pk = self.bank()
        P.mm(pb[:, 0:2], self.onesF[0:1, :], ls[0:1, 4:6], r=["ls3"], w=[pk])
        P.copy("dve", neglam, pb[:, 0:2], r=[pk], w=["neglam"])
        P.ts("dve", gsub, cC(2), 1.0 - lam_init, None, ALU.mult, w=["gsub"])
        hT = A.alloc("o_hT", [128, DC, T], BF16)
        self.mkrot("slab", 1, [128, DC, 512], BF16)
        self.mkrot("sq", 1, [128, DC, TB], BF16)
        self.rots["slab"][0].append(self.rots["sq"][0][0])
        self.mkrot("sq1", 2, [128, TB], BF16)
        self.mkrot("rs", 2, [128, TB], F32)
        w_in_d = d["od_w_in"][o].rearrange("(c p) n -> p c n", p=128)
        for tb in range(NTB):
            self.norm_block(tb, 0, l, hT[:, :, tb * TB:(tb + 1) * TB], ("o_hT", tb))

        def load_slab(c0, n):
            sb, sk = self.rot("slab")
            wk = [sk, ("sq", 0)] if sk == ("slab", 1) else [sk]
            P.dma("pool", sb[:, :, 0:n], w_in_d[:, :, c0:c0 + n], w=wk)
            return sb, sk

        def pn_a(src, srck, ones_l, inv_n, gain, out, outk):
            sq, sqk = self.rot("sq1")
            P.act(sq, src, AF.Square, r=[srck], w=[sqk])
            return (src, srck, ones_l, inv_n, gain, out, outk, sq, sqk)

        def pn_b(stt_):
            src, srck, ones_l, inv_n, gain, out, outk, sq, sqk = stt_
            ss, ssk = self.bank()
            P.mm(ss, ones_l, sq, r=[sqk], w=[ssk])
            rs, rsk = self.rot("rs")
            P.act(rs, ss, AF.Ln, r=[ssk], w=[rsk], scale=inv_n, bias=EPS)
            P.act(rs, rs, AF.Exp, r=[rsk], w=[rsk], scale=-0.5)
            P.stt("dve", out, src, gain, rs, ALU.mult, ALU.mult, r=[srck, rsk], w=[outk])

        slab_q = load_slab(0, 512)
        slab_k = load_slab(512, 512)
        prev = None
        for dst, dname, gj, (sb, sk) in ((qcT, "qcT", 0, slab_q), (kcT, "kcT", 1, slab_k)):
            for ch in range(4):
                for tb in range(NTB):
                    sl = slice(tb * TB, (tb + 1) * TB)
                    pp, ppk = self.bank()
                    for c in range(DC):
                        P.mm(pp, sb[:, c, ch * 128:(ch + 1) * 128], hT[:, c, sl], start=(c == 0), stop=(c == DC - 1),
                             r=[sk, (("o_hT", tb), c)], w=[ppk])
                    cur = pn_a(pp, ppk, self.blockB, 1.0 / 64, cC(gj), dst[:, ch, sl], (dname, ch, tb))
                    if prev is not None:
                        pn_b(prev)
                    prev = cur
        sb, sk = load_slab(1024, 512)
        pn_b(prev)
        for tt in range(NT):
            pp, ppk = self.bank()
            for c in range(DC):
                P.mm(pp, hT[:, c, tt * 128:(tt + 1) * 128], sb[:, c, :], start=(c == 0), stop=(c == DC - 1),
                     r=[sk, (("o_hT", tt // 4), c)], w=[ppk])
            P.copy("act", vc[:, tt, :], pp, r=[ppk], w=[("vc", tt)])
        sb, sk = load_slab(1536, 416)
        for tb in range(NTB):
            sl = slice(tb * TB, (tb + 1) * TB)
            qq = [self.bank(), self.bank()]
            for ci, (pp, ppk) in enumerate(qq):
                for c in range(DC):
                    P.mm(pp, sb[:, c, ci * 128:(ci + 1) * 128], hT[:, c, sl], start=(c == 0), stop=(c == DC - 1),
                         r=[sk, (("o_hT", tb), c)], w=[ppk])
            ss, ssk = self.bank()
            for ci, (pp, ppk) in enumerate(qq):
                sq, sqk = self.rot("sq1")
                P.act(sq, pp, AF.Square, r=[ppk], w=[sqk])
                P.mm(ss, self.onesB, sq, start=(ci == 0), stop=(ci == 1), r=[sqk], w=[ssk])
            rs, rsk = self.rot("rs")
            P.act(rs, ss, AF.Ln, r=[ssk], w=[rsk], scale=1.0 / 256, bias=EPS)
            P.act(rs, rs, AF.Exp, r=[rsk], w=[rsk], scale=-0.5)
            for ci, (pp, ppk) in enumerate(qq):
                P.stt("dve", qlatT[:, ci, sl], pp, cC(3 + ci), rs, ALU.mult, ALU.mult, r=[ppk, rsk],
                      w=[("qlatT", ci, tb)])
            pp, ppk = self.bank()
            for c in range(DC):
                P.mm(pp, sb[:, c, 256:384], hT[:, c, sl], start=(c == 0), stop=(c == DC - 1),
                     r=[sk, (("o_hT", tb), c)], w=[ppk])
            self.pnorm(pp, ppk, 128, self.onesB, 1.0 / 128, cC(5), kvlatT[:, sl], ("kvlatT", tb))
            pp, ppk = self.bank()
            for c in range(DC):
                P.mm(pp[64:96, :], sb[:, c, 384:416], hT[:, c, sl], start=(c == 0), stop=(c == DC - 1),
                     r=[sk, (("o_hT", tb), c)], w=[ppk])
            P.copy("act", kropeT[64:96, sl], pp[64:96, :], r=[ppk], w=[("kropeT", tb)])
        P.fence()
        A.reset(m2)
        self.rots = {}
        wo = A.alloc("o_wo", [128, 4, D], BF16)
        P.dma("pool", wo, d["od_w_out"][o].rearrange("(h p) n -> p h n", p=128)[:, 0:4, :], w=["o_wo"])
        self.mkrot("posfb", 2, [128, TB], F32)
        self.mkrot("pkm", 2, [NT, 128], F32)
        self.mkrot("dist", 3, [128, TB], F32)
        self.mkrot("tS", 2, [128, TB], F32)
        self.mkrot("pT", 4, [128, TB], BF16)
        self.mkrot("yT", 2, [128, 4, TB], BF16)
        self.mkrot("rs", 3, [128, TB], F32)
        self.mkrot("sq1", 2, [128, TB], BF16)
        self.mkrot("ya", 2, [128, TB], F32)
        self.mkrot("yb", 1, [128, TB], F32)
        self.mkrot("lsum", 4, [128, TB], F32)
        self.mkrot("lsb", 2, [128, TB], BF16)
        self.score_banks = [4, 5, 6, 7]
        self.si = 0
        self.rot_banks = [6, 7]
        self.bi = 0
        items = [(qb, h, kt) for qb in range(NTB) for h in range(4) for kt in range(4 * qb + 4)]
        st = {}
        deferred = []

        def defer(n, fn):
            deferred.append([n, fn])

        def tick(flush=False):
            while deferred and (flush or deferred[0][0] <= 0):
                deferred.pop(0)[1]()
            for dd_ in deferred:
                dd_[0] -= 1

        def stageA(it):
            qb, h, kt = it
            if h == 0 and kt == 0:
                st[("pf", qb)] = self.build_posfb(qb)
                st[("yT", qb)] = self.rot("yT")
            pf, pfk = st[("pf", qb)]
            j = kt - 4 * qb
            c0 = max(j, 0) * 128
            dist, dkk = self.rot("dist")
            P.act(dist[:, c0:], pf[:, c0:], AF.Abs, r=[pfk], w=[dkk], bias=self.negposk[:, kt:kt + 1])
            sps = []
            for mi in range(2):
                hs = slice(mi * 64, (mi + 1) * 64)
                sp_, spk = self.sbank()
                P.mm(sp_[:, c0:], kcT[hs, h, kt * 128:(kt + 1) * 128], qcT[hs, h, qb * TB + c0:(qb + 1) * TB],
                     r=[("kcT", h, kt // 4), ("qcT", h, qb)], w=[spk])
                sps.append((sp_, spk))
            st[it] = (dist, dkk, sps)

        def stageB(it):
            qb, h, kt = it
            yT, yk = st[("yT", qb)]
            nkt = 4 * qb + 4
            j = kt - 4 * qb
            c0 = max(j, 0) * 128
            dist, dkk, sps = st.pop(it)
            cur_banks = [spk[1] for _, spk in sps]
            if kt == 0:
                st[("ls", qb, h)] = [self.rot("lsum"), self.rot("lsum")]
            lss = st[("ls", qb, h)]
            pts = []
            for mi in range(2):
                sp_, spk = sps[mi]
                tS, tSk = self.rot("tS")
                P.stt("dve", tS[:, c0:], dist[:, c0:], -slopes[h] / SC_C, sp_[:, c0:], ALU.mult, ALU.add,
                      r=[dkk, spk], w=[tSk])
                pT, pTk = self.rot("pT")
                P.act(pT[:, c0:], tS[:, c0:], AF.Exp, r=[tSk], w=[pTk], scale=SC_C)
                pts.append((pT, pTk))
            for mi in range(2):
                pT, pTk = pts[mi]
                if j >= 0:
                    P.op("dve", "memset", pT[64:128, c0:c0 + 64], 0.0, w=[pTk])
                ab = mi
                Ob, Ok = self.ps[ab], ("ps", ab)
                P.mm(Ob[:, c0:], vc[:, kt, h * 128:(h + 1) * 128], pT[:, c0:], start=(kt == 0),
                     stop=(kt == nkt - 1), r=[("vc", kt), pTk], w=[Ok])
                lb_, lbk_ = self.ps[2 + mi], ("ps", 2 + mi)
                P.mm(lb_[:, c0:], self.onesB, pT[:, c0:], start=(kt == 0), stop=(kt == nkt - 1), r=[pTk], w=[lbk_])
            self.rot_banks = cur_banks
            self.bi = 0
            tick()
            if kt == nkt - 1:
                del st[("ls", qb, h)]
                ya, yak = self.rot("ya")
                yb, ybk = self.rot("yb")
                sq, sqk = self.rot("sq1")

                def step1(lss=lss, ya=ya, yak=yak, yb=yb, ybk=ybk, sq=sq, sqk=sqk, h=h):
                    for mi, (y_, y_k) in enumerate(((ya, yak), (yb, ybk))):
                        ab = mi
                        r0, r0k = self.rot("rs")
                        self.recip_act(r0, r0k, self.ps[2 + mi], ("ps", 2 + mi))
                        P.tt("dve", y_, self.ps[ab], r0, ALU.mult, r=[("ps", ab), r0k], w=[y_k])
                    P.stt("dve", ya, yb, neglam[:, 0:1], ya, ALU.mult, ALU.add, r=[ybk, yak], w=[yak])
                    P.act(sq, ya, AF.Square, r=[yak], w=[sqk])

                def step2(h=h, ya=ya, yak=yak, sq=sq, sqk=sqk, yT=yT, yk=yk):
                    ss, ssk = self.bank()
                    P.mm(ss, self.onesB, sq, r=[sqk], w=[ssk])
                    rs, rsk = self.rot("rs")
                    P.act(rs, ss, AF.Ln, r=[ssk], w=[rsk], scale=1.0 / 128, bias=EPS)
                    P.act(rs, rs, AF.Exp, r=[rsk], w=[rsk], scale=-0.5)
                    P.stt("dve", yT[:, h, :], ya, gsub, rs, ALU.mult, ALU.mult, r=[yak, rsk], w=[(yk, h)])

                step1()
                defer(1, step2)
                if h == 3:
                    defer(3, lambda qb=qb, yT=yT, yk=yk: self.out_proj_block(wo, "o_wo", yT, yk, 4, qb))

        LOOK2 = 1
        for i in range(min(LOOK2, len(items))):
            stageA(items[i])
        for i in range(len(items)):
            if i + LOOK2 < len(items):
                stageA(items[i + LOOK2])
            stageB(items[i])
        tick(flush=True)
        self.rot_banks = list(range(8))
        self.bi = 0
        self.score_banks = [2, 3, 4, 5]
        self.si = 0
        P.fence()
        A.reset(m1)
        self.rots = {}
        qdT = A.alloc("qdT", [128, 4, T], BF16)
        kdT = A.alloc("kdT", [128, 4, T], BF16)
        vd = A.alloc("vd", [128, NT, 512], BF16)
        wuq = A.alloc("wuq", [128, 2, 384], BF16)
        wukv = A.alloc("wukv", [128, 768], BF16)
        P.dma("pool", wuq, d["od_w_uq"][o].rearrange("(c p) n -> p c n", p=128), w=["wuq"])
        P.dma("pool", wukv, d["od_w_ukv"][o], w=["wukv"])
        m3 = A.mark()
        self.mkrot("posfb", 2, [128, TB], F32)
        self.mkrot("pkm", 2, [NT, 128], F32)
        self.mkrot("ang", 3, [128, TB], F32)
        self.mkrot("angi", 1, [128, TB], I32)
        self.mkrot("sin", 2, [128, TB], F32)
        self.mkrot("cos", 2, [128, TB], F32)
        KO3 = 3
        oslots = []
        for j in range(KO3):
            sd = {"sq": A.alloc("o3_sq", [128, TB], BF16)}
            for nm in ("rs", "kraw", "rtmp", "rtmp2"):
                sd[nm] = A.alloc("o3_" + nm, [128, TB], F32)
            oslots.append(sd)
        self.rot_banks = list(range(8))
        self.bi = 0
        ones96 = self.onesB[0:96, 0:96]

        def qk_chain(tb, h, isk, cosb, cosk, sinb, sink, j):
            sd = oslots[j]
            K_ = lambda nm: ("o3s", j, nm)
            sl = slice(tb * TB, (tb + 1) * TB)
            if not isk:
                dst, dkey, gcol = qdT[:, h, sl], ("qdT", h, tb), cC(6)[0:96, :]
                qp, qpk = self.bank()
                for c in range(2):
                    P.mm(qp[0:96, :], wuq[:, c, h * 96:(h + 1) * 96], qlatT[:, c, sl], start=(c == 0), stop=(c == 1),
                         r=["wuq", ("qlatT", c, tb)], w=[qpk])
                src, srck = qp[0:96, :], [qpk]
            else:
                dst, dkey, gcol = kdT[:, h, sl], ("kdT", h, tb), cC(7)[0:96, :]
                kp, kpk = self.bank()
                P.mm(kp[0:64, :], wukv[:, h * 192:h * 192 + 64], kvlatT[:, sl], r=["wukv", ("kvlatT", tb)], w=[kpk])
                kr = sd["kraw"]
                P.copy("act", kr[0:64, :], kp[0:64, :], r=[kpk], w=[K_("kraw0")])
                P.copy("act", kr[64:96, :], kropeT[64:96, sl], r=[("kropeT", tb)], w=[K_("kraw1")])
                src, srck = kr[0:96, :], [K_("kraw0"), K_("kraw1")]
            P.act(sd["sq"][0:96, :], src, AF.Square, r=srck, w=[K_("sq")])
            yield
            ss, ssk = self.bank()
            P.mm(ss[0:96, :], ones96, sd["sq"][0:96, :], r=[K_("sq")], w=[ssk])
            rs = sd["rs"]
            P.act(rs[0:96, :], ss[0:96, :], AF.Ln, r=[ssk], w=[K_("rs")], scale=1.0 / 96, bias=EPS)
            P.act(rs[0:96, :], rs[0:96, :], AF.Exp, r=[K_("rs")], w=[K_("rs")], scale=-0.5)
            P.stt("dve", dst[0:96, :], src, gcol, rs[0:96, :], ALU.mult, ALU.mult, r=srck + [K_("rs")], w=[dkey])
            yield
            rp, rpk = self.bank()
            P.mm(rp[0:96, :], self.rotTB[0:96, 0:96], dst[0:96, :], r=[dkey], w=[rpk])
            t1, t2 = sd["rtmp"], sd["rtmp2"]
            P.tt("dve", t1[64:96, :], dst[64:96, :], cosb[64:96, :], ALU.mult, r=[dkey, cosk], w=[K_("t1")])
            P.tt("dve", t2[64:96, :], rp[64:96, :], sinb[64:96, :], ALU.mult, r=[rpk, sink], w=[K_("t2")])
            yield
            P.tt("dve", dst[64:96, :], t1[64:96, :], t2[64:96, :], ALU.add, r=[K_("t1"), K_("t2")], w=[dkey])

        for tb in range(NTB):
            tabs = self.rope_tables(tb)
            cosb, cosk = tabs["cos"]
            sinb, sink = tabs["sin"]
            makers = []
            for h in range(4):
                for isk in (False, True):
                    makers.append(lambda j, tb=tb, h=h, isk=isk, cosb=cosb, cosk=cosk, sinb=sinb, sink=sink:
                                  qk_chain(tb, h, isk, cosb, cosk, sinb, sink, j))
            self.run_chains(makers, KO3)
        wv = wukv.rearrange("p (h e) -> p h e", h=4)[:, :, 64:192]
        for tt in range(NT):
            pp, ppk = self.bank()
            P.mm(pp.rearrange("p (h e) -> p h e", h=4), kvlatT[:, tt * 128:(tt + 1) * 128], wv,
                 r=["wukv", ("kvlatT", tt // 4)], w=[ppk])
            P.copy("act", vd[:, tt, :], pp, r=[ppk], w=[("vd", tt)])
        P.fence()
        A.reset(m3)
        self.rots = {}
        wo2 = A.alloc("o_wo2", [128, 4, D], BF16)
        P.dma("pool", wo2, d["od_w_out"][o].rearrange("(h p) n -> p h n", p=128)[:, 4:8, :], w=["o_wo2"])
        self.mkrot("rs", 3, [128, TB], F32)
        self.mkrot("pT", 6, [128, TB], BF16)
        self.mkrot("yT", 2, [128, 4, TB], BF16)
        self.rot_banks = [6, 7]
        self.bi = 0
        if self.cfg.get("pe_l", True):
            self.score_banks = [4, 5, 6, 7]
            self.si = 0
        self.mkrot("lsum", 2, [128, TB], F32)
        for qb in range(NTB):
            yT, yk = self.rot("yT")
            items = [(h, kt) for h in range(4) for kt in range(4 * qb + 4)]
            st = {}

            def stageA(it, qb=qb):
                h, kt = it
                c0 = max(kt - 4 * qb, 0) * 128
                sp_, spk = self.sbank()
                P.mm(sp_[:, c0:], kdT[0:96, h, kt * 128:(kt + 1) * 128], qdT[0:96, h, qb * TB + c0:(qb + 1) * TB],
                     r=[("kdT", h, kt // 4), ("qdT", h, qb)], w=[spk])
                st[it] = (sp_, spk)

            def stageB(it, qb=qb, yT=yT, yk=yk):
                h, kt = it
                nkt = 4 * qb + 4
                j = kt - 4 * qb
                c0 = max(j, 0) * 128
                sp_, spk = st.pop(it)
                if kt == 0:
                    st[("ls", h)] = self.rot("lsum")
                ls, lsk = st[("ls", h)]
                pT, pTk = self.rot("pT")
                P.act(pT[:, c0:], sp_[:, c0:], AF.Exp, r=[spk], w=[pTk], scale=SC_D)
                if j >= 0:
                    P.op("dve", "memset", pT[64:128, c0:c0 + 64], 0.0, w=[pTk])
                Ob, Ok = self.ps[h % 2], ("ps", h % 2)
                P.mm(Ob[:, c0:], vd[:, kt, h * 128:(h + 1) * 128], pT[:, c0:], start=(kt == 0), stop=(kt == nkt - 1),
                     r=[("vd", kt), pTk], w=[Ok])
                if self.cfg.get("pe_l", True):
                    lb_, lbk_ = self.ps[2 + h % 2], ("ps", 2 + h % 2)
                    P.mm(lb_[:, c0:], self.onesB, pT[:, c0:], start=(kt == 0), stop=(kt == nkt - 1), r=[pTk], w=[lbk_])
                    if kt == nkt - 1:
                        r0, r0k = self.rot("rs")
                        self.recip_act(r0, r0k, lb_, lbk_)
                        P.tt("dve", yT[:, h, :], Ob, r0, ALU.mult, r=[Ok, r0k], w=[(yk, h)])
                        del st[("ls", h)]
                else:
                    le = "dve" if kt % 3 else "pool"
                    if kt == 0:
                        P.copy(le, ls, pT, r=[pTk], w=[lsk])
                    else:
                        P.tt(le, ls[:, c0:], ls[:, c0:], pT[:, c0:], ALU.add, r=[pTk, lsk], w=[lsk])
                    if kt == nkt - 1:
                        lp, lpk = self.bank()
                        P.mm(lp, self.onesF, ls, r=[lsk], w=[lpk])
                        r0, r0k = self.rot("rs")
                        self.recip_act(r0, r0k, lp, lpk)
                        P.tt("dve", yT[:, h, :], Ob, r0, ALU.mult, r=[Ok, r0k], w=[(yk, h)])
                        del st[("ls", h)]

            LOOK = 3
            for i in range(min(LOOK, len(items))):
                stageA(items[i])
            for i in range(len(items)):
                if i + LOOK < len(items):
                    stageA(items[i + LOOK])
                stageB(items[i])
            self.out_proj_block(wo2, "o_wo2", yT, yk, 4, qb)
        self.rot_banks = list(range(8))
        self.bi = 0
        P.fence()
        A.reset(m0)

    def build(self):
        cfg = self.cfg
        self.setup()
        for l in range(cfg.get("layers", DEPTH)):
            if cfg.get("mixer", True):
                if l % 2 == 0:
                    if not cfg.get("skip_even"):
                        self.even_mixer(l // 2, l)
                elif not cfg.get("skip_odd"):
                    self.odd_mixer(l // 2, l)
            if cfg.get("xattn", True):
                self.xattn(l)
            if cfg.get("ffn", True):
                self.ffn(l)
        self.store()
        self.P.emit()


def build_nc(cfg=None):
    nc = bass.Bass("TRN2", target_bir_lowering=False)
    b = Builder(nc, cfg or {})
    b.build()
    return nc, b


def make_in_maps(inputs, n):
    consts = host_consts()
    maps = []
    for i in range(n):
        mp = {
            "x": np.ascontiguousarray(inputs["x"][i]),
            "mem": np.ascontiguousarray(inputs["mem"][i]),
            "positions": np.ascontiguousarray(inputs["positions"][i:i + 1]),
        }
        for name, _ in WEIGHTS:
            mp[name] = np.ascontiguousarray(inputs[name])
        mp.update(consts)
        maps.append(mp)
    return maps


def kernel(**inputs):
    inputs = {k: np.asarray(v) for k, v in inputs.items()}
    n = inputs["x"].shape[0]
    nc, _ = build_nc({})
    in_maps = make_in_maps(inputs, n)
    res = run_bass_kernel_spmd(nc, in_maps, core_ids=list(range(n)))
    return np.stack([np.asarray(r["y"]) for r in res.results], axis=0).astype(np.float32)
```

```python
from contextlib import ExitStack
import math
import numpy as np
import concourse.bass as bass
import concourse.mybir as mybir
from concourse.bass_utils import run_bass_kernel_spmd

F32 = mybir.dt.float32
BF16 = mybir.dt.bfloat16
F16 = mybir.dt.float16
I32 = mybir.dt.int32
AF = mybir.ActivationFunctionType
ALU = mybir.AluOpType
AX = mybir.AxisListType

EPOCH = 30000
STRICT_SAME_ENGINE = True
NSLOT = 8
ENGS = ("pe", "act", "dve", "pool", "sp")


class Prog:
    def __init__(self, nc):
        self.nc = nc
        self.ops = {e: [] for e in ENGS}
        self.ncomp = {e: 0 for e in ENGS}
        self.ndma = {e: 0 for e in ENGS}
        self.last_w = {}
        self.readers = {}
        self.waited = {e: {} for e in ENGS}
        self.semkeys = set()
        self.sems = {}
        self.last_tok = {}
        self.gdep = None

    def add(self, eng, fn, r=(), w=(), dma=False, nofence=False):
        nowait_only = fn is None
        raw = {}
        oth = {}
        if eng != "pe" and not nowait_only:
            locks = [("pslock", k[1]) for k in r if isinstance(k, tuple) and len(k) == 2 and k[0] == "ps"]
            if locks:
                w = list(w) + locks

        def put(d, tok):
            sk, v, e2, d2 = tok
            if d.get(sk, (0,))[0] < v:
                d[sk] = (v, e2, d2)

        for k in r:
            t = self.last_w.get(k)
            if t is not None:
                put(raw, t)
        for k in w:
            t = self.last_w.get(k)
            if t is not None:
                put(oth, t)
            for sk, (v, e2, d2) in self.readers.get(k, {}).items():
                put(oth, (sk, v, e2, d2))
        if self.gdep is not None and not nofence:
            put(raw, self.gdep)
        if nowait_only:
            semkey, val = None, 0
        elif dma:
            i = self.ndma[eng]
            self.ndma[eng] += 1
            slot, rnd = i % NSLOT, i // NSLOT
            semkey = ("d", eng, slot)
            val = 16 * (rnd + 1)
            if rnd > 0:
                put(raw, (semkey, 16 * rnd, eng, True))
        else:
            i = self.ncomp[eng]
            self.ncomp[eng] += 1
            semkey = ("c", eng, i // EPOCH)
            val = i % EPOCH + 1
        tok = (semkey, val, eng, dma)
        waits = []
        wd = self.waited[eng]
        for d, is_raw in ((raw, True), (oth, False)):
            for sk, (v, e2, d2) in d.items():
                if not d2 and e2 == eng:
                    if eng == "pe" or (not is_raw and not STRICT_SAME_ENGINE):
                        continue
                if wd.get(sk, 0) >= v:
                    continue
                wd[sk] = v
                waits.append((sk, v))
        if nowait_only:
            self.ops[eng].append((None, waits, None, False))
            return None
        self.semkeys.add(semkey)
        self.last_tok[semkey] = tok
        for k in w:
            self.last_w[k] = tok
            self.readers[k] = {}
        for k in r:
            d = self.readers.setdefault(k, {})
            if d.get(semkey, (0,))[0] < val:
                d[semkey] = (val, eng, dma)
        self.ops[eng].append((fn, waits, semkey, dma))
        return tok

    def op(self, eng, name, *args, r=(), w=(), **kw):
        return self.add(eng, lambda e: getattr(e, name)(*args, **kw), r, w)

    def mm(self, out, lhsT, rhs, start=True, stop=True, r=(), w=(), **kw):
        return self.add("pe", lambda e: e.matmul(out, lhsT, rhs, start=start, stop=stop, **kw), r, w)

    def tr(self, out, in_, ident, r=(), w=()):
        return self.add("pe", lambda e: e.transpose(out, in_, ident), r, w)

    def act(self, out, in_, func, r=(), w=(), **kw):
        return self.add("act", lambda e: e.activation(out, in_, func, **kw), r, w)

    def ts(self, eng, out, in0, s1, s2, op0, op1=None, r=(), w=()):
        if op1 is None:
            return self.add(eng, lambda e: e.tensor_scalar(out, in0, s1, None, op0), r, w)
        return self.add(eng, lambda e: e.tensor_scalar(out, in0, s1, s2, op0, op1), r, w)

    def tt(self, eng, out, in0, in1, op, r=(), w=()):
        return self.add(eng, lambda e: e.tensor_tensor(out, in0, in1, op), r, w)

    def stt(self, eng, out, in0, scalar, in1, op0, op1, r=(), w=()):
        return self.add(eng, lambda e: e.scalar_tensor_tensor(out, in0, scalar, in1, op0, op1), r, w)

    def copy(self, eng, out, in_, r=(), w=()):
        if eng == "act":
            return self.add(eng, lambda e: e.copy(out, in_), r, w)
        return self.add(eng, lambda e: e.tensor_copy(out, in_), r, w)

    def dma(self, eng, out, in_, r=(), w=(), nofence=False, **kw):
        return self.add(eng, lambda e: e.dma_start(out, in_, **kw), r, w, dma=True, nofence=nofence)

    def fence(self):
        keys = []
        for sk, tok in list(self.last_tok.items()):
            k = ("_fence", sk)
            self.last_w[k] = tok
            self.readers[k] = {}
            keys.append(k)
        self.gdep = None
        tok = self.add("sp", lambda e: e.nop(), r=keys, w=[])
        self.gdep = tok

    def finish(self, eng, keys):
        self.add(eng, None, r=keys, w=())

    def emit(self):
        nc = self.nc
        with ExitStack() as st:
            for sk in sorted(self.semkeys, key=str):
                self.sems[sk] = st.enter_context(nc.semaphore("s_%s_%s_%d" % sk))
            with nc.Block() as block:
                def mk(name):
                    def body(e):
                        for fn, waits, semkey, dma in self.ops[name]:
                            for sk, v in waits:
                                e.wait_ge(self.sems[sk], v)
                            if fn is None:
                                continue
                            ins = fn(e)
                            ins.then_inc(self.sems[semkey], 16 if dma else 1)
                    return body
                block.tensor(mk("pe"))
                block.scalar(mk("act"))
                block.vector(mk("dve"))
                block.gpsimd(mk("pool"))
                block.sync(mk("sp"))


T = 2048
D = 1024
TB = 512
NTB = 4
NT = 16
DC = 8
FF = 2816
FC = 22
NMEM = 256
EPS = 1e-6
SB_BASE = 16512
SB_END = 229344
DEPTH = 4

WEIGHTS = [
    ("norm_mix", [4, 1024]), ("norm_x", [4, 1024]), ("norm_mem", [4, 1024]),
    ("x_wq", [4, 1024, 512]), ("x_wkv", [4, 1024, 1024]), ("x_q_norm", [4, 128]), ("x_k_norm", [4, 128]),
    ("x_wo", [4, 512, 1024]), ("norm_ffn", [4, 1024]), ("ffn_w_in", [4, 1024, 5632]),
    ("ffn_w_out", [4, 2816, 1024]),
    ("ev_w_in", [2, 1024, 3080]), ("ev_conv_qkv", [2, 4, 1536]), ("ev_a_log", [2, 4]), ("ev_dt_bias", [2, 4]),
    ("ev_o_norm", [2, 128]), ("ev_conv_b_w", [2, 4, 512]), ("ev_conv_b_b", [2, 512]),
    ("ev_gate_a_w", [2, 8, 64, 64]), ("ev_gate_a_b", [2, 512]), ("ev_gate_x_w", [2, 8, 64, 64]),
    ("ev_gate_x_b", [2, 512]), ("ev_lru_l", [2, 512]), ("ev_w_out", [2, 1024, 1024]),
    ("od_w_in", [2, 1024, 1952]), ("od_c_q_norm", [2, 64]), ("od_c_k_norm", [2, 64]),
    ("od_lam_q1", [2, 64]), ("od_lam_k1", [2, 64]), ("od_lam_q2", [2, 64]), ("od_lam_k2", [2, 64]),
    ("od_c_sub_norm", [2, 128]), ("od_q_lat_norm", [2, 256]), ("od_w_uq", [2, 256, 384]),
    ("od_kv_lat_norm", [2, 128]), ("od_w_ukv", [2, 128, 768]), ("od_d_q_norm", [2, 96]),
    ("od_d_k_norm", [2, 96]), ("od_w_out", [2, 1024, 1024]),
]


def host_consts():
    c = {}
    c["c_ident"] = np.eye(128, dtype=np.float32)
    rt = np.zeros((128, 128), np.float32)
    for i in range(16):
        rt[80 + i, 64 + i] = -1.0
        rt[64 + i, 80 + i] = 1.0
    c["c_rotT"] = rt
    fr = np.zeros((128, 2), np.float32)
    for i in range(16):
        f = 10000.0 ** (-(i / 16.0))
        fr[64 + i, 0] = fr[80 + i, 0] = np.float32(f)
    fr[:, 1] = fr[:, 0] / np.float32(2 * np.pi)
    c["c_freq"] = fr
    bo = np.zeros((128, 128), np.float32)
    bo[0:64, 0:64] = 1.0
    bo[64:128, 64:128] = 1.0
    c["c_blockones"] = bo
    ii = np.arange(128)
    same = (ii[:, None] // 64) == (ii[None, :] // 64)
    c["c_maskSL"] = (same & (ii[None, :] < ii[:, None])).astype(np.float32)
    c["c_maskIU"] = (same & (ii[None, :] >= ii[:, None])).astype(np.float32)
    c["c_lsel"] = (ii[:, None] == (ii[None, :] // 64) * 64 + 63).astype(np.float32)
    return c


class Arena:
    def __init__(self, nc, base, end):
        self.nc, self.p, self.end, self.n = nc, base, end, 0

    def alloc(self, name, shape, dtype):
        esz = 4 if dtype in (F32, I32) else 2
        nbytes = int(np.prod(shape[1:])) * esz
        off = (self.p + 31) // 32 * 32
        self.p = off + nbytes
        assert self.p <= self.end, ("SBUF overflow", name, self.p, self.end)
        self.n += 1
        return self.nc.alloc_sbuf_tensor_at("%s_%d" % (name, self.n), list(shape), dtype, offset=off).ap()

    def mark(self):
        return self.p

    def reset(self, m):
        self.p = m


class Builder:
    def __init__(self, nc, cfg):
        self.nc = nc
        self.cfg = cfg
        self.P = Prog(nc)
        self.d = {}
        P = self.P
        d = self.d
        d["x"] = nc.dram_tensor("x", [T, D], F32, kind="ExternalInput").ap()
        d["mem"] = nc.dram_tensor("mem", [NMEM, D], F32, kind="ExternalInput").ap()
        d["positions"] = nc.dram_tensor("positions", [1, T], I32, kind="ExternalInput").ap()
        for name, shp in WEIGHTS:
            d[name] = nc.dram_tensor(name, shp, F32, kind="ExternalInput").ap()
        for name, arr in host_consts().items():
            d[name] = nc.dram_tensor(name, list(arr.shape), F32, kind="ExternalInput").ap()
        d["y"] = nc.dram_tensor("y", [T, D], F32, kind="ExternalOutput").ap()
        self.A = Arena(nc, SB_BASE, SB_END)
        A = self.A
        self.ps = [nc.alloc_psum_tensor("psb%d" % i, [128, 512], F32).ap() for i in range(8)]
        self.bi = 0
        self.rot_banks = list(range(8))
        self.misc_banks = [6, 7]
        self.score_banks = [2, 3, 4, 5]
        self.mi = 0
        self.si = 0
        self.rots = {}
        self.xT = A.alloc("xT", [128, DC, T], F32)
        self.identF = A.alloc("identF", [128, 128], F32)
        self.identB = A.alloc("identB", [128, 128], BF16)
        self.onesB = A.alloc("onesB", [128, 128], BF16)
        self.onesF = A.alloc("onesF", [128, 128], F32)
        self.colsA = A.alloc("colsA", [128, 128], F32)
        self.colsB = A.alloc("colsB", [128, 128], F32)
        self.colsC = A.alloc("colsC", [128, 128], F32)
        self.blockB = A.alloc("blockB", [128, 128], BF16)
        self.rotTB = A.alloc("rotTB", [128, 128], BF16)
        self.freq = A.alloc("freq", [128, 2], F32)
        self.posk = A.alloc("posk", [128, NT], F32)
        self.negposk = A.alloc("negposk", [128, NT], F32)
        self.pkf = A.alloc("pkf", [NT, 128], F32)
        self.colsD = A.alloc("colsD", [128, 256], F32)
        self.maskSL = A.alloc("maskSL", [128, 128], F32)
        self.maskIU = A.alloc("maskIU", [128, 128], F32)
        self.lsel = A.alloc("lsel", [128, 128], F32)
        self.memTn = A.alloc("memTn", [128, DC, NMEM], F32)
        self.kTx = A.alloc("kTx", [128, 4, NMEM], BF16)
        self.vx = A.alloc("vx", [128, 2, 512], BF16)
        self.phase_base = A.mark()

    def bank(self):
        i = self.rot_banks[self.bi % len(self.rot_banks)]
        self.bi += 1
        return self.ps[i], ("ps", i)

    def mkrot(self, name, n, shape, dtype):
        self.rots[name] = [[self.A.alloc(name, shape, dtype) for _ in range(n)], 0]

    def rot(self, name):
        lst, i = self.rots[name]
        self.rots[name][1] = (i + 1) % len(lst)
        return lst[i], (name, i)

    def gcol(self, which, l, c):
        j = which * 32 + l * 8 + c
        return self.colsA[:, j:j + 1]

    def setup(self):
        P, d, A = self.P, self.d, self.A
        P.dma("sp", self.identF, d["c_ident"], w=["identF"])
        P.copy("dve", self.identB, self.identF, r=["identF"], w=["identB"])
        P.op("dve", "memset", self.onesB, 1.0, w=["onesB"])
        P.op("dve", "memset", self.onesF, 1.0, w=["onesF"])
        m = A.mark()
        stg = A.alloc("stg", [128, 128], F32)
        for i, nm in enumerate(["norm_mix", "norm_x", "norm_ffn", "norm_mem"]):
            P.dma("sp", stg[i * 32:(i + 1) * 32, :], d[nm].rearrange("l (c p) -> (l c) p", p=128), w=[("stg", i)])
        pb, pk = self.bank()
        P.tr(pb[:, 0:128], stg, self.identF, r=[("stg", i) for i in range(4)] + ["identF"], w=[pk])
        P.copy("dve", self.colsA, pb[:, 0:128], r=[pk], w=["colsA"])
        stg2 = A.alloc("stg2", [128, 128], F32)
        P.op("dve", "memset", stg2, 0.0, w=["stg2"])
        P.dma("sp", stg2[0:4, :], d["x_q_norm"], r=[], w=["stg2"])
        P.dma("sp", stg2[4:8, :], d["x_k_norm"], r=["stg2"], w=["stg2b"])
        pb, pk = self.bank()
        P.tr(pb[:, 0:128], stg2, self.identF, r=["stg2", "stg2b", "identF"], w=[pk])
        P.copy("dve", self.colsB, pb[:, 0:128], r=[pk], w=["colsB"])
        stg3 = A.alloc("stg3", [128, 128], F32)
        P.op("dve", "memset", stg3, 0.0, w=["stg3z"])
        k3 = []
        def ld3(row, c0, src):
            k = ("stg3", len(k3))
            k3.append(k)
            P.dma("sp", stg3[row:row + 1, c0:c0 + src.shape[1]], src, r=["stg3z"], w=[k])
        for o in range(2):
            for half in range(2):
                ld3(o * 8 + 0, half * 64, d["od_c_q_norm"][o:o + 1, :])
                ld3(o * 8 + 1, half * 64, d["od_c_k_norm"][o:o + 1, :])
            ld3(o * 8 + 2, 0, d["od_c_sub_norm"][o:o + 1, :])
            ld3(o * 8 + 3, 0, d["od_q_lat_norm"][o:o + 1, 0:128])
            ld3(o * 8 + 4, 0, d["od_q_lat_norm"][o:o + 1, 128:256])
            ld3(o * 8 + 5, 0, d["od_kv_lat_norm"][o:o + 1, :])
            ld3(o * 8 + 6, 0, d["od_d_q_norm"][o:o + 1, :])
            ld3(o * 8 + 7, 0, d["od_d_k_norm"][o:o + 1, :])
        pb, pk = self.bank()
        P.tr(pb[:, 0:128], stg3, self.identF, r=k3 + ["identF"], w=[pk])
        P.copy("dve", self.colsC, pb[:, 0:128], r=[pk], w=["colsC"])
        P.dma("sp", self.maskSL, d["c_maskSL"], w=["maskSL"])
        P.dma("sp", self.maskIU, d["c_maskIU"], w=["maskIU"])
        P.dma("sp", self.lsel, d["c_lsel"], w=["lsel"])
        for e in range(2):
            st4 = A.alloc("stg4", [128, 128], F32)
            P.op("dve", "memset", st4, 0.0, w=[("st4z", e)])
            k4 = []
            def ld4(r0, src):
                k = ("stg4", e, len(k4))
                k4.append(k)
                P.dma("sp", st4[r0:r0 + src.shape[0], :], src, r=[("st4z", e)], w=[k])
            ld4(0, d["ev_conv_qkv"][e].rearrange("j (c p) -> (j c) p", p=128))
            ld4(48, d["ev_conv_b_w"][e].rearrange("j (c p) -> (j c) p", p=128))
            ld4(64, d["ev_conv_b_b"][e:e + 1, :].rearrange("o (c p) -> (o c) p", p=128))
            ld4(68, d["ev_gate_a_b"][e:e + 1, :].rearrange("o (c p) -> (o c) p", p=128))
            ld4(72, d["ev_gate_x_b"][e:e + 1, :].rearrange("o (c p) -> (o c) p", p=128))
            ld4(76, d["ev_lru_l"][e:e + 1, :].rearrange("o (c p) -> (o c) p", p=128))
            ld4(80, d["ev_o_norm"][e:e + 1, :])
            pb, pk = self.bank()
            P.tr(pb[:, 0:128], st4, self.identF, r=k4 + ["identF"], w=[pk])
            P.copy("dve", self.colsD[:, e * 128:(e + 1) * 128], pb[:, 0:128], r=[pk], w=[("colsD", e)])
        cst = A.alloc("cst", [128, 128], F32)
        P.dma("sp", cst, d["c_blockones"], w=["cst"])
        P.copy("dve", self.blockB, cst, r=["cst"], w=["blockB"])
        cst2 = A.alloc("cst2", [128, 128], F32)
        P.dma("sp", cst2, d["c_rotT"], w=["cst2"])
        P.copy("dve", self.rotTB, cst2, r=["cst2"], w=["rotTB"])
        P.dma("sp", self.freq, d["c_freq"], w=["freq"])
        pk_i = A.alloc("pk_i", [NT, 128], I32)
        pk_f = self.pkf
        P.dma("sp", pk_i, d["positions"].rearrange("o (t p) -> (o t) p", p=128), w=["pk_i"])
        P.copy("dve", pk_f, pk_i, r=["pk_i"], w=["pk_f"])
        pb, pk = self.bank()
        P.tr(pb[:, 0:NT], pk_f, self.identF[0:NT, 0:NT], r=["pk_f", "identF"], w=[pk])
        P.copy("dve", self.posk, pb[:, 0:NT], r=[pk], w=["posk"])
        P.ts("dve", self.negposk, self.posk, -1.0, None, ALU.mult, r=["posk"], w=["negposk"])
        xin = [A.alloc("xin", [128, D], F32) for _ in range(2)]
        for tt in range(NT):
            xb = xin[tt % 2]
            xk = ("xin", tt % 2)
            P.dma("sp", xb, d["x"][tt * 128:(tt + 1) * 128, :], w=[xk])
            for hb in range(2):
                pb, pk = self.bank()
                for q in range(4):
                    c = hb * 4 + q
                    P.tr(pb[:, q * 128:(q + 1) * 128], xb[:, c * 128:(c + 1) * 128], self.identF,
                         r=[xk, "identF"], w=[pk])
                eng = "dve" if hb == 0 else "act"
                P.copy(eng, self.xT[:, hb * 4:(hb + 1) * 4, tt * 128:(tt + 1) * 128],
                       pb.rearrange("p (a b) -> p a b", a=4),
                       r=[pk], w=[("xT", c, tt // 4) for c in range(hb * 4, hb * 4 + 4)])
        mm_ = [A.alloc("memin", [128, D], F32) for _ in range(2)]
        msq = A.alloc("msq", [128, D], F32)
        mss = A.alloc("mss", [128, 2], F32)
        for mt in range(2):
            P.dma("sp", mm_[mt], d["mem"][mt * 128:(mt + 1) * 128, :], w=[("memin", mt)])
            P.act(msq, mm_[mt], AF.Square, r=[("memin", mt)], w=["msq"], accum_out=mss[:, mt:mt + 1])
            P.act(mss[:, mt:mt + 1], mss[:, mt:mt + 1], AF.Sqrt, r=["msq"], w=[("mss", mt)], scale=1.0 / D, bias=EPS)
            P.op("dve", "reciprocal", mss[:, mt:mt + 1], mss[:, mt:mt + 1], r=[("mss", mt)], w=[("mss", mt)])
            P.ts("dve", mm_[mt], mm_[mt], mss[:, mt:mt + 1], None, ALU.mult, r=[("memin", mt), ("mss", mt)],
                 w=[("memin", mt)])
            for hb in range(2):
                pb, pk = self.bank()
                for q in range(4):
                    c = hb * 4 + q
                    P.tr(pb[:, q * 128:(q + 1) * 128], mm_[mt][:, c * 128:(c + 1) * 128], self.identF,
                         r=[("memin", mt), "identF"], w=[pk])
                P.copy("dve", self.memTn[:, hb * 4:(hb + 1) * 4, mt * 128:(mt + 1) * 128],
                       pb.rearrange("p (a b) -> p a b", a=4), r=[pk], w=["memTn"])
        P.fence()
        A.reset(m)

    def norm_block(self, tb, which, l, hT_out, hkey):
        P = self.P
        sl = slice(tb * TB, (tb + 1) * TB)
        sq, sqk = self.rot("sq")
        P.act(sq, self.xT[:, :, sl], AF.Square, r=[("xT", c, tb) for c in range(DC)], w=[sqk])
        ss, ssk = self.bank()
        for c in range(DC):
            P.mm(ss, self.onesB, sq[:, c, :], start=(c == 0), stop=(c == DC - 1), r=[sqk, "onesB"], w=[ssk])
        rs, rsk = self.rot("rs")
        P.act(rs, ss, AF.Ln, r=[ssk], w=[rsk], scale=1.0 / D, bias=EPS)
        P.act(rs, rs, AF.Exp, r=[rsk], w=[rsk], scale=-0.5)
        for c in range(DC):
            P.stt("dve", hT_out[:, c, :], self.xT[:, c, sl], self.gcol(which, l, c), rs, ALU.mult, ALU.mult,
                  r=[("xT", c, tb), rsk, "colsA"], w=[(hkey, c)])

    def pnorm(self, src, srck, npart, ones_l, inv_n, gain, out, outk):
        P = self.P
        srcks = srck if isinstance(srck, list) else [srck]
        sq, sqk = self.rot("sq1")
        P.act(sq[0:npart, :], src, AF.Square, r=srcks, w=[sqk])
        ss, ssk = self.bank()
        P.mm(ss[0:npart, :], ones_l, sq[0:npart, :], r=[sqk, "onesB"], w=[ssk])
        rs, rsk = self.rot("rs")
        P.act(rs[0:npart, :], ss[0:npart, :], AF.Ln, r=[ssk], w=[rsk], scale=inv_n, bias=EPS)
        P.act(rs[0:npart, :], rs[0:npart, :], AF.Exp, r=[rsk], w=[rsk], scale=-0.5)
        if gain is None:
            P.tt("dve", out, src, rs[0:npart, :], ALU.mult, r=srcks + [rsk], w=[outk])
        elif isinstance(gain, float):
            P.stt("dve", out, src, gain, rs[0:npart, :], ALU.mult, ALU.mult, r=srcks + [rsk], w=[outk])
        else:
            P.stt("dve", out, src, gain, rs[0:npart, :], ALU.mult, ALU.mult, r=srcks + [rsk], w=[outk])

    def pnorm_pipe(self, blocks):
        P = self.P
        prev = None

        def part_b(stt_):
            (src, srck, npart, ones_l, inv_n, gain, out, outk), sq, sqk = stt_
            srcks = srck if isinstance(srck, list) else [srck]
            ss, ssk = self.bank()
            P.mm(ss[0:npart, :], ones_l, sq[0:npart, :], r=[sqk], w=[ssk])
            rs, rsk = self.rot("rs")
            P.act(rs[0:npart, :], ss[0:npart, :], AF.Ln, r=[ssk], w=[rsk], scale=inv_n, bias=EPS)
            P.act(rs[0:npart, :], rs[0:npart, :], AF.Exp, r=[rsk], w=[rsk], scale=-0.5)
            if gain is None:
                P.tt("dve", out, src, rs[0:npart, :], ALU.mult, r=srcks + [rsk], w=[outk])
            else:
                P.stt("dve", out, src, gain, rs[0:npart, :], ALU.mult, ALU.mult, r=srcks + [rsk], w=[outk])

        for b in blocks:
            src, srck, npart = b[0], b[1], b[2]
            srcks = srck if isinstance(srck, list) else [srck]
            sq, sqk = self.rot("sq1")
            P.act(sq[0:npart, :], src, AF.Square, r=srcks, w=[sqk])
            if prev is not None:
                part_b(prev)
            prev = (b, sq, sqk)
        if prev is not None:
            part_b(prev)

    def run_chains(self, makers, K):
        pending = list(makers)
        active = []
        free = list(range(K))
        while pending or active:
            while pending and free:
                j = free.pop(0)
                active.append((pending.pop(0)(j), j))
            nxt = []
            for g, j in active:
                try:
                    next(g)
                    nxt.append((g, j))
                except StopIteration:
                    free.append(j)
            active = nxt

    def recip_act(self, out, outk, src, srck):
        P = self.P
        P.act(out, src, AF.Ln, r=[srck], w=[outk])
        P.act(out, out, AF.Exp, r=[outk], w=[outk], scale=-1.0)

    def mbank(self):
        i = self.misc_banks[self.mi % len(self.misc_banks)]
        self.mi += 1
        return self.ps[i], ("ps", i)

    def sbank(self):
        i = self.score_banks[self.si % len(self.score_banks)]
        self.si += 1
        return self.ps[i], ("ps", i)

    def ffn(self, l):
        P, d, A = self.P, self.d, self.A
        m = A.mark()
        self.rots = {}
        hT = A.alloc("ffn_hT", [128, DC, 1024], BF16)
        act = A.alloc("ffn_act", [128, FC, 1024], BF16)
        self.mkrot("win", 2, [128, DC, 1024], BF16)
        self.mkrot("wout", 2, [128, FC, 128], BF16)
        self.mkrot("sq", 1, [128, DC, TB], BF16)
        self.mkrot("rs", 2, [128, TB], F32)
        self.mkrot("sg", 2, [128, TB], F32)
        w_in_d = d["ffn_w_in"][l].rearrange("(c p) n -> p c n", p=128)
        w_out_d = d["ffn_w_out"][l].rearrange("(f p) n -> p f n", p=128)
        SLW = 512
        nsl = (FF + SLW - 1) // SLW
        for half in range(2):
            if half == 0:
                for j in range(2):
                    self.norm_block(j, 2, l, hT[:, :, j * TB:(j + 1) * TB], ("ffn_hT", j))
            for s in range(nsl):
                c0 = s * SLW
                ncol = min(SLW, FF - c0)
                wb, wk = self.rot("win")
                P.dma("pool", wb[:, :, 0:ncol], w_in_d[:, :, c0:c0 + ncol], w=[(wk, "g")])
                P.dma("pool", wb[:, :, SLW:SLW + ncol], w_in_d[:, :, FF + c0:FF + c0 + ncol], w=[(wk, "u")])
                for fi in range(ncol // 128):
                    f = (c0 // 128) + fi
                    for j in range(2):
                        gps, gk = self.bank()
                        ups, uk = self.bank()
                        for c in range(DC):
                            P.mm(gps, wb[:, c, fi * 128:(fi + 1) * 128], hT[:, c, j * TB:(j + 1) * TB],
                                 start=(c == 0), stop=(c == DC - 1), r=[(wk, "g"), (("ffn_hT", j), c)], w=[gk])
                        for c in range(DC):
                            P.mm(ups, wb[:, c, SLW + fi * 128:SLW + (fi + 1) * 128], hT[:, c, j * TB:(j + 1) * TB],
                                 start=(c == 0), stop=(c == DC - 1), r=[(wk, "u"), (("ffn_hT", j), c)], w=[uk])
                        sg, sgk = self.rot("sg")
                        P.act(sg, gps, AF.Silu, r=[gk], w=[sgk])
                        P.tt("dve", act[:, f, j * TB:(j + 1) * TB], sg, ups, ALU.mult, r=[sgk, uk], w=[("act", f, j)])
            if half == 0:
                for j in range(2):
                    self.norm_block(2 + j, 2, l, hT[:, :, j * TB:(j + 1) * TB], ("ffn_hT", j))
            for dc in range(DC):
                wo, wok = self.rot("wout")
                P.dma("pool", wo, w_out_d[:, :, dc * 128:(dc + 1) * 128], w=[wok])
                for j in range(2):
                    tb = half * 2 + j
                    sl = slice(tb * TB, (tb + 1) * TB)
                    yps, yk = self.bank()
                    for f in range(FC):
                        P.mm(yps, wo[:, f, :], act[:, f, j * TB:(j + 1) * TB], start=(f == 0), stop=(f == FC - 1),
                             r=[wok, ("act", f, j)], w=[yk])
                    P.tt("dve", self.xT[:, dc, sl], self.xT[:, dc, sl], yps, ALU.add, r=[("xT", dc, tb), yk],
                         w=[("xT", dc, tb)])
        P.fence()
        A.reset(m)

    def out_proj_block(self, wo, wok, oT, ok, nk, tb):
        P = self.P
        sl = slice(tb * TB, (tb + 1) * TB)
        for dc in range(DC):
            yps, yk = self.bank()
            for h in range(nk):
                P.mm(yps, wo[:, h, dc * 128:(dc + 1) * 128], oT[:, h, :], start=(h == 0), stop=(h == nk - 1),
                     r=[wok, (ok, h)], w=[yk])
            P.tt("dve", self.xT[:, dc, sl], self.xT[:, dc, sl], yps, ALU.add, r=[("xT", dc, tb), yk],
                 w=[("xT", dc, tb)])

    def xattn(self, l):
        P, d, A = self.P, self.d, self.A
        m = A.mark()
        self.rots = {}
        wq = A.alloc("x_wq", [128, DC, 512], BF16)
        wo = A.alloc("x_wo", [128, 4, D], BF16)
        wkv = A.alloc("x_wkv", [128, DC, D], BF16)
        mh = A.alloc("x_mh", [128, DC, NMEM], BF16)
        hT = A.alloc("x_hT", [128, DC, T], BF16)
        qall = A.alloc("x_q", [128, 4, T], BF16)
        oall = A.alloc("x_o", [128, 4, T], BF16)
        self.mkrot("sq", 1, [128, DC, TB], BF16)
        self.mkrot("sq1", 3, [128, TB], BF16)
        self.mkrot("rs", 3, [128, TB], F32)
        self.mkrot("pT", 6, [128, TB], BF16)
        P.dma("pool", wq, d["x_wq"][l].rearrange("(c p) n -> p c n", p=128), w=["x_wq"])
        for hh in range(2):
            P.dma("pool", wkv[:, :, hh * 512:(hh + 1) * 512],
                  d["x_wkv"][l].rearrange("(c p) n -> p c n", p=128)[:, :, hh * 512:(hh + 1) * 512], w=[("x_wkv", hh)])
        P.dma("pool", wo, d["x_wo"][l].rearrange("(h p) n -> p h n", p=128), w=["x_wo"])
        self.rot_banks = list(range(8))
        self.bi = 0
        for tb in range(NTB):
            self.norm_block(tb, 1, l, hT[:, :, tb * TB:(tb + 1) * TB], ("x_hT", tb))
        prev = None

        def q_b(stt_):
            qp, qk, sq, sqk, h, tb = stt_
            ss, ssk = self.bank()
            P.mm(ss, self.onesB, sq, r=[sqk], w=[ssk])
            rs, rsk = self.rot("rs")
            P.act(rs, ss, AF.Ln, r=[ssk], w=[rsk], scale=1.0 / 128, bias=EPS)
            P.act(rs, rs, AF.Exp, r=[rsk], w=[rsk], scale=-0.5)
            P.stt("dve", qall[:, h, tb * TB:(tb + 1) * TB], qp, self.colsB[:, l:l + 1], rs, ALU.mult, ALU.mult,
                  r=[qk, rsk], w=[("x_q", h, tb)])

        for tb in range(NTB):
            sl = slice(tb * TB, (tb + 1) * TB)
            for h in range(4):
                qp, qk = self.bank()
                for c in range(DC):
                    P.mm(qp, wq[:, c, h * 128:(h + 1) * 128], hT[:, c, sl], start=(c == 0), stop=(c == DC - 1),
                         r=["x_wq", (("x_hT", tb), c)], w=[qk])
                sq, sqk = self.rot("sq1")
                P.act(sq, qp, AF.Square, r=[qk], w=[sqk])
                if prev is not None:
                    q_b(prev)
                prev = (qp, qk, sq, sqk, h, tb)
        for c in range(DC):
            P.ts("dve", mh[:, c, :], self.memTn[:, c, :], self.gcol(3, l, c), None, ALU.mult, w=[("x_mh", c)])
        q_b(prev)
        for h in range(4):
            kp, kk = self.bank()
            for c in range(DC):
                P.mm(kp[:, 0:NMEM], wkv[:, c, h * 128:(h + 1) * 128], mh[:, c, :], start=(c == 0), stop=(c == DC - 1),
                     r=[("x_wkv", 0), ("x_mh", c)], w=[kk])
            sq, sqk = self.rot("sq1")
            P.act(sq[:, 0:NMEM], kp[:, 0:NMEM], AF.Square, r=[kk], w=[sqk])
            ss, ssk = self.bank()
            P.mm(ss[:, 0:NMEM], self.onesB, sq[:, 0:NMEM], r=[sqk], w=[ssk])
            rs, rsk = self.rot("rs")
            P.act(rs[:, 0:NMEM], ss[:, 0:NMEM], AF.Ln, r=[ssk], w=[rsk], scale=1.0 / 128, bias=EPS)
            P.act(rs[:, 0:NMEM], rs[:, 0:NMEM], AF.Exp, r=[rsk], w=[rsk], scale=-0.5)
            P.stt("dve", self.kTx[:, h, :], kp[:, 0:NMEM], self.colsB[:, 4 + l:5 + l], rs[:, 0:NMEM], ALU.mult, ALU.mult,
                  r=[kk, rsk], w=[("kTx", h)])
        for mt in range(2):
            vp, vk = self.bank()
            for c in range(DC):
                P.mm(vp, mh[:, c, mt * 128:(mt + 1) * 128], wkv[:, c, 512:1024], start=(c == 0), stop=(c == DC - 1),
                     r=[("x_wkv", 1), ("x_mh", c)], w=[vk])
            P.copy("act", self.vx[:, mt, :], vp, r=[vk], w=[("vx", mt)])
        self.score_banks = [4, 5, 6, 7]
        self.si = 0
        items = [(tb, h, mt) for tb in range(NTB) for h in range(4) for mt in range(2)]
        st = {}

        def stageA(it):
            tb, h, mt = it
            sp_, spk = self.sbank()
            P.mm(sp_, self.kTx[:, h, mt * 128:(mt + 1) * 128], qall[:, h, tb * TB:(tb + 1) * TB],
                 r=[("kTx", h), ("x_q", h, tb)], w=[spk])
            st[it] = (sp_, spk)

        def stageB(it):
            tb, h, mt = it
            g = tb * 4 + h
            sp_, spk = st.pop(it)
            pT, pTk = self.rot("pT")
            P.act(pT, sp_, AF.Exp, r=[spk], w=[pTk], scale=128 ** -0.5)
            ob, obk = self.ps[g % 2], ("ps", g % 2)
            lb, lbk = self.ps[2 + g % 2], ("ps", 2 + g % 2)
            P.mm(ob, self.vx[:, mt, h * 128:(h + 1) * 128], pT, start=(mt == 0), stop=(mt == 1),
                 r=[("vx", mt), pTk], w=[obk])
            P.mm(lb, self.onesB, pT, start=(mt == 0), stop=(mt == 1), r=[pTk], w=[lbk])
            if mt == 1:
                rs, rsk = self.rot("rs")
                self.recip_act(rs, rsk, lb, lbk)
                P.tt("dve", oall[:, h, tb * TB:(tb + 1) * TB], ob, rs, ALU.mult, r=[obk, rsk], w=[(("x_o", tb), h)])

        LOOKX = 3
        for i in range(min(LOOKX, len(items))):
            stageA(items[i])
        for i in range(len(items)):
            if i + LOOKX < len(items):
                stageA(items[i + LOOKX])
            stageB(items[i])
        self.rot_banks = list(range(8))
        self.bi = 0
        for tb in range(NTB):
            self.out_proj_block(wo, "x_wo", oall[:, :, tb * TB:(tb + 1) * TB], ("x_o", tb), 4, tb)
        self.score_banks = [2, 3, 4, 5]
        self.si = 0
        P.fence()
        A.reset(m)

    def store(self):
        P, d, A = self.P, self.d, self.A
        m = A.mark()
        yo = [A.alloc("yout", [128, D], F32) for _ in range(2)]
        keys = []
        for tt in range(NT):
            yb = yo[tt % 2]
            for hb in range(2):
                pb, pk = self.bank()
                for q in range(4):
                    c = hb * 4 + q
                    P.tr(pb[:, q * 128:(q + 1) * 128], self.xT[:, c, tt * 128:(tt + 1) * 128], self.identF,
                         r=[("xT", c, tt // 4), "identF"], w=[pk])
                eng = "dve" if hb == 0 else "act"
                P.copy(eng, yb[:, hb * 512:(hb + 1) * 512], pb, r=[pk], w=[("yout", tt % 2, hb)])
            P.dma("sp", d["y"][tt * 128:(tt + 1) * 128, :], yb, r=[("yout", tt % 2, 0), ("yout", tt % 2, 1)],
                  w=[("y", tt)])
            keys.append(("y", tt))
        P.finish("sp", keys)
        A.reset(m)

    def even_mixer(self, e, l):
        P, d, A = self.P, self.d, self.A
        m0 = A.mark()
        self.rots = {}
        self.rot_banks = list(range(8))
        cD = lambda j: self.colsD[:, e * 128 + j:e * 128 + j + 1]
        w_in_d = d["ev_w_in"][e].rearrange("(c p) n -> p c n", p=128)
        hT = A.alloc("e_hT", [128, DC, T], BF16)
        m1 = A.mark()
        self.mkrot("sq", 1, [128, DC, TB], BF16)
        self.mkrot("rs", 2, [128, TB], F32)
        for tb in range(NTB):
            self.norm_block(tb, 0, l, hT[:, :, tb * TB:(tb + 1) * TB], ("e_hT", tb))
        P.fence()
        A.reset(m1)
        self.rots = {}

        def proj(sbw, sk, tb, dst_ps, ppk):
            sl = slice(tb * TB, (tb + 1) * TB)
            for c in range(DC):
                P.mm(dst_ps, sbw[:, c, :], hT[:, c, sl], start=(c == 0), stop=(c == DC - 1),
                     r=[sk, (("e_hT", tb), c)], w=[ppk])

        if not self.cfg.get("skip_e1"):
            self._even_e1(e, l, hT, w_in_d, cD, proj, m1)
        if not self.cfg.get("skip_e2"):
            self._even_e2(e, l, hT, w_in_d, cD, proj, m1)
        P.fence()
        A.reset(m0)

    def _even_e1(self, e, l, hT, w_in_d, cD, proj, m1):
        P, d, A = self.P, self.d, self.A
        ybT = A.alloc("ybT", [128, 4, T], BF16)
        wo = A.alloc("e_wo", [128, 4, D], BF16)
        xraw = A.alloc("xraw", [128, 3 + T], F32)
        xc = A.alloc("xc", [128, T], F32)
        av = A.alloc("av", [128, T], F32)
        hs = A.alloc("hs", [128, T], F32)
        rfull = A.alloc("rfull", [128, T], F32)
        ifull = A.alloc("ifull", [128, T], F32)
        xcb = A.alloc("xcb", [128, T], BF16)
        gts = [A.alloc("gate", [128, 128], BF16) for _ in range(8)]
        c1 = A.alloc("c1", [128, 4], F32)
        self.mkrot("slab", 2, [128, DC, 256], BF16)
        self.mkrot("gg", 2, [128, TB], F32)
        slabs = []
        for cc in range(2):
            sb, sk = self.rot("slab")
            P.dma("pool", sb[:, :, 0:128], w_in_d[:, :, 2056 + cc * 128:2056 + (cc + 1) * 128], w=[(sk, 0)])
            P.dma("pool", sb[:, :, 128:256], w_in_d[:, :, 2568 + cc * 128:2568 + (cc + 1) * 128], w=[(sk, 1)])
            slabs.append((sb, sk))
        lcols = self.colsD[:, e * 128 + 76:e * 128 + 80]
        P.act(c1, lcols, AF.Exp, w=["c1"], scale=-1.0)
        P.act(c1, c1, AF.Ln, r=["c1"], w=["c1"], bias=1.0)
        P.ts("dve", c1, c1, -8.0, None, ALU.mult, r=["c1"], w=["c1"])
        P.op("dve", "memset", xraw[:, 0:3], 0.0, w=["xraw_pad"])
        for gi, g in enumerate(gts):
            P.op("dve", "memset", g, 0.0, w=[("gz", gi)])
        for cc in range(4):
            for which, nm in enumerate(["ev_gate_a_w", "ev_gate_x_w"]):
                gi = which * 4 + cc
                P.dma("pool", gts[gi][0:64, 0:64], d[nm][e, 2 * cc], r=[("gz", gi)], w=[("gate", gi, 0)])
                P.dma("pool", gts[gi][64:128, 64:128], d[nm][e, 2 * cc + 1], r=[("gz", gi)], w=[("gate", gi, 1)])
        P.dma("pool", wo, d["ev_w_out"][e].rearrange("(h p) n -> p h n", p=128)[:, 4:8, :], w=["e_wo"])
        allT = list(range(NTB))
        for cc in range(4):
            if cc < 2:
                sb, sk = slabs[cc]
            else:
                sb, sk = self.rot("slab")
                P.dma("pool", sb[:, :, 0:128], w_in_d[:, :, 2056 + cc * 128:2056 + (cc + 1) * 128], w=[(sk, 0)])
                P.dma("pool", sb[:, :, 128:256], w_in_d[:, :, 2568 + cc * 128:2568 + (cc + 1) * 128], w=[(sk, 1)])
            for tb in range(NTB):
                pp, ppk = self.bank()
                proj(sb[:, :, 0:128], (sk, 0), tb, pp, ppk)
                P.copy("act", xraw[:, 3 + tb * TB:3 + (tb + 1) * TB], pp, r=[ppk], w=[("xraw", tb)])
            xrk = [("xraw", tb) for tb in range(NTB)] + ["xraw_pad"]
            P.ts("dve", xc, xraw[:, 3:3 + T], cD(48 + 3 * 4 + cc), cD(64 + cc), ALU.mult, ALU.add, r=xrk, w=["xc"])
            for j in range(3):
                P.stt("dve", xc, xraw[:, j:j + T], cD(48 + j * 4 + cc), xc, ALU.mult, ALU.add, r=xrk + ["xc"], w=["xc"])
            P.copy("act", xcb, xc, r=["xc"], w=["xcb"])
            for tb in range(NTB):
                sl = slice(tb * TB, (tb + 1) * TB)
                rp, rpk = self.bank()
                P.mm(rp, gts[cc], xcb[:, sl], r=["xcb", ("gate", cc, 0), ("gate", cc, 1)], w=[rpk])
                ip, ipk = self.bank()
                P.mm(ip, gts[4 + cc], xcb[:, sl], r=["xcb", ("gate", 4 + cc, 0), ("gate", 4 + cc, 1)], w=[ipk])
                P.act(rfull[:, sl], rp, AF.Sigmoid, r=[rpk], w=[("rfull", tb)], bias=cD(68 + cc))
                P.act(ifull[:, sl], ip, AF.Sigmoid, r=[ipk], w=[("ifull", tb)], bias=cD(72 + cc))
            rk_all = [("rfull", tb) for tb in allT]
            ik_all = [("ifull", tb) for tb in allT]
            P.act(av, rfull, AF.Exp, r=rk_all + ["c1"], w=["av"], scale=c1[:, cc:cc + 1])
            P.tt("dve", rfull, av, av, ALU.mult, r=["av"], w=rk_all)
            P.act(rfull, rfull, AF.Sqrt, r=rk_all, w=rk_all, scale=-1.0, bias=1.0)
            P.tt("dve", ifull, ifull, xc, ALU.mult, r=ik_all + ["xc"], w=ik_all)
            P.tt("dve", xc, rfull, ifull, ALU.mult, r=rk_all + ik_all, w=["xc"])
            P.op("dve", "tensor_tensor_scan", hs, av, xc, 0.0, ALU.mult, ALU.add, r=["av", "xc"], w=["hs"])
            for tb in range(NTB):
                sl = slice(tb * TB, (tb + 1) * TB)
                gp, gpk = self.bank()
                proj(sb[:, :, 128:256], (sk, 1), tb, gp, gpk)
                gg, ggk = self.rot("gg")
                P.act(gg, gp, AF.Gelu_apprx_tanh, r=[gpk], w=[ggk])
                P.tt("dve", ybT[:, cc, sl], gg, hs[:, sl], ALU.mult, r=[ggk, "hs"], w=[(("ybT", tb), cc)])
        for tb in range(NTB):
            self.out_proj_block(wo, "e_wo", ybT[:, :, tb * TB:(tb + 1) * TB], ("ybT", tb), 4, tb)
        P.fence()
        A.reset(m1)

    def _even_e2(self, e, l, hT, w_in_d, cD, proj, m1):
        P, d, A = self.P, self.d, self.A
        self.rots = {}
        yaT = A.alloc("yaT", [128, 4, T], BF16)
        woa = A.alloc("e_woa", [128, 4, D], BF16)
        P.dma("pool", woa, d["ev_w_out"][e].rearrange("(h p) n -> p h n", p=128)[:, 0:4, :], w=["e_woa"])
        scn = ["beta", "g", "cum", "cl", "kbe", "kdec", "negb", "tmp"]
        sc = {nm: A.alloc("sc_" + nm, [128, NT, 4], F32) for nm in scn}
        scf = {nm: sc[nm].rearrange("p t h -> p (t h)") for nm in scn}
        lastc = A.alloc("lastc", [128, 4, 32], F32)
        bd = A.alloc("bdrow", [1, 8], F32)
        bdb = A.alloc("bdb", [128, 8], F32)
        slab8 = A.alloc("slab8", [128, DC, 8], BF16)
        P.dma("sp", bd[0:1, 0:4], d["ev_dt_bias"][e:e + 1, :], w=[("bd", 0)])
        P.dma("sp", bd[0:1, 4:8], d["ev_a_log"][e:e + 1, :], w=[("bd", 1)])
        pb, pk = self.bank()
        P.mm(pb[:, 0:8], self.onesF[0:1, :], bd, r=[("bd", 0), ("bd", 1)], w=[pk])
        P.copy("dve", bdb, pb[:, 0:8], r=[pk], w=["bdb"])
        P.act(bdb[:, 4:8], bdb[:, 4:8], AF.Exp, r=["bdb"], w=["bdb2"])
        P.ts("dve", bdb[:, 4:8], bdb[:, 4:8], -1.0, None, ALU.mult, r=["bdb2"], w=["bdb2"])
        P.dma("pool", slab8, w_in_d[:, :, 2048:2056], w=["slab8"])
        for tt in range(NT):
            pp, ppk = self.bank()
            for c in range(DC):
                P.mm(pp[:, 0:8], hT[:, c, tt * 128:(tt + 1) * 128], slab8[:, c, :], start=(c == 0), stop=(c == DC - 1),
                     r=["slab8", (("e_hT", tt // 4), c)], w=[ppk])
            P.act(sc["beta"][:, tt, :], pp[:, 0:4], AF.Sigmoid, r=[ppk], w=[("sc_beta", tt)])
            P.tt("dve", sc["tmp"][:, tt, :], pp[:, 4:8], bdb[:, 0:4], ALU.add, r=[ppk, "bdb"], w=[("sc_tmp", tt)])
        alltmp = [("sc_tmp", tt) for tt in range(NT)]
        P.act(scf["tmp"], scf["tmp"], AF.Exp, r=alltmp, w=["sc_tmp2"])
        P.act(scf["tmp"], scf["tmp"], AF.Ln, r=["sc_tmp2"], w=["sc_tmp2"], bias=1.0)
        for tt in range(NT):
            P.tt("dve", sc["g"][:, tt, :], sc["tmp"][:, tt, :], bdb[:, 4:8], ALU.mult, r=["sc_tmp2", "bdb2"], w=[("sc_g", tt)])
        pb, pk = self.bank()
        P.mm(pb[:, 0:64], self.maskIU, scf["g"], r=[("sc_g", tt) for tt in range(NT)], w=[pk])
        P.copy("dve", scf["cum"], pb[:, 0:64], r=[pk], w=["sc_cum"])
        pb, pk = self.bank()
        P.mm(pb[:, 0:64], self.lsel, scf["cum"], r=["sc_cum"], w=[pk])
        P.copy("dve", scf["cl"], pb[:, 0:64], r=[pk], w=["sc_cl"])
        allb = [("sc_beta", tt) for tt in range(NT)]
        P.act(scf["kbe"], scf["cum"], AF.Exp, r=["sc_cum"], w=["sc_kbe"])
        P.tt("dve", scf["kbe"], scf["kbe"], scf["beta"], ALU.mult, r=["sc_kbe"] + allb, w=["sc_kbe"])
        P.tt("dve", scf["kdec"], scf["cl"], scf["cum"], ALU.subtract, r=["sc_cl", "sc_cum"], w=["sc_kdec"])
        P.act(scf["kdec"], scf["kdec"], AF.Exp, r=["sc_kdec"], w=["sc_kdec"])
        P.ts("dve", scf["negb"], scf["beta"], -1.0, None, ALU.mult, r=allb, w=["sc_negb"])
        P.fence()
        slabq = [A.alloc("slabq", [128, DC, 128], BF16) for _ in range(1)]

        def load_head_slabs(hh):
            cols = [hh * 128]
            for i_, c_ in enumerate(cols):
                P.dma("pool", slabq[i_], w_in_d[:, :, c_:c_ + 128], w=[("slabq", i_)], nofence=True)

        load_head_slabs(0)
        mh = A.mark()
        for h in range(4):
            A.reset(mh)
            self.rots = {}
            self.rot_banks = list(range(8))
            qT = A.alloc("qT", [128, T], BF16)
            kT = A.alloc("kT", [128, T], BF16)
            vT = A.alloc("vT", [128, T], BF16)
            zs = A.alloc("zs", [128, T], BF16)
            mA = A.mark()
            raws = [A.alloc("raw", [128, 3 + T], F32) for _ in range(3)]
            cvs = [A.alloc("cv", [128, T], F32) for _ in range(2)]
            self.mkrot("sq1", 2, [128, TB], BF16)
            self.mkrot("rs", 1, [128, TB], F32)
            self.mkrot("slabv", 2, [128, DC, 128], BF16)
            kinds = [(h * 128, qT, "qT"), (512 + h * 128, kT, "kT"), (1024 + h * 128, vT, "vT")]
            for kind, (col0, dst, dn) in enumerate(kinds):
                raw = raws[kind]
                P.op("dve", "memset", raw[:, 0:3], 0.0, w=[("raw_pad", kind)])
                if kind == 0:
                    sb, sk = slabq[0], ("slabq", 0)
                else:
                    sb, sk = self.rot("slabv")
                    P.dma("pool", sb, w_in_d[:, :, col0:col0 + 128], w=[sk])
                for tb in range(NTB):
                    pp, ppk = self.bank()
                    proj(sb, sk, tb, pp, ppk)
                    P.copy("act", raw[:, 3 + tb * TB:3 + (tb + 1) * TB], pp, r=[ppk], w=[("raw", kind, tb)])
            sb, sk = self.rot("slabv")
            P.dma("pool", sb, w_in_d[:, :, 1536 + h * 128:1536 + (h + 1) * 128], w=[sk])
            zps = []
            for tb in range(NTB):
                pp, ppk = self.bank()
                proj(sb, sk, tb, pp, ppk)
                zps.append((pp, ppk))

            def conv(kind, cv, cvk, eng):
                raw = raws[kind]
                cc = kind * 4 + h
                rk_ = [("raw", kind, tb) for tb in range(NTB)] + [("raw_pad", kind)]
                P.ts(eng, cv, raw[:, 3:3 + T], cD(3 * 12 + cc), None, ALU.mult, r=rk_, w=[cvk])
                for j in range(3):
                    P.stt(eng, cv, raw[:, j:j + T], cD(j * 12 + cc), cv, ALU.mult, ALU.add, r=rk_ + [cvk], w=[cvk])

            conv(0, cvs[0], "cv0", "dve")
            conv(1, cvs[1], "cv1", "dve")
            silq = raws[0][:, 3:3 + T]
            silk = raws[1][:, 3:3 + T]
            P.act(silq, cvs[0], AF.Silu, r=["cv0"], w=["silq"] + [("raw", 0, tb) for tb in range(NTB)])
            P.act(silk, cvs[1], AF.Silu, r=["cv1"], w=["silk"] + [("raw", 1, tb) for tb in range(NTB)])
            conv(2, cvs[0], "cv0", "dve")
            for tb in range(NTB):
                pp, ppk = zps[tb]
                P.act(zs[:, tb * TB:(tb + 1) * TB], pp, AF.Silu, r=[ppk], w=[("zs", tb)])
            P.act(vT, cvs[0], AF.Silu, r=["cv0"], w=["vT"])
            blocks = []
            for sil_, silkey, dst, dn, gsc in ((silq, "silq", qT, "qT", 128 ** -0.5), (silk, "silk", kT, "kT", None)):
                for tb in range(NTB):
                    sl = slice(tb * TB, (tb + 1) * TB)
                    blocks.append((sil_[:, sl], silkey, 128, self.onesB, 1.0, gsc, dst[:, sl], (dn, tb)))
            self.pnorm_pipe(blocks)
            P.fence()
            A.reset(mA)
            self.rots = {}
            if h + 1 < 4:
                load_head_slabs(h + 1)
            kdec = A.alloc("ktm_dec", [128, NT, 128], BF16)
            utm = A.alloc("u_tm", [128, NT, 128], BF16)
            wT = A.alloc("wT", [128, T], BF16)
            qdecT = A.alloc("qdecT", [128, T], BF16)
            qkT = A.alloc("qkT", [128, NT, 128], BF16)
            S = A.alloc("S", [128, 128], F32)
            Sb = A.alloc("Sb", [128, 128], BF16)
            KCH = 5
            slots = []
            for j in range(KCH):
                sd = {}
                for nm in ("dg", "t1"):
                    sd[nm] = A.alloc("sl_" + nm, [128, 128], F32)
                sd["t2"] = sd["dg"]
                for nm in ("Dt", "Dm", "E", "p0", "p1", "pT0", "pT1", "kbe", "vtb", "AT0", "AT1"):
                    sd[nm] = A.alloc("sl_" + nm, [128, 128], BF16)
                slots.append(sd)
            self.mkrot("vn", 2, [128, 128], BF16)
            self.mkrot("ot", 2, [128, TB], F32)
            self.mkrot("sq1", 1, [128, TB], BF16)
            self.mkrot("rs", 1, [128, TB], F32)
            iF, iB = self.identF, self.identB
            self.rot_banks = [1, 2, 3, 4, 5, 6, 7]
            self.bi = 0

            def tile_chain(tt, j, h=h):
                sd = slots[j]
                K_ = lambda nm: ("sl", j, nm)
                tsl = slice(tt * 128, (tt + 1) * 128)
                tb = tt // 4
                cumc = sc["cum"][:, tt, h:h + 1]
                kp, kpk = self.bank()
                P.mm(kp[:, 0:128], kT[:, tsl], iB, r=[("kT", tb)], w=[kpk])
                P.ts("dve", sd["kbe"], kp[:, 0:128], sc["kbe"][:, tt, h:h + 1], None, ALU.mult, r=[kpk], w=[K_("kbe")])
                P.act(kdec[:, tt, :], kp[:, 0:128], AF.Copy, r=[kpk], w=[("kdec", tt)], scale=sc["kdec"][:, tt, h:h + 1])
                vp, vpk = self.bank()
                P.mm(vp[:, 0:128], vT[:, tsl], iB, r=["vT"], w=[vpk])
                P.act(sd["vtb"], vp[:, 0:128], AF.Copy, r=[vpk], w=[K_("vtb")], scale=sc["beta"][:, tt, h:h + 1])
                P.ts("pool", sd["dg"], iF, cumc, None, ALU.mult, w=[K_("dg")])
                yield
                bc, bck = self.bank()
                P.mm(bc[:, 0:128], self.onesF, sd["dg"], r=[K_("dg")], w=[bck])
                P.ts("dve", sd["t1"], bc[:, 0:128], cumc, 0.0, ALU.subtract, ALU.min, r=[bck], w=[K_("t1")])
                P.act(sd["Dt"], sd["t1"], AF.Exp, r=[K_("t1")], w=[K_("Dt")])
                P.ts("dve", sd["t2"], bc[:, 0:128], cumc, 0.0, ALU.subtract, ALU.max, r=[bck], w=[K_("t2"), K_("dg")])
                P.act(sd["Dm"], sd["t2"], AF.Exp, r=[K_("t2")], w=[K_("Dm")], scale=-1.0)
                P.act(sd["E"], bc[:, 0:128], AF.Exp, r=[bck], w=[K_("E")])
                P.act(lastc[:, h, 2 * tt:2 * tt + 2], bc[:, 63:128:64], AF.Exp, r=[bck], w=[("lastc", tt)])
                P.tt("pool", qdecT[:, tsl], qT[:, tsl], sd["E"], ALU.mult, r=[("qT", tb), K_("E")], w=[("qdecT", tt)])
                P.tt("pool", sd["Dm"], sd["Dm"], self.maskSL, ALU.mult, r=[K_("Dm")], w=[K_("Dm")])
                P.tt("pool", sd["Dt"], sd["Dt"], self.maskIU, ALU.mult, r=[K_("Dt")], w=[K_("Dt")])
                yield
                KK, KKk = self.bank()
                P.mm(KK[:, 0:128], kT[:, tsl], kT[:, tsl], r=[("kT", tb)], w=[KKk])
                QK, QKk = self.bank()
                P.mm(QK[:, 0:128], kT[:, tsl], qT[:, tsl], r=[("kT", tb), ("qT", tb)], w=[QKk])
                p, pk_ = sd["p0"], K_("p0")
                P.stt("dve", p, KK[:, 0:128], sc["negb"][:, tt, h:h + 1], sd["Dm"], ALU.mult, ALU.mult,
                      r=[KKk, K_("Dm")], w=[pk_])
                P.tt("dve", qkT[:, tt, :], QK[:, 0:128], sd["Dt"], ALU.mult, r=[QKk, K_("Dt")], w=[("qkT", tt)])
                yield
                tp, tpk = self.bank()
                P.mm(tp[:, 0:128], p, iB, r=[pk_], w=[tpk])
                pT, pTk = sd["pT0"], K_("pT0")
                P.copy("act", pT, tp[:, 0:128], r=[tpk], w=[pTk])
                AT, ATk = sd["AT0"], K_("AT0")
                P.tt("dve", AT, tp[:, 0:128], iF, ALU.add, r=[tpk], w=[ATk])
                yield
                def a_update(pcur, pcurk, AT, ATk, s_):
                    an, ank = self.bank()
                    P.mm(an[:, 0:128], iB, AT, start=True, stop=False, r=[ATk], w=[ank])
                    P.mm(an[:, 0:128], pcur, AT, start=False, stop=True, r=[pcurk, ATk], w=[ank])
                    nx_ = "1" if (s_ % 2 == 0) else "0"
                    ATn, ATnk = sd["AT" + nx_], K_("AT" + nx_)
                    P.copy("dve" if s_ % 2 else "act", ATn, an[:, 0:128], r=[ank], w=[ATnk])
                    return ATn, ATnk

                for s_ in range(5):
                    nx = "1" if (s_ % 2 == 0) else "0"
                    p2, p2k = self.bank()
                    P.mm(p2[:, 0:128], pT, p, r=[pTk, pk_], w=[p2k])
                    pn, pnk = sd["p" + nx], K_("p" + nx)
                    P.copy("act", pn, p2[:, 0:128], r=[p2k], w=[pnk])
                    if s_ < 4:
                        p2T, p2Tk = self.bank()
                        P.mm(p2T[:, 0:128], p, pT, r=[pTk, pk_], w=[p2Tk])
                        pTn, pTnk = sd["pT" + nx], K_("pT" + nx)
                        P.copy("act" if s_ % 2 else "dve", pTn, p2T[:, 0:128], r=[p2Tk], w=[pTnk])
                    if s_ > 0:
                        AT, ATk = a_update(p, pk_, AT, ATk, s_ - 1)
                    yield
                    if s_ < 4:
                        p, pk_, pT, pTk = pn, pnk, pTn, pTnk
                    else:
                        p, pk_ = pn, pnk
                AT, ATk = a_update(p, pk_, AT, ATk, 4)
                yield
                wp, wpk = self.bank()
                P.mm(wp[:, 0:128], sd["kbe"], AT, r=[K_("kbe"), ATk], w=[wpk])
                P.copy("act", wT[:, tsl], wp[:, 0:128], r=[wpk], w=[("wT", tt)])
                up, upk = self.bank()
                P.mm(up[:, 0:128], AT, sd["vtb"], r=[K_("vtb"), ATk], w=[upk])
                P.copy("dve", utm[:, tt, :], up[:, 0:128], r=[upk], w=[("utm", tt)])

            def scan_chain(h=h):
                P.op("dve", "memset", S, 0.0, w=["S"])
                P.op("dve", "memset", Sb, 0.0, w=["Sb"])
                ob, obk = self.ps[0], ("ps", 0)
                for ck in range(32):
                    tt, half = ck // 2, ck % 2
                    yield ("need", tt)
                    hp = slice(half * 64, half * 64 + 64)
                    csl = slice(ck * 64, (ck + 1) * 64)
                    col = (ck % 8) * 64
                    a_ps, ak = self.bank()
                    P.mm(a_ps[hp, 0:128], wT[:, csl], Sb, r=[("wT", tt), "Sb"], w=[ak])
                    P.mm(ob[:, col:col + 64], Sb, qdecT[:, csl], start=True, stop=False, r=["Sb", ("qdecT", tt)], w=[obk])
                    vn, vnk = self.rot("vn")
                    P.tt("dve", vn[hp, :], utm[hp, tt, :], a_ps[hp, 0:128], ALU.subtract, r=[("utm", tt), ak], w=[vnk])
                    yield None
                    s_ps, spk = self.bank()
                    P.mm(s_ps[:, 0:128], kdec[hp, tt, :], vn[hp, :], r=[("kdec", tt), vnk], w=[spk])
                    P.mm(ob[:, col:col + 64], vn[hp, :], qkT[hp, tt, half * 64:half * 64 + 64], start=False, stop=True,
                         r=[vnk, ("qkT", tt)], w=[obk])
                    P.stt("dve", Sb, S, lastc[:, h, ck:ck + 1], s_ps[:, 0:128], ALU.mult, ALU.add,
                          r=[spk, "S", ("lastc", tt)], w=["Sb"])
                    P.stt("dve", S, S, lastc[:, h, ck:ck + 1], s_ps[:, 0:128], ALU.mult, ALU.add,
                          r=[spk, "S", ("lastc", tt)], w=["S"])
                    if ck % 8 == 7:
                        tb = ck // 8
                        sl = slice(tb * TB, (tb + 1) * TB)
                        ot, otk = self.rot("ot")
                        P.copy("act", ot, ob, r=[obk], w=[otk])
                        on, onk = self.rot("ot")
                        self.pnorm(ot, otk, 128, self.onesB, 1.0 / 128, cD(80), on, onk)
                        P.tt("dve", yaT[:, h, sl], on, zs[:, sl], ALU.mult, r=[onk, ("zs", tb)], w=[(("yaT", tb), h)])
                    yield None

            pending = list(range(NT))
            active = []
            free_slots = list(range(KCH))
            done_tiles = set()
            scan = scan_chain()
            scan_wait = next(scan)
            scan_done = False
            while pending or active or not scan_done:
                while pending and free_slots:
                    tt = pending.pop(0)
                    j = free_slots.pop(0)
                    active.append((tile_chain(tt, j), tt, j))
                nxt = []
                for g, tt, j in active:
                    try:
                        next(g)
                        nxt.append((g, tt, j))
                    except StopIteration:
                        done_tiles.add(tt)
                        free_slots.append(j)
                active = nxt
                for _ in range(3):
                    if not scan_done:
                        if scan_wait is None or scan_wait[1] in done_tiles:
                            try:
                                scan_wait = next(scan)
                            except StopIteration:
                                scan_done = True
            self.rot_banks = list(range(8))
            self.bi = 0
            P.fence()
        for tb in range(NTB):
            self.out_proj_block(woa, "e_woa", yaT[:, :, tb * TB:(tb + 1) * TB], ("yaT", tb), 4, tb)
        P.fence()
        A.reset(m1)

    def build_posfb(self, qb):
        P = self.P
        pf, pfk = self.rot("posfb")
        pb, pk = self.bank()
        for j in range(4):
            tt = qb * 4 + j
            tm, tmk = self.rot("pkm")
            P.ts("dve", tm, self.pkf, self.identF[0:NT, tt:tt + 1], None, ALU.mult, w=[tmk])
            P.mm(pb[:, j * 128:(j + 1) * 128], self.onesF[0:NT, :], tm, r=[tmk], w=[pk])
        P.copy("act", pf, pb, r=[pk], w=[pfk])
        return pf, pfk

    def rope_tables(self, tb):
        P = self.P
        pf, pfk = self.build_posfb(tb)
        R = slice(64, 96)
        f0 = self.freq[R, 0:1]
        out = {}
        for nm, shift in (("sin", 0.0), ("cos", math.pi / 2)):
            ang, ak = self.rot("ang")
            P.ts("dve", ang[R, :], pf[R, :], f0, shift, ALU.mult, ALU.add, r=[pfk], w=[ak])
            ki, kik = self.rot("angi")
            P.ts("dve", ki[R, :], ang[R, :], 1.0 / (2 * math.pi), None, ALU.mult, r=[ak], w=[kik])
            kf, kfk = self.rot("ang")
            P.copy("dve", kf[R, :], ki[R, :], r=[kik], w=[kfk])
            P.stt("dve", ang[R, :], kf[R, :], -2 * math.pi, ang[R, :], ALU.mult, ALU.add, r=[kfk, ak], w=[ak])
            P.ts("dve", ang[R, :], ang[R, :], math.pi, -math.pi, ALU.min, ALU.max, r=[ak], w=[ak])
            tab, tk = self.rot(nm)
            P.act(tab[R, :], ang[R, :], AF.Sin, r=[ak], w=[tk])
            out[nm] = (tab, tk)
        return out

    def odd_mixer(self, o, l):
        P, d, A = self.P, self.d, self.A
        m0 = A.mark()
        self.rots = {}
        cC = lambda j: self.colsC[:, o * 8 + j:o * 8 + j + 1]
        SC_C = 64 ** -0.5
        SC_D = 96 ** -0.5
        slopes = [2.0 ** (-8.0 * (i + 1) / 4) for i in range(4)]
        lam_init = 0.8 - 0.6 * math.exp(-0.3 * l)
        qlatT = A.alloc("qlatT", [128, 2, T], BF16)
        kvlatT = A.alloc("kvlatT", [128, T], BF16)
        kropeT = A.alloc("kropeT", [128, T], BF16)
        neglam = A.alloc("neglam", [128, 2], F32)
        gsub = A.alloc("gsub", [128, 1], F32)
        lamt = A.alloc("lamt", [1, 4, 64], F32)
        lamp = A.alloc("lamp", [1, 2, 64], F32)
        ls = A.alloc("lams", [1, 8], F32)
        m1 = A.mark()
        qcT = A.alloc("qcT", [128, 4, T], BF16)
        kcT = A.alloc("kcT", [128, 4, T], BF16)
        vc = A.alloc("vc", [128, NT, 512], BF16)
        m2 = A.mark()
        for i, nm in enumerate(["od_lam_q1", "od_lam_k1", "od_lam_q2", "od_lam_k2"]):
            P.dma("sp", lamt[0:1, i, :], d[nm][o:o + 1, :], w=[("lamt", i)])
        P.tt("dve", lamp[0:1, 0, :], lamt[0:1, 0, :], lamt[0:1, 1, :], ALU.mult, r=[("lamt", 0), ("lamt", 1)], w=["lp0"])
        P.tt("dve", lamp[0:1, 1, :], lamt[0:1, 2, :], lamt[0:1, 3, :], ALU.mult, r=[("lamt", 2), ("lamt", 3)], w=["lp1"])
        P.op("dve", "memset", ls, 0.0, w=["ls"])
        P.op("dve", "reduce_sum", ls[0:1, 0:1], lamp[0:1, 0, :], AX.X, r=["lp0", "ls"], w=["ls0"])
        P.op("dve", "reduce_sum", ls[0:1, 1:2], lamp[0:1, 1, :], AX.X, r=["lp1", "ls"], w=["ls1"])
        P.act(ls[0:1, 0:2], ls[0:1, 0:2], AF.Exp, r=["ls0", "ls1"], w=["lse"])
        P.tt("dve", ls[0:1, 2:3], ls[0:1, 1:2], ls[0:1, 0:1], ALU.subtract, r=["lse"], w=["ls2"])
        P.ts("dve", ls[0:1, 4:5], ls[0:1, 2:3], -lam_init, None, ALU.add, r=["ls2"], w=["ls3"])
        pb, pk = self.bank()
        P.mm(pb[:, 0:2], self.onesF[0:1, :], ls[0:1, 4:6], r=["ls3"], w=[pk])
        P.copy("dve", neglam, pb[:, 0:2], r=[pk], w=["neglam"])
        P.ts("dve", gsub, cC(2), 1.0 - lam_init, None, ALU.mult, w=["gsub"])
        hT = A.alloc("o_hT", [128, DC, T], BF16)
        self.mkrot("slab", 1, [128, DC, 512], BF16)
        self.mkrot("sq", 1, [128, DC, TB], BF16)
        self.rots["slab"][0].append(self.rots["sq"][0][0])
        self.mkrot("sq1", 2, [128, TB], BF16)
        self.mkrot("rs", 2, [128, TB], F32)
        w_in_d = d["od_w_in"][o].rearrange("(c p) n -> p c n", p=128)
        for tb in range(NTB):
            self.norm_block(tb, 0, l, hT[:, :, tb * TB:(tb + 1) * TB], ("o_hT", tb))

        def load_slab(c0, n):
            sb, sk = self.rot("slab")
            wk = [sk, ("sq", 0)] if sk == ("slab", 1) else [sk]
            P.dma("pool", sb[:, :, 0:n], w_in_d[:, :, c0:c0 + n], w=wk)
            return sb, sk

        def pn_a(src, srck, ones_l, inv_n, gain, out, outk):
            sq, sqk = self.rot("sq1")
            P.act(sq, src, AF.Square, r=[srck], w=[sqk])
            return (src, srck, ones_l, inv_n, gain, out, outk, sq, sqk)

        def pn_b(stt_):
            src, srck, ones_l, inv_n, gain, out, outk, sq, sqk = stt_
            ss, ssk = self.bank()
            P.mm(ss, ones_l, sq, r=[sqk], w=[ssk])
            rs, rsk = self.rot("rs")
            P.act(rs, ss, AF.Ln, r=[ssk], w=[rsk], scale=inv_n, bias=EPS)
            P.act(rs, rs, AF.Exp, r=[rsk], w=[rsk], scale=-0.5)
            P.stt("dve", out, src, gain, rs, ALU.mult, ALU.mult, r=[srck, rsk], w=[outk])

        slab_q = load_slab(0, 512)
        slab_k = load_slab(512, 512)
        prev = None
        for dst, dname, gj, (sb, sk) in ((qcT, "qcT", 0, slab_q), (kcT, "kcT", 1, slab_k)):
            for ch in range(4):
                for tb in range(NTB):
                    sl = slice(tb * TB, (tb + 1) * TB)
                    pp, ppk = self.bank()
                    for c in range(DC):
                        P.mm(pp, sb[:, c, ch * 128:(ch + 1) * 128], hT[:, c, sl], start=(c == 0), stop=(c == DC - 1),
                             r=[sk, (("o_hT", tb), c)], w=[ppk])
                    cur = pn_a(pp, ppk, self.blockB, 1.0 / 64, cC(gj), dst[:, ch, sl], (dname, ch, tb))
                    if prev is not None:
                        pn_b(prev)
                    prev = cur
        sb, sk = load_slab(1024, 512)
        pn_b(prev)
        for tt in range(NT):
            pp, ppk = self.bank()
            for c in range(DC):
                P.mm(pp, hT[:, c, tt * 128:(tt + 1) * 128], sb[:, c, :], start=(c == 0), stop=(c == DC - 1),
                     r=[sk, (("o_hT", tt // 4), c)], w=[ppk])
            P.copy("act", vc[:, tt, :], pp, r=[ppk], w=[("vc", tt)])
        sb, sk = load_slab(1536, 416)
        for tb in range(NTB):
            sl = slice(tb * TB, (tb + 1) * TB)
            qq = [self.bank(), self.bank()]
            for ci, (pp, ppk) in enumerate(qq):
                for c in range(DC):
                    P.mm(pp, sb[:, c, ci * 128:(ci + 1) * 128], hT[:, c, sl], start=(c == 0), stop=(c == DC - 1),
                         r=[sk, (("o_hT", tb), c)], w=[ppk])
            ss, ssk = self.bank()
            for ci, (pp, ppk) in enumerate(qq):
                sq, sqk = self.rot("sq1")
                P.act(sq, pp, AF.Square, r=[ppk], w=[sqk])
                P.mm(ss, self.onesB, sq, start=(ci == 0), stop=(ci == 1), r=[sqk], w=[ssk])
            rs, rsk = self.rot("rs")
            P.act(rs, ss, AF.Ln, r=[ssk], w=[rsk], scale=1.0 / 256, bias=EPS)
            P.act(rs, rs, AF.Exp, r=[rsk], w=[rsk], scale=-0.5)
            for ci, (pp, ppk) in enumerate(qq):
                P.stt("dve", qlatT[:, ci, sl], pp, cC(3 + ci), rs, ALU.mult, ALU.mult, r=[ppk, rsk],
                      w=[("qlatT", ci, tb)])
            pp, ppk = self.bank()
            for c in range(DC):
                P.mm(pp, sb[:, c, 256:384], hT[:, c, sl], start=(c == 0), stop=(c == DC - 1),
                     r=[sk, (("o_hT", tb), c)], w=[ppk])
            self.pnorm(pp, ppk, 128, self.onesB, 1.0 / 128, cC(5), kvlatT[:, sl], ("kvlatT", tb))
            pp, ppk = self.bank()
            for c in range(DC):
                P.mm(pp[64:96, :], sb[:, c, 384:416], hT[:, c, sl], start=(c == 0), stop=(c == DC - 1),
                     r=[sk, (("o_hT", tb), c)], w=[ppk])
            P.copy("act", kropeT[64:96, sl], pp[64:96, :], r=[ppk], w=[("kropeT", tb)])
        P.fence()
        A.reset(m2)
        self.rots = {}
        wo = A.alloc("o_wo", [128, 4, D], BF16)
        P.dma("pool", wo, d["od_w_out"][o].rearrange("(h p) n -> p h n", p=128)[:, 0:4, :], w=["o_wo"])
        self.mkrot("posfb", 2, [128, TB], F32)
        self.mkrot("pkm", 2, [NT, 128], F32)
        dcache = [A.alloc("dcache", [128, TB], F16) for _ in range(NT)]
        self.mkrot("tS", 2, [128, TB], F32)
        self.mkrot("pT", 4, [128, TB], BF16)
        self.mkrot("yT", 2, [128, 4, TB], BF16)
        self.mkrot("rs", 3, [128, TB], F32)
        self.mkrot("sq1", 2, [128, TB], BF16)
        self.mkrot("ya", 1, [128, TB], F32)
        self.mkrot("yb", 1, [128, TB], F32)
        self.score_banks = [4, 5, 6, 7]
        self.si = 0
        self.rot_banks = [6, 7]
        self.bi = 0
        items = [(qb, h, kt) for qb in range(NTB) for h in range(4) for kt in range(4 * qb + 4)]
        st = {}
        deferred = []

        def defer(n, fn):
            deferred.append([n, fn])

        def tick(flush=False):
            while deferred and (flush or deferred[0][0] <= 0):
                deferred.pop(0)[1]()
            for dd_ in deferred:
                dd_[0] -= 1

        def stageA(it):
            qb, h, kt = it
            if h == 0 and kt == 0:
                st[("pf", qb)] = self.build_posfb(qb)
                st[("yT", qb)] = self.rot("yT")
            pf, pfk = st[("pf", qb)]
            j = kt - 4 * qb
            c0 = max(j, 0) * 128
            dist, dkk = dcache[kt], ("dcache", kt)
            if h == 0:
                P.act(dist[:, c0:], pf[:, c0:], AF.Abs, r=[pfk], w=[dkk], bias=self.negposk[:, kt:kt + 1])
            sps = []
            for mi in range(2):
                hs = slice(mi * 64, (mi + 1) * 64)
                sp_, spk = self.sbank()
                P.mm(sp_[:, c0:], kcT[hs, h, kt * 128:(kt + 1) * 128], qcT[hs, h, qb * TB + c0:(qb + 1) * TB],
                     r=[("kcT", h, kt // 4), ("qcT", h, qb)], w=[spk])
                sps.append((sp_, spk))
            st[it] = (dist, dkk, sps)

        def stageB(it):
            qb, h, kt = it
            yT, yk = st[("yT", qb)]
            nkt = 4 * qb + 4
            j = kt - 4 * qb
            c0 = max(j, 0) * 128
            dist, dkk, sps = st.pop(it)
            cur_banks = [spk[1] for _, spk in sps]
            lss = None
            pts = []
            for mi in range(2):
                sp_, spk = sps[mi]
                tS, tSk = self.rot("tS")
                P.stt("dve", tS[:, c0:], dist[:, c0:], -slopes[h] / SC_C, sp_[:, c0:], ALU.mult, ALU.add,
                      r=[dkk, spk], w=[tSk])
                pT, pTk = self.rot("pT")
                P.act(pT[:, c0:], tS[:, c0:], AF.Exp, r=[tSk], w=[pTk], scale=SC_C)
                pts.append((pT, pTk))
            for mi in range(2):
                pT, pTk = pts[mi]
                if j >= 0:
                    P.op("pool", "memset", pT[64:128, c0:c0 + 64], 0.0, w=[pTk])
                ab = mi
                Ob, Ok = self.ps[ab], ("ps", ab)
                P.mm(Ob[:, c0:], vc[:, kt, h * 128:(h + 1) * 128], pT[:, c0:], start=(kt == 0),
                     stop=(kt == nkt - 1), r=[("vc", kt), pTk], w=[Ok])
                lb_, lbk_ = self.ps[2 + mi], ("ps", 2 + mi)
                P.mm(lb_[:, c0:], self.onesB, pT[:, c0:], start=(kt == 0), stop=(kt == nkt - 1), r=[pTk], w=[lbk_])
            self.rot_banks = cur_banks
            self.bi = 0
            tick()
            if kt == nkt - 1:
                ya, yak = self.rot("ya")
                yb, ybk = self.rot("yb")
                sq, sqk = self.rot("sq1")

                def step1(lss=lss, ya=ya, yak=yak, yb=yb, ybk=ybk, sq=sq, sqk=sqk, h=h):
                    for mi, (y_, y_k) in enumerate(((ya, yak), (yb, ybk))):
                        ab = mi
                        r0, r0k = self.rot("rs")
                        self.recip_act(r0, r0k, self.ps[2 + mi], ("ps", 2 + mi))
                        P.tt("dve", y_, self.ps[ab], r0, ALU.mult, r=[("ps", ab), r0k], w=[y_k])
                    P.stt("dve", ya, yb, neglam[:, 0:1], ya, ALU.mult, ALU.add, r=[ybk, yak], w=[yak])
                    P.act(sq, ya, AF.Square, r=[yak], w=[sqk])

                def step2(h=h, ya=ya, yak=yak, sq=sq, sqk=sqk, yT=yT, yk=yk):
                    ss, ssk = self.bank()
                    P.mm(ss, self.onesB, sq, r=[sqk], w=[ssk])
                    rs, rsk = self.rot("rs")
                    P.act(rs, ss, AF.Ln, r=[ssk], w=[rsk], scale=1.0 / 128, bias=EPS)
                    P.act(rs, rs, AF.Exp, r=[rsk], w=[rsk], scale=-0.5)
                    P.stt("dve", yT[:, h, :], ya, gsub, rs, ALU.mult, ALU.mult, r=[yak, rsk], w=[(yk, h)])

                step1()
                defer(1, step2)
                if h == 3:
                    defer(3, lambda qb=qb, yT=yT, yk=yk: self.out_proj_block(wo, "o_wo", yT, yk, 4, qb))

        LOOK2 = 1
        for i in range(min(LOOK2, len(items))):
            stageA(items[i])
        for i in range(len(items)):
            if i + LOOK2 < len(items):
                stageA(items[i + LOOK2])
            stageB(items[i])
        tick(flush=True)
        self.rot_banks = list(range(8))
        self.bi = 0
        self.score_banks = [2, 3, 4, 5]
        self.si = 0
        P.fence()
        A.reset(m1)
        self.rots = {}
        qdT = A.alloc("qdT", [128, 4, T], BF16)
        kdT = A.alloc("kdT", [128, 4, T], BF16)
        vd = A.alloc("vd", [128, NT, 512], BF16)
        wuq = A.alloc("wuq", [128, 2, 384], BF16)
        wukv = A.alloc("wukv", [128, 768], BF16)
        P.dma("pool", wuq, d["od_w_uq"][o].rearrange("(c p) n -> p c n", p=128), w=["wuq"])
        P.dma("pool", wukv, d["od_w_ukv"][o], w=["wukv"])
        m3 = A.mark()
        self.mkrot("posfb", 2, [128, TB], F32)
        self.mkrot("pkm", 2, [NT, 128], F32)
        self.mkrot("ang", 3, [128, TB], F32)
        self.mkrot("angi", 1, [128, TB], I32)
        self.mkrot("sin", 2, [128, TB], F32)
        self.mkrot("cos", 2, [128, TB], F32)
        KO3 = 3
        oslots = []
        for j in range(KO3):
            sd = {"sq": A.alloc("o3_sq", [128, TB], BF16)}
            for nm in ("rs", "kraw", "rtmp", "rtmp2"):
                sd[nm] = A.alloc("o3_" + nm, [128, TB], F32)
            oslots.append(sd)
        self.rot_banks = list(range(8))
        self.bi = 0
        ones96 = self.onesB[0:96, 0:96]

        def qk_chain(tb, h, isk, cosb, cosk, sinb, sink, j):
            sd = oslots[j]
            K_ = lambda nm: ("o3s", j, nm)
            sl = slice(tb * TB, (tb + 1) * TB)
            if not isk:
                dst, dkey, gcol = qdT[:, h, sl], ("qdT", h, tb), cC(6)[0:96, :]
                qp, qpk = self.bank()
                for c in range(2):
                    P.mm(qp[0:96, :], wuq[:, c, h * 96:(h + 1) * 96], qlatT[:, c, sl], start=(c == 0), stop=(c == 1),
                         r=["wuq", ("qlatT", c, tb)], w=[qpk])
                src, srck = qp[0:96, :], [qpk]
            else:
                dst, dkey, gcol = kdT[:, h, sl], ("kdT", h, tb), cC(7)[0:96, :]
                kp, kpk = self.bank()
                P.mm(kp[0:64, :], wukv[:, h * 192:h * 192 + 64], kvlatT[:, sl], r=["wukv", ("kvlatT", tb)], w=[kpk])
                kr = sd["kraw"]
                P.copy("act", kr[0:64, :], kp[0:64, :], r=[kpk], w=[K_("kraw0")])
                P.copy("act", kr[64:96, :], kropeT[64:96, sl], r=[("kropeT", tb)], w=[K_("kraw1")])
                src, srck = kr[0:96, :], [K_("kraw0"), K_("kraw1")]
            P.act(sd["sq"][0:96, :], src, AF.Square, r=srck, w=[K_("sq")])
            yield
            ss, ssk = self.bank()
            P.mm(ss[0:96, :], ones96, sd["sq"][0:96, :], r=[K_("sq")], w=[ssk])
            rs = sd["rs"]
            P.act(rs[0:96, :], ss[0:96, :], AF.Ln, r=[ssk], w=[K_("rs")], scale=1.0 / 96, bias=EPS)
            P.act(rs[0:96, :], rs[0:96, :], AF.Exp, r=[K_("rs")], w=[K_("rs")], scale=-0.5)
            P.stt("dve", dst[0:96, :], src, gcol, rs[0:96, :], ALU.mult, ALU.mult, r=srck + [K_("rs")], w=[dkey])
            yield
            rp, rpk = self.bank()
            P.mm(rp[0:96, :], self.rotTB[0:96, 0:96], dst[0:96, :], r=[dkey], w=[rpk])
            t1, t2 = sd["rtmp"], sd["rtmp2"]
            P.tt("dve", t1[64:96, :], dst[64:96, :], cosb[64:96, :], ALU.mult, r=[dkey, cosk], w=[K_("t1")])
            P.tt("dve", t2[64:96, :], rp[64:96, :], sinb[64:96, :], ALU.mult, r=[rpk, sink], w=[K_("t2")])
            yield
            P.tt("dve", dst[64:96, :], t1[64:96, :], t2[64:96, :], ALU.add, r=[K_("t1"), K_("t2")], w=[dkey])

        for tb in range(NTB):
            tabs = self.rope_tables(tb)
            cosb, cosk = tabs["cos"]
            sinb, sink = tabs["sin"]
            makers = []
            for h in range(4):
                for isk in (False, True):
                    makers.append(lambda j, tb=tb, h=h, isk=isk, cosb=cosb, cosk=cosk, sinb=sinb, sink=sink:
                                  qk_chain(tb, h, isk, cosb, cosk, sinb, sink, j))
            self.run_chains(makers, KO3)
        wv = wukv.rearrange("p (h e) -> p h e", h=4)[:, :, 64:192]
        for tt in range(NT):
            pp, ppk = self.bank()
            P.mm(pp.rearrange("p (h e) -> p h e", h=4), kvlatT[:, tt * 128:(tt + 1) * 128], wv,
                 r=["wukv", ("kvlatT", tt // 4)], w=[ppk])
            P.copy("act", vd[:, tt, :], pp, r=[ppk], w=[("vd", tt)])
        P.fence()
        A.reset(m3)
        self.rots = {}
        wo2 = A.alloc("o_wo2", [128, 4, D], BF16)
        P.dma("pool", wo2, d["od_w_out"][o].rearrange("(h p) n -> p h n", p=128)[:, 4:8, :], w=["o_wo2"])
        self.mkrot("rs", 3, [128, TB], F32)
        self.mkrot("pT", 6, [128, TB], BF16)
        self.mkrot("yT", 2, [128, 4, TB], BF16)
        self.rot_banks = [6, 7]
        self.bi = 0
        if self.cfg.get("pe_l", True):
            self.score_banks = [4, 5, 6, 7]
            self.si = 0
        self.mkrot("lsum", 2, [128, TB], F32)
        for qb in range(NTB):
            yT, yk = self.rot("yT")
            items = [(h, kt) for h in range(4) for kt in range(4 * qb + 4)]
            st = {}

            def stageA(it, qb=qb):
                h, kt = it
                c0 = max(kt - 4 * qb, 0) * 128
                sp_, spk = self.sbank()
                P.mm(sp_[:, c0:], kdT[0:96, h, kt * 128:(kt + 1) * 128], qdT[0:96, h, qb * TB + c0:(qb + 1) * TB],
                     r=[("kdT", h, kt // 4), ("qdT", h, qb)], w=[spk])
                st[it] = (sp_, spk)

            def stageB(it, qb=qb, yT=yT, yk=yk):
                h, kt = it
                nkt = 4 * qb + 4
                j = kt - 4 * qb
                c0 = max(j, 0) * 128
                sp_, spk = st.pop(it)
                if kt == 0:
                    st[("ls", h)] = self.rot("lsum")
                ls, lsk = st[("ls", h)]
                pT, pTk = self.rot("pT")
                P.act(pT[:, c0:], sp_[:, c0:], AF.Exp, r=[spk], w=[pTk], scale=SC_D)
                if j >= 0:
                    P.op("dve", "memset", pT[64:128, c0:c0 + 64], 0.0, w=[pTk])
                Ob, Ok = self.ps[h % 2], ("ps", h % 2)
                P.mm(Ob[:, c0:], vd[:, kt, h * 128:(h + 1) * 128], pT[:, c0:], start=(kt == 0), stop=(kt == nkt - 1),
                     r=[("vd", kt), pTk], w=[Ok])
                if self.cfg.get("pe_l", True):
                    lb_, lbk_ = self.ps[2 + h % 2], ("ps", 2 + h % 2)
                    P.mm(lb_[:, c0:], self.onesB, pT[:, c0:], start=(kt == 0), stop=(kt == nkt - 1), r=[pTk], w=[lbk_])
                    if kt == nkt - 1:
                        r0, r0k = self.rot("rs")
                        self.recip_act(r0, r0k, lb_, lbk_)
                        P.tt("dve", yT[:, h, :], Ob, r0, ALU.mult, r=[Ok, r0k], w=[(yk, h)])
                        del st[("ls", h)]
                else:
                    le = "dve" if kt % 3 else "pool"
                    if kt == 0:
                        P.copy(le, ls, pT, r=[pTk], w=[lsk])
                    else:
                        P.tt(le, ls[:, c0:], ls[:, c0:], pT[:, c0:], ALU.add, r=[pTk, lsk], w=[lsk])
                    if kt == nkt - 1:
                        lp, lpk = self.bank()
                        P.mm(lp, self.onesF, ls, r=[lsk], w=[lpk])
                        r0, r0k = self.rot("rs")
                        self.recip_act(r0, r0k, lp, lpk)
                        P.tt("dve", yT[:, h, :], Ob, r0, ALU.mult, r=[Ok, r0k], w=[(yk, h)])
                        del st[("ls", h)]

            LOOK = 3
            for i in range(min(LOOK, len(items))):
                stageA(items[i])
            for i in range(len(items)):
                if i + LOOK < len(items):
                    stageA(items[i + LOOK])
                stageB(items[i])
            self.out_proj_block(wo2, "o_wo2", yT, yk, 4, qb)
        self.rot_banks = list(range(8))
        self.bi = 0
        P.fence()
        A.reset(m0)

    def build(self):
        cfg = self.cfg
        self.setup()
        for l in range(cfg.get("layers", DEPTH)):
            if cfg.get("mixer", True):
                if l % 2 == 0:
                    if not cfg.get("skip_even"):
                        self.even_mixer(l // 2, l)
                elif not cfg.get("skip_odd"):
                    self.odd_mixer(l // 2, l)
            if cfg.get("xattn", True):
                self.xattn(l)
            if cfg.get("ffn", True):
                self.ffn(l)
        self.store()
        self.P.emit()


def build_nc(cfg=None):
    nc = bass.Bass("TRN2", target_bir_lowering=False)
    b = Builder(nc, cfg or {})
    b.build()
    return nc, b


def make_in_maps(inputs, n):
    consts = host_consts()
    maps = []
    for i in range(n):
        mp = {
            "x": np.ascontiguousarray(inputs["x"][i]),
            "mem": np.ascontiguousarray(inputs["mem"][i]),
            "positions": np.ascontiguousarray(inputs["positions"][i:i + 1]),
        }
        for name, _ in WEIGHTS:
            mp[name] = np.ascontiguousarray(inputs[name])
        mp.update(consts)
        maps.append(mp)
    return maps


def kernel(**inputs):
    inputs = {k: np.asarray(v) for k, v in inputs.items()}
    n = inputs["x"].shape[0]
    nc, _ = build_nc({})
    in_maps = make_in_maps(inputs, n)
    res = run_bass_kernel_spmd(nc, in_maps, core_ids=list(range(n)))
    return np.stack([np.asarray(r["y"]) for r in res.results], axis=0).astype(np.float32)
```

```python
from contextlib import ExitStack
import math
import numpy as np
import concourse.bass as bass
import concourse.mybir as mybir
from concourse.bass_utils import run_bass_kernel_spmd

F32 = mybir.dt.float32
BF16 = mybir.dt.bfloat16
F16 = mybir.dt.float16
I32 = mybir.dt.int32
AF = mybir.ActivationFunctionType
ALU = mybir.AluOpType
AX = mybir.AxisListType

EPOCH = 30000
STRICT_SAME_ENGINE = True
NSLOT = 8
ENGS = ("pe", "act", "dve", "pool", "sp")


class Prog:
    def __init__(self, nc):
        self.nc = nc
        self.ops = {e: [] for e in ENGS}
        self.ncomp = {e: 0 for e in ENGS}
        self.ndma = {e: 0 for e in ENGS}
        self.last_w = {}
        self.readers = {}
        self.waited = {e: {} for e in ENGS}
        self.semkeys = set()
        self.sems = {}
        self.last_tok = {}
        self.gdep = None

    def add(self, eng, fn, r=(), w=(), dma=False, nofence=False):
        nowait_only = fn is None
        raw = {}
        oth = {}
        if eng != "pe" and not nowait_only:
            locks = [("pslock", k[1]) for k in r if isinstance(k, tuple) and len(k) == 2 and k[0] == "ps"]
            if locks:
                w = list(w) + locks

        def put(d, tok):
            sk, v, e2, d2 = tok
            if d.get(sk, (0,))[0] < v:
                d[sk] = (v, e2, d2)

        for k in r:
            t = self.last_w.get(k)
            if t is not None:
                put(raw, t)
        for k in w:
            t = self.last_w.get(k)
            if t is not None:
                put(oth, t)
            for sk, (v, e2, d2) in self.readers.get(k, {}).items():
                put(oth, (sk, v, e2, d2))
        if self.gdep is not None and not nofence:
            put(raw, self.gdep)
        if nowait_only:
            semkey, val = None, 0
        elif dma:
            i = self.ndma[eng]
            self.ndma[eng] += 1
            slot, rnd = i % NSLOT, i // NSLOT
            semkey = ("d", eng, slot)
            val = 16 * (rnd + 1)
            if rnd > 0:
                put(raw, (semkey, 16 * rnd, eng, True))
        else:
            i = self.ncomp[eng]
            self.ncomp[eng] += 1
            semkey = ("c", eng, i // EPOCH)
            val = i % EPOCH + 1
        tok = (semkey, val, eng, dma)
        waits = []
        wd = self.waited[eng]
        for d, is_raw in ((raw, True), (oth, False)):
            for sk, (v, e2, d2) in d.items():
                if not d2 and e2 == eng:
                    if eng == "pe" or (not is_raw and not STRICT_SAME_ENGINE):
                        continue
                if wd.get(sk, 0) >= v:
                    continue
                wd[sk] = v
                waits.append((sk, v))
        if nowait_only:
            self.ops[eng].append((None, waits, None, False))
            return None
        self.semkeys.add(semkey)
        self.last_tok[semkey] = tok
        for k in w:
            self.last_w[k] = tok
            self.readers[k] = {}
        for k in r:
            d = self.readers.setdefault(k, {})
            if d.get(semkey, (0,))[0] < val:
                d[semkey] = (val, eng, dma)
        self.ops[eng].append((fn, waits, semkey, dma))
        return tok

    def op(self, eng, name, *args, r=(), w=(), **kw):
        return self.add(eng, lambda e: getattr(e, name)(*args, **kw), r, w)

    def mm(self, out, lhsT, rhs, start=True, stop=True, r=(), w=(), **kw):
        return self.add("pe", lambda e: e.matmul(out, lhsT, rhs, start=start, stop=stop, **kw), r, w)

    def tr(self, out, in_, ident, r=(), w=()):
        return self.add("pe", lambda e: e.transpose(out, in_, ident), r, w)

    def act(self, out, in_, func, r=(), w=(), **kw):
        return self.add("act", lambda e: e.activation(out, in_, func, **kw), r, w)

    def ts(self, eng, out, in0, s1, s2, op0, op1=None, r=(), w=()):
        if op1 is None:
            return self.add(eng, lambda e: e.tensor_scalar(out, in0, s1, None, op0), r, w)
        return self.add(eng, lambda e: e.tensor_scalar(out, in0, s1, s2, op0, op1), r, w)

    def tt(self, eng, out, in0, in1, op, r=(), w=()):
        return self.add(eng, lambda e: e.tensor_tensor(out, in0, in1, op), r, w)

    def stt(self, eng, out, in0, scalar, in1, op0, op1, r=(), w=()):
        return self.add(eng, lambda e: e.scalar_tensor_tensor(out, in0, scalar, in1, op0, op1), r, w)

    def copy(self, eng, out, in_, r=(), w=()):
        if eng == "act":
            return self.add(eng, lambda e: e.copy(out, in_), r, w)
        return self.add(eng, lambda e: e.tensor_copy(out, in_), r, w)

    def dma(self, eng, out, in_, r=(), w=(), nofence=False, **kw):
        return self.add(eng, lambda e: e.dma_start(out, in_, **kw), r, w, dma=True, nofence=nofence)

    def fence(self):
        keys = []
        for sk, tok in list(self.last_tok.items()):
            k = ("_fence", sk)
            self.last_w[k] = tok
            self.readers[k] = {}
            keys.append(k)
        self.gdep = None
        tok = self.add("sp", lambda e: e.nop(), r=keys, w=[])
        self.gdep = tok

    def finish(self, eng, keys):
        self.add(eng, None, r=keys, w=())

    def emit(self):
        nc = self.nc
        with ExitStack() as st:
            for sk in sorted(self.semkeys, key=str):
                self.sems[sk] = st.enter_context(nc.semaphore("s_%s_%s_%d" % sk))
            with nc.Block() as block:
                def mk(name):
                    def body(e):
                        for fn, waits, semkey, dma in self.ops[name]:
                            for sk, v in waits:
                                e.wait_ge(self.sems[sk], v)
                            if fn is None:
                                continue
                            ins = fn(e)
                            ins.then_inc(self.sems[semkey], 16 if dma else 1)
                    return body
                block.tensor(mk("pe"))
                block.scalar(mk("act"))
                block.vector(mk("dve"))
                block.gpsimd(mk("pool"))
                block.sync(mk("sp"))


T = 2048
D = 1024
TB = 512
NTB = 4
NT = 16
DC = 8
FF = 2816
FC = 22
NMEM = 256
EPS = 1e-6
SB_BASE = 16512
SB_END = 229344
DEPTH = 4
SM_SHIFT = 10.0

WEIGHTS = [
    ("norm_mix", [4, 1024]), ("norm_x", [4, 1024]), ("norm_mem", [4, 1024]),
    ("x_wq", [4, 1024, 512]), ("x_wkv", [4, 1024, 1024]), ("x_q_norm", [4, 128]), ("x_k_norm", [4, 128]),
    ("x_wo", [4, 512, 1024]), ("norm_ffn", [4, 1024]), ("ffn_w_in", [4, 1024, 5632]),
    ("ffn_w_out", [4, 2816, 1024]),
    ("ev_w_in", [2, 1024, 3080]), ("ev_conv_qkv", [2, 4, 1536]), ("ev_a_log", [2, 4]), ("ev_dt_bias", [2, 4]),
    ("ev_o_norm", [2, 128]), ("ev_conv_b_w", [2, 4, 512]), ("ev_conv_b_b", [2, 512]),
    ("ev_gate_a_w", [2, 8, 64, 64]), ("ev_gate_a_b", [2, 512]), ("ev_gate_x_w", [2, 8, 64, 64]),
    ("ev_gate_x_b", [2, 512]), ("ev_lru_l", [2, 512]), ("ev_w_out", [2, 1024, 1024]),
    ("od_w_in", [2, 1024, 1952]), ("od_c_q_norm", [2, 64]), ("od_c_k_norm", [2, 64]),
    ("od_lam_q1", [2, 64]), ("od_lam_k1", [2, 64]), ("od_lam_q2", [2, 64]), ("od_lam_k2", [2, 64]),
    ("od_c_sub_norm", [2, 128]), ("od_q_lat_norm", [2, 256]), ("od_w_uq", [2, 256, 384]),
    ("od_kv_lat_norm", [2, 128]), ("od_w_ukv", [2, 128, 768]), ("od_d_q_norm", [2, 96]),
    ("od_d_k_norm", [2, 96]), ("od_w_out", [2, 1024, 1024]),
]


def host_consts():
    c = {}
    c["c_ident"] = np.eye(128, dtype=np.float32)
    rt = np.zeros((128, 128), np.float32)
    for i in range(16):
        rt[80 + i, 64 + i] = -1.0
        rt[64 + i, 80 + i] = 1.0
    c["c_rotT"] = rt
    fr = np.zeros((128, 2), np.float32)
    for i in range(16):
        f = 10000.0 ** (-(i / 16.0))
        fr[64 + i, 0] = fr[80 + i, 0] = np.float32(f)
    fr[:, 1] = fr[:, 0] / np.float32(2 * np.pi)
    c["c_freq"] = fr
    bo = np.zeros((128, 128), np.float32)
    bo[0:64, 0:64] = 1.0
    bo[64:128, 64:128] = 1.0
    c["c_blockones"] = bo
    ii = np.arange(128)
    same = (ii[:, None] // 64) == (ii[None, :] // 64)
    c["c_maskSL"] = (same & (ii[None, :] < ii[:, None])).astype(np.float32)
    c["c_maskIU"] = (same & (ii[None, :] >= ii[:, None])).astype(np.float32)
    c["c_lsel"] = (ii[:, None] == (ii[None, :] // 64) * 64 + 63).astype(np.float32)
    return c


class Arena:
    def __init__(self, nc, base, end):
        self.nc, self.p, self.end, self.n = nc, base, end, 0

    def alloc(self, name, shape, dtype):
        esz = 4 if dtype in (F32, I32) else 2
        nbytes = int(np.prod(shape[1:])) * esz
        off = (self.p + 31) // 32 * 32
        self.p = off + nbytes
        assert self.p <= self.end, ("SBUF overflow", name, self.p, self.end)
        self.n += 1
        return self.nc.alloc_sbuf_tensor_at("%s_%d" % (name, self.n), list(shape), dtype, offset=off).ap()

    def mark(self):
        return self.p

    def reset(self, m):
        self.p = m


class Builder:
    def __init__(self, nc, cfg):
        self.nc = nc
        self.cfg = cfg
        self.P = Prog(nc)
        self.d = {}
        P = self.P
        d = self.d
        d["x"] = nc.dram_tensor("x", [T, D], F32, kind="ExternalInput").ap()
        d["mem"] = nc.dram_tensor("mem", [NMEM, D], F32, kind="ExternalInput").ap()
        d["positions"] = nc.dram_tensor("positions", [1, T], I32, kind="ExternalInput").ap()
        for name, shp in WEIGHTS:
            d[name] = nc.dram_tensor(name, shp, F32, kind="ExternalInput").ap()
        for name, arr in host_consts().items():
            d[name] = nc.dram_tensor(name, list(arr.shape), F32, kind="ExternalInput").ap()
        d["y"] = nc.dram_tensor("y", [T, D], F32, kind="ExternalOutput").ap()
        self.A = Arena(nc, SB_BASE, SB_END)
        A = self.A
        self.ps = [nc.alloc_psum_tensor("psb%d" % i, [128, 512], F32).ap() for i in range(8)]
        self.bi = 0
        self.rot_banks = list(range(8))
        self.misc_banks = [6, 7]
        self.score_banks = [2, 3, 4, 5]
        self.mi = 0
        self.si = 0
        self.rots = {}
        self.xT = A.alloc("xT", [128, DC, T], F32)
        self.identF = A.alloc("identF", [128, 128], F32)
        self.identB = A.alloc("identB", [128, 128], BF16)
        self.onesB = A.alloc("onesB", [128, 128], BF16)
        self.onesF = A.alloc("onesF", [128, 128], F32)
        self.colsA = A.alloc("colsA", [128, 128], F32)
        self.colsB = A.alloc("colsB", [128, 128], F32)
        self.colsC = A.alloc("colsC", [128, 128], F32)
        self.blockB = A.alloc("blockB", [128, 128], BF16)
        self.rotTB = A.alloc("rotTB", [128, 128], BF16)
        self.freq = A.alloc("freq", [128, 2], F32)
        self.posk = A.alloc("posk", [128, NT], F32)
        self.negposk = A.alloc("negposk", [128, NT], F32)
        self.pkf = A.alloc("pkf", [NT, 128], F32)
        self.colsD = A.alloc("colsD", [128, 256], F32)
        self.maskSL = A.alloc("maskSL", [128, 128], F32)
        self.maskIU = A.alloc("maskIU", [128, 128], F32)
        self.lsel = A.alloc("lsel", [128, 128], F32)
        self.memTn = A.alloc("memTn", [128, DC, NMEM], F32)
        self.kTx = A.alloc("kTx", [128, 4, NMEM], BF16)
        self.vx = A.alloc("vx", [128, 2, 512], BF16)
        self.phase_base = A.mark()

    def bank(self):
        i = self.rot_banks[self.bi % len(self.rot_banks)]
        self.bi += 1
        return self.ps[i], ("ps", i)

    def mkrot(self, name, n, shape, dtype):
        self.rots[name] = [[self.A.alloc(name, shape, dtype) for _ in range(n)], 0]

    def rot(self, name):
        lst, i = self.rots[name]
        self.rots[name][1] = (i + 1) % len(lst)
        return lst[i], (name, i)

    def gcol(self, which, l, c):
        j = which * 32 + l * 8 + c
        return self.colsA[:, j:j + 1]

    def setup(self):
        P, d, A = self.P, self.d, self.A
        P.dma("sp", self.identF, d["c_ident"], w=["identF"])
        P.copy("dve", self.identB, self.identF, r=["identF"], w=["identB"])
        P.op("dve", "memset", self.onesB, 1.0, w=["onesB"])
        P.op("dve", "memset", self.onesF, 1.0, w=["onesF"])
        m = A.mark()
        stg = A.alloc("stg", [128, 128], F32)
        for i, nm in enumerate(["norm_mix", "norm_x", "norm_ffn", "norm_mem"]):
            P.dma("sp", stg[i * 32:(i + 1) * 32, :], d[nm].rearrange("l (c p) -> (l c) p", p=128), w=[("stg", i)])
        pb, pk = self.bank()
        P.tr(pb[:, 0:128], stg, self.identF, r=[("stg", i) for i in range(4)] + ["identF"], w=[pk])
        P.copy("dve", self.colsA, pb[:, 0:128], r=[pk], w=["colsA"])
        stg2 = A.alloc("stg2", [128, 128], F32)
        P.op("dve", "memset", stg2, 0.0, w=["stg2"])
        P.dma("sp", stg2[0:4, :], d["x_q_norm"], r=[], w=["stg2"])
        P.dma("sp", stg2[4:8, :], d["x_k_norm"], r=["stg2"], w=["stg2b"])
        pb, pk = self.bank()
        P.tr(pb[:, 0:128], stg2, self.identF, r=["stg2", "stg2b", "identF"], w=[pk])
        P.copy("dve", self.colsB, pb[:, 0:128], r=[pk], w=["colsB"])
        stg3 = A.alloc("stg3", [128, 128], F32)
        P.op("dve", "memset", stg3, 0.0, w=["stg3z"])
        k3 = []
        def ld3(row, c0, src):
            k = ("stg3", len(k3))
            k3.append(k)
            P.dma("sp", stg3[row:row + 1, c0:c0 + src.shape[1]], src, r=["stg3z"], w=[k])
        for o in range(2):
            for half in range(2):
                ld3(o * 8 + 0, half * 64, d["od_c_q_norm"][o:o + 1, :])
                ld3(o * 8 + 1, half * 64, d["od_c_k_norm"][o:o + 1, :])
            ld3(o * 8 + 2, 0, d["od_c_sub_norm"][o:o + 1, :])
            ld3(o * 8 + 3, 0, d["od_q_lat_norm"][o:o + 1, 0:128])
            ld3(o * 8 + 4, 0, d["od_q_lat_norm"][o:o + 1, 128:256])
            ld3(o * 8 + 5, 0, d["od_kv_lat_norm"][o:o + 1, :])
            ld3(o * 8 + 6, 0, d["od_d_q_norm"][o:o + 1, :])
            ld3(o * 8 + 7, 0, d["od_d_k_norm"][o:o + 1, :])
        pb, pk = self.bank()
        P.tr(pb[:, 0:128], stg3, self.identF, r=k3 + ["identF"], w=[pk])
        P.copy("dve", self.colsC, pb[:, 0:128], r=[pk], w=["colsC"])
        P.dma("sp", self.maskSL, d["c_maskSL"], w=["maskSL"])
        P.dma("sp", self.maskIU, d["c_maskIU"], w=["maskIU"])
        P.dma("sp", self.lsel, d["c_lsel"], w=["lsel"])
        for e in range(2):
            st4 = A.alloc("stg4", [128, 128], F32)
            P.op("dve", "memset", st4, 0.0, w=[("st4z", e)])
            k4 = []
            def ld4(r0, src):
                k = ("stg4", e, len(k4))
                k4.append(k)
                P.dma("sp", st4[r0:r0 + src.shape[0], :], src, r=[("st4z", e)], w=[k])
            ld4(0, d["ev_conv_qkv"][e].rearrange("j (c p) -> (j c) p", p=128))
            ld4(48, d["ev_conv_b_w"][e].rearrange("j (c p) -> (j c) p", p=128))
            ld4(64, d["ev_conv_b_b"][e:e + 1, :].rearrange("o (c p) -> (o c) p", p=128))
            ld4(68, d["ev_gate_a_b"][e:e + 1, :].rearrange("o (c p) -> (o c) p", p=128))
            ld4(72, d["ev_gate_x_b"][e:e + 1, :].rearrange("o (c p) -> (o c) p", p=128))
            ld4(76, d["ev_lru_l"][e:e + 1, :].rearrange("o (c p) -> (o c) p", p=128))
            ld4(80, d["ev_o_norm"][e:e + 1, :])
            pb, pk = self.bank()
            P.tr(pb[:, 0:128], st4, self.identF, r=k4 + ["identF"], w=[pk])
            P.copy("dve", self.colsD[:, e * 128:(e + 1) * 128], pb[:, 0:128], r=[pk], w=[("colsD", e)])
        cst = A.alloc("cst", [128, 128], F32)
        P.dma("sp", cst, d["c_blockones"], w=["cst"])
        P.copy("dve", self.blockB, cst, r=["cst"], w=["blockB"])
        cst2 = A.alloc("cst2", [128, 128], F32)
        P.dma("sp", cst2, d["c_rotT"], w=["cst2"])
        P.copy("dve", self.rotTB, cst2, r=["cst2"], w=["rotTB"])
        P.dma("sp", self.freq, d["c_freq"], w=["freq"])
        pk_i = A.alloc("pk_i", [NT, 128], I32)
        pk_f = self.pkf
        P.dma("sp", pk_i, d["positions"].rearrange("o (t p) -> (o t) p", p=128), w=["pk_i"])
        P.copy("dve", pk_f, pk_i, r=["pk_i"], w=["pk_f"])
        pb, pk = self.bank()
        P.tr(pb[:, 0:NT], pk_f, self.identF[0:NT, 0:NT], r=["pk_f", "identF"], w=[pk])
        P.copy("dve", self.posk, pb[:, 0:NT], r=[pk], w=["posk"])
        P.ts("dve", self.negposk, self.posk, -1.0, None, ALU.mult, r=["posk"], w=["negposk"])
        xin = [A.alloc("xin", [128, D], F32) for _ in range(2)]
        for tt in range(NT):
            xb = xin[tt % 2]
            xk = ("xin", tt % 2)
            P.dma("sp", xb, d["x"][tt * 128:(tt + 1) * 128, :], w=[xk])
            for hb in range(2):
                pb, pk = self.bank()
                for q in range(4):
                    c = hb * 4 + q
                    P.tr(pb[:, q * 128:(q + 1) * 128], xb[:, c * 128:(c + 1) * 128], self.identF,
                         r=[xk, "identF"], w=[pk])
                eng = "dve" if hb == 0 else "act"
                P.copy(eng, self.xT[:, hb * 4:(hb + 1) * 4, tt * 128:(tt + 1) * 128],
                       pb.rearrange("p (a b) -> p a b", a=4),
                       r=[pk], w=[("xT", c, tt // 4) for c in range(hb * 4, hb * 4 + 4)])
        mm_ = [A.alloc("memin", [128, D], F32) for _ in range(2)]
        msq = A.alloc("msq", [128, D], F32)
        mss = A.alloc("mss", [128, 2], F32)
        for mt in range(2):
            P.dma("sp", mm_[mt], d["mem"][mt * 128:(mt + 1) * 128, :], w=[("memin", mt)])
            P.act(msq, mm_[mt], AF.Square, r=[("memin", mt)], w=["msq"], accum_out=mss[:, mt:mt + 1])
            P.act(mss[:, mt:mt + 1], mss[:, mt:mt + 1], AF.Sqrt, r=["msq"], w=[("mss", mt)], scale=1.0 / D, bias=EPS)
            P.op("dve", "reciprocal", mss[:, mt:mt + 1], mss[:, mt:mt + 1], r=[("mss", mt)], w=[("mss", mt)])
            P.ts("dve", mm_[mt], mm_[mt], mss[:, mt:mt + 1], None, ALU.mult, r=[("memin", mt), ("mss", mt)],
                 w=[("memin", mt)])
            for hb in range(2):
                pb, pk = self.bank()
                for q in range(4):
                    c = hb * 4 + q
                    P.tr(pb[:, q * 128:(q + 1) * 128], mm_[mt][:, c * 128:(c + 1) * 128], self.identF,
                         r=[("memin", mt), "identF"], w=[pk])
                P.copy("dve", self.memTn[:, hb * 4:(hb + 1) * 4, mt * 128:(mt + 1) * 128],
                       pb.rearrange("p (a b) -> p a b", a=4), r=[pk], w=["memTn"])
        P.fence()
        A.reset(m)

    def norm_block(self, tb, which, l, hT_out, hkey):
        P = self.P
        sl = slice(tb * TB, (tb + 1) * TB)
        sq, sqk = self.rot("sq")
        P.act(sq, self.xT[:, :, sl], AF.Square, r=[("xT", c, tb) for c in range(DC)], w=[sqk])
        ss, ssk = self.bank()
        for c in range(DC):
            P.mm(ss, self.onesB, sq[:, c, :], start=(c == 0), stop=(c == DC - 1), r=[sqk, "onesB"], w=[ssk])
        rs, rsk = self.rot("rs")
        P.act(rs, ss, AF.Ln, r=[ssk], w=[rsk], scale=1.0 / D, bias=EPS)
        P.act(rs, rs, AF.Exp, r=[rsk], w=[rsk], scale=-0.5)
        for c in range(DC):
            P.stt("dve", hT_out[:, c, :], self.xT[:, c, sl], self.gcol(which, l, c), rs, ALU.mult, ALU.mult,
                  r=[("xT", c, tb), rsk, "colsA"], w=[(hkey, c)])

    def pnorm(self, src, srck, npart, ones_l, inv_n, gain, out, outk):
        P = self.P
        srcks = srck if isinstance(srck, list) else [srck]
        sq, sqk = self.rot("sq1")
        P.act(sq[0:npart, :], src, AF.Square, r=srcks, w=[sqk])
        ss, ssk = self.bank()
        P.mm(ss[0:npart, :], ones_l, sq[0:npart, :], r=[sqk, "onesB"], w=[ssk])
        rs, rsk = self.rot("rs")
        P.act(rs[0:npart, :], ss[0:npart, :], AF.Ln, r=[ssk], w=[rsk], scale=inv_n, bias=EPS)
        P.act(rs[0:npart, :], rs[0:npart, :], AF.Exp, r=[rsk], w=[rsk], scale=-0.5)
        if gain is None:
            P.tt("dve", out, src, rs[0:npart, :], ALU.mult, r=srcks + [rsk], w=[outk])
        elif isinstance(gain, float):
            P.stt("dve", out, src, gain, rs[0:npart, :], ALU.mult, ALU.mult, r=srcks + [rsk], w=[outk])
        else:
            P.stt("dve", out, src, gain, rs[0:npart, :], ALU.mult, ALU.mult, r=srcks + [rsk], w=[outk])

    def pnorm_pipe(self, blocks):
        P = self.P
        prev = None

        def part_b(stt_):
            (src, srck, npart, ones_l, inv_n, gain, out, outk), sq, sqk = stt_
            srcks = srck if isinstance(srck, list) else [srck]
            ss, ssk = self.bank()
            P.mm(ss[0:npart, :], ones_l, sq[0:npart, :], r=[sqk], w=[ssk])
            rs, rsk = self.rot("rs")
            P.act(rs[0:npart, :], ss[0:npart, :], AF.Ln, r=[ssk], w=[rsk], scale=inv_n, bias=EPS)
            P.act(rs[0:npart, :], rs[0:npart, :], AF.Exp, r=[rsk], w=[rsk], scale=-0.5)
            if gain is None:
                P.tt("dve", out, src, rs[0:npart, :], ALU.mult, r=srcks + [rsk], w=[outk])
            else:
                P.stt("dve", out, src, gain, rs[0:npart, :], ALU.mult, ALU.mult, r=srcks + [rsk], w=[outk])

        for b in blocks:
            src, srck, npart = b[0], b[1], b[2]
            srcks = srck if isinstance(srck, list) else [srck]
            sq, sqk = self.rot("sq1")
            P.act(sq[0:npart, :], src, AF.Square, r=srcks, w=[sqk])
            if prev is not None:
                part_b(prev)
            prev = (b, sq, sqk)
        if prev is not None:
            part_b(prev)

    def run_chains(self, makers, K):
        pending = list(makers)
        active = []
        free = list(range(K))
        while pending or active:
            while pending and free:
                j = free.pop(0)
                active.append((pending.pop(0)(j), j))
            nxt = []
            for g, j in active:
                try:
                    next(g)
                    nxt.append((g, j))
                except StopIteration:
                    free.append(j)
            active = nxt

    def recip_act(self, out, outk, src, srck):
        P = self.P
        P.act(out, src, AF.Ln, r=[srck], w=[outk])
        P.act(out, out, AF.Exp, r=[outk], w=[outk], scale=-1.0)

    def mbank(self):
        i = self.misc_banks[self.mi % len(self.misc_banks)]
        self.mi += 1
        return self.ps[i], ("ps", i)

    def sbank(self):
        i = self.score_banks[self.si % len(self.score_banks)]
        self.si += 1
        return self.ps[i], ("ps", i)

    def ffn(self, l):
        P, d, A = self.P, self.d, self.A
        m = A.mark()
        self.rots = {}
        hT = A.alloc("ffn_hT", [128, DC, 1024], BF16)
        act = A.alloc("ffn_act", [128, FC, 1024], BF16)
        self.mkrot("win", 2, [128, DC, 1024], BF16)
        self.mkrot("wout", 2, [128, FC, 128], BF16)
        self.mkrot("sq", 1, [128, DC, TB], BF16)
        self.mkrot("rs", 2, [128, TB], F32)
        self.mkrot("sg", 2, [128, TB], F32)
        w_in_d = d["ffn_w_in"][l].rearrange("(c p) n -> p c n", p=128)
        w_out_d = d["ffn_w_out"][l].rearrange("(f p) n -> p f n", p=128)
        SLW = 512
        nsl = (FF + SLW - 1) // SLW
        for half in range(2):
            if half == 0:
                for j in range(2):
                    self.norm_block(j, 2, l, hT[:, :, j * TB:(j + 1) * TB], ("ffn_hT", j))
            for s in range(nsl):
                c0 = s * SLW
                ncol = min(SLW, FF - c0)
                wb, wk = self.rot("win")
                P.dma("pool", wb[:, :, 0:ncol], w_in_d[:, :, c0:c0 + ncol], w=[(wk, "g")])
                P.dma("pool", wb[:, :, SLW:SLW + ncol], w_in_d[:, :, FF + c0:FF + c0 + ncol], w=[(wk, "u")])
                for fi in range(ncol // 128):
                    f = (c0 // 128) + fi
                    for j in range(2):
                        gps, gk = self.bank()
                        ups, uk = self.bank()
                        for c in range(DC):
                            P.mm(gps, wb[:, c, fi * 128:(fi + 1) * 128], hT[:, c, j * TB:(j + 1) * TB],
                                 start=(c == 0), stop=(c == DC - 1), r=[(wk, "g"), (("ffn_hT", j), c)], w=[gk])
                        for c in range(DC):
                            P.mm(ups, wb[:, c, SLW + fi * 128:SLW + (fi + 1) * 128], hT[:, c, j * TB:(j + 1) * TB],
                                 start=(c == 0), stop=(c == DC - 1), r=[(wk, "u"), (("ffn_hT", j), c)], w=[uk])
                        sg, sgk = self.rot("sg")
                        P.act(sg, gps, AF.Silu, r=[gk], w=[sgk])
                        P.tt("dve", act[:, f, j * TB:(j + 1) * TB], sg, ups, ALU.mult, r=[sgk, uk], w=[("act", f, j)])
            if half == 0:
                for j in range(2):
                    self.norm_block(2 + j, 2, l, hT[:, :, j * TB:(j + 1) * TB], ("ffn_hT", j))
            for dc in range(DC):
                wo, wok = self.rot("wout")
                P.dma("pool", wo, w_out_d[:, :, dc * 128:(dc + 1) * 128], w=[wok])
                for j in range(2):
                    tb = half * 2 + j
                    sl = slice(tb * TB, (tb + 1) * TB)
                    yps, yk = self.bank()
                    for f in range(FC):
                        P.mm(yps, wo[:, f, :], act[:, f, j * TB:(j + 1) * TB], start=(f == 0), stop=(f == FC - 1),
                             r=[wok, ("act", f, j)], w=[yk])
                    P.tt("dve", self.xT[:, dc, sl], self.xT[:, dc, sl], yps, ALU.add, r=[("xT", dc, tb), yk],
                         w=[("xT", dc, tb)])
        P.fence()
        A.reset(m)

    def out_proj_block(self, wo, wok, oT, ok, nk, tb):
        P = self.P
        sl = slice(tb * TB, (tb + 1) * TB)
        for dc in range(DC):
            yps, yk = self.bank()
            for h in range(nk):
                P.mm(yps, wo[:, h, dc * 128:(dc + 1) * 128], oT[:, h, :], start=(h == 0), stop=(h == nk - 1),
                     r=[wok, (ok, h)], w=[yk])
            P.tt("dve", self.xT[:, dc, sl], self.xT[:, dc, sl], yps, ALU.add, r=[("xT", dc, tb), yk],
                 w=[("xT", dc, tb)])

    def xattn(self, l):
        P, d, A = self.P, self.d, self.A
        m = A.mark()
        self.rots = {}
        wq = A.alloc("x_wq", [128, DC, 512], BF16)
        wo = A.alloc("x_wo", [128, 4, D], BF16)
        wkv = A.alloc("x_wkv", [128, DC, D], BF16)
        mh = A.alloc("x_mh", [128, DC, NMEM], BF16)
        hT = A.alloc("x_hT", [128, DC, T], BF16)
        qall = A.alloc("x_q", [128, 4, T], BF16)
        oall = A.alloc("x_o", [128, 4, T], BF16)
        self.mkrot("sq", 1, [128, DC, TB], BF16)
        self.mkrot("sq1", 3, [128, TB], BF16)
        self.mkrot("rs", 3, [128, TB], F32)
        self.mkrot("pT", 6, [128, TB], BF16)
        P.dma("pool", wq, d["x_wq"][l].rearrange("(c p) n -> p c n", p=128), w=["x_wq"])
        for hh in range(2):
            P.dma("pool", wkv[:, :, hh * 512:(hh + 1) * 512],
                  d["x_wkv"][l].rearrange("(c p) n -> p c n", p=128)[:, :, hh * 512:(hh + 1) * 512], w=[("x_wkv", hh)])
        P.dma("pool", wo, d["x_wo"][l].rearrange("(h p) n -> p h n", p=128), w=["x_wo"])
        self.rot_banks = list(range(8))
        self.bi = 0
        for tb in range(NTB):
            self.norm_block(tb, 1, l, hT[:, :, tb * TB:(tb + 1) * TB], ("x_hT", tb))
        prev = None

        def q_b(stt_):
            qp, qk, sq, sqk, h, tb = stt_
            ss, ssk = self.bank()
            P.mm(ss, self.onesB, sq, r=[sqk], w=[ssk])
            rs, rsk = self.rot("rs")
            P.act(rs, ss, AF.Ln, r=[ssk], w=[rsk], scale=1.0 / 128, bias=EPS)
            P.act(rs, rs, AF.Exp, r=[rsk], w=[rsk], scale=-0.5)
            P.stt("dve", qall[:, h, tb * TB:(tb + 1) * TB], qp, self.colsB[:, l:l + 1], rs, ALU.mult, ALU.mult,
                  r=[qk, rsk], w=[("x_q", h, tb)])

        for tb in range(NTB):
            sl = slice(tb * TB, (tb + 1) * TB)
            for h in range(4):
                qp, qk = self.bank()
                for c in range(DC):
                    P.mm(qp, wq[:, c, h * 128:(h + 1) * 128], hT[:, c, sl], start=(c == 0), stop=(c == DC - 1),
                         r=["x_wq", (("x_hT", tb), c)], w=[qk])
                sq, sqk = self.rot("sq1")
                P.act(sq, qp, AF.Square, r=[qk], w=[sqk])
                if prev is not None:
                    q_b(prev)
                prev = (qp, qk, sq, sqk, h, tb)
        for c in range(DC):
            P.ts("dve", mh[:, c, :], self.memTn[:, c, :], self.gcol(3, l, c), None, ALU.mult, w=[("x_mh", c)])
        q_b(prev)
        for h in range(4):
            kp, kk = self.bank()
            for c in range(DC):
                P.mm(kp[:, 0:NMEM], wkv[:, c, h * 128:(h + 1) * 128], mh[:, c, :], start=(c == 0), stop=(c == DC - 1),
                     r=[("x_wkv", 0), ("x_mh", c)], w=[kk])
            sq, sqk = self.rot("sq1")
            P.act(sq[:, 0:NMEM], kp[:, 0:NMEM], AF.Square, r=[kk], w=[sqk])
            ss, ssk = self.bank()
            P.mm(ss[:, 0:NMEM], self.onesB, sq[:, 0:NMEM], r=[sqk], w=[ssk])
            rs, rsk = self.rot("rs")
            P.act(rs[:, 0:NMEM], ss[:, 0:NMEM], AF.Ln, r=[ssk], w=[rsk], scale=1.0 / 128, bias=EPS)
            P.act(rs[:, 0:NMEM], rs[:, 0:NMEM], AF.Exp, r=[rsk], w=[rsk], scale=-0.5)
            P.stt("dve", self.kTx[:, h, :], kp[:, 0:NMEM], self.colsB[:, 4 + l:5 + l], rs[:, 0:NMEM], ALU.mult, ALU.mult,
                  r=[kk, rsk], w=[("kTx", h)])
        for mt in range(2):
            vp, vk = self.bank()
            for c in range(DC):
                P.mm(vp, mh[:, c, mt * 128:(mt + 1) * 128], wkv[:, c, 512:1024], start=(c == 0), stop=(c == DC - 1),
                     r=[("x_wkv", 1), ("x_mh", c)], w=[vk])
            P.copy("act", self.vx[:, mt, :], vp, r=[vk], w=[("vx", mt)])
        self.score_banks = [4, 5, 6, 7]
        self.si = 0
        items = [(tb, h, mt) for tb in range(NTB) for h in range(4) for mt in range(2)]
        st = {}

        def stageA(it):
            tb, h, mt = it
            sp_, spk = self.sbank()
            P.mm(sp_, self.kTx[:, h, mt * 128:(mt + 1) * 128], qall[:, h, tb * TB:(tb + 1) * TB],
                 r=[("kTx", h), ("x_q", h, tb)], w=[spk])
            st[it] = (sp_, spk)

        def stageB(it):
            tb, h, mt = it
            g = tb * 4 + h
            sp_, spk = st.pop(it)
            pT, pTk = self.rot("pT")
            P.act(pT, sp_, AF.Exp, r=[spk], w=[pTk], scale=128 ** -0.5, bias=-SM_SHIFT)
            ob, obk = self.ps[g % 2], ("ps", g % 2)
            lb, lbk = self.ps[2 + g % 2], ("ps", 2 + g % 2)
            P.mm(ob, self.vx[:, mt, h * 128:(h + 1) * 128], pT, start=(mt == 0), stop=(mt == 1),
                 r=[("vx", mt), pTk], w=[obk])
            P.mm(lb, self.onesB, pT, start=(mt == 0), stop=(mt == 1), r=[pTk], w=[lbk])
            if mt == 1:
                rs, rsk = self.rot("rs")
                self.recip_act(rs, rsk, lb, lbk)
                P.tt("dve", oall[:, h, tb * TB:(tb + 1) * TB], ob, rs, ALU.mult, r=[obk, rsk], w=[(("x_o", tb), h)])

        LOOKX = 3
        for i in range(min(LOOKX, len(items))):
            stageA(items[i])
        for i in range(len(items)):
            if i + LOOKX < len(items):
                stageA(items[i + LOOKX])
            stageB(items[i])
        self.rot_banks = list(range(8))
        self.bi = 0
        for tb in range(NTB):
            self.out_proj_block(wo, "x_wo", oall[:, :, tb * TB:(tb + 1) * TB], ("x_o", tb), 4, tb)
        self.score_banks = [2, 3, 4, 5]
        self.si = 0
        P.fence()
        A.reset(m)

    def store(self):
        P, d, A = self.P, self.d, self.A
        m = A.mark()
        yo = [A.alloc("yout", [128, D], F32) for _ in range(2)]
        keys = []
        for tt in range(NT):
            yb = yo[tt % 2]
            for hb in range(2):
                pb, pk = self.bank()
                for q in range(4):
                    c = hb * 4 + q
                    P.tr(pb[:, q * 128:(q + 1) * 128], self.xT[:, c, tt * 128:(tt + 1) * 128], self.identF,
                         r=[("xT", c, tt // 4), "identF"], w=[pk])
                eng = "dve" if hb == 0 else "act"
                P.copy(eng, yb[:, hb * 512:(hb + 1) * 512], pb, r=[pk], w=[("yout", tt % 2, hb)])
            P.dma("sp", d["y"][tt * 128:(tt + 1) * 128, :], yb, r=[("yout", tt % 2, 0), ("yout", tt % 2, 1)],
                  w=[("y", tt)])
            keys.append(("y", tt))
        P.finish("sp", keys)
        A.reset(m)

    def even_mixer(self, e, l):
        P, d, A = self.P, self.d, self.A
        m0 = A.mark()
        self.rots = {}
        self.rot_banks = list(range(8))
        cD = lambda j: self.colsD[:, e * 128 + j:e * 128 + j + 1]
        w_in_d = d["ev_w_in"][e].rearrange("(c p) n -> p c n", p=128)
        hT = A.alloc("e_hT", [128, DC, T], BF16)
        m1 = A.mark()
        self.mkrot("sq", 1, [128, DC, TB], BF16)
        self.mkrot("rs", 2, [128, TB], F32)
        for tb in range(NTB):
            self.norm_block(tb, 0, l, hT[:, :, tb * TB:(tb + 1) * TB], ("e_hT", tb))
        P.fence()
        A.reset(m1)
        self.rots = {}

        def proj(sbw, sk, tb, dst_ps, ppk):
            sl = slice(tb * TB, (tb + 1) * TB)
            for c in range(DC):
                P.mm(dst_ps, sbw[:, c, :], hT[:, c, sl], start=(c == 0), stop=(c == DC - 1),
                     r=[sk, (("e_hT", tb), c)], w=[ppk])

        if not self.cfg.get("skip_e1"):
            self._even_e1(e, l, hT, w_in_d, cD, proj, m1)
        if not self.cfg.get("skip_e2"):
            self._even_e2(e, l, hT, w_in_d, cD, proj, m1)
        P.fence()
        A.reset(m0)

    def _even_e1(self, e, l, hT, w_in_d, cD, proj, m1):
        P, d, A = self.P, self.d, self.A
        ybT = A.alloc("ybT", [128, 4, T], BF16)
        wo = A.alloc("e_wo", [128, 4, D], BF16)
        xraw = A.alloc("xraw", [128, 3 + T], F32)
        xc = A.alloc("xc", [128, T], F32)
        av = A.alloc("av", [128, T], F32)
        hs = A.alloc("hs", [128, T], F32)
        rfull = A.alloc("rfull", [128, T], F32)
        ifull = A.alloc("ifull", [128, T], F32)
        xcb = A.alloc("xcb", [128, T], BF16)
        gts = [A.alloc("gate", [128, 128], BF16) for _ in range(8)]
        c1 = A.alloc("c1", [128, 4], F32)
        self.mkrot("slab", 2, [128, DC, 256], BF16)
        self.mkrot("gg", 2, [128, TB], F32)
        slabs = []
        for cc in range(2):
            sb, sk = self.rot("slab")
            P.dma("pool", sb[:, :, 0:128], w_in_d[:, :, 2056 + cc * 128:2056 + (cc + 1) * 128], w=[(sk, 0)])
            P.dma("pool", sb[:, :, 128:256], w_in_d[:, :, 2568 + cc * 128:2568 + (cc + 1) * 128], w=[(sk, 1)])
            slabs.append((sb, sk))
        lcols = self.colsD[:, e * 128 + 76:e * 128 + 80]
        P.act(c1, lcols, AF.Exp, w=["c1"], scale=-1.0)
        P.act(c1, c1, AF.Ln, r=["c1"], w=["c1"], bias=1.0)
        P.ts("dve", c1, c1, -8.0, None, ALU.mult, r=["c1"], w=["c1"])
        P.op("dve", "memset", xraw[:, 0:3], 0.0, w=["xraw_pad"])
        for gi, g in enumerate(gts):
            P.op("dve", "memset", g, 0.0, w=[("gz", gi)])
        for cc in range(4):
            for which, nm in enumerate(["ev_gate_a_w", "ev_gate_x_w"]):
                gi = which * 4 + cc
                P.dma("pool", gts[gi][0:64, 0:64], d[nm][e, 2 * cc], r=[("gz", gi)], w=[("gate", gi, 0)])
                P.dma("pool", gts[gi][64:128, 64:128], d[nm][e, 2 * cc + 1], r=[("gz", gi)], w=[("gate", gi, 1)])
        P.dma("pool", wo, d["ev_w_out"][e].rearrange("(h p) n -> p h n", p=128)[:, 4:8, :], w=["e_wo"])
        allT = list(range(NTB))
        for cc in range(4):
            if cc < 2:
                sb, sk = slabs[cc]
            else:
                sb, sk = self.rot("slab")
                P.dma("pool", sb[:, :, 0:128], w_in_d[:, :, 2056 + cc * 128:2056 + (cc + 1) * 128], w=[(sk, 0)])
                P.dma("pool", sb[:, :, 128:256], w_in_d[:, :, 2568 + cc * 128:2568 + (cc + 1) * 128], w=[(sk, 1)])
            for tb in range(NTB):
                pp, ppk = self.bank()
                proj(sb[:, :, 0:128], (sk, 0), tb, pp, ppk)
                P.copy("act", xraw[:, 3 + tb * TB:3 + (tb + 1) * TB], pp, r=[ppk], w=[("xraw", tb)])
            xrk = [("xraw", tb) for tb in range(NTB)] + ["xraw_pad"]
            P.ts("dve", xc, xraw[:, 3:3 + T], cD(48 + 3 * 4 + cc), cD(64 + cc), ALU.mult, ALU.add, r=xrk, w=["xc"])
            for j in range(3):
                P.stt("dve", xc, xraw[:, j:j + T], cD(48 + j * 4 + cc), xc, ALU.mult, ALU.add, r=xrk + ["xc"], w=["xc"])
            P.copy("act", xcb, xc, r=["xc"], w=["xcb"])
            for tb in range(NTB):
                sl = slice(tb * TB, (tb + 1) * TB)
                rp, rpk = self.bank()
                P.mm(rp, gts[cc], xcb[:, sl], r=["xcb", ("gate", cc, 0), ("gate", cc, 1)], w=[rpk])
                ip, ipk = self.bank()
                P.mm(ip, gts[4 + cc], xcb[:, sl], r=["xcb", ("gate", 4 + cc, 0), ("gate", 4 + cc, 1)], w=[ipk])
                P.act(rfull[:, sl], rp, AF.Sigmoid, r=[rpk], w=[("rfull", tb)], bias=cD(68 + cc))
                P.act(ifull[:, sl], ip, AF.Sigmoid, r=[ipk], w=[("ifull", tb)], bias=cD(72 + cc))
            rk_all = [("rfull", tb) for tb in allT]
            ik_all = [("ifull", tb) for tb in allT]
            P.act(av, rfull, AF.Exp, r=rk_all + ["c1"], w=["av"], scale=c1[:, cc:cc + 1])
            P.tt("dve", rfull, av, av, ALU.mult, r=["av"], w=rk_all)
            P.act(rfull, rfull, AF.Sqrt, r=rk_all, w=rk_all, scale=-1.0, bias=1.0)
            P.tt("dve", ifull, ifull, xc, ALU.mult, r=ik_all + ["xc"], w=ik_all)
            P.tt("dve", xc, rfull, ifull, ALU.mult, r=rk_all + ik_all, w=["xc"])
            P.op("dve", "tensor_tensor_scan", hs, av, xc, 0.0, ALU.mult, ALU.add, r=["av", "xc"], w=["hs"])
            for tb in range(NTB):
                sl = slice(tb * TB, (tb + 1) * TB)
                gp, gpk = self.bank()
                proj(sb[:, :, 128:256], (sk, 1), tb, gp, gpk)
                gg, ggk = self.rot("gg")
                P.act(gg, gp, AF.Gelu_apprx_tanh, r=[gpk], w=[ggk])
                P.tt("dve", ybT[:, cc, sl], gg, hs[:, sl], ALU.mult, r=[ggk, "hs"], w=[(("ybT", tb), cc)])
        for tb in range(NTB):
            self.out_proj_block(wo, "e_wo", ybT[:, :, tb * TB:(tb + 1) * TB], ("ybT", tb), 4, tb)
        P.fence()
        A.reset(m1)

    def _even_e2(self, e, l, hT, w_in_d, cD, proj, m1):
        P, d, A = self.P, self.d, self.A
        self.rots = {}
        yaT = A.alloc("yaT", [128, 4, T], BF16)
        woa = A.alloc("e_woa", [128, 4, D], BF16)
        P.dma("pool", woa, d["ev_w_out"][e].rearrange("(h p) n -> p h n", p=128)[:, 0:4, :], w=["e_woa"])
        scn = ["beta", "g", "cum", "cl", "kbe", "kdec", "negb", "tmp"]
        sc = {nm: A.alloc("sc_" + nm, [128, NT, 4], F32) for nm in scn}
        scf = {nm: sc[nm].rearrange("p t h -> p (t h)") for nm in scn}
        lastc = A.alloc("lastc", [128, 4, 32], F32)
        bd = A.alloc("bdrow", [1, 8], F32)
        bdb = A.alloc("bdb", [128, 8], F32)
        slab8 = A.alloc("slab8", [128, DC, 8], BF16)
        P.dma("sp", bd[0:1, 0:4], d["ev_dt_bias"][e:e + 1, :], w=[("bd", 0)])
        P.dma("sp", bd[0:1, 4:8], d["ev_a_log"][e:e + 1, :], w=[("bd", 1)])
        pb, pk = self.bank()
        P.mm(pb[:, 0:8], self.onesF[0:1, :], bd, r=[("bd", 0), ("bd", 1)], w=[pk])
        P.copy("dve", bdb, pb[:, 0:8], r=[pk], w=["bdb"])
        P.act(bdb[:, 4:8], bdb[:, 4:8], AF.Exp, r=["bdb"], w=["bdb2"])
        P.ts("dve", bdb[:, 4:8], bdb[:, 4:8], -1.0, None, ALU.mult, r=["bdb2"], w=["bdb2"])
        P.dma("pool", slab8, w_in_d[:, :, 2048:2056], w=["slab8"])
        for tt in range(NT):
            pp, ppk = self.bank()
            for c in range(DC):
                P.mm(pp[:, 0:8], hT[:, c, tt * 128:(tt + 1) * 128], slab8[:, c, :], start=(c == 0), stop=(c == DC - 1),
                     r=["slab8", (("e_hT", tt // 4), c)], w=[ppk])
            P.act(sc["beta"][:, tt, :], pp[:, 0:4], AF.Sigmoid, r=[ppk], w=[("sc_beta", tt)])
            P.tt("dve", sc["tmp"][:, tt, :], pp[:, 4:8], bdb[:, 0:4], ALU.add, r=[ppk, "bdb"], w=[("sc_tmp", tt)])
        alltmp = [("sc_tmp", tt) for tt in range(NT)]
        P.act(scf["tmp"], scf["tmp"], AF.Exp, r=alltmp, w=["sc_tmp2"])
        P.act(scf["tmp"], scf["tmp"], AF.Ln, r=["sc_tmp2"], w=["sc_tmp2"], bias=1.0)
        for tt in range(NT):
            P.tt("dve", sc["g"][:, tt, :], sc["tmp"][:, tt, :], bdb[:, 4:8], ALU.mult, r=["sc_tmp2", "bdb2"], w=[("sc_g", tt)])
        pb, pk = self.bank()
        P.mm(pb[:, 0:64], self.maskIU, scf["g"], r=[("sc_g", tt) for tt in range(NT)], w=[pk])
        P.copy("dve", scf["cum"], pb[:, 0:64], r=[pk], w=["sc_cum"])
        pb, pk = self.bank()
        P.mm(pb[:, 0:64], self.lsel, scf["cum"], r=["sc_cum"], w=[pk])
        P.copy("dve", scf["cl"], pb[:, 0:64], r=[pk], w=["sc_cl"])
        allb = [("sc_beta", tt) for tt in range(NT)]
        P.act(scf["kbe"], scf["cum"], AF.Exp, r=["sc_cum"], w=["sc_kbe"])
        P.tt("dve", scf["kbe"], scf["kbe"], scf["beta"], ALU.mult, r=["sc_kbe"] + allb, w=["sc_kbe"])
        P.tt("dve", scf["kdec"], scf["cl"], scf["cum"], ALU.subtract, r=["sc_cl", "sc_cum"], w=["sc_kdec"])
        P.act(scf["kdec"], scf["kdec"], AF.Exp, r=["sc_kdec"], w=["sc_kdec"])
        P.ts("dve", scf["negb"], scf["beta"], -1.0, None, ALU.mult, r=allb, w=["sc_negb"])
        P.fence()
        slabq = [A.alloc("slabq", [128, DC, 128], BF16) for _ in range(1)]

        def load_head_slabs(hh):
            cols = [hh * 128]
            for i_, c_ in enumerate(cols):
                P.dma("pool", slabq[i_], w_in_d[:, :, c_:c_ + 128], w=[("slabq", i_)], nofence=True)

        load_head_slabs(0)
        mh = A.mark()
        for h in range(4):
            A.reset(mh)
            self.rots = {}
            self.rot_banks = list(range(8))
            qT = A.alloc("qT", [128, T], BF16)
            kT = A.alloc("kT", [128, T], BF16)
            vT = A.alloc("vT", [128, T], BF16)
            zs = A.alloc("zs", [128, T], BF16)
            mA = A.mark()
            raws = [A.alloc("raw", [128, 3 + T], F32) for _ in range(3)]
            cvs = [A.alloc("cv", [128, T], F32) for _ in range(2)]
            self.mkrot("sq1", 2, [128, TB], BF16)
            self.mkrot("rs", 1, [128, TB], F32)
            self.mkrot("slabv", 2, [128, DC, 128], BF16)
            kinds = [(h * 128, qT, "qT"), (512 + h * 128, kT, "kT"), (1024 + h * 128, vT, "vT")]
            for kind, (col0, dst, dn) in enumerate(kinds):
                raw = raws[kind]
                P.op("dve", "memset", raw[:, 0:3], 0.0, w=[("raw_pad", kind)])
                if kind == 0:
                    sb, sk = slabq[0], ("slabq", 0)
                else:
                    sb, sk = self.rot("slabv")
                    P.dma("pool", sb, w_in_d[:, :, col0:col0 + 128], w=[sk])
                for tb in range(NTB):
                    pp, ppk = self.bank()
                    proj(sb, sk, tb, pp, ppk)
                    P.copy("act", raw[:, 3 + tb * TB:3 + (tb + 1) * TB], pp, r=[ppk], w=[("raw", kind, tb)])
            sb, sk = self.rot("slabv")
            P.dma("pool", sb, w_in_d[:, :, 1536 + h * 128:1536 + (h + 1) * 128], w=[sk])
            zps = []
            for tb in range(NTB):
                pp, ppk = self.bank()
                proj(sb, sk, tb, pp, ppk)
                zps.append((pp, ppk))

            def conv(kind, cv, cvk, eng):
                raw = raws[kind]
                cc = kind * 4 + h
                rk_ = [("raw", kind, tb) for tb in range(NTB)] + [("raw_pad", kind)]
                P.ts(eng, cv, raw[:, 3:3 + T], cD(3 * 12 + cc), None, ALU.mult, r=rk_, w=[cvk])
                for j in range(3):
                    P.stt(eng, cv, raw[:, j:j + T], cD(j * 12 + cc), cv, ALU.mult, ALU.add, r=rk_ + [cvk], w=[cvk])

            conv(0, cvs[0], "cv0", "dve")
            conv(1, cvs[1], "cv1", "dve")
            silq = raws[0][:, 3:3 + T]
            silk = raws[1][:, 3:3 + T]
            P.act(silq, cvs[0], AF.Silu, r=["cv0"], w=["silq"] + [("raw", 0, tb) for tb in range(NTB)])
            P.act(silk, cvs[1], AF.Silu, r=["cv1"], w=["silk"] + [("raw", 1, tb) for tb in range(NTB)])
            conv(2, cvs[0], "cv0", "dve")
            for tb in range(NTB):
                pp, ppk = zps[tb]
                P.act(zs[:, tb * TB:(tb + 1) * TB], pp, AF.Silu, r=[ppk], w=[("zs", tb)])
            P.act(vT, cvs[0], AF.Silu, r=["cv0"], w=["vT"])
            blocks = []
            for sil_, silkey, dst, dn, gsc in ((silq, "silq", qT, "qT", 128 ** -0.5), (silk, "silk", kT, "kT", None)):
                for tb in range(NTB):
                    sl = slice(tb * TB, (tb + 1) * TB)
                    blocks.append((sil_[:, sl], silkey, 128, self.onesB, 1.0, gsc, dst[:, sl], (dn, tb)))
            self.pnorm_pipe(blocks)
            P.fence()
            A.reset(mA)
            self.rots = {}
            if h + 1 < 4:
                load_head_slabs(h + 1)
            kdec = A.alloc("ktm_dec", [128, NT, 128], BF16)
            utm = A.alloc("u_tm", [128, NT, 128], BF16)
            wT = A.alloc("wT", [128, T], BF16)
            qdecT = A.alloc("qdecT", [128, T], BF16)
            qkT = A.alloc("qkT", [128, NT, 128], BF16)
            S = A.alloc("S", [128, 128], F32)
            Sb = A.alloc("Sb", [128, 128], BF16)
            KCH = 5
            slots = []
            for j in range(KCH):
                sd = {}
                for nm in ("dg", "t1"):
                    sd[nm] = A.alloc("sl_" + nm, [128, 128], F32)
                sd["t2"] = sd["dg"]
                for nm in ("Dt", "Dm", "E", "p0", "p1", "pT0", "pT1", "kbe", "vtb", "AT0", "AT1"):
                    sd[nm] = A.alloc("sl_" + nm, [128, 128], BF16)
                slots.append(sd)
            self.mkrot("vn", 2, [128, 128], BF16)
            self.mkrot("ot", 2, [128, TB], F32)
            self.mkrot("sq1", 1, [128, TB], BF16)
            self.mkrot("rs", 1, [128, TB], F32)
            iF, iB = self.identF, self.identB
            self.rot_banks = [1, 2, 3, 4, 5, 6, 7]
            self.bi = 0

            def tile_chain(tt, j, h=h):
                sd = slots[j]
                K_ = lambda nm: ("sl", j, nm)
                tsl = slice(tt * 128, (tt + 1) * 128)
                tb = tt // 4
                cumc = sc["cum"][:, tt, h:h + 1]
                kp, kpk = self.bank()
                P.mm(kp[:, 0:128], kT[:, tsl], iB, r=[("kT", tb)], w=[kpk])
                P.ts("dve", sd["kbe"], kp[:, 0:128], sc["kbe"][:, tt, h:h + 1], None, ALU.mult, r=[kpk], w=[K_("kbe")])
                P.act(kdec[:, tt, :], kp[:, 0:128], AF.Copy, r=[kpk], w=[("kdec", tt)], scale=sc["kdec"][:, tt, h:h + 1])
                vp, vpk = self.bank()
                P.mm(vp[:, 0:128], vT[:, tsl], iB, r=["vT"], w=[vpk])
                P.act(sd["vtb"], vp[:, 0:128], AF.Copy, r=[vpk], w=[K_("vtb")], scale=sc["beta"][:, tt, h:h + 1])
                P.ts("pool", sd["dg"], iF, cumc, None, ALU.mult, w=[K_("dg")])
                yield
                bc, bck = self.bank()
                P.mm(bc[:, 0:128], self.onesF, sd["dg"], r=[K_("dg")], w=[bck])
                P.ts("dve", sd["t1"], bc[:, 0:128], cumc, 0.0, ALU.subtract, ALU.min, r=[bck], w=[K_("t1")])
                P.act(sd["Dt"], sd["t1"], AF.Exp, r=[K_("t1")], w=[K_("Dt")])
                P.ts("dve", sd["t2"], bc[:, 0:128], cumc, 0.0, ALU.subtract, ALU.max, r=[bck], w=[K_("t2"), K_("dg")])
                P.act(sd["Dm"], sd["t2"], AF.Exp, r=[K_("t2")], w=[K_("Dm")], scale=-1.0)
                P.act(sd["E"], bc[:, 0:128], AF.Exp, r=[bck], w=[K_("E")])
                P.act(lastc[:, h, 2 * tt:2 * tt + 2], bc[:, 63:128:64], AF.Exp, r=[bck], w=[("lastc", tt)])
                P.tt("pool", qdecT[:, tsl], qT[:, tsl], sd["E"], ALU.mult, r=[("qT", tb), K_("E")], w=[("qdecT", tt)])
                P.tt("pool", sd["Dm"], sd["Dm"], self.maskSL, ALU.mult, r=[K_("Dm")], w=[K_("Dm")])
                P.tt("pool", sd["Dt"], sd["Dt"], self.maskIU, ALU.mult, r=[K_("Dt")], w=[K_("Dt")])
                yield
                KK, KKk = self.bank()
                P.mm(KK[:, 0:128], kT[:, tsl], kT[:, tsl], r=[("kT", tb)], w=[KKk])
                QK, QKk = self.bank()
                P.mm(QK[:, 0:128], kT[:, tsl], qT[:, tsl], r=[("kT", tb), ("qT", tb)], w=[QKk])
                p, pk_ = sd["p0"], K_("p0")
                P.stt("dve", p, KK[:, 0:128], sc["negb"][:, tt, h:h + 1], sd["Dm"], ALU.mult, ALU.mult,
                      r=[KKk, K_("Dm")], w=[pk_])
                P.tt("dve", qkT[:, tt, :], QK[:, 0:128], sd["Dt"], ALU.mult, r=[QKk, K_("Dt")], w=[("qkT", tt)])
                yield
                tp, tpk = self.bank()
                P.mm(tp[:, 0:128], p, iB, r=[pk_], w=[tpk])
                pT, pTk = sd["pT0"], K_("pT0")
                P.copy("act", pT, tp[:, 0:128], r=[tpk], w=[pTk])
                AT, ATk = sd["AT0"], K_("AT0")
                P.tt("dve", AT, tp[:, 0:128], iF, ALU.add, r=[tpk], w=[ATk])
                yield
                def a_update(pcur, pcurk, AT, ATk, s_):
                    an, ank = self.bank()
                    P.mm(an[:, 0:128], iB, AT, start=True, stop=False, r=[ATk], w=[ank])
                    P.mm(an[:, 0:128], pcur, AT, start=False, stop=True, r=[pcurk, ATk], w=[ank])
                    nx_ = "1" if (s_ % 2 == 0) else "0"
                    ATn, ATnk = sd["AT" + nx_], K_("AT" + nx_)
                    P.copy("dve" if s_ % 2 else "act", ATn, an[:, 0:128], r=[ank], w=[ATnk])
                    return ATn, ATnk

                for s_ in range(5):
                    nx = "1" if (s_ % 2 == 0) else "0"
                    p2, p2k = self.bank()
                    P.mm(p2[:, 0:128], pT, p, r=[pTk, pk_], w=[p2k])
                    pn, pnk = sd["p" + nx], K_("p" + nx)
                    P.copy("act", pn, p2[:, 0:128], r=[p2k], w=[pnk])
                    if s_ < 4:
                        p2T, p2Tk = self.bank()
                        P.mm(p2T[:, 0:128], p, pT, r=[pTk, pk_], w=[p2Tk])
                        pTn, pTnk = sd["pT" + nx], K_("pT" + nx)
                        P.copy("act" if s_ % 2 else "dve", pTn, p2T[:, 0:128], r=[p2Tk], w=[pTnk])
                    if s_ > 0:
                        AT, ATk = a_update(p, pk_, AT, ATk, s_ - 1)
                    yield
                    if s_ < 4:
                        p, pk_, pT, pTk = pn, pnk, pTn, pTnk
                    else:
                        p, pk_ = pn, pnk
                AT, ATk = a_update(p, pk_, AT, ATk, 4)
                yield
                wp, wpk = self.bank()
                P.mm(wp[:, 0:128], sd["kbe"], AT, r=[K_("kbe"), ATk], w=[wpk])
                P.copy("act", wT[:, tsl], wp[:, 0:128], r=[wpk], w=[("wT", tt)])
                up, upk = self.bank()
                P.mm(up[:, 0:128], AT, sd["vtb"], r=[K_("vtb"), ATk], w=[upk])
                P.copy("dve", utm[:, tt, :], up[:, 0:128], r=[upk], w=[("utm", tt)])

            def scan_chain(h=h):
                P.op("dve", "memset", S, 0.0, w=["S"])
                P.op("dve", "memset", Sb, 0.0, w=["Sb"])
                ob, obk = self.ps[0], ("ps", 0)
                for ck in range(32):
                    tt, half = ck // 2, ck % 2
                    yield ("need", tt)
                    hp = slice(half * 64, half * 64 + 64)
                    csl = slice(ck * 64, (ck + 1) * 64)
                    col = (ck % 8) * 64
                    a_ps, ak = self.bank()
                    P.mm(a_ps[hp, 0:128], wT[:, csl], Sb, r=[("wT", tt), "Sb"], w=[ak])
                    P.mm(ob[:, col:col + 64], Sb, qdecT[:, csl], start=True, stop=False, r=["Sb", ("qdecT", tt)], w=[obk])
                    vn, vnk = self.rot("vn")
                    P.tt("dve", vn[hp, :], utm[hp, tt, :], a_ps[hp, 0:128], ALU.subtract, r=[("utm", tt), ak], w=[vnk])
                    yield None
                    s_ps, spk = self.bank()
                    P.mm(s_ps[:, 0:128], kdec[hp, tt, :], vn[hp, :], r=[("kdec", tt), vnk], w=[spk])
                    P.mm(ob[:, col:col + 64], vn[hp, :], qkT[hp, tt, half * 64:half * 64 + 64], start=False, stop=True,
                         r=[vnk, ("qkT", tt)], w=[obk])
                    P.stt("dve", Sb, S, lastc[:, h, ck:ck + 1], s_ps[:, 0:128], ALU.mult, ALU.add,
                          r=[spk, "S", ("lastc", tt)], w=["Sb"])
                    P.stt("dve", S, S, lastc[:, h, ck:ck + 1], s_ps[:, 0:128], ALU.mult, ALU.add,
                          r=[spk, "S", ("lastc", tt)], w=["S"])
                    if ck % 8 == 7:
                        tb = ck // 8
                        sl = slice(tb * TB, (tb + 1) * TB)
                        ot, otk = self.rot("ot")
                        P.copy("act", ot, ob, r=[obk], w=[otk])
                        on, onk = self.rot("ot")
                        self.pnorm(ot, otk, 128, self.onesB, 1.0 / 128, cD(80), on, onk)
                        P.tt("dve", yaT[:, h, sl], on, zs[:, sl], ALU.mult, r=[onk, ("zs", tb)], w=[(("yaT", tb), h)])
                    yield None

            pending = list(range(NT))
            active = []
            free_slots = list(range(KCH))
            done_tiles = set()
            scan = scan_chain()
            scan_wait = next(scan)
            scan_done = False
            while pending or active or not scan_done:
                while pending and free_slots:
                    tt = pending.pop(0)
                    j = free_slots.pop(0)
                    active.append((tile_chain(tt, j), tt, j))
                nxt = []
                for g, tt, j in active:
                    try:
                        next(g)
                        nxt.append((g, tt, j))
                    except StopIteration:
                        done_tiles.add(tt)
                        free_slots.append(j)
                active = nxt
                for _ in range(3):
                    if not scan_done:
                        if scan_wait is None or scan_wait[1] in done_tiles:
                            try:
                                scan_wait = next(scan)
                            except StopIteration:
                                scan_done = True
            self.rot_banks = list(range(8))
            self.bi = 0
            P.fence()
        for tb in range(NTB):
            self.out_proj_block(woa, "e_woa", yaT[:, :, tb * TB:(tb + 1) * TB], ("yaT", tb), 4, tb)
        P.fence()
        A.reset(m1)

    def build_posfb(self, qb):
        P = self.P
        pf, pfk = self.rot("posfb")
        pb, pk = self.bank()
        for j in range(4):
            tt = qb * 4 + j
            tm, tmk = self.rot("pkm")
            P.ts("dve", tm, self.pkf, self.identF[0:NT, tt:tt + 1], None, ALU.mult, w=[tmk])
            P.mm(pb[:, j * 128:(j + 1) * 128], self.onesF[0:NT, :], tm, r=[tmk], w=[pk])
        P.copy("act", pf, pb, r=[pk], w=[pfk])
        return pf, pfk

    def rope_tables(self, tb):
        P = self.P
        pf, pfk = self.build_posfb(tb)
        R = slice(64, 96)
        f0 = self.freq[R, 0:1]
        out = {}
        for nm, shift in (("sin", 0.0), ("cos", math.pi / 2)):
            ang, ak = self.rot("ang")
            P.ts("dve", ang[R, :], pf[R, :], f0, shift, ALU.mult, ALU.add, r=[pfk], w=[ak])
            ki, kik = self.rot("angi")
            P.ts("dve", ki[R, :], ang[R, :], 1.0 / (2 * math.pi), None, ALU.mult, r=[ak], w=[kik])
            kf, kfk = self.rot("ang")
            P.copy("dve", kf[R, :], ki[R, :], r=[kik], w=[kfk])
            P.stt("dve", ang[R, :], kf[R, :], -2 * math.pi, ang[R, :], ALU.mult, ALU.add, r=[kfk, ak], w=[ak])
            P.ts("dve", ang[R, :], ang[R, :], math.pi, -math.pi, ALU.min, ALU.max, r=[ak], w=[ak])
            tab, tk = self.rot(nm)
            P.act(tab[R, :], ang[R, :], AF.Sin, r=[ak], w=[tk])
            out[nm] = (tab, tk)
        return out

    def odd_mixer(self, o, l):
        P, d, A = self.P, self.d, self.A
        m0 = A.mark()
        self.rots = {}
        cC = lambda j: self.colsC[:, o * 8 + j:o * 8 + j + 1]
        SC_C = 64 ** -0.5
        SC_D = 96 ** -0.5
        slopes = [2.0 ** (-8.0 * (i + 1) / 4) for i in range(4)]
        lam_init = 0.8 - 0.6 * math.exp(-0.3 * l)
        qlatT = A.alloc("qlatT", [128, 2, T], BF16)
        kvlatT = A.alloc("kvlatT", [128, T], BF16)
        kropeT = A.alloc("kropeT", [128, T], BF16)
        neglam = A.alloc("neglam", [128, 2], F32)
        gsub = A.alloc("gsub", [128, 1], F32)
        lamt = A.alloc("lamt", [1, 4, 64], F32)
        lamp = A.alloc("lamp", [1, 2, 64], F32)
        ls = A.alloc("lams", [1, 8], F32)
        m1 = A.mark()
        qcT = A.alloc("qcT", [128, 4, T], BF16)
        kcT = A.alloc("kcT", [128, 4, T], BF16)
        vc = A.alloc("vc", [128, NT, 512], BF16)
        m2 = A.mark()
        for i, nm in enumerate(["od_lam_q1", "od_lam_k1", "od_lam_q2", "od_lam_k2"]):
            P.dma("sp", lamt[0:1, i, :], d[nm][o:o + 1, :], w=[("lamt", i)])
        P.tt("dve", lamp[0:1, 0, :], lamt[0:1, 0, :], lamt[0:1, 1, :], ALU.mult, r=[("lamt", 0), ("lamt", 1)], w=["lp0"])
        P.tt("dve", lamp[0:1, 1, :], lamt[0:1, 2, :], lamt[0:1, 3, :], ALU.mult, r=[("lamt", 2), ("lamt", 3)], w=["lp1"])
        P.op("dve", "memset", ls, 0.0, w=["ls"])
        P.op("dve", "reduce_sum", ls[0:1, 0:1], lamp[0:1, 0, :], AX.X, r=["lp0", "ls"], w=["ls0"])
        P.op("dve", "reduce_sum", ls[0:1, 1:2], lamp[0:1, 1, :], AX.X, r=["lp1", "ls"], w=["ls1"])
        P.act(ls[0:1, 0:2], ls[0:1, 0:2], AF.Exp, r=["ls0", "ls1"], w=["lse"])
        P.tt("dve", ls[0:1, 2:3], ls[0:1, 1:2], ls[0:1, 0:1], ALU.subtract, r=["lse"], w=["ls2"])
        P.ts("dve", ls[0:1, 4:5], ls[0:1, 2:3], -lam_init, None, ALU.add, r=["ls2"], w=["ls3"])
        pb, pk = self.bank()
        P.mm(pb[:, 0:2], self.onesF[0:1, :], ls[0:1, 4:6], r=["ls3"], w=[pk])
        P.copy("dve", neglam, pb[:, 0:2], r=[pk], w=["neglam"])
        P.ts("dve", gsub, cC(2), 1.0 - lam_init, None, ALU.mult, w=["gsub"])
        hT = A.alloc("o_hT", [128, DC, T], BF16)
        self.mkrot("slab", 1, [128, DC, 512], BF16)
        self.mkrot("sq", 1, [128, DC, TB], BF16)
        self.rots["slab"][0].append(self.rots["sq"][0][0])
        self.mkrot("sq1", 2, [128, TB], BF16)
        self.mkrot("rs", 2, [128, TB], F32)
        w_in_d = d["od_w_in"][o].rearrange("(c p) n -> p c n", p=128)
        for tb in range(NTB):
            self.norm_block(tb, 0, l, hT[:, :, tb * TB:(tb + 1) * TB], ("o_hT", tb))

        def load_slab(c0, n):
            sb, sk = self.rot("slab")
            wk = [sk, ("sq", 0)] if sk == ("slab", 1) else [sk]
            P.dma("pool", sb[:, :, 0:n], w_in_d[:, :, c0:c0 + n], w=wk)
            return sb, sk

        def pn_a(src, srck, ones_l, inv_n, gain, out, outk):
            sq, sqk = self.rot("sq1")
            P.act(sq, src, AF.Square, r=[srck], w=[sqk])
            return (src, srck, ones_l, inv_n, gain, out, outk, sq, sqk)

        def pn_b(stt_):
            src, srck, ones_l, inv_n, gain, out, outk, sq, sqk = stt_
            ss, ssk = self.bank()
            P.mm(ss, ones_l, sq, r=[sqk], w=[ssk])
            rs, rsk = self.rot("rs")
            P.act(rs, ss, AF.Ln, r=[ssk], w=[rsk], scale=inv_n, bias=EPS)
            P.act(rs, rs, AF.Exp, r=[rsk], w=[rsk], scale=-0.5)
            P.stt("dve", out, src, gain, rs, ALU.mult, ALU.mult, r=[srck, rsk], w=[outk])

        slab_q = load_slab(0, 512)
        slab_k = load_slab(512, 512)
        prev = None
        for dst, dname, gj, (sb, sk) in ((qcT, "qcT", 0, slab_q), (kcT, "kcT", 1, slab_k)):
            for ch in range(4):
                for tb in range(NTB):
                    sl = slice(tb * TB, (tb + 1) * TB)
                    pp, ppk = self.bank()
                    for c in range(DC):
                        P.mm(pp, sb[:, c, ch * 128:(ch + 1) * 128], hT[:, c, sl], start=(c == 0), stop=(c == DC - 1),
                             r=[sk, (("o_hT", tb), c)], w=[ppk])
                    cur = pn_a(pp, ppk, self.blockB, 1.0 / 64, cC(gj), dst[:, ch, sl], (dname, ch, tb))
                    if prev is not None:
                        pn_b(prev)
                    prev = cur
        sb, sk = load_slab(1024, 512)
        pn_b(prev)
        for tt in range(NT):
            pp, ppk = self.bank()
            for c in range(DC):
                P.mm(pp, hT[:, c, tt * 128:(tt + 1) * 128], sb[:, c, :], start=(c == 0), stop=(c == DC - 1),
                     r=[sk, (("o_hT", tt // 4), c)], w=[ppk])
            P.copy("act", vc[:, tt, :], pp, r=[ppk], w=[("vc", tt)])
        sb, sk = load_slab(1536, 416)
        for tb in range(NTB):
            sl = slice(tb * TB, (tb + 1) * TB)
            qq = [self.bank(), self.bank()]
            for ci, (pp, ppk) in enumerate(qq):
                for c in range(DC):
                    P.mm(pp, sb[:, c, ci * 128:(ci + 1) * 128], hT[:, c, sl], start=(c == 0), stop=(c == DC - 1),
                         r=[sk, (("o_hT", tb), c)], w=[ppk])
            ss, ssk = self.bank()
            for ci, (pp, ppk) in enumerate(qq):
                sq, sqk = self.rot("sq1")
                P.act(sq, pp, AF.Square, r=[ppk], w=[sqk])
                P.mm(ss, self.onesB, sq, start=(ci == 0), stop=(ci == 1), r=[sqk], w=[ssk])
            rs, rsk = self.rot("rs")
            P.act(rs, ss, AF.Ln, r=[ssk], w=[rsk], scale=1.0 / 256, bias=EPS)
            P.act(rs, rs, AF.Exp, r=[rsk], w=[rsk], scale=-0.5)
            for ci, (pp, ppk) in enumerate(qq):
                P.stt("dve", qlatT[:, ci, sl], pp, cC(3 + ci), rs, ALU.mult, ALU.mult, r=[ppk, rsk],
                      w=[("qlatT", ci, tb)])
            pp, ppk = self.bank()
            for c in range(DC):
                P.mm(pp, sb[:, c, 256:384], hT[:, c, sl], start=(c == 0), stop=(c == DC - 1),
                     r=[sk, (("o_hT", tb), c)], w=[ppk])
            self.pnorm(pp, ppk, 128, self.onesB, 1.0 / 128, cC(5), kvlatT[:, sl], ("kvlatT", tb))
            pp, ppk = self.bank()
            for c in range(DC):
                P.mm(pp[64:96, :], sb[:, c, 384:416], hT[:, c, sl], start=(c == 0), stop=(c == DC - 1),
                     r=[sk, (("o_hT", tb), c)], w=[ppk])
            P.copy("act", kropeT[64:96, sl], pp[64:96, :], r=[ppk], w=[("kropeT", tb)])
        P.fence()
        A.reset(m2)
        self.rots = {}
        wo = A.alloc("o_wo", [128, 4, D], BF16)
        P.dma("pool", wo, d["od_w_out"][o].rearrange("(h p) n -> p h n", p=128)[:, 0:4, :], w=["o_wo"])
        self.mkrot("posfb", 2, [128, TB], F32)
        self.mkrot("pkm", 2, [NT, 128], F32)
        dcache = [A.alloc("dcache", [128, TB], F16) for _ in range(NT)]
        self.mkrot("tS", 2, [128, TB], F32)
        self.mkrot("pT", 4, [128, TB], BF16)
        self.mkrot("yT", 2, [128, 4, TB], BF16)
        self.mkrot("rs", 3, [128, TB], F32)
        self.mkrot("sq1", 2, [128, TB], BF16)
        self.mkrot("ya", 1, [128, TB], F32)
        self.mkrot("yb", 1, [128, TB], F32)
        self.score_banks = [4, 5, 6, 7]
        self.si = 0
        self.rot_banks = [6, 7]
        self.bi = 0
        items = [(qb, h, kt) for qb in range(NTB) for h in range(4) for kt in range(4 * qb + 4)]
        st = {}
        deferred = []

        def defer(n, fn):
            deferred.append([n, fn])

        def tick(flush=False):
            while deferred and (flush or deferred[0][0] <= 0):
                deferred.pop(0)[1]()
            for dd_ in deferred:
                dd_[0] -= 1

        def stageA(it):
            qb, h, kt = it
            if h == 0 and kt == 0:
                st[("pf", qb)] = self.build_posfb(qb)
                st[("yT", qb)] = self.rot("yT")
            pf, pfk = st[("pf", qb)]
            j = kt - 4 * qb
            c0 = max(j, 0) * 128
            dist, dkk = dcache[kt], ("dcache", kt)
            if h == 0:
                P.act(dist[:, c0:], pf[:, c0:], AF.Abs, r=[pfk], w=[dkk], bias=self.negposk[:, kt:kt + 1])
            sps = []
            for mi in range(2):
                hs = slice(mi * 64, (mi + 1) * 64)
                sp_, spk = self.sbank()
                P.mm(sp_[:, c0:], kcT[hs, h, kt * 128:(kt + 1) * 128], qcT[hs, h, qb * TB + c0:(qb + 1) * TB],
                     r=[("kcT", h, kt // 4), ("qcT", h, qb)], w=[spk])
                sps.append((sp_, spk))
            st[it] = (dist, dkk, sps)

        def stageB(it):
            qb, h, kt = it
            yT, yk = st[("yT", qb)]
            nkt = 4 * qb + 4
            j = kt - 4 * qb
            c0 = max(j, 0) * 128
            dist, dkk, sps = st.pop(it)
            cur_banks = [spk[1] for _, spk in sps]
            lss = None
            pts = []
            for mi in range(2):
                sp_, spk = sps[mi]
                tS, tSk = self.rot("tS")
                P.stt("dve", tS[:, c0:], dist[:, c0:], -slopes[h] / SC_C, sp_[:, c0:], ALU.mult, ALU.add,
                      r=[dkk, spk], w=[tSk])
                pT, pTk = self.rot("pT")
                P.act(pT[:, c0:], tS[:, c0:], AF.Exp, r=[tSk], w=[pTk], scale=SC_C, bias=-SM_SHIFT)
                pts.append((pT, pTk))
            for mi in range(2):
                pT, pTk = pts[mi]
                if j >= 0:
                    P.op("pool", "memset", pT[64:128, c0:c0 + 64], 0.0, w=[pTk])
                ab = mi
                Ob, Ok = self.ps[ab], ("ps", ab)
                P.mm(Ob[:, c0:], vc[:, kt, h * 128:(h + 1) * 128], pT[:, c0:], start=(kt == 0),
                     stop=(kt == nkt - 1), r=[("vc", kt), pTk], w=[Ok])
                lb_, lbk_ = self.ps[2 + mi], ("ps", 2 + mi)
                P.mm(lb_[:, c0:], self.onesB, pT[:, c0:], start=(kt == 0), stop=(kt == nkt - 1), r=[pTk], w=[lbk_])
            self.rot_banks = cur_banks
            self.bi = 0
            tick()
            if kt == nkt - 1:
                ya, yak = self.rot("ya")
                yb, ybk = self.rot("yb")
                sq, sqk = self.rot("sq1")

                def step1(lss=lss, ya=ya, yak=yak, yb=yb, ybk=ybk, sq=sq, sqk=sqk, h=h):
                    for mi, (y_, y_k) in enumerate(((ya, yak), (yb, ybk))):
                        ab = mi
                        r0, r0k = self.rot("rs")
                        self.recip_act(r0, r0k, self.ps[2 + mi], ("ps", 2 + mi))
                        P.tt("dve", y_, self.ps[ab], r0, ALU.mult, r=[("ps", ab), r0k], w=[y_k])
                    P.stt("dve", ya, yb, neglam[:, 0:1], ya, ALU.mult, ALU.add, r=[ybk, yak], w=[yak])
                    P.act(sq, ya, AF.Square, r=[yak], w=[sqk])

                def step2(h=h, ya=ya, yak=yak, sq=sq, sqk=sqk, yT=yT, yk=yk):
                    ss, ssk = self.bank()
                    P.mm(ss, self.onesB, sq, r=[sqk], w=[ssk])
                    rs, rsk = self.rot("rs")
                    P.act(rs, ss, AF.Ln, r=[ssk], w=[rsk], scale=1.0 / 128, bias=EPS)
                    P.act(rs, rs, AF.Exp, r=[rsk], w=[rsk], scale=-0.5)
                    P.stt("dve", yT[:, h, :], ya, gsub, rs, ALU.mult, ALU.mult, r=[yak, rsk], w=[(yk, h)])

                step1()
                defer(1, step2)
                if h == 3:
                    defer(3, lambda qb=qb, yT=yT, yk=yk: self.out_proj_block(wo, "o_wo", yT, yk, 4, qb))

        LOOK2 = 1
        for i in range(min(LOOK2, len(items))):
            stageA(items[i])
        for i in range(len(items)):
            if i + LOOK2 < len(items):
                stageA(items[i + LOOK2])
            stageB(items[i])
        tick(flush=True)
        self.rot_banks = list(range(8))
        self.bi = 0
        self.score_banks = [2, 3, 4, 5]
        self.si = 0
        P.fence()
        A.reset(m1)
        self.rots = {}
        qdT = A.alloc("qdT", [128, 4, T], BF16)
        kdT = A.alloc("kdT", [128, 4, T], BF16)
        vd = A.alloc("vd", [128, NT, 512], BF16)
        wuq = A.alloc("wuq", [128, 2, 384], BF16)
        wukv = A.alloc("wukv", [128, 768], BF16)
        P.dma("pool", wuq, d["od_w_uq"][o].rearrange("(c p) n -> p c n", p=128), w=["wuq"])
        P.dma("pool", wukv, d["od_w_ukv"][o], w=["wukv"])
        m3 = A.mark()
        self.mkrot("posfb", 2, [128, TB], F32)
        self.mkrot("pkm", 2, [NT, 128], F32)
        self.mkrot("ang", 3, [128, TB], F32)
        self.mkrot("angi", 1, [128, TB], I32)
        self.mkrot("sin", 2, [128, TB], F32)
        self.mkrot("cos", 2, [128, TB], F32)
        KO3 = 3
        oslots = []
        for j in range(KO3):
            sd = {"sq": A.alloc("o3_sq", [128, TB], BF16)}
            for nm in ("rs", "kraw", "rtmp", "rtmp2"):
                sd[nm] = A.alloc("o3_" + nm, [128, TB], F32)
            oslots.append(sd)
        self.rot_banks = list(range(8))
        self.bi = 0
        ones96 = self.onesB[0:96, 0:96]

        def qk_chain(tb, h, isk, cosb, cosk, sinb, sink, j):
            sd = oslots[j]
            K_ = lambda nm: ("o3s", j, nm)
            sl = slice(tb * TB, (tb + 1) * TB)
            if not isk:
                dst, dkey, gcol = qdT[:, h, sl], ("qdT", h, tb), cC(6)[0:96, :]
                qp, qpk = self.bank()
                for c in range(2):
                    P.mm(qp[0:96, :], wuq[:, c, h * 96:(h + 1) * 96], qlatT[:, c, sl], start=(c == 0), stop=(c == 1),
                         r=["wuq", ("qlatT", c, tb)], w=[qpk])
                src, srck = qp[0:96, :], [qpk]
            else:
                dst, dkey, gcol = kdT[:, h, sl], ("kdT", h, tb), cC(7)[0:96, :]
                kp, kpk = self.bank()
                P.mm(kp[0:64, :], wukv[:, h * 192:h * 192 + 64], kvlatT[:, sl], r=["wukv", ("kvlatT", tb)], w=[kpk])
                kr = sd["kraw"]
                P.copy("act", kr[0:64, :], kp[0:64, :], r=[kpk], w=[K_("kraw0")])
                P.copy("act", kr[64:96, :], kropeT[64:96, sl], r=[("kropeT", tb)], w=[K_("kraw1")])
                src, srck = kr[0:96, :], [K_("kraw0"), K_("kraw1")]
            P.act(sd["sq"][0:96, :], src, AF.Square, r=srck, w=[K_("sq")])
            yield
            ss, ssk = self.bank()
            P.mm(ss[0:96, :], ones96, sd["sq"][0:96, :], r=[K_("sq")], w=[ssk])
            rs = sd["rs"]
            P.act(rs[0:96, :], ss[0:96, :], AF.Ln, r=[ssk], w=[K_("rs")], scale=1.0 / 96, bias=EPS)
            P.act(rs[0:96, :], rs[0:96, :], AF.Exp, r=[K_("rs")], w=[K_("rs")], scale=-0.5)
            P.stt("dve", dst[0:96, :], src, gcol, rs[0:96, :], ALU.mult, ALU.mult, r=srck + [K_("rs")], w=[dkey])
            yield
            rp, rpk = self.bank()
            P.mm(rp[0:96, :], self.rotTB[0:96, 0:96], dst[0:96, :], r=[dkey], w=[rpk])
            t1, t2 = sd["rtmp"], sd["rtmp2"]
            P.tt("dve", t1[64:96, :], dst[64:96, :], cosb[64:96, :], ALU.mult, r=[dkey, cosk], w=[K_("t1")])
            P.tt("dve", t2[64:96, :], rp[64:96, :], sinb[64:96, :], ALU.mult, r=[rpk, sink], w=[K_("t2")])
            yield
            P.tt("dve", dst[64:96, :], t1[64:96, :], t2[64:96, :], ALU.add, r=[K_("t1"), K_("t2")], w=[dkey])

        for tb in range(NTB):
            tabs = self.rope_tables(tb)
            cosb, cosk = tabs["cos"]
            sinb, sink = tabs["sin"]
            makers = []
            for h in range(4):
                for isk in (False, True):
                    makers.append(lambda j, tb=tb, h=h, isk=isk, cosb=cosb, cosk=cosk, sinb=sinb, sink=sink:
                                  qk_chain(tb, h, isk, cosb, cosk, sinb, sink, j))
            self.run_chains(makers, KO3)
        wv = wukv.rearrange("p (h e) -> p h e", h=4)[:, :, 64:192]
        for tt in range(NT):
            pp, ppk = self.bank()
            P.mm(pp.rearrange("p (h e) -> p h e", h=4), kvlatT[:, tt * 128:(tt + 1) * 128], wv,
                 r=["wukv", ("kvlatT", tt // 4)], w=[ppk])
            P.copy("act", vd[:, tt, :], pp, r=[ppk], w=[("vd", tt)])
        P.fence()
        A.reset(m3)
        self.rots = {}
        wo2 = A.alloc("o_wo2", [128, 4, D], BF16)
        P.dma("pool", wo2, d["od_w_out"][o].rearrange("(h p) n -> p h n", p=128)[:, 4:8, :], w=["o_wo2"])
        self.mkrot("rs", 3, [128, TB], F32)
        self.mkrot("pT", 6, [128, TB], BF16)
        self.mkrot("yT", 2, [128, 4, TB], BF16)
        self.rot_banks = [6, 7]
        self.bi = 0
        if self.cfg.get("pe_l", True):
            self.score_banks = [4, 5, 6, 7]
            self.si = 0
        self.mkrot("lsum", 2, [128, TB], F32)
        for qb in range(NTB):
            yT, yk = self.rot("yT")
            items = [(h, kt) for h in range(4) for kt in range(4 * qb + 4)]
            st = {}

            def stageA(it, qb=qb):
                h, kt = it
                c0 = max(kt - 4 * qb, 0) * 128
                sp_, spk = self.sbank()
                P.mm(sp_[:, c0:], kdT[0:96, h, kt * 128:(kt + 1) * 128], qdT[0:96, h, qb * TB + c0:(qb + 1) * TB],
                     r=[("kdT", h, kt // 4), ("qdT", h, qb)], w=[spk])
                st[it] = (sp_, spk)

            def stageB(it, qb=qb, yT=yT, yk=yk):
                h, kt = it
                nkt = 4 * qb + 4
                j = kt - 4 * qb
                c0 = max(j, 0) * 128
                sp_, spk = st.pop(it)
                if kt == 0:
                    st[("ls", h)] = self.rot("lsum")
                ls, lsk = st[("ls", h)]
                pT, pTk = self.rot("pT")
                P.act(pT[:, c0:], sp_[:, c0:], AF.Exp, r=[spk], w=[pTk], scale=SC_D, bias=-SM_SHIFT)
                if j >= 0:
                    P.op("dve", "memset", pT[64:128, c0:c0 + 64], 0.0, w=[pTk])
                Ob, Ok = self.ps[h % 2], ("ps", h % 2)
                P.mm(Ob[:, c0:], vd[:, kt, h * 128:(h + 1) * 128], pT[:, c0:], start=(kt == 0), stop=(kt == nkt - 1),
                     r=[("vd", kt), pTk], w=[Ok])
                if self.cfg.get("pe_l", True):
                    lb_, lbk_ = self.ps[2 + h % 2], ("ps", 2 + h % 2)
                    P.mm(lb_[:, c0:], self.onesB, pT[:, c0:], start=(kt == 0), stop=(kt == nkt - 1), r=[pTk], w=[lbk_])
                    if kt == nkt - 1:
                        r0, r0k = self.rot("rs")
                        self.recip_act(r0, r0k, lb_, lbk_)
                        P.tt("dve", yT[:, h, :], Ob, r0, ALU.mult, r=[Ok, r0k], w=[(yk, h)])
                        del st[("ls", h)]
                else:
                    le = "dve" if kt % 3 else "pool"
                    if kt == 0:
                        P.copy(le, ls, pT, r=[pTk], w=[lsk])
                    else:
                        P.tt(le, ls[:, c0:], ls[:, c0:], pT[:, c0:], ALU.add, r=[pTk, lsk], w=[lsk])
                    if kt == nkt - 1:
                        lp, lpk = self.bank()
                        P.mm(lp, self.onesF, ls, r=[lsk], w=[lpk])
                        r0, r0k = self.rot("rs")
                        self.recip_act(r0, r0k, lp, lpk)
                        P.tt("dve", yT[:, h, :], Ob, r0, ALU.mult, r=[Ok, r0k], w=[(yk, h)])
                        del st[("ls", h)]

            LOOK = 3
            for i in range(min(LOOK, len(items))):
                stageA(items[i])
            for i in range(len(items)):
                if i + LOOK < len(items):
                    stageA(items[i + LOOK])
                stageB(items[i])
            self.out_proj_block(wo2, "o_wo2", yT, yk, 4, qb)
        self.rot_banks = list(range(8))
        self.bi = 0
        P.fence()
        A.reset(m0)

    def build(self):
        cfg = self.cfg
        self.setup()
        for l in range(cfg.get("layers", DEPTH)):
            if cfg.get("mixer", True):
                if l % 2 == 0:
                    if not cfg.get("skip_even"):
                        self.even_mixer(l // 2, l)
                elif not cfg.get("skip_odd"):
                    self.odd_mixer(l // 2, l)
            if cfg.get("xattn", True):
                self.xattn(l)
            if cfg.get("ffn", True):
                self.ffn(l)
        self.store()
        self.P.emit()


def build_nc(cfg=None):
    nc = bass.Bass("TRN2", target_bir_lowering=False)
    b = Builder(nc, cfg or {})
    b.build()
    return nc, b


def make_in_maps(inputs, n):
    consts = host_consts()
    maps = []
    for i in range(n):
        mp = {
            "x": np.ascontiguousarray(inputs["x"][i]),
            "mem": np.ascontiguousarray(inputs["mem"][i]),
            "positions": np.ascontiguousarray(inputs["positions"][i:i + 1]),
        }
        for name, _ in WEIGHTS:
            mp[name] = np.ascontiguousarray(inputs[name])
        mp.update(consts)
        maps.append(mp)
    return maps


def kernel(**inputs):
    inputs = {k: np.asarray(v) for k, v in inputs.items()}
    n = inputs["x"].shape[0]
    nc, _ = build_nc({})
    in_maps = make_in_maps(inputs, n)
    res = run_bass_kernel_spmd(nc, in_maps, core_ids=list(range(n)))
    return np.stack([np.asarray(r["y"]) for r in res.results], axis=0).astype(np.float32)
```

```python
from contextlib import ExitStack
import math
import numpy as np
import concourse.bass as bass
import concourse.mybir as mybir
from concourse.bass_utils import run_bass_kernel_spmd

F32 = mybir.dt.float32
BF16 = mybir.dt.bfloat16
F16 = mybir.dt.float16
I32 = mybir.dt.int32
AF = mybir.ActivationFunctionType
ALU = mybir.AluOpType
AX = mybir.AxisListType

EPOCH = 30000
STRICT_SAME_ENGINE = True
NSLOT = 8
ENGS = ("pe", "act", "dve", "pool", "sp")


class Prog:
    def __init__(self, nc):
        self.nc = nc
        self.ops = {e: [] for e in ENGS}
        self.ncomp = {e: 0 for e in ENGS}
        self.ndma = {e: 0 for e in ENGS}
        self.last_w = {}
        self.readers = {}
        self.waited = {e: {} for e in ENGS}
        self.semkeys = set()
        self.sems = {}
        self.last_tok = {}
        self.gdep = None

    def add(self, eng, fn, r=(), w=(), dma=False, nofence=False):
        nowait_only = fn is None
        raw = {}
        oth = {}
        if eng != "pe" and not nowait_only:
            locks = [("pslock", k[1]) for k in r if isinstance(k, tuple) and len(k) == 2 and k[0] == "ps"]
            if locks:
                w = list(w) + locks

        def put(d, tok):
            sk, v, e2, d2 = tok
            if d.get(sk, (0,))[0] < v:
                d[sk] = (v, e2, d2)

        for k in r:
            t = self.last_w.get(k)
            if t is not None:
                put(raw, t)
        for k in w:
            t = self.last_w.get(k)
            if t is not None:
                put(oth, t)
            for sk, (v, e2, d2) in self.readers.get(k, {}).items():
                put(oth, (sk, v, e2, d2))
        if self.gdep is not None and not nofence:
            put(raw, self.gdep)
        if nowait_only:
            semkey, val = None, 0
        elif dma:
            i = self.ndma[eng]
            self.ndma[eng] += 1
            slot, rnd = i % NSLOT, i // NSLOT
            semkey = ("d", eng, slot)
            val = 16 * (rnd + 1)
            if rnd > 0:
                put(raw, (semkey, 16 * rnd, eng, True))
        else:
            i = self.ncomp[eng]
            self.ncomp[eng] += 1
            semkey = ("c", eng, i // EPOCH)
            val = i % EPOCH + 1
        tok = (semkey, val, eng, dma)
        waits = []
        wd = self.waited[eng]
        for d, is_raw in ((raw, True), (oth, False)):
            for sk, (v, e2, d2) in d.items():
                if not d2 and e2 == eng:
                    if eng == "pe" or (not is_raw and not STRICT_SAME_ENGINE):
                        continue
                if wd.get(sk, 0) >= v:
                    continue
                wd[sk] = v
                waits.append((sk, v))
        if nowait_only:
            self.ops[eng].append((None, waits, None, False))
            return None
        self.semkeys.add(semkey)
        self.last_tok[semkey] = tok
        for k in w:
            self.last_w[k] = tok
            self.readers[k] = {}
        for k in r:
            d = self.readers.setdefault(k, {})
            if d.get(semkey, (0,))[0] < val:
                d[semkey] = (val, eng, dma)
        self.ops[eng].append((fn, waits, semkey, dma))
        return tok

    def op(self, eng, name, *args, r=(), w=(), **kw):
        return self.add(eng, lambda e: getattr(e, name)(*args, **kw), r, w)

    def mm(self, out, lhsT, rhs, start=True, stop=True, r=(), w=(), **kw):
        return self.add("pe", lambda e: e.matmul(out, lhsT, rhs, start=start, stop=stop, **kw), r, w)

    def tr(self, out, in_, ident, r=(), w=()):
        return self.add("pe", lambda e: e.transpose(out, in_, ident), r, w)

    def act(self, out, in_, func, r=(), w=(), **kw):
        return self.add("act", lambda e: e.activation(out, in_, func, **kw), r, w)

    def ts(self, eng, out, in0, s1, s2, op0, op1=None, r=(), w=()):
        if op1 is None:
            return self.add(eng, lambda e: e.tensor_scalar(out, in0, s1, None, op0), r, w)
        return self.add(eng, lambda e: e.tensor_scalar(out, in0, s1, s2, op0, op1), r, w)

    def tt(self, eng, out, in0, in1, op, r=(), w=()):
        return self.add(eng, lambda e: e.tensor_tensor(out, in0, in1, op), r, w)

    def stt(self, eng, out, in0, scalar, in1, op0, op1, r=(), w=()):
        return self.add(eng, lambda e: e.scalar_tensor_tensor(out, in0, scalar, in1, op0, op1), r, w)

    def copy(self, eng, out, in_, r=(), w=()):
        if eng == "act":
            return self.add(eng, lambda e: e.copy(out, in_), r, w)
        return self.add(eng, lambda e: e.tensor_copy(out, in_), r, w)

    def dma(self, eng, out, in_, r=(), w=(), nofence=False, **kw):
        return self.add(eng, lambda e: e.dma_start(out, in_, **kw), r, w, dma=True, nofence=nofence)

    def fence(self):
        keys = []
        for sk, tok in list(self.last_tok.items()):
            k = ("_fence", sk)
            self.last_w[k] = tok
            self.readers[k] = {}
            keys.append(k)
        self.gdep = None
        tok = self.add("sp", lambda e: e.nop(), r=keys, w=[])
        self.gdep = tok

    def finish(self, eng, keys):
        self.add(eng, None, r=keys, w=())

    def emit(self):
        nc = self.nc
        with ExitStack() as st:
            for sk in sorted(self.semkeys, key=str):
                self.sems[sk] = st.enter_context(nc.semaphore("s_%s_%s_%d" % sk))
            with nc.Block() as block:
                def mk(name):
                    def body(e):
                        for fn, waits, semkey, dma in self.ops[name]:
                            for sk, v in waits:
                                e.wait_ge(self.sems[sk], v)
                            if fn is None:
                                continue
                            ins = fn(e)
                            ins.then_inc(self.sems[semkey], 16 if dma else 1)
                    return body
                block.tensor(mk("pe"))
                block.scalar(mk("act"))
                block.vector(mk("dve"))
                block.gpsimd(mk("pool"))
                block.sync(mk("sp"))


T = 2048
D = 1024
TB = 512
NTB = 4
NT = 16
DC = 8
FF = 2816
FC = 22
NMEM = 256
EPS = 1e-6
SB_BASE = 16512
SB_END = 229344
DEPTH = 4
SM_SHIFT = 10.0

WEIGHTS = [
    ("norm_mix", [4, 1024]), ("norm_x", [4, 1024]), ("norm_mem", [4, 1024]),
    ("x_wq", [4, 1024, 512]), ("x_wkv", [4, 1024, 1024]), ("x_q_norm", [4, 128]), ("x_k_norm", [4, 128]),
    ("x_wo", [4, 512, 1024]), ("norm_ffn", [4, 1024]), ("ffn_w_in", [4, 1024, 5632]),
    ("ffn_w_out", [4, 2816, 1024]),
    ("ev_w_in", [2, 1024, 3080]), ("ev_conv_qkv", [2, 4, 1536]), ("ev_a_log", [2, 4]), ("ev_dt_bias", [2, 4]),
    ("ev_o_norm", [2, 128]), ("ev_conv_b_w", [2, 4, 512]), ("ev_conv_b_b", [2, 512]),
    ("ev_gate_a_w", [2, 8, 64, 64]), ("ev_gate_a_b", [2, 512]), ("ev_gate_x_w", [2, 8, 64, 64]),
    ("ev_gate_x_b", [2, 512]), ("ev_lru_l", [2, 512]), ("ev_w_out", [2, 1024, 1024]),
    ("od_w_in", [2, 1024, 1952]), ("od_c_q_norm", [2, 64]), ("od_c_k_norm", [2, 64]),
    ("od_lam_q1", [2, 64]), ("od_lam_k1", [2, 64]), ("od_lam_q2", [2, 64]), ("od_lam_k2", [2, 64]),
    ("od_c_sub_norm", [2, 128]), ("od_q_lat_norm", [2, 256]), ("od_w_uq", [2, 256, 384]),
    ("od_kv_lat_norm", [2, 128]), ("od_w_ukv", [2, 128, 768]), ("od_d_q_norm", [2, 96]),
    ("od_d_k_norm", [2, 96]), ("od_w_out", [2, 1024, 1024]),
]


def host_consts():
    c = {}
    c["c_ident"] = np.eye(128, dtype=np.float32)
    rt = np.zeros((128, 128), np.float32)
    for i in range(16):
        rt[80 + i, 64 + i] = -1.0
        rt[64 + i, 80 + i] = 1.0
    c["c_rotT"] = rt
    fr = np.zeros((128, 2), np.float32)
    for i in range(16):
        f = 10000.0 ** (-(i / 16.0))
        fr[64 + i, 0] = fr[80 + i, 0] = np.float32(f)
    fr[:, 1] = fr[:, 0] / np.float32(2 * np.pi)
    c["c_freq"] = fr
    bo = np.zeros((128, 128), np.float32)
    bo[0:64, 0:64] = 1.0
    bo[64:128, 64:128] = 1.0
    c["c_blockones"] = bo
    ii = np.arange(128)
    same = (ii[:, None] // 64) == (ii[None, :] // 64)
    c["c_maskSL"] = (same & (ii[None, :] < ii[:, None])).astype(np.float32)
    c["c_maskIU"] = (same & (ii[None, :] >= ii[:, None])).astype(np.float32)
    c["c_lsel"] = (ii[:, None] == (ii[None, :] // 64) * 64 + 63).astype(np.float32)
    return c


class Arena:
    def __init__(self, nc, base, end):
        self.nc, self.p, self.end, self.n = nc, base, end, 0

    def alloc(self, name, shape, dtype):
        esz = 4 if dtype in (F32, I32) else 2
        nbytes = int(np.prod(shape[1:])) * esz
        off = (self.p + 31) // 32 * 32
        self.p = off + nbytes
        assert self.p <= self.end, ("SBUF overflow", name, self.p, self.end)
        self.n += 1
        return self.nc.alloc_sbuf_tensor_at("%s_%d" % (name, self.n), list(shape), dtype, offset=off).ap()

    def mark(self):
        return self.p

    def reset(self, m):
        self.p = m


class Builder:
    def __init__(self, nc, cfg):
        self.nc = nc
        self.cfg = cfg
        self.P = Prog(nc)
        self.d = {}
        P = self.P
        d = self.d
        d["x"] = nc.dram_tensor("x", [T, D], F32, kind="ExternalInput").ap()
        d["mem"] = nc.dram_tensor("mem", [NMEM, D], F32, kind="ExternalInput").ap()
        d["positions"] = nc.dram_tensor("positions", [1, T], I32, kind="ExternalInput").ap()
        for name, shp in WEIGHTS:
            d[name] = nc.dram_tensor(name, shp, F32, kind="ExternalInput").ap()
        for name, arr in host_consts().items():
            d[name] = nc.dram_tensor(name, list(arr.shape), F32, kind="ExternalInput").ap()
        d["y"] = nc.dram_tensor("y", [T, D], F32, kind="ExternalOutput").ap()
        self.A = Arena(nc, SB_BASE, SB_END)
        A = self.A
        self.ps = [nc.alloc_psum_tensor("psb%d" % i, [128, 512], F32).ap() for i in range(8)]
        self.bi = 0
        self.rot_banks = list(range(8))
        self.misc_banks = [6, 7]
        self.score_banks = [2, 3, 4, 5]
        self.mi = 0
        self.si = 0
        self.rots = {}
        self.xT = A.alloc("xT", [128, DC, T], F32)
        self.identF = A.alloc("identF", [128, 128], F32)
        self.identB = A.alloc("identB", [128, 128], BF16)
        self.onesB = A.alloc("onesB", [128, 128], BF16)
        self.onesF = A.alloc("onesF", [128, 128], F32)
        self.colsA = A.alloc("colsA", [128, 128], F32)
        self.colsB = A.alloc("colsB", [128, 128], F32)
        self.colsC = A.alloc("colsC", [128, 128], F32)
        self.blockB = A.alloc("blockB", [128, 128], BF16)
        self.rotTB = A.alloc("rotTB", [128, 128], BF16)
        self.freq = A.alloc("freq", [128, 2], F32)
        self.posk = A.alloc("posk", [128, NT], F32)
        self.negposk = A.alloc("negposk", [128, NT], F32)
        self.pkf = A.alloc("pkf", [NT, 128], F32)
        self.colsD = A.alloc("colsD", [128, 256], F32)
        self.maskSL = A.alloc("maskSL", [128, 128], F32)
        self.maskIU = A.alloc("maskIU", [128, 128], F32)
        self.lsel = A.alloc("lsel", [128, 128], F32)
        self.memTn = A.alloc("memTn", [128, DC, NMEM], F32)
        self.kTx = A.alloc("kTx", [128, 4, NMEM], BF16)
        self.vx = A.alloc("vx", [128, 2, 512], BF16)
        self.phase_base = A.mark()

    def bank(self):
        i = self.rot_banks[self.bi % len(self.rot_banks)]
        self.bi += 1
        return self.ps[i], ("ps", i)

    def mkrot(self, name, n, shape, dtype):
        self.rots[name] = [[self.A.alloc(name, shape, dtype) for _ in range(n)], 0]

    def rot(self, name):
        lst, i = self.rots[name]
        self.rots[name][1] = (i + 1) % len(lst)
        return lst[i], (name, i)

    def gcol(self, which, l, c):
        j = which * 32 + l * 8 + c
        return self.colsA[:, j:j + 1]

    def setup(self):
        P, d, A = self.P, self.d, self.A
        P.dma("sp", self.identF, d["c_ident"], w=["identF"])
        P.copy("dve", self.identB, self.identF, r=["identF"], w=["identB"])
        P.op("dve", "memset", self.onesB, 1.0, w=["onesB"])
        P.op("dve", "memset", self.onesF, 1.0, w=["onesF"])
        m = A.mark()
        stg = A.alloc("stg", [128, 128], F32)
        for i, nm in enumerate(["norm_mix", "norm_x", "norm_ffn", "norm_mem"]):
            P.dma("sp", stg[i * 32:(i + 1) * 32, :], d[nm].rearrange("l (c p) -> (l c) p", p=128), w=[("stg", i)])
        pb, pk = self.bank()
        P.tr(pb[:, 0:128], stg, self.identF, r=[("stg", i) for i in range(4)] + ["identF"], w=[pk])
        P.copy("dve", self.colsA, pb[:, 0:128], r=[pk], w=["colsA"])
        stg2 = A.alloc("stg2", [128, 128], F32)
        P.op("dve", "memset", stg2, 0.0, w=["stg2"])
        P.dma("sp", stg2[0:4, :], d["x_q_norm"], r=[], w=["stg2"])
        P.dma("sp", stg2[4:8, :], d["x_k_norm"], r=["stg2"], w=["stg2b"])
        pb, pk = self.bank()
        P.tr(pb[:, 0:128], stg2, self.identF, r=["stg2", "stg2b", "identF"], w=[pk])
        P.copy("dve", self.colsB, pb[:, 0:128], r=[pk], w=["colsB"])
        stg3 = A.alloc("stg3", [128, 128], F32)
        P.op("dve", "memset", stg3, 0.0, w=["stg3z"])
        k3 = []
        def ld3(row, c0, src):
            k = ("stg3", len(k3))
            k3.append(k)
            P.dma("sp", stg3[row:row + 1, c0:c0 + src.shape[1]], src, r=["stg3z"], w=[k])
        for o in range(2):
            for half in range(2):
                ld3(o * 8 + 0, half * 64, d["od_c_q_norm"][o:o + 1, :])
                ld3(o * 8 + 1, half * 64, d["od_c_k_norm"][o:o + 1, :])
            ld3(o * 8 + 2, 0, d["od_c_sub_norm"][o:o + 1, :])
            ld3(o * 8 + 3, 0, d["od_q_lat_norm"][o:o + 1, 0:128])
            ld3(o * 8 + 4, 0, d["od_q_lat_norm"][o:o + 1, 128:256])
            ld3(o * 8 + 5, 0, d["od_kv_lat_norm"][o:o + 1, :])
            ld3(o * 8 + 6, 0, d["od_d_q_norm"][o:o + 1, :])
            ld3(o * 8 + 7, 0, d["od_d_k_norm"][o:o + 1, :])
        pb, pk = self.bank()
        P.tr(pb[:, 0:128], stg3, self.identF, r=k3 + ["identF"], w=[pk])
        P.copy("dve", self.colsC, pb[:, 0:128], r=[pk], w=["colsC"])
        P.dma("sp", self.maskSL, d["c_maskSL"], w=["maskSL"])
        P.dma("sp", self.maskIU, d["c_maskIU"], w=["maskIU"])
        P.dma("sp", self.lsel, d["c_lsel"], w=["lsel"])
        for e in range(2):
            st4 = A.alloc("stg4", [128, 128], F32)
            P.op("dve", "memset", st4, 0.0, w=[("st4z", e)])
            k4 = []
            def ld4(r0, src):
                k = ("stg4", e, len(k4))
                k4.append(k)
                P.dma("sp", st4[r0:r0 + src.shape[0], :], src, r=[("st4z", e)], w=[k])
            ld4(0, d["ev_conv_qkv"][e].rearrange("j (c p) -> (j c) p", p=128))
            ld4(48, d["ev_conv_b_w"][e].rearrange("j (c p) -> (j c) p", p=128))
            ld4(64, d["ev_conv_b_b"][e:e + 1, :].rearrange("o (c p) -> (o c) p", p=128))
            ld4(68, d["ev_gate_a_b"][e:e + 1, :].rearrange("o (c p) -> (o c) p", p=128))
            ld4(72, d["ev_gate_x_b"][e:e + 1, :].rearrange("o (c p) -> (o c) p", p=128))
            ld4(76, d["ev_lru_l"][e:e + 1, :].rearrange("o (c p) -> (o c) p", p=128))
            ld4(80, d["ev_o_norm"][e:e + 1, :])
            pb, pk = self.bank()
            P.tr(pb[:, 0:128], st4, self.identF, r=k4 + ["identF"], w=[pk])
            P.copy("dve", self.colsD[:, e * 128:(e + 1) * 128], pb[:, 0:128], r=[pk], w=[("colsD", e)])
        cst = A.alloc("cst", [128, 128], F32)
        P.dma("sp", cst, d["c_blockones"], w=["cst"])
        P.copy("dve", self.blockB, cst, r=["cst"], w=["blockB"])
        cst2 = A.alloc("cst2", [128, 128], F32)
        P.dma("sp", cst2, d["c_rotT"], w=["cst2"])
        P.copy("dve", self.rotTB, cst2, r=["cst2"], w=["rotTB"])
        P.dma("sp", self.freq, d["c_freq"], w=["freq"])
        pk_i = A.alloc("pk_i", [NT, 128], I32)
        pk_f = self.pkf
        P.dma("sp", pk_i, d["positions"].rearrange("o (t p) -> (o t) p", p=128), w=["pk_i"])
        P.copy("dve", pk_f, pk_i, r=["pk_i"], w=["pk_f"])
        pb, pk = self.bank()
        P.tr(pb[:, 0:NT], pk_f, self.identF[0:NT, 0:NT], r=["pk_f", "identF"], w=[pk])
        P.copy("dve", self.posk, pb[:, 0:NT], r=[pk], w=["posk"])
        P.ts("dve", self.negposk, self.posk, -1.0, None, ALU.mult, r=["posk"], w=["negposk"])
        xin = [A.alloc("xin", [128, D], F32) for _ in range(2)]
        for tt in range(NT):
            xb = xin[tt % 2]
            xk = ("xin", tt % 2)
            P.dma("sp", xb, d["x"][tt * 128:(tt + 1) * 128, :], w=[xk])
            for hb in range(2):
                pb, pk = self.bank()
                for q in range(4):
                    c = hb * 4 + q
                    P.tr(pb[:, q * 128:(q + 1) * 128], xb[:, c * 128:(c + 1) * 128], self.identF,
                         r=[xk, "identF"], w=[pk])
                eng = "dve" if hb == 0 else "act"
                P.copy(eng, self.xT[:, hb * 4:(hb + 1) * 4, tt * 128:(tt + 1) * 128],
                       pb.rearrange("p (a b) -> p a b", a=4),
                       r=[pk], w=[("xT", c, tt // 4) for c in range(hb * 4, hb * 4 + 4)])
        mm_ = [A.alloc("memin", [128, D], F32) for _ in range(2)]
        msq = A.alloc("msq", [128, D], F32)
        mss = A.alloc("mss", [128, 2], F32)
        for mt in range(2):
            P.dma("sp", mm_[mt], d["mem"][mt * 128:(mt + 1) * 128, :], w=[("memin", mt)])
            P.act(msq, mm_[mt], AF.Square, r=[("memin", mt)], w=["msq"], accum_out=mss[:, mt:mt + 1])
            P.act(mss[:, mt:mt + 1], mss[:, mt:mt + 1], AF.Sqrt, r=["msq"], w=[("mss", mt)], scale=1.0 / D, bias=EPS)
            P.op("dve", "reciprocal", mss[:, mt:mt + 1], mss[:, mt:mt + 1], r=[("mss", mt)], w=[("mss", mt)])
            P.ts("dve", mm_[mt], mm_[mt], mss[:, mt:mt + 1], None, ALU.mult, r=[("memin", mt), ("mss", mt)],
                 w=[("memin", mt)])
            for hb in range(2):
                pb, pk = self.bank()
                for q in range(4):
                    c = hb * 4 + q
                    P.tr(pb[:, q * 128:(q + 1) * 128], mm_[mt][:, c * 128:(c + 1) * 128], self.identF,
                         r=[("memin", mt), "identF"], w=[pk])
                P.copy("dve", self.memTn[:, hb * 4:(hb + 1) * 4, mt * 128:(mt + 1) * 128],
                       pb.rearrange("p (a b) -> p a b", a=4), r=[pk], w=["memTn"])
        P.fence()
        A.reset(m)

    def norm_block(self, tb, which, l, hT_out, hkey):
        P = self.P
        sl = slice(tb * TB, (tb + 1) * TB)
        sq, sqk = self.rot("sq")
        P.act(sq, self.xT[:, :, sl], AF.Square, r=[("xT", c, tb) for c in range(DC)], w=[sqk])
        ss, ssk = self.bank()
        for c in range(DC):
            P.mm(ss, self.onesB, sq[:, c, :], start=(c == 0), stop=(c == DC - 1), r=[sqk, "onesB"], w=[ssk])
        rs, rsk = self.rot("rs")
        P.act(rs, ss, AF.Ln, r=[ssk], w=[rsk], scale=1.0 / D, bias=EPS)
        P.act(rs, rs, AF.Exp, r=[rsk], w=[rsk], scale=-0.5)
        for c in range(DC):
            P.stt("dve", hT_out[:, c, :], self.xT[:, c, sl], self.gcol(which, l, c), rs, ALU.mult, ALU.mult,
                  r=[("xT", c, tb), rsk, "colsA"], w=[(hkey, c)])

    def pnorm(self, src, srck, npart, ones_l, inv_n, gain, out, outk):
        P = self.P
        srcks = srck if isinstance(srck, list) else [srck]
        sq, sqk = self.rot("sq1")
        P.act(sq[0:npart, :], src, AF.Square, r=srcks, w=[sqk])
        ss, ssk = self.bank()
        P.mm(ss[0:npart, :], ones_l, sq[0:npart, :], r=[sqk, "onesB"], w=[ssk])
        rs, rsk = self.rot("rs")
        P.act(rs[0:npart, :], ss[0:npart, :], AF.Ln, r=[ssk], w=[rsk], scale=inv_n, bias=EPS)
        P.act(rs[0:npart, :], rs[0:npart, :], AF.Exp, r=[rsk], w=[rsk], scale=-0.5)
        if gain is None:
            P.tt("dve", out, src, rs[0:npart, :], ALU.mult, r=srcks + [rsk], w=[outk])
        elif isinstance(gain, float):
            P.stt("dve", out, src, gain, rs[0:npart, :], ALU.mult, ALU.mult, r=srcks + [rsk], w=[outk])
        else:
            P.stt("dve", out, src, gain, rs[0:npart, :], ALU.mult, ALU.mult, r=srcks + [rsk], w=[outk])

    def pnorm_pipe(self, blocks):
        P = self.P
        prev = None

        def part_b(stt_):
            (src, srck, npart, ones_l, inv_n, gain, out, outk), sq, sqk = stt_
            srcks = srck if isinstance(srck, list) else [srck]
            ss, ssk = self.bank()
            P.mm(ss[0:npart, :], ones_l, sq[0:npart, :], r=[sqk], w=[ssk])
            rs, rsk = self.rot("rs")
            P.act(rs[0:npart, :], ss[0:npart, :], AF.Ln, r=[ssk], w=[rsk], scale=inv_n, bias=EPS)
            P.act(rs[0:npart, :], rs[0:npart, :], AF.Exp, r=[rsk], w=[rsk], scale=-0.5)
            if gain is None:
                P.tt("dve", out, src, rs[0:npart, :], ALU.mult, r=srcks + [rsk], w=[outk])
            else:
                P.stt("dve", out, src, gain, rs[0:npart, :], ALU.mult, ALU.mult, r=srcks + [rsk], w=[outk])

        for b in blocks:
            src, srck, npart = b[0], b[1], b[2]
            srcks = srck if isinstance(srck, list) else [srck]
            sq, sqk = self.rot("sq1")
            P.act(sq[0:npart, :], src, AF.Square, r=srcks, w=[sqk])
            if prev is not None:
                part_b(prev)
            prev = (b, sq, sqk)
        if prev is not None:
            part_b(prev)

    def run_chains(self, makers, K):
        pending = list(makers)
        active = []
        free = list(range(K))
        while pending or active:
            while pending and free:
                j = free.pop(0)
                active.append((pending.pop(0)(j), j))
            nxt = []
            for g, j in active:
                try:
                    next(g)
                    nxt.append((g, j))
                except StopIteration:
                    free.append(j)
            active = nxt

    def recip_act(self, out, outk, src, srck):
        P = self.P
        P.act(out, src, AF.Ln, r=[srck], w=[outk])
        P.act(out, out, AF.Exp, r=[outk], w=[outk], scale=-1.0)

    def mbank(self):
        i = self.misc_banks[self.mi % len(self.misc_banks)]
        self.mi += 1
        return self.ps[i], ("ps", i)

    def sbank(self):
        i = self.score_banks[self.si % len(self.score_banks)]
        self.si += 1
        return self.ps[i], ("ps", i)

    def ffn(self, l):
        P, d, A = self.P, self.d, self.A
        m = A.mark()
        self.rots = {}
        hT = A.alloc("ffn_hT", [128, DC, 1024], BF16)
        act = A.alloc("ffn_act", [128, FC, 1024], BF16)
        self.mkrot("win", 2, [128, DC, 1024], BF16)
        self.mkrot("wout", 2, [128, FC, 128], BF16)
        self.mkrot("sq", 1, [128, DC, TB], BF16)
        self.mkrot("rs", 2, [128, TB], F32)
        self.mkrot("sg", 2, [128, TB], F32)
        w_in_d = d["ffn_w_in"][l].rearrange("(c p) n -> p c n", p=128)
        w_out_d = d["ffn_w_out"][l].rearrange("(f p) n -> p f n", p=128)
        SLW = 512
        nsl = (FF + SLW - 1) // SLW
        for half in range(2):
            if half == 0:
                for j in range(2):
                    self.norm_block(j, 2, l, hT[:, :, j * TB:(j + 1) * TB], ("ffn_hT", j))
            for s in range(nsl):
                c0 = s * SLW
                ncol = min(SLW, FF - c0)
                wb, wk = self.rot("win")
                P.dma("pool", wb[:, :, 0:ncol], w_in_d[:, :, c0:c0 + ncol], w=[(wk, "g")])
                P.dma("pool", wb[:, :, SLW:SLW + ncol], w_in_d[:, :, FF + c0:FF + c0 + ncol], w=[(wk, "u")])
                for fi in range(ncol // 128):
                    f = (c0 // 128) + fi
                    for j in range(2):
                        gps, gk = self.bank()
                        ups, uk = self.bank()
                        for c in range(DC):
                            P.mm(gps, wb[:, c, fi * 128:(fi + 1) * 128], hT[:, c, j * TB:(j + 1) * TB],
                                 start=(c == 0), stop=(c == DC - 1), r=[(wk, "g"), (("ffn_hT", j), c)], w=[gk])
                        for c in range(DC):
                            P.mm(ups, wb[:, c, SLW + fi * 128:SLW + (fi + 1) * 128], hT[:, c, j * TB:(j + 1) * TB],
                                 start=(c == 0), stop=(c == DC - 1), r=[(wk, "u"), (("ffn_hT", j), c)], w=[uk])
                        sg, sgk = self.rot("sg")
                        P.act(sg, gps, AF.Silu, r=[gk], w=[sgk])
                        P.tt("dve", act[:, f, j * TB:(j + 1) * TB], sg, ups, ALU.mult, r=[sgk, uk], w=[("act", f, j)])
            if half == 0:
                for j in range(2):
                    self.norm_block(2 + j, 2, l, hT[:, :, j * TB:(j + 1) * TB], ("ffn_hT", j))
            for dc in range(DC):
                wo, wok = self.rot("wout")
                P.dma("pool", wo, w_out_d[:, :, dc * 128:(dc + 1) * 128], w=[wok])
                for j in range(2):
                    tb = half * 2 + j
                    sl = slice(tb * TB, (tb + 1) * TB)
                    yps, yk = self.bank()
                    for f in range(FC):
                        P.mm(yps, wo[:, f, :], act[:, f, j * TB:(j + 1) * TB], start=(f == 0), stop=(f == FC - 1),
                             r=[wok, ("act", f, j)], w=[yk])
                    P.tt("dve", self.xT[:, dc, sl], self.xT[:, dc, sl], yps, ALU.add, r=[("xT", dc, tb), yk],
                         w=[("xT", dc, tb)])
        P.fence()
        A.reset(m)

    def out_proj_block(self, wo, wok, oT, ok, nk, tb):
        P = self.P
        sl = slice(tb * TB, (tb + 1) * TB)
        for dc in range(DC):
            yps, yk = self.bank()
            for h in range(nk):
                P.mm(yps, wo[:, h, dc * 128:(dc + 1) * 128], oT[:, h, :], start=(h == 0), stop=(h == nk - 1),
                     r=[wok, (ok, h)], w=[yk])
            P.tt("dve", self.xT[:, dc, sl], self.xT[:, dc, sl], yps, ALU.add, r=[("xT", dc, tb), yk],
                 w=[("xT", dc, tb)])

    def xattn(self, l):
        P, d, A = self.P, self.d, self.A
        m = A.mark()
        self.rots = {}
        wq = A.alloc("x_wq", [128, DC, 512], BF16)
        wo = A.alloc("x_wo", [128, 4, D], BF16)
        wkv = A.alloc("x_wkv", [128, DC, D], BF16)
        mh = A.alloc("x_mh", [128, DC, NMEM], BF16)
        hT = A.alloc("x_hT", [128, DC, T], BF16)
        qall = A.alloc("x_q", [128, 4, T], BF16)
        oall = A.alloc("x_o", [128, 4, T], BF16)
        self.mkrot("sq", 1, [128, DC, TB], BF16)
        self.mkrot("sq1", 3, [128, TB], BF16)
        self.mkrot("rs", 3, [128, TB], F32)
        self.mkrot("pT", 6, [128, TB], BF16)
        P.dma("pool", wq, d["x_wq"][l].rearrange("(c p) n -> p c n", p=128), w=["x_wq"])
        for hh in range(2):
            P.dma("pool", wkv[:, :, hh * 512:(hh + 1) * 512],
                  d["x_wkv"][l].rearrange("(c p) n -> p c n", p=128)[:, :, hh * 512:(hh + 1) * 512], w=[("x_wkv", hh)])
        P.dma("pool", wo, d["x_wo"][l].rearrange("(h p) n -> p h n", p=128), w=["x_wo"])
        self.rot_banks = list(range(8))
        self.bi = 0
        for tb in range(NTB):
            self.norm_block(tb, 1, l, hT[:, :, tb * TB:(tb + 1) * TB], ("x_hT", tb))
        prev = None

        def q_b(stt_):
            qp, qk, sq, sqk, h, tb = stt_
            ss, ssk = self.bank()
            P.mm(ss, self.onesB, sq, r=[sqk], w=[ssk])
            rs, rsk = self.rot("rs")
            P.act(rs, ss, AF.Ln, r=[ssk], w=[rsk], scale=1.0 / 128, bias=EPS)
            P.act(rs, rs, AF.Exp, r=[rsk], w=[rsk], scale=-0.5)
            P.stt("dve", qall[:, h, tb * TB:(tb + 1) * TB], qp, self.colsB[:, l:l + 1], rs, ALU.mult, ALU.mult,
                  r=[qk, rsk], w=[("x_q", h, tb)])

        for tb in range(NTB):
            sl = slice(tb * TB, (tb + 1) * TB)
            for h in range(4):
                qp, qk = self.bank()
                for c in range(DC):
                    P.mm(qp, wq[:, c, h * 128:(h + 1) * 128], hT[:, c, sl], start=(c == 0), stop=(c == DC - 1),
                         r=["x_wq", (("x_hT", tb), c)], w=[qk])
                sq, sqk = self.rot("sq1")
                P.act(sq, qp, AF.Square, r=[qk], w=[sqk])
                if prev is not None:
                    q_b(prev)
                prev = (qp, qk, sq, sqk, h, tb)
        for c in range(DC):
            P.ts("dve", mh[:, c, :], self.memTn[:, c, :], self.gcol(3, l, c), None, ALU.mult, w=[("x_mh", c)])
        q_b(prev)
        for h in range(4):
            kp, kk = self.bank()
            for c in range(DC):
                P.mm(kp[:, 0:NMEM], wkv[:, c, h * 128:(h + 1) * 128], mh[:, c, :], start=(c == 0), stop=(c == DC - 1),
                     r=[("x_wkv", 0), ("x_mh", c)], w=[kk])
            sq, sqk = self.rot("sq1")
            P.act(sq[:, 0:NMEM], kp[:, 0:NMEM], AF.Square, r=[kk], w=[sqk])
            ss, ssk = self.bank()
            P.mm(ss[:, 0:NMEM], self.onesB, sq[:, 0:NMEM], r=[sqk], w=[ssk])
            rs, rsk = self.rot("rs")
            P.act(rs[:, 0:NMEM], ss[:, 0:NMEM], AF.Ln, r=[ssk], w=[rsk], scale=1.0 / 128, bias=EPS)
            P.act(rs[:, 0:NMEM], rs[:, 0:NMEM], AF.Exp, r=[rsk], w=[rsk], scale=-0.5)
            P.stt("dve", self.kTx[:, h, :], kp[:, 0:NMEM], self.colsB[:, 4 + l:5 + l], rs[:, 0:NMEM], ALU.mult, ALU.mult,
                  r=[kk, rsk], w=[("kTx", h)])
        for mt in range(2):
            vp, vk = self.bank()
            for c in range(DC):
                P.mm(vp, mh[:, c, mt * 128:(mt + 1) * 128], wkv[:, c, 512:1024], start=(c == 0), stop=(c == DC - 1),
                     r=[("x_wkv", 1), ("x_mh", c)], w=[vk])
            P.copy("act", self.vx[:, mt, :], vp, r=[vk], w=[("vx", mt)])
        self.score_banks = [4, 5, 6, 7]
        self.si = 0
        items = [(tb, h, mt) for tb in range(NTB) for h in range(4) for mt in range(2)]
        st = {}

        def stageA(it):
            tb, h, mt = it
            sp_, spk = self.sbank()
            P.mm(sp_, self.kTx[:, h, mt * 128:(mt + 1) * 128], qall[:, h, tb * TB:(tb + 1) * TB],
                 r=[("kTx", h), ("x_q", h, tb)], w=[spk])
            st[it] = (sp_, spk)

        def stageB(it):
            tb, h, mt = it
            g = tb * 4 + h
            sp_, spk = st.pop(it)
            pT, pTk = self.rot("pT")
            P.act(pT, sp_, AF.Exp, r=[spk], w=[pTk], scale=128 ** -0.5, bias=-SM_SHIFT)
            ob, obk = self.ps[g % 2], ("ps", g % 2)
            lb, lbk = self.ps[2 + g % 2], ("ps", 2 + g % 2)
            P.mm(ob, self.vx[:, mt, h * 128:(h + 1) * 128], pT, start=(mt == 0), stop=(mt == 1),
                 r=[("vx", mt), pTk], w=[obk])
            P.mm(lb, self.onesB, pT, start=(mt == 0), stop=(mt == 1), r=[pTk], w=[lbk])
            if mt == 1:
                rs, rsk = self.rot("rs")
                self.recip_act(rs, rsk, lb, lbk)
                P.tt("dve", oall[:, h, tb * TB:(tb + 1) * TB], ob, rs, ALU.mult, r=[obk, rsk], w=[(("x_o", tb), h)])

        LOOKX = 3
        for i in range(min(LOOKX, len(items))):
            stageA(items[i])
        for i in range(len(items)):
            if i + LOOKX < len(items):
                stageA(items[i + LOOKX])
            stageB(items[i])
        self.rot_banks = list(range(8))
        self.bi = 0
        for tb in range(NTB):
            self.out_proj_block(wo, "x_wo", oall[:, :, tb * TB:(tb + 1) * TB], ("x_o", tb), 4, tb)
        self.score_banks = [2, 3, 4, 5]
        self.si = 0
        P.fence()
        A.reset(m)

    def store(self):
        P, d, A = self.P, self.d, self.A
        m = A.mark()
        yo = [A.alloc("yout", [128, D], F32) for _ in range(2)]
        keys = []
        for tt in range(NT):
            yb = yo[tt % 2]
            for hb in range(2):
                pb, pk = self.bank()
                for q in range(4):
                    c = hb * 4 + q
                    P.tr(pb[:, q * 128:(q + 1) * 128], self.xT[:, c, tt * 128:(tt + 1) * 128], self.identF,
                         r=[("xT", c, tt // 4), "identF"], w=[pk])
                eng = "dve" if hb == 0 else "act"
                P.copy(eng, yb[:, hb * 512:(hb + 1) * 512], pb, r=[pk], w=[("yout", tt % 2, hb)])
            P.dma("sp", d["y"][tt * 128:(tt + 1) * 128, :], yb, r=[("yout", tt % 2, 0), ("yout", tt % 2, 1)],
                  w=[("y", tt)])
            keys.append(("y", tt))
        P.finish("sp", keys)
        A.reset(m)

    def even_mixer(self, e, l):
        P, d, A = self.P, self.d, self.A
        m0 = A.mark()
        self.rots = {}
        self.rot_banks = list(range(8))
        cD = lambda j: self.colsD[:, e * 128 + j:e * 128 + j + 1]
        w_in_d = d["ev_w_in"][e].rearrange("(c p) n -> p c n", p=128)
        hT = A.alloc("e_hT", [128, DC, T], BF16)
        m1 = A.mark()
        self.mkrot("sq", 1, [128, DC, TB], BF16)
        self.mkrot("rs", 2, [128, TB], F32)
        for tb in range(NTB):
            self.norm_block(tb, 0, l, hT[:, :, tb * TB:(tb + 1) * TB], ("e_hT", tb))
        P.fence()
        A.reset(m1)
        self.rots = {}

        def proj(sbw, sk, tb, dst_ps, ppk):
            sl = slice(tb * TB, (tb + 1) * TB)
            for c in range(DC):
                P.mm(dst_ps, sbw[:, c, :], hT[:, c, sl], start=(c == 0), stop=(c == DC - 1),
                     r=[sk, (("e_hT", tb), c)], w=[ppk])

        if not self.cfg.get("skip_e1"):
            self._even_e1(e, l, hT, w_in_d, cD, proj, m1)
        if not self.cfg.get("skip_e2"):
            self._even_e2(e, l, hT, w_in_d, cD, proj, m1)
        P.fence()
        A.reset(m0)

    def _even_e1(self, e, l, hT, w_in_d, cD, proj, m1):
        P, d, A = self.P, self.d, self.A
        ybT = A.alloc("ybT", [128, 4, T], BF16)
        wo = A.alloc("e_wo", [128, 4, D], BF16)
        xraw = A.alloc("xraw", [128, 3 + T], F32)
        xc = A.alloc("xc", [128, T], F32)
        av = A.alloc("av", [128, T], F32)
        hs = A.alloc("hs", [128, T], F32)
        rfull = A.alloc("rfull", [128, T], F32)
        ifull = A.alloc("ifull", [128, T], F32)
        xcb = A.alloc("xcb", [128, T], BF16)
        gts = [A.alloc("gate", [128, 128], BF16) for _ in range(8)]
        c1 = A.alloc("c1", [128, 4], F32)
        self.mkrot("slab", 2, [128, DC, 256], BF16)
        self.mkrot("gg", 2, [128, TB], F32)
        slabs = []
        for cc in range(2):
            sb, sk = self.rot("slab")
            P.dma("pool", sb[:, :, 0:128], w_in_d[:, :, 2056 + cc * 128:2056 + (cc + 1) * 128], w=[(sk, 0)])
            P.dma("pool", sb[:, :, 128:256], w_in_d[:, :, 2568 + cc * 128:2568 + (cc + 1) * 128], w=[(sk, 1)])
            slabs.append((sb, sk))
        lcols = self.colsD[:, e * 128 + 76:e * 128 + 80]
        P.act(c1, lcols, AF.Exp, w=["c1"], scale=-1.0)
        P.act(c1, c1, AF.Ln, r=["c1"], w=["c1"], bias=1.0)
        P.ts("dve", c1, c1, -8.0, None, ALU.mult, r=["c1"], w=["c1"])
        P.op("dve", "memset", xraw[:, 0:3], 0.0, w=["xraw_pad"])
        for gi, g in enumerate(gts):
            P.op("dve", "memset", g, 0.0, w=[("gz", gi)])
        for cc in range(4):
            for which, nm in enumerate(["ev_gate_a_w", "ev_gate_x_w"]):
                gi = which * 4 + cc
                P.dma("pool", gts[gi][0:64, 0:64], d[nm][e, 2 * cc], r=[("gz", gi)], w=[("gate", gi, 0)])
                P.dma("pool", gts[gi][64:128, 64:128], d[nm][e, 2 * cc + 1], r=[("gz", gi)], w=[("gate", gi, 1)])
        P.dma("pool", wo, d["ev_w_out"][e].rearrange("(h p) n -> p h n", p=128)[:, 4:8, :], w=["e_wo"])
        allT = list(range(NTB))
        for cc in range(4):
            if cc < 2:
                sb, sk = slabs[cc]
            else:
                sb, sk = self.rot("slab")
                P.dma("pool", sb[:, :, 0:128], w_in_d[:, :, 2056 + cc * 128:2056 + (cc + 1) * 128], w=[(sk, 0)])
                P.dma("pool", sb[:, :, 128:256], w_in_d[:, :, 2568 + cc * 128:2568 + (cc + 1) * 128], w=[(sk, 1)])
            for tb in range(NTB):
                pp, ppk = self.bank()
                proj(sb[:, :, 0:128], (sk, 0), tb, pp, ppk)
                P.copy("act", xraw[:, 3 + tb * TB:3 + (tb + 1) * TB], pp, r=[ppk], w=[("xraw", tb)])
            xrk = [("xraw", tb) for tb in range(NTB)] + ["xraw_pad"]
            P.ts("dve", xc, xraw[:, 3:3 + T], cD(48 + 3 * 4 + cc), cD(64 + cc), ALU.mult, ALU.add, r=xrk, w=["xc"])
            for j in range(3):
                P.stt("dve", xc, xraw[:, j:j + T], cD(48 + j * 4 + cc), xc, ALU.mult, ALU.add, r=xrk + ["xc"], w=["xc"])
            P.copy("act", xcb, xc, r=["xc"], w=["xcb"])
            for tb in range(NTB):
                sl = slice(tb * TB, (tb + 1) * TB)
                rp, rpk = self.bank()
                P.mm(rp, gts[cc], xcb[:, sl], r=["xcb", ("gate", cc, 0), ("gate", cc, 1)], w=[rpk])
                ip, ipk = self.bank()
                P.mm(ip, gts[4 + cc], xcb[:, sl], r=["xcb", ("gate", 4 + cc, 0), ("gate", 4 + cc, 1)], w=[ipk])
                P.act(rfull[:, sl], rp, AF.Sigmoid, r=[rpk], w=[("rfull", tb)], bias=cD(68 + cc))
                P.act(ifull[:, sl], ip, AF.Sigmoid, r=[ipk], w=[("ifull", tb)], bias=cD(72 + cc))
            rk_all = [("rfull", tb) for tb in allT]
            ik_all = [("ifull", tb) for tb in allT]
            P.act(av, rfull, AF.Exp, r=rk_all + ["c1"], w=["av"], scale=c1[:, cc:cc + 1])
            P.tt("dve", rfull, av, av, ALU.mult, r=["av"], w=rk_all)
            P.act(rfull, rfull, AF.Sqrt, r=rk_all, w=rk_all, scale=-1.0, bias=1.0)
            P.tt("dve", ifull, ifull, xc, ALU.mult, r=ik_all + ["xc"], w=ik_all)
            P.tt("dve", xc, rfull, ifull, ALU.mult, r=rk_all + ik_all, w=["xc"])
            P.op("dve", "tensor_tensor_scan", hs, av, xc, 0.0, ALU.mult, ALU.add, r=["av", "xc"], w=["hs"])
            for tb in range(NTB):
                sl = slice(tb * TB, (tb + 1) * TB)
                gp, gpk = self.bank()
                proj(sb[:, :, 128:256], (sk, 1), tb, gp, gpk)
                gg, ggk = self.rot("gg")
                P.act(gg, gp, AF.Gelu_apprx_tanh, r=[gpk], w=[ggk])
                P.tt("dve", ybT[:, cc, sl], gg, hs[:, sl], ALU.mult, r=[ggk, "hs"], w=[(("ybT", tb), cc)])
        for tb in range(NTB):
            self.out_proj_block(wo, "e_wo", ybT[:, :, tb * TB:(tb + 1) * TB], ("ybT", tb), 4, tb)
        P.fence()
        A.reset(m1)

    def _even_e2(self, e, l, hT, w_in_d, cD, proj, m1):
        P, d, A = self.P, self.d, self.A
        self.rots = {}
        yaT = A.alloc("yaT", [128, 4, T], BF16)
        woa = A.alloc("e_woa", [128, 4, D], BF16)
        P.dma("pool", woa, d["ev_w_out"][e].rearrange("(h p) n -> p h n", p=128)[:, 0:4, :], w=["e_woa"])
        scn = ["beta", "g", "cum", "cl", "kbe", "kdec", "negb", "tmp"]
        sc = {nm: A.alloc("sc_" + nm, [128, NT, 4], F32) for nm in scn}
        scf = {nm: sc[nm].rearrange("p t h -> p (t h)") for nm in scn}
        lastc = A.alloc("lastc", [128, 4, 32], F32)
        bd = A.alloc("bdrow", [1, 8], F32)
        bdb = A.alloc("bdb", [128, 8], F32)
        slab8 = A.alloc("slab8", [128, DC, 8], BF16)
        P.dma("sp", bd[0:1, 0:4], d["ev_dt_bias"][e:e + 1, :], w=[("bd", 0)])
        P.dma("sp", bd[0:1, 4:8], d["ev_a_log"][e:e + 1, :], w=[("bd", 1)])
        pb, pk = self.bank()
        P.mm(pb[:, 0:8], self.onesF[0:1, :], bd, r=[("bd", 0), ("bd", 1)], w=[pk])
        P.copy("dve", bdb, pb[:, 0:8], r=[pk], w=["bdb"])
        P.act(bdb[:, 4:8], bdb[:, 4:8], AF.Exp, r=["bdb"], w=["bdb2"])
        P.ts("dve", bdb[:, 4:8], bdb[:, 4:8], -1.0, None, ALU.mult, r=["bdb2"], w=["bdb2"])
        P.dma("pool", slab8, w_in_d[:, :, 2048:2056], w=["slab8"])
        for tt in range(NT):
            pp, ppk = self.bank()
            for c in range(DC):
                P.mm(pp[:, 0:8], hT[:, c, tt * 128:(tt + 1) * 128], slab8[:, c, :], start=(c == 0), stop=(c == DC - 1),
                     r=["slab8", (("e_hT", tt // 4), c)], w=[ppk])
            P.act(sc["beta"][:, tt, :], pp[:, 0:4], AF.Sigmoid, r=[ppk], w=[("sc_beta", tt)])
            P.tt("dve", sc["tmp"][:, tt, :], pp[:, 4:8], bdb[:, 0:4], ALU.add, r=[ppk, "bdb"], w=[("sc_tmp", tt)])
        alltmp = [("sc_tmp", tt) for tt in range(NT)]
        P.act(scf["tmp"], scf["tmp"], AF.Exp, r=alltmp, w=["sc_tmp2"])
        P.act(scf["tmp"], scf["tmp"], AF.Ln, r=["sc_tmp2"], w=["sc_tmp2"], bias=1.0)
        for tt in range(NT):
            P.tt("dve", sc["g"][:, tt, :], sc["tmp"][:, tt, :], bdb[:, 4:8], ALU.mult, r=["sc_tmp2", "bdb2"], w=[("sc_g", tt)])
        pb, pk = self.bank()
        P.mm(pb[:, 0:64], self.maskIU, scf["g"], r=[("sc_g", tt) for tt in range(NT)], w=[pk])
        P.copy("dve", scf["cum"], pb[:, 0:64], r=[pk], w=["sc_cum"])
        pb, pk = self.bank()
        P.mm(pb[:, 0:64], self.lsel, scf["cum"], r=["sc_cum"], w=[pk])
        P.copy("dve", scf["cl"], pb[:, 0:64], r=[pk], w=["sc_cl"])
        allb = [("sc_beta", tt) for tt in range(NT)]
        P.act(scf["kbe"], scf["cum"], AF.Exp, r=["sc_cum"], w=["sc_kbe"])
        P.tt("dve", scf["kbe"], scf["kbe"], scf["beta"], ALU.mult, r=["sc_kbe"] + allb, w=["sc_kbe"])
        P.tt("dve", scf["kdec"], scf["cl"], scf["cum"], ALU.subtract, r=["sc_cl", "sc_cum"], w=["sc_kdec"])
        P.act(scf["kdec"], scf["kdec"], AF.Exp, r=["sc_kdec"], w=["sc_kdec"])
        P.ts("dve", scf["negb"], scf["beta"], -1.0, None, ALU.mult, r=allb, w=["sc_negb"])
        P.fence()
        slabq = [A.alloc("slabq", [128, DC, 128], BF16) for _ in range(1)]

        def load_head_slabs(hh):
            cols = [hh * 128]
            for i_, c_ in enumerate(cols):
                P.dma("pool", slabq[i_], w_in_d[:, :, c_:c_ + 128], w=[("slabq", i_)], nofence=True)

        load_head_slabs(0)
        mh = A.mark()
        for h in range(4):
            A.reset(mh)
            self.rots = {}
            self.rot_banks = list(range(8))
            qT = A.alloc("qT", [128, T], BF16)
            kT = A.alloc("kT", [128, T], BF16)
            vT = A.alloc("vT", [128, T], BF16)
            zs = A.alloc("zs", [128, T], BF16)
            mA = A.mark()
            raws = [A.alloc("raw", [128, 3 + T], F32) for _ in range(3)]
            cvs = [A.alloc("cv", [128, T], F32) for _ in range(2)]
            self.mkrot("sq1", 2, [128, TB], BF16)
            self.mkrot("rs", 1, [128, TB], F32)
            self.mkrot("slabv", 2, [128, DC, 128], BF16)
            kinds = [(h * 128, qT, "qT"), (512 + h * 128, kT, "kT"), (1024 + h * 128, vT, "vT")]
            for kind, (col0, dst, dn) in enumerate(kinds):
                raw = raws[kind]
                P.op("dve", "memset", raw[:, 0:3], 0.0, w=[("raw_pad", kind)])
                if kind == 0:
                    sb, sk = slabq[0], ("slabq", 0)
                else:
                    sb, sk = self.rot("slabv")
                    P.dma("pool", sb, w_in_d[:, :, col0:col0 + 128], w=[sk])
                for tb in range(NTB):
                    pp, ppk = self.bank()
                    proj(sb, sk, tb, pp, ppk)
                    P.copy("act", raw[:, 3 + tb * TB:3 + (tb + 1) * TB], pp, r=[ppk], w=[("raw", kind, tb)])
            sb, sk = self.rot("slabv")
            P.dma("pool", sb, w_in_d[:, :, 1536 + h * 128:1536 + (h + 1) * 128], w=[sk])
            zps = []
            for tb in range(NTB):
                pp, ppk = self.bank()
                proj(sb, sk, tb, pp, ppk)
                zps.append((pp, ppk))

            def conv(kind, cv, cvk, eng):
                raw = raws[kind]
                cc = kind * 4 + h
                rk_ = [("raw", kind, tb) for tb in range(NTB)] + [("raw_pad", kind)]
                P.ts(eng, cv, raw[:, 3:3 + T], cD(3 * 12 + cc), None, ALU.mult, r=rk_, w=[cvk])
                for j in range(3):
                    P.stt(eng, cv, raw[:, j:j + T], cD(j * 12 + cc), cv, ALU.mult, ALU.add, r=rk_ + [cvk], w=[cvk])

            conv(0, cvs[0], "cv0", "dve")
            conv(1, cvs[1], "cv1", "dve")
            silq = raws[0][:, 3:3 + T]
            silk = raws[1][:, 3:3 + T]
            P.act(silq, cvs[0], AF.Silu, r=["cv0"], w=["silq"] + [("raw", 0, tb) for tb in range(NTB)])
            P.act(silk, cvs[1], AF.Silu, r=["cv1"], w=["silk"] + [("raw", 1, tb) for tb in range(NTB)])
            conv(2, cvs[0], "cv0", "dve")
            for tb in range(NTB):
                pp, ppk = zps[tb]
                P.act(zs[:, tb * TB:(tb + 1) * TB], pp, AF.Silu, r=[ppk], w=[("zs", tb)])
            P.act(vT, cvs[0], AF.Silu, r=["cv0"], w=["vT"])
            blocks = []
            for sil_, silkey, dst, dn, gsc in ((silq, "silq", qT, "qT", 128 ** -0.5), (silk, "silk", kT, "kT", None)):
                for tb in range(NTB):
                    sl = slice(tb * TB, (tb + 1) * TB)
                    blocks.append((sil_[:, sl], silkey, 128, self.onesB, 1.0, gsc, dst[:, sl], (dn, tb)))
            self.pnorm_pipe(blocks)
            P.fence()
            A.reset(mA)
            self.rots = {}
            if h + 1 < 4:
                load_head_slabs(h + 1)
            kdec = A.alloc("ktm_dec", [128, NT, 128], BF16)
            utm = A.alloc("u_tm", [128, NT, 128], BF16)
            wT = A.alloc("wT", [128, T], BF16)
            qdecT = A.alloc("qdecT", [128, T], BF16)
            qkT = A.alloc("qkT", [128, NT, 128], BF16)
            S = A.alloc("S", [128, 128], F32)
            Sb = A.alloc("Sb", [128, 128], BF16)
            KCH = 5
            slots = []
            for j in range(KCH):
                sd = {}
                for nm in ("dg", "t1"):
                    sd[nm] = A.alloc("sl_" + nm, [128, 128], F32)
                sd["t2"] = sd["dg"]
                for nm in ("Dt", "Dm", "E", "p0", "p1", "pT0", "pT1", "kbe", "vtb", "AT0", "AT1"):
                    sd[nm] = A.alloc("sl_" + nm, [128, 128], BF16)
                slots.append(sd)
            self.mkrot("vn", 2, [128, 128], BF16)
            self.mkrot("ot", 2, [128, TB], F32)
            self.mkrot("sq1", 1, [128, TB], BF16)
            self.mkrot("rs", 1, [128, TB], F32)
            iF, iB = self.identF, self.identB
            self.rot_banks = [1, 2, 3, 4, 5, 6, 7]
            self.bi = 0

            def tile_chain(tt, j, h=h):
                sd = slots[j]
                K_ = lambda nm: ("sl", j, nm)
                tsl = slice(tt * 128, (tt + 1) * 128)
                tb = tt // 4
                cumc = sc["cum"][:, tt, h:h + 1]
                kp, kpk = self.bank()
                P.mm(kp[:, 0:128], kT[:, tsl], iB, r=[("kT", tb)], w=[kpk])
                P.ts("dve", sd["kbe"], kp[:, 0:128], sc["kbe"][:, tt, h:h + 1], None, ALU.mult, r=[kpk], w=[K_("kbe")])
                P.act(kdec[:, tt, :], kp[:, 0:128], AF.Copy, r=[kpk], w=[("kdec", tt)], scale=sc["kdec"][:, tt, h:h + 1])
                vp, vpk = self.bank()
                P.mm(vp[:, 0:128], vT[:, tsl], iB, r=["vT"], w=[vpk])
                P.act(sd["vtb"], vp[:, 0:128], AF.Copy, r=[vpk], w=[K_("vtb")], scale=sc["beta"][:, tt, h:h + 1])
                P.ts("pool", sd["dg"], iF, cumc, None, ALU.mult, w=[K_("dg")])
                yield
                bc, bck = self.bank()
                P.mm(bc[:, 0:128], self.onesF, sd["dg"], r=[K_("dg")], w=[bck])
                P.ts("dve", sd["t1"], bc[:, 0:128], cumc, 0.0, ALU.subtract, ALU.min, r=[bck], w=[K_("t1")])
                P.act(sd["Dt"], sd["t1"], AF.Exp, r=[K_("t1")], w=[K_("Dt")])
                P.ts("dve", sd["t2"], bc[:, 0:128], cumc, 0.0, ALU.subtract, ALU.max, r=[bck], w=[K_("t2"), K_("dg")])
                P.act(sd["Dm"], sd["t2"], AF.Exp, r=[K_("t2")], w=[K_("Dm")], scale=-1.0)
                P.act(sd["E"], bc[:, 0:128], AF.Exp, r=[bck], w=[K_("E")])
                P.act(lastc[:, h, 2 * tt:2 * tt + 2], bc[:, 63:128:64], AF.Exp, r=[bck], w=[("lastc", tt)])
                P.tt("pool", qdecT[:, tsl], qT[:, tsl], sd["E"], ALU.mult, r=[("qT", tb), K_("E")], w=[("qdecT", tt)])
                P.tt("pool", sd["Dm"], sd["Dm"], self.maskSL, ALU.mult, r=[K_("Dm")], w=[K_("Dm")])
                P.tt("pool", sd["Dt"], sd["Dt"], self.maskIU, ALU.mult, r=[K_("Dt")], w=[K_("Dt")])
                yield
                KK, KKk = self.bank()
                P.mm(KK[:, 0:128], kT[:, tsl], kT[:, tsl], r=[("kT", tb)], w=[KKk])
                QK, QKk = self.bank()
                P.mm(QK[:, 0:128], kT[:, tsl], qT[:, tsl], r=[("kT", tb), ("qT", tb)], w=[QKk])
                p, pk_ = sd["p0"], K_("p0")
                P.stt("dve", p, KK[:, 0:128], sc["negb"][:, tt, h:h + 1], sd["Dm"], ALU.mult, ALU.mult,
                      r=[KKk, K_("Dm")], w=[pk_])
                P.tt("dve", qkT[:, tt, :], QK[:, 0:128], sd["Dt"], ALU.mult, r=[QKk, K_("Dt")], w=[("qkT", tt)])
                yield
                tp, tpk = self.bank()
                P.mm(tp[:, 0:128], p, iB, r=[pk_], w=[tpk])
                pT, pTk = sd["pT0"], K_("pT0")
                P.copy("act", pT, tp[:, 0:128], r=[tpk], w=[pTk])
                AT, ATk = sd["AT0"], K_("AT0")
                P.tt("dve", AT, tp[:, 0:128], iF, ALU.add, r=[tpk], w=[ATk])
                yield
                def a_update(pcur, pcurk, AT, ATk, s_):
                    an, ank = self.bank()
                    P.mm(an[:, 0:128], iB, AT, start=True, stop=False, r=[ATk], w=[ank])
                    P.mm(an[:, 0:128], pcur, AT, start=False, stop=True, r=[pcurk, ATk], w=[ank])
                    nx_ = "1" if (s_ % 2 == 0) else "0"
                    ATn, ATnk = sd["AT" + nx_], K_("AT" + nx_)
                    P.copy("dve" if s_ % 2 else "act", ATn, an[:, 0:128], r=[ank], w=[ATnk])
                    return ATn, ATnk

                for s_ in range(5):
                    nx = "1" if (s_ % 2 == 0) else "0"
                    p2, p2k = self.bank()
                    P.mm(p2[:, 0:128], pT, p, r=[pTk, pk_], w=[p2k])
                    pn, pnk = sd["p" + nx], K_("p" + nx)
                    P.copy("act", pn, p2[:, 0:128], r=[p2k], w=[pnk])
                    if s_ < 4:
                        p2T, p2Tk = self.bank()
                        P.mm(p2T[:, 0:128], p, pT, r=[pTk, pk_], w=[p2Tk])
                        pTn, pTnk = sd["pT" + nx], K_("pT" + nx)
                        P.copy("act" if s_ % 2 else "dve", pTn, p2T[:, 0:128], r=[p2Tk], w=[pTnk])
                    if s_ > 0:
                        AT, ATk = a_update(p, pk_, AT, ATk, s_ - 1)
                    yield
                    if s_ < 4:
                        p, pk_, pT, pTk = pn, pnk, pTn, pTnk
                    else:
                        p, pk_ = pn, pnk
                AT, ATk = a_update(p, pk_, AT, ATk, 4)
                yield
                wp, wpk = self.bank()
                P.mm(wp[:, 0:128], sd["kbe"], AT, r=[K_("kbe"), ATk], w=[wpk])
                P.copy("act", wT[:, tsl], wp[:, 0:128], r=[wpk], w=[("wT", tt)])
                up, upk = self.bank()
                P.mm(up[:, 0:128], AT, sd["vtb"], r=[K_("vtb"), ATk], w=[upk])
                P.copy("dve", utm[:, tt, :], up[:, 0:128], r=[upk], w=[("utm", tt)])

            def scan_chain(h=h):
                P.op("dve", "memset", S, 0.0, w=["S"])
                P.op("dve", "memset", Sb, 0.0, w=["Sb"])
                ob, obk = self.ps[0], ("ps", 0)
                for ck in range(32):
                    tt, half = ck // 2, ck % 2
                    yield ("need", tt)
                    hp = slice(half * 64, half * 64 + 64)
                    csl = slice(ck * 64, (ck + 1) * 64)
                    col = (ck % 8) * 64
                    a_ps, ak = self.bank()
                    P.mm(a_ps[hp, 0:128], wT[:, csl], Sb, r=[("wT", tt), "Sb"], w=[ak])
                    P.mm(ob[:, col:col + 64], Sb, qdecT[:, csl], start=True, stop=False, r=["Sb", ("qdecT", tt)], w=[obk])
                    vn, vnk = self.rot("vn")
                    P.tt("dve", vn[hp, :], utm[hp, tt, :], a_ps[hp, 0:128], ALU.subtract, r=[("utm", tt), ak], w=[vnk])
                    yield None
                    s_ps, spk = self.bank()
                    P.mm(s_ps[:, 0:128], kdec[hp, tt, :], vn[hp, :], r=[("kdec", tt), vnk], w=[spk])
                    P.mm(ob[:, col:col + 64], vn[hp, :], qkT[hp, tt, half * 64:half * 64 + 64], start=False, stop=True,
                         r=[vnk, ("qkT", tt)], w=[obk])
                    P.stt("dve", Sb, S, lastc[:, h, ck:ck + 1], s_ps[:, 0:128], ALU.mult, ALU.add,
                          r=[spk, "S", ("lastc", tt)], w=["Sb"])
                    P.stt("dve", S, S, lastc[:, h, ck:ck + 1], s_ps[:, 0:128], ALU.mult, ALU.add,
                          r=[spk, "S", ("lastc", tt)], w=["S"])
                    if ck % 8 == 7:
                        tb = ck // 8
                        sl = slice(tb * TB, (tb + 1) * TB)
                        ot, otk = self.rot("ot")
                        P.copy("act", ot, ob, r=[obk], w=[otk])
                        on, onk = self.rot("ot")
                        self.pnorm(ot, otk, 128, self.onesB, 1.0 / 128, cD(80), on, onk)
                        P.tt("dve", yaT[:, h, sl], on, zs[:, sl], ALU.mult, r=[onk, ("zs", tb)], w=[(("yaT", tb), h)])
                    yield None

            pending = list(range(NT))
            active = []
            free_slots = list(range(KCH))
            done_tiles = set()
            scan = scan_chain()
            scan_wait = next(scan)
            scan_done = False
            while pending or active or not scan_done:
                while pending and free_slots:
                    tt = pending.pop(0)
                    j = free_slots.pop(0)
                    active.append((tile_chain(tt, j), tt, j))
                def scan_step():
                    nonlocal scan_wait, scan_done
                    if not scan_done:
                        if scan_wait is None or scan_wait[1] in done_tiles:
                            try:
                                scan_wait = next(scan)
                            except StopIteration:
                                scan_done = True

                nxt = []
                for gi_, (g, tt, j) in enumerate(active):
                    try:
                        next(g)
                        nxt.append((g, tt, j))
                    except StopIteration:
                        done_tiles.add(tt)
                        free_slots.append(j)
                    if gi_ % 2 == 0:
                        scan_step()
                active = nxt
                if not active:
                    scan_step()
            self.rot_banks = list(range(8))
            self.bi = 0
            P.fence()
        for tb in range(NTB):
            self.out_proj_block(woa, "e_woa", yaT[:, :, tb * TB:(tb + 1) * TB], ("yaT", tb), 4, tb)
        P.fence()
        A.reset(m1)

    def build_posfb(self, qb):
        P = self.P
        pf, pfk = self.rot("posfb")
        pb, pk = self.bank()
        for j in range(4):
            tt = qb * 4 + j
            tm, tmk = self.rot("pkm")
            P.ts("dve", tm, self.pkf, self.identF[0:NT, tt:tt + 1], None, ALU.mult, w=[tmk])
            P.mm(pb[:, j * 128:(j + 1) * 128], self.onesF[0:NT, :], tm, r=[tmk], w=[pk])
        P.copy("act", pf, pb, r=[pk], w=[pfk])
        return pf, pfk

    def rope_tables(self, tb):
        P = self.P
        pf, pfk = self.build_posfb(tb)
        R = slice(64, 96)
        f0 = self.freq[R, 0:1]
        out = {}
        for nm, shift in (("sin", 0.0), ("cos", math.pi / 2)):
            ang, ak = self.rot("ang")
            P.ts("dve", ang[R, :], pf[R, :], f0, shift, ALU.mult, ALU.add, r=[pfk], w=[ak])
            ki, kik = self.rot("angi")
            P.ts("dve", ki[R, :], ang[R, :], 1.0 / (2 * math.pi), None, ALU.mult, r=[ak], w=[kik])
            kf, kfk = self.rot("ang")
            P.copy("dve", kf[R, :], ki[R, :], r=[kik], w=[kfk])
            P.stt("dve", ang[R, :], kf[R, :], -2 * math.pi, ang[R, :], ALU.mult, ALU.add, r=[kfk, ak], w=[ak])
            P.ts("dve", ang[R, :], ang[R, :], math.pi, -math.pi, ALU.min, ALU.max, r=[ak], w=[ak])
            tab, tk = self.rot(nm)
            P.act(tab[R, :], ang[R, :], AF.Sin, r=[ak], w=[tk])
            out[nm] = (tab, tk)
        return out

    def odd_mixer(self, o, l):
        P, d, A = self.P, self.d, self.A
        m0 = A.mark()
        self.rots = {}
        cC = lambda j: self.colsC[:, o * 8 + j:o * 8 + j + 1]
        SC_C = 64 ** -0.5
        SC_D = 96 ** -0.5
        slopes = [2.0 ** (-8.0 * (i + 1) / 4) for i in range(4)]
        lam_init = 0.8 - 0.6 * math.exp(-0.3 * l)
        qlatT = A.alloc("qlatT", [128, 2, T], BF16)
        kvlatT = A.alloc("kvlatT", [128, T], BF16)
        kropeT = A.alloc("kropeT", [128, T], BF16)
        neglam = A.alloc("neglam", [128, 2], F32)
        gsub = A.alloc("gsub", [128, 1], F32)
        lamt = A.alloc("lamt", [1, 4, 64], F32)
        lamp = A.alloc("lamp", [1, 2, 64], F32)
        ls = A.alloc("lams", [1, 8], F32)
        m1 = A.mark()
        qcT = A.alloc("qcT", [128, 4, T], BF16)
        kcT = A.alloc("kcT", [128, 4, T], BF16)
        vc = A.alloc("vc", [128, NT, 512], BF16)
        m2 = A.mark()
        for i, nm in enumerate(["od_lam_q1", "od_lam_k1", "od_lam_q2", "od_lam_k2"]):
            P.dma("sp", lamt[0:1, i, :], d[nm][o:o + 1, :], w=[("lamt", i)])
        P.tt("dve", lamp[0:1, 0, :], lamt[0:1, 0, :], lamt[0:1, 1, :], ALU.mult, r=[("lamt", 0), ("lamt", 1)], w=["lp0"])
        P.tt("dve", lamp[0:1, 1, :], lamt[0:1, 2, :], lamt[0:1, 3, :], ALU.mult, r=[("lamt", 2), ("lamt", 3)], w=["lp1"])
        P.op("dve", "memset", ls, 0.0, w=["ls"])
        P.op("dve", "reduce_sum", ls[0:1, 0:1], lamp[0:1, 0, :], AX.X, r=["lp0", "ls"], w=["ls0"])
        P.op("dve", "reduce_sum", ls[0:1, 1:2], lamp[0:1, 1, :], AX.X, r=["lp1", "ls"], w=["ls1"])
        P.act(ls[0:1, 0:2], ls[0:1, 0:2], AF.Exp, r=["ls0", "ls1"], w=["lse"])
        P.tt("dve", ls[0:1, 2:3], ls[0:1, 1:2], ls[0:1, 0:1], ALU.subtract, r=["lse"], w=["ls2"])
        P.ts("dve", ls[0:1, 4:5], ls[0:1, 2:3], -lam_init, None, ALU.add, r=["ls2"], w=["ls3"])
        pb, pk = self.bank()
        P.mm(pb[:, 0:2], self.onesF[0:1, :], ls[0:1, 4:6], r=["ls3"], w=[pk])
        P.copy("dve", neglam, pb[:, 0:2], r=[pk], w=["neglam"])
        P.ts("dve", gsub, cC(2), 1.0 - lam_init, None, ALU.mult, w=["gsub"])
        hT = A.alloc("o_hT", [128, DC, T], BF16)
        self.mkrot("slab", 1, [128, DC, 512], BF16)
        self.mkrot("sq", 1, [128, DC, TB], BF16)
        self.rots["slab"][0].append(self.rots["sq"][0][0])
        self.mkrot("sq1", 2, [128, TB], BF16)
        self.mkrot("rs", 2, [128, TB], F32)
        w_in_d = d["od_w_in"][o].rearrange("(c p) n -> p c n", p=128)
        for tb in range(NTB):
            self.norm_block(tb, 0, l, hT[:, :, tb * TB:(tb + 1) * TB], ("o_hT", tb))

        def load_slab(c0, n):
            sb, sk = self.rot("slab")
            wk = [sk, ("sq", 0)] if sk == ("slab", 1) else [sk]
            P.dma("pool", sb[:, :, 0:n], w_in_d[:, :, c0:c0 + n], w=wk)
            return sb, sk

        def pn_a(src, srck, ones_l, inv_n, gain, out, outk):
            sq, sqk = self.rot("sq1")
            P.act(sq, src, AF.Square, r=[srck], w=[sqk])
            return (src, srck, ones_l, inv_n, gain, out, outk, sq, sqk)

        def pn_b(stt_):
            src, srck, ones_l, inv_n, gain, out, outk, sq, sqk = stt_
            ss, ssk = self.bank()
            P.mm(ss, ones_l, sq, r=[sqk], w=[ssk])
            rs, rsk = self.rot("rs")
            P.act(rs, ss, AF.Ln, r=[ssk], w=[rsk], scale=inv_n, bias=EPS)
            P.act(rs, rs, AF.Exp, r=[rsk], w=[rsk], scale=-0.5)
            P.stt("dve", out, src, gain, rs, ALU.mult, ALU.mult, r=[srck, rsk], w=[outk])

        slab_q = load_slab(0, 512)
        slab_k = load_slab(512, 512)
        prev = None
        for dst, dname, gj, (sb, sk) in ((qcT, "qcT", 0, slab_q), (kcT, "kcT", 1, slab_k)):
            for ch in range(4):
                for tb in range(NTB):
                    sl = slice(tb * TB, (tb + 1) * TB)
                    pp, ppk = self.bank()
                    for c in range(DC):
                        P.mm(pp, sb[:, c, ch * 128:(ch + 1) * 128], hT[:, c, sl], start=(c == 0), stop=(c == DC - 1),
                             r=[sk, (("o_hT", tb), c)], w=[ppk])
                    cur = pn_a(pp, ppk, self.blockB, 1.0 / 64, cC(gj), dst[:, ch, sl], (dname, ch, tb))
                    if prev is not None:
                        pn_b(prev)
                    prev = cur
        sb, sk = load_slab(1024, 512)
        pn_b(prev)
        for tt in range(NT):
            pp, ppk = self.bank()
            for c in range(DC):
                P.mm(pp, hT[:, c, tt * 128:(tt + 1) * 128], sb[:, c, :], start=(c == 0), stop=(c == DC - 1),
                     r=[sk, (("o_hT", tt // 4), c)], w=[ppk])
            P.copy("act", vc[:, tt, :], pp, r=[ppk], w=[("vc", tt)])
        sb, sk = load_slab(1536, 416)
        for tb in range(NTB):
            sl = slice(tb * TB, (tb + 1) * TB)
            qq = [self.bank(), self.bank()]
            for ci, (pp, ppk) in enumerate(qq):
                for c in range(DC):
                    P.mm(pp, sb[:, c, ci * 128:(ci + 1) * 128], hT[:, c, sl], start=(c == 0), stop=(c == DC - 1),
                         r=[sk, (("o_hT", tb), c)], w=[ppk])
            ss, ssk = self.bank()
            for ci, (pp, ppk) in enumerate(qq):
                sq, sqk = self.rot("sq1")
                P.act(sq, pp, AF.Square, r=[ppk], w=[sqk])
                P.mm(ss, self.onesB, sq, start=(ci == 0), stop=(ci == 1), r=[sqk], w=[ssk])
            rs, rsk = self.rot("rs")
            P.act(rs, ss, AF.Ln, r=[ssk], w=[rsk], scale=1.0 / 256, bias=EPS)
            P.act(rs, rs, AF.Exp, r=[rsk], w=[rsk], scale=-0.5)
            for ci, (pp, ppk) in enumerate(qq):
                P.stt("dve", qlatT[:, ci, sl], pp, cC(3 + ci), rs, ALU.mult, ALU.mult, r=[ppk, rsk],
                      w=[("qlatT", ci, tb)])
            pp, ppk = self.bank()
            for c in range(DC):
                P.mm(pp, sb[:, c, 256:384], hT[:, c, sl], start=(c == 0), stop=(c == DC - 1),
                     r=[sk, (("o_hT", tb), c)], w=[ppk])
            self.pnorm(pp, ppk, 128, self.onesB, 1.0 / 128, cC(5), kvlatT[:, sl], ("kvlatT", tb))
            pp, ppk = self.bank()
            for c in range(DC):
                P.mm(pp[64:96, :], sb[:, c, 384:416], hT[:, c, sl], start=(c == 0), stop=(c == DC - 1),
                     r=[sk, (("o_hT", tb), c)], w=[ppk])
            P.copy("act", kropeT[64:96, sl], pp[64:96, :], r=[ppk], w=[("kropeT", tb)])
        P.fence()
        A.reset(m2)
        self.rots = {}
        wo = A.alloc("o_wo", [128, 4, D], BF16)
        P.dma("pool", wo, d["od_w_out"][o].rearrange("(h p) n -> p h n", p=128)[:, 0:4, :], w=["o_wo"])
        self.mkrot("posfb", 2, [128, TB], F32)
        self.mkrot("pkm", 2, [NT, 128], F32)
        dcache = [A.alloc("dcache", [128, TB], F16) for _ in range(NT)]
        self.mkrot("tS", 2, [128, TB], F32)
        self.mkrot("pT", 4, [128, TB], BF16)
        self.mkrot("yT", 2, [128, 4, TB], BF16)
        self.mkrot("rs", 3, [128, TB], F32)
        self.mkrot("sq1", 2, [128, TB], BF16)
        self.mkrot("ya", 1, [128, TB], F32)
        self.mkrot("yb", 1, [128, TB], F32)
        self.score_banks = [4, 5, 6, 7]
        self.si = 0
        self.rot_banks = [6, 7]
        self.bi = 0
        items = [(qb, h, kt) for qb in range(NTB) for h in range(4) for kt in range(4 * qb + 4)]
        st = {}
        deferred = []

        def defer(n, fn):
            deferred.append([n, fn])

        def tick(flush=False):
            while deferred and (flush or deferred[0][0] <= 0):
                deferred.pop(0)[1]()
            for dd_ in deferred:
                dd_[0] -= 1

        def stageA(it):
            qb, h, kt = it
            if h == 0 and kt == 0:
                st[("pf", qb)] = self.build_posfb(qb)
                st[("yT", qb)] = self.rot("yT")
            pf, pfk = st[("pf", qb)]
            j = kt - 4 * qb
            c0 = max(j, 0) * 128
            dist, dkk = dcache[kt], ("dcache", kt)
            if h == 0:
                P.act(dist[:, c0:], pf[:, c0:], AF.Abs, r=[pfk], w=[dkk], bias=self.negposk[:, kt:kt + 1])
            sps = []
            for mi in range(2):
                hs = slice(mi * 64, (mi + 1) * 64)
                sp_, spk = self.sbank()
                P.mm(sp_[:, c0:], kcT[hs, h, kt * 128:(kt + 1) * 128], qcT[hs, h, qb * TB + c0:(qb + 1) * TB],
                     r=[("kcT", h, kt // 4), ("qcT", h, qb)], w=[spk])
                sps.append((sp_, spk))
            st[it] = (dist, dkk, sps)

        def stageB(it):
            qb, h, kt = it
            yT, yk = st[("yT", qb)]
            nkt = 4 * qb + 4
            j = kt - 4 * qb
            c0 = max(j, 0) * 128
            dist, dkk, sps = st.pop(it)
            cur_banks = [spk[1] for _, spk in sps]
            lss = None
            pts = []
            for mi in range(2):
                sp_, spk = sps[mi]
                tS, tSk = self.rot("tS")
                P.stt("dve", tS[:, c0:], dist[:, c0:], -slopes[h] / SC_C, sp_[:, c0:], ALU.mult, ALU.add,
                      r=[dkk, spk], w=[tSk])
                pT, pTk = self.rot("pT")
                P.act(pT[:, c0:], tS[:, c0:], AF.Exp, r=[tSk], w=[pTk], scale=SC_C, bias=-SM_SHIFT)
                pts.append((pT, pTk))
            for mi in range(2):
                pT, pTk = pts[mi]
                if j >= 0:
                    P.op("pool", "memset", pT[64:128, c0:c0 + 64], 0.0, w=[pTk])
                ab = mi
                Ob, Ok = self.ps[ab], ("ps", ab)
                P.mm(Ob[:, c0:], vc[:, kt, h * 128:(h + 1) * 128], pT[:, c0:], start=(kt == 0),
                     stop=(kt == nkt - 1), r=[("vc", kt), pTk], w=[Ok])
                lb_, lbk_ = self.ps[2 + mi], ("ps", 2 + mi)
                P.mm(lb_[:, c0:], self.onesB, pT[:, c0:], start=(kt == 0), stop=(kt == nkt - 1), r=[pTk], w=[lbk_])
            self.rot_banks = cur_banks
            self.bi = 0
            tick()
            if kt == nkt - 1:
                ya, yak = self.rot("ya")
                yb, ybk = self.rot("yb")
                sq, sqk = self.rot("sq1")

                def step1(lss=lss, ya=ya, yak=yak, yb=yb, ybk=ybk, sq=sq, sqk=sqk, h=h):
                    for mi, (y_, y_k) in enumerate(((ya, yak), (yb, ybk))):
                        ab = mi
                        r0, r0k = self.rot("rs")
                        self.recip_act(r0, r0k, self.ps[2 + mi], ("ps", 2 + mi))
                        P.tt("dve", y_, self.ps[ab], r0, ALU.mult, r=[("ps", ab), r0k], w=[y_k])
                    P.stt("dve", ya, yb, neglam[:, 0:1], ya, ALU.mult, ALU.add, r=[ybk, yak], w=[yak])
                    P.act(sq, ya, AF.Square, r=[yak], w=[sqk])

                def step2(h=h, ya=ya, yak=yak, sq=sq, sqk=sqk, yT=yT, yk=yk):
                    ss, ssk = self.bank()
                    P.mm(ss, self.onesB, sq, r=[sqk], w=[ssk])
                    rs, rsk = self.rot("rs")
                    P.act(rs, ss, AF.Ln, r=[ssk], w=[rsk], scale=1.0 / 128, bias=EPS)
                    P.act(rs, rs, AF.Exp, r=[rsk], w=[rsk], scale=-0.5)
                    P.stt("dve", yT[:, h, :], ya, gsub, rs, ALU.mult, ALU.mult, r=[yak, rsk], w=[(yk, h)])

                step1()
                defer(1, step2)
                if h == 3:
                    defer(3, lambda qb=qb, yT=yT, yk=yk: self.out_proj_block(wo, "o_wo", yT, yk, 4, qb))

        LOOK2 = 1
        for i in range(min(LOOK2, len(items))):
            stageA(items[i])
        for i in range(len(items)):
            if i + LOOK2 < len(items):
                stageA(items[i + LOOK2])
            stageB(items[i])
        tick(flush=True)
        self.rot_banks = list(range(8))
        self.bi = 0
        self.score_banks = [2, 3, 4, 5]
        self.si = 0
        P.fence()
        A.reset(m1)
        self.rots = {}
        qdT = A.alloc("qdT", [128, 4, T], BF16)
        kdT = A.alloc("kdT", [128, 4, T], BF16)
        vd = A.alloc("vd", [128, NT, 512], BF16)
        wuq = A.alloc("wuq", [128, 2, 384], BF16)
        wukv = A.alloc("wukv", [128, 768], BF16)
        P.dma("pool", wuq, d["od_w_uq"][o].rearrange("(c p) n -> p c n", p=128), w=["wuq"])
        P.dma("pool", wukv, d["od_w_ukv"][o], w=["wukv"])
        m3 = A.mark()
        self.mkrot("posfb", 2, [128, TB], F32)
        self.mkrot("pkm", 2, [NT, 128], F32)
        self.mkrot("ang", 3, [128, TB], F32)
        self.mkrot("angi", 1, [128, TB], I32)
        self.mkrot("sin", 2, [128, TB], F32)
        self.mkrot("cos", 2, [128, TB], F32)
        KO3 = 3
        oslots = []
        for j in range(KO3):
            sd = {"sq": A.alloc("o3_sq", [128, TB], BF16)}
            for nm in ("rs", "kraw", "rtmp", "rtmp2"):
                sd[nm] = A.alloc("o3_" + nm, [128, TB], F32)
            oslots.append(sd)
        self.rot_banks = list(range(8))
        self.bi = 0
        ones96 = self.onesB[0:96, 0:96]

        def qk_chain(tb, h, isk, cosb, cosk, sinb, sink, j):
            sd = oslots[j]
            K_ = lambda nm: ("o3s", j, nm)
            sl = slice(tb * TB, (tb + 1) * TB)
            if not isk:
                dst, dkey, gcol = qdT[:, h, sl], ("qdT", h, tb), cC(6)[0:96, :]
                qp, qpk = self.bank()
                for c in range(2):
                    P.mm(qp[0:96, :], wuq[:, c, h * 96:(h + 1) * 96], qlatT[:, c, sl], start=(c == 0), stop=(c == 1),
                         r=["wuq", ("qlatT", c, tb)], w=[qpk])
                src, srck = qp[0:96, :], [qpk]
            else:
                dst, dkey, gcol = kdT[:, h, sl], ("kdT", h, tb), cC(7)[0:96, :]
                kp, kpk = self.bank()
                P.mm(kp[0:64, :], wukv[:, h * 192:h * 192 + 64], kvlatT[:, sl], r=["wukv", ("kvlatT", tb)], w=[kpk])
                kr = sd["kraw"]
                P.copy("act", kr[0:64, :], kp[0:64, :], r=[kpk], w=[K_("kraw0")])
                P.copy("act", kr[64:96, :], kropeT[64:96, sl], r=[("kropeT", tb)], w=[K_("kraw1")])
                src, srck = kr[0:96, :], [K_("kraw0"), K_("kraw1")]
            P.act(sd["sq"][0:96, :], src, AF.Square, r=srck, w=[K_("sq")])
            yield
            ss, ssk = self.bank()
            P.mm(ss[0:96, :], ones96, sd["sq"][0:96, :], r=[K_("sq")], w=[ssk])
            rs = sd["rs"]
            P.act(rs[0:96, :], ss[0:96, :], AF.Ln, r=[ssk], w=[K_("rs")], scale=1.0 / 96, bias=EPS)
            P.act(rs[0:96, :], rs[0:96, :], AF.Exp, r=[K_("rs")], w=[K_("rs")], scale=-0.5)
            P.stt("dve", dst[0:96, :], src, gcol, rs[0:96, :], ALU.mult, ALU.mult, r=srck + [K_("rs")], w=[dkey])
            yield
            rp, rpk = self.bank()
            P.mm(rp[0:96, :], self.rotTB[0:96, 0:96], dst[0:96, :], r=[dkey], w=[rpk])
            t1, t2 = sd["rtmp"], sd["rtmp2"]
            P.tt("dve", t1[64:96, :], dst[64:96, :], cosb[64:96, :], ALU.mult, r=[dkey, cosk], w=[K_("t1")])
            P.tt("dve", t2[64:96, :], rp[64:96, :], sinb[64:96, :], ALU.mult, r=[rpk, sink], w=[K_("t2")])
            yield
            P.tt("dve", dst[64:96, :], t1[64:96, :], t2[64:96, :], ALU.add, r=[K_("t1"), K_("t2")], w=[dkey])

        for tb in range(NTB):
            tabs = self.rope_tables(tb)
            cosb, cosk = tabs["cos"]
            sinb, sink = tabs["sin"]
            makers = []
            for h in range(4):
                for isk in (False, True):
                    makers.append(lambda j, tb=tb, h=h, isk=isk, cosb=cosb, cosk=cosk, sinb=sinb, sink=sink:
                                  qk_chain(tb, h, isk, cosb, cosk, sinb, sink, j))
            self.run_chains(makers, KO3)
        wv = wukv.rearrange("p (h e) -> p h e", h=4)[:, :, 64:192]
        for tt in range(NT):
            pp, ppk = self.bank()
            P.mm(pp.rearrange("p (h e) -> p h e", h=4), kvlatT[:, tt * 128:(tt + 1) * 128], wv,
                 r=["wukv", ("kvlatT", tt // 4)], w=[ppk])
            P.copy("act", vd[:, tt, :], pp, r=[ppk], w=[("vd", tt)])
        P.fence()
        A.reset(m3)
        self.rots = {}
        wo2 = A.alloc("o_wo2", [128, 4, D], BF16)
        P.dma("pool", wo2, d["od_w_out"][o].rearrange("(h p) n -> p h n", p=128)[:, 4:8, :], w=["o_wo2"])
        self.mkrot("rs", 3, [128, TB], F32)
        self.mkrot("pT", 6, [128, TB], BF16)
        self.mkrot("yT", 2, [128, 4, TB], BF16)
        self.rot_banks = [6, 7]
        self.bi = 0
        if self.cfg.get("pe_l", True):
            self.score_banks = [4, 5, 6, 7]
            self.si = 0
        self.mkrot("lsum", 2, [128, TB], F32)
        for qb in range(NTB):
            yT, yk = self.rot("yT")
            items = [(h, kt) for h in range(4) for kt in range(4 * qb + 4)]
            st = {}

            def stageA(it, qb=qb):
                h, kt = it
                c0 = max(kt - 4 * qb, 0) * 128
                sp_, spk = self.sbank()
                P.mm(sp_[:, c0:], kdT[0:96, h, kt * 128:(kt + 1) * 128], qdT[0:96, h, qb * TB + c0:(qb + 1) * TB],
                     r=[("kdT", h, kt // 4), ("qdT", h, qb)], w=[spk])
                st[it] = (sp_, spk)

            def stageB(it, qb=qb, yT=yT, yk=yk):
                h, kt = it
                nkt = 4 * qb + 4
                j = kt - 4 * qb
                c0 = max(j, 0) * 128
                sp_, spk = st.pop(it)
                if kt == 0:
                    st[("ls", h)] = self.rot("lsum")
                ls, lsk = st[("ls", h)]
                pT, pTk = self.rot("pT")
                P.act(pT[:, c0:], sp_[:, c0:], AF.Exp, r=[spk], w=[pTk], scale=SC_D, bias=-SM_SHIFT)
                if j >= 0:
                    P.op("dve", "memset", pT[64:128, c0:c0 + 64], 0.0, w=[pTk])
                Ob, Ok = self.ps[h % 2], ("ps", h % 2)
                P.mm(Ob[:, c0:], vd[:, kt, h * 128:(h + 1) * 128], pT[:, c0:], start=(kt == 0), stop=(kt == nkt - 1),
                     r=[("vd", kt), pTk], w=[Ok])
                if self.cfg.get("pe_l", True):
                    lb_, lbk_ = self.ps[2 + h % 2], ("ps", 2 + h % 2)
                    P.mm(lb_[:, c0:], self.onesB, pT[:, c0:], start=(kt == 0), stop=(kt == nkt - 1), r=[pTk], w=[lbk_])
                    if kt == nkt - 1:
                        r0, r0k = self.rot("rs")
                        self.recip_act(r0, r0k, lb_, lbk_)
                        P.tt("dve", yT[:, h, :], Ob, r0, ALU.mult, r=[Ok, r0k], w=[(yk, h)])
                        del st[("ls", h)]
                else:
                    le = "dve" if kt % 3 else "pool"
                    if kt == 0:
                        P.copy(le, ls, pT, r=[pTk], w=[lsk])
                    else:
                        P.tt(le, ls[:, c0:], ls[:, c0:], pT[:, c0:], ALU.add, r=[pTk, lsk], w=[lsk])
                    if kt == nkt - 1:
                        lp, lpk = self.bank()
                        P.mm(lp, self.onesF, ls, r=[lsk], w=[lpk])
                        r0, r0k = self.rot("rs")
                        self.recip_act(r0, r0k, lp, lpk)
                        P.tt("dve", yT[:, h, :], Ob, r0, ALU.mult, r=[Ok, r0k], w=[(yk, h)])
                        del st[("ls", h)]

            LOOK = 3
            for i in range(min(LOOK, len(items))):
                stageA(items[i])
            for i in range(len(items)):
                if i + LOOK < len(items):
                    stageA(items[i + LOOK])
                stageB(items[i])
            self.out_proj_block(wo2, "o_wo2", yT, yk, 4, qb)
        self.rot_banks = list(range(8))
        self.bi = 0
        P.fence()
        A.reset(m0)

    def build(self):
        cfg = self.cfg
        self.setup()
        for l in range(cfg.get("layers", DEPTH)):
            if cfg.get("mixer", True):
                if l % 2 == 0:
                    if not cfg.get("skip_even"):
                        self.even_mixer(l // 2, l)
                elif not cfg.get("skip_odd"):
                    self.odd_mixer(l // 2, l)
            if cfg.get("xattn", True):
                self.xattn(l)
            if cfg.get("ffn", True):
                self.ffn(l)
        self.store()
        self.P.emit()


def build_nc(cfg=None):
    nc = bass.Bass("TRN2", target_bir_lowering=False)
    b = Builder(nc, cfg or {})
    b.build()
    return nc, b


def make_in_maps(inputs, n):
    consts = host_consts()
    maps = []
    for i in range(n):
        mp = {
            "x": np.ascontiguousarray(inputs["x"][i]),
            "mem": np.ascontiguousarray(inputs["mem"][i]),
            "positions": np.ascontiguousarray(inputs["positions"][i:i + 1]),
        }
        for name, _ in WEIGHTS:
            mp[name] = np.ascontiguousarray(inputs[name])
        mp.update(consts)
        maps.append(mp)
    return maps


def kernel(**inputs):
    inputs = {k: np.asarray(v) for k, v in inputs.items()}
    n = inputs["x"].shape[0]
    nc, _ = build_nc({})
    in_maps = make_in_maps(inputs, n)
    res = run_bass_kernel_spmd(nc, in_maps, core_ids=list(range(n)))
    return np.stack([np.asarray(r["y"]) for r in res.results], axis=0).astype(np.float32)
```

```python
from contextlib import ExitStack
import math
import numpy as np
import concourse.bass as bass
import concourse.mybir as mybir
from concourse.bass_utils import run_bass_kernel_spmd

F32 = mybir.dt.float32
BF16 = mybir.dt.bfloat16
F16 = mybir.dt.float16
I32 = mybir.dt.int32
AF = mybir.ActivationFunctionType
ALU = mybir.AluOpType
AX = mybir.AxisListType

EPOCH = 30000
STRICT_SAME_ENGINE = True
NSLOT = 8
ENGS = ("pe", "act", "dve", "pool", "sp")


class Prog:
    def __init__(self, nc):
        self.nc = nc
        self.ops = {e: [] for e in ENGS}
        self.ncomp = {e: 0 for e in ENGS}
        self.ndma = {e: 0 for e in ENGS}
        self.last_w = {}
        self.readers = {}
        self.waited = {e: {} for e in ENGS}
        self.semkeys = set()
        self.sems = {}
        self.last_tok = {}
        self.gdep = None

    def add(self, eng, fn, r=(), w=(), dma=False, nofence=False):
        nowait_only = fn is None
        raw = {}
        oth = {}
        if eng != "pe" and not nowait_only:
            locks = [("pslock", k[1]) for k in r if isinstance(k, tuple) and len(k) == 2 and k[0] == "ps"]
            if locks:
                w = list(w) + locks

        def put(d, tok):
            sk, v, e2, d2 = tok
            if d.get(sk, (0,))[0] < v:
                d[sk] = (v, e2, d2)

        for k in r:
            t = self.last_w.get(k)
            if t is not None:
                put(raw, t)
        for k in w:
            t = self.last_w.get(k)
            if t is not None:
                put(oth, t)
            for sk, (v, e2, d2) in self.readers.get(k, {}).items():
                put(oth, (sk, v, e2, d2))
        if self.gdep is not None and not nofence:
            put(raw, self.gdep)
        if nowait_only:
            semkey, val = None, 0
        elif dma:
            i = self.ndma[eng]
            self.ndma[eng] += 1
            slot, rnd = i % NSLOT, i // NSLOT
            semkey = ("d", eng, slot)
            val = 16 * (rnd + 1)
            if rnd > 0:
                put(raw, (semkey, 16 * rnd, eng, True))
        else:
            i = self.ncomp[eng]
            self.ncomp[eng] += 1
            semkey = ("c", eng, i // EPOCH)
            val = i % EPOCH + 1
        tok = (semkey, val, eng, dma)
        waits = []
        wd = self.waited[eng]
        for d, is_raw in ((raw, True), (oth, False)):
            for sk, (v, e2, d2) in d.items():
                if not d2 and e2 == eng:
                    if eng == "pe" or (not is_raw and not STRICT_SAME_ENGINE):
                        continue
                if wd.get(sk, 0) >= v:
                    continue
                wd[sk] = v
                waits.append((sk, v))
        if nowait_only:
            self.ops[eng].append((None, waits, None, False))
            return None
        self.semkeys.add(semkey)
        self.last_tok[semkey] = tok
        for k in w:
            self.last_w[k] = tok
            self.readers[k] = {}
        for k in r:
            d = self.readers.setdefault(k, {})
            if d.get(semkey, (0,))[0] < val:
                d[semkey] = (val, eng, dma)
        self.ops[eng].append((fn, waits, semkey, dma))
        return tok

    def op(self, eng, name, *args, r=(), w=(), **kw):
        return self.add(eng, lambda e: getattr(e, name)(*args, **kw), r, w)

    def mm(self, out, lhsT, rhs, start=True, stop=True, r=(), w=(), **kw):
        return self.add("pe", lambda e: e.matmul(out, lhsT, rhs, start=start, stop=stop, **kw), r, w)

    def tr(self, out, in_, ident, r=(), w=()):
        return self.add("pe", lambda e: e.transpose(out, in_, ident), r, w)

    def act(self, out, in_, func, r=(), w=(), **kw):
        return self.add("act", lambda e: e.activation(out, in_, func, **kw), r, w)

    def ts(self, eng, out, in0, s1, s2, op0, op1=None, r=(), w=()):
        if op1 is None:
            return self.add(eng, lambda e: e.tensor_scalar(out, in0, s1, None, op0), r, w)
        return self.add(eng, lambda e: e.tensor_scalar(out, in0, s1, s2, op0, op1), r, w)

    def tt(self, eng, out, in0, in1, op, r=(), w=()):
        return self.add(eng, lambda e: e.tensor_tensor(out, in0, in1, op), r, w)

    def stt(self, eng, out, in0, scalar, in1, op0, op1, r=(), w=()):
        return self.add(eng, lambda e: e.scalar_tensor_tensor(out, in0, scalar, in1, op0, op1), r, w)

    def copy(self, eng, out, in_, r=(), w=()):
        if eng == "act":
            return self.add(eng, lambda e: e.copy(out, in_), r, w)
        return self.add(eng, lambda e: e.tensor_copy(out, in_), r, w)

    def dma(self, eng, out, in_, r=(), w=(), nofence=False, **kw):
        return self.add(eng, lambda e: e.dma_start(out, in_, **kw), r, w, dma=True, nofence=nofence)

    def fence(self):
        keys = []
        for sk, tok in list(self.last_tok.items()):
            k = ("_fence", sk)
            self.last_w[k] = tok
            self.readers[k] = {}
            keys.append(k)
        self.gdep = None
        tok = self.add("sp", lambda e: e.nop(), r=keys, w=[])
        self.gdep = tok

    def finish(self, eng, keys):
        self.add(eng, None, r=keys, w=())

    def emit(self):
        nc = self.nc
        with ExitStack() as st:
            for sk in sorted(self.semkeys, key=str):
                self.sems[sk] = st.enter_context(nc.semaphore("s_%s_%s_%d" % sk))
            with nc.Block() as block:
                def mk(name):
                    def body(e):
                        for fn, waits, semkey, dma in self.ops[name]:
                            for sk, v in waits:
                                e.wait_ge(self.sems[sk], v)
                            if fn is None:
                                continue
                            ins = fn(e)
                            ins.then_inc(self.sems[semkey], 16 if dma else 1)
                    return body
                block.tensor(mk("pe"))
                block.scalar(mk("act"))
                block.vector(mk("dve"))
                block.gpsimd(mk("pool"))
                block.sync(mk("sp"))


T = 2048
D = 1024
TB = 512
NTB = 4
NT = 16
DC = 8
FF = 2816
FC = 22
NMEM = 256
EPS = 1e-6
SB_BASE = 16512
SB_END = 229344
DEPTH = 4
SM_SHIFT = 10.0

WEIGHTS = [
    ("norm_mix", [4, 1024]), ("norm_x", [4, 1024]), ("norm_mem", [4, 1024]),
    ("x_wq", [4, 1024, 512]), ("x_wkv", [4, 1024, 1024]), ("x_q_norm", [4, 128]), ("x_k_norm", [4, 128]),
    ("x_wo", [4, 512, 1024]), ("norm_ffn", [4, 1024]), ("ffn_w_in", [4, 1024, 5632]),
    ("ffn_w_out", [4, 2816, 1024]),
    ("ev_w_in", [2, 1024, 3080]), ("ev_conv_qkv", [2, 4, 1536]), ("ev_a_log", [2, 4]), ("ev_dt_bias", [2, 4]),
    ("ev_o_norm", [2, 128]), ("ev_conv_b_w", [2, 4, 512]), ("ev_conv_b_b", [2, 512]),
    ("ev_gate_a_w", [2, 8, 64, 64]), ("ev_gate_a_b", [2, 512]), ("ev_gate_x_w", [2, 8, 64, 64]),
    ("ev_gate_x_b", [2, 512]), ("ev_lru_l", [2, 512]), ("ev_w_out", [2, 1024, 1024]),
    ("od_w_in", [2, 1024, 1952]), ("od_c_q_norm", [2, 64]), ("od_c_k_norm", [2, 64]),
    ("od_lam_q1", [2, 64]), ("od_lam_k1", [2, 64]), ("od_lam_q2", [2, 64]), ("od_lam_k2", [2, 64]),
    ("od_c_sub_norm", [2, 128]), ("od_q_lat_norm", [2, 256]), ("od_w_uq", [2, 256, 384]),
    ("od_kv_lat_norm", [2, 128]), ("od_w_ukv", [2, 128, 768]), ("od_d_q_norm", [2, 96]),
    ("od_d_k_norm", [2, 96]), ("od_w_out", [2, 1024, 1024]),
]


def host_consts():
    c = {}
    c["c_ident"] = np.eye(128, dtype=np.float32)
    rt = np.zeros((128, 128), np.float32)
    for i in range(16):
        rt[80 + i, 64 + i] = -1.0
        rt[64 + i, 80 + i] = 1.0
    c["c_rotT"] = rt
    fr = np.zeros((128, 2), np.float32)
    for i in range(16):
        f = 10000.0 ** (-(i / 16.0))
        fr[64 + i, 0] = fr[80 + i, 0] = np.float32(f)
    fr[:, 1] = fr[:, 0] / np.float32(2 * np.pi)
    c["c_freq"] = fr
    bo = np.zeros((128, 128), np.float32)
    bo[0:64, 0:64] = 1.0
    bo[64:128, 64:128] = 1.0
    c["c_blockones"] = bo
    ii = np.arange(128)
    same = (ii[:, None] // 64) == (ii[None, :] // 64)
    c["c_maskSL"] = (same & (ii[None, :] < ii[:, None])).astype(np.float32)
    c["c_maskIU"] = (same & (ii[None, :] >= ii[:, None])).astype(np.float32)
    c["c_lsel"] = (ii[:, None] == (ii[None, :] // 64) * 64 + 63).astype(np.float32)
    return c


class Arena:
    def __init__(self, nc, base, end):
        self.nc, self.p, self.end, self.n = nc, base, end, 0

    def alloc(self, name, shape, dtype):
        esz = 4 if dtype in (F32, I32) else 2
        nbytes = int(np.prod(shape[1:])) * esz
        off = (self.p + 31) // 32 * 32
        self.p = off + nbytes
        assert self.p <= self.end, ("SBUF overflow", name, self.p, self.end)
        self.n += 1
        return self.nc.alloc_sbuf_tensor_at("%s_%d" % (name, self.n), list(shape), dtype, offset=off).ap()

    def mark(self):
        return self.p

    def reset(self, m):
        self.p = m


class Builder:
    def __init__(self, nc, cfg):
        self.nc = nc
        self.cfg = cfg
        self.P = Prog(nc)
        self.d = {}
        P = self.P
        d = self.d
        d["x"] = nc.dram_tensor("x", [T, D], F32, kind="ExternalInput").ap()
        d["mem"] = nc.dram_tensor("mem", [NMEM, D], F32, kind="ExternalInput").ap()
        d["positions"] = nc.dram_tensor("positions", [1, T], I32, kind="ExternalInput").ap()
        for name, shp in WEIGHTS:
            d[name] = nc.dram_tensor(name, shp, F32, kind="ExternalInput").ap()
        for name, arr in host_consts().items():
            d[name] = nc.dram_tensor(name, list(arr.shape), F32, kind="ExternalInput").ap()
        d["y"] = nc.dram_tensor("y", [T, D], F32, kind="ExternalOutput").ap()
        self.A = Arena(nc, SB_BASE, SB_END)
        A = self.A
        self.ps = [nc.alloc_psum_tensor("psb%d" % i, [128, 512], F32).ap() for i in range(8)]
        self.bi = 0
        self.rot_banks = list(range(8))
        self.misc_banks = [6, 7]
        self.score_banks = [2, 3, 4, 5]
        self.mi = 0
        self.si = 0
        self.rots = {}
        self.xT = A.alloc("xT", [128, DC, T], F32)
        self.identF = A.alloc("identF", [128, 128], F32)
        self.identB = A.alloc("identB", [128, 128], BF16)
        self.onesB = A.alloc("onesB", [128, 128], BF16)
        self.onesF = A.alloc("onesF", [128, 128], F32)
        self.colsA = A.alloc("colsA", [128, 128], F32)
        self.colsB = A.alloc("colsB", [128, 128], F32)
        self.colsC = A.alloc("colsC", [128, 128], F32)
        self.blockB = A.alloc("blockB", [128, 128], BF16)
        self.rotTB = A.alloc("rotTB", [128, 128], BF16)
        self.freq = A.alloc("freq", [128, 2], F32)
        self.posk = A.alloc("posk", [128, NT], F32)
        self.negposk = A.alloc("negposk", [128, NT], F32)
        self.pkf = A.alloc("pkf", [NT, 128], F32)
        self.colsD = A.alloc("colsD", [128, 256], F32)
        self.maskSL = A.alloc("maskSL", [128, 128], F32)
        self.maskIU = A.alloc("maskIU", [128, 128], F32)
        self.lsel = A.alloc("lsel", [128, 128], F32)
        self.memTn = A.alloc("memTn", [128, DC, NMEM], F32)
        self.kTx = A.alloc("kTx", [128, 4, NMEM], BF16)
        self.vx = A.alloc("vx", [128, 2, 512], BF16)
        self.phase_base = A.mark()

    def bank(self):
        i = self.rot_banks[self.bi % len(self.rot_banks)]
        self.bi += 1
        return self.ps[i], ("ps", i)

    def mkrot(self, name, n, shape, dtype):
        self.rots[name] = [[self.A.alloc(name, shape, dtype) for _ in range(n)], 0]

    def rot(self, name):
        lst, i = self.rots[name]
        self.rots[name][1] = (i + 1) % len(lst)
        return lst[i], (name, i)

    def gcol(self, which, l, c):
        j = which * 32 + l * 8 + c
        return self.colsA[:, j:j + 1]

    def setup(self):
        P, d, A = self.P, self.d, self.A
        P.dma("sp", self.identF, d["c_ident"], w=["identF"])
        P.copy("dve", self.identB, self.identF, r=["identF"], w=["identB"])
        P.op("dve", "memset", self.onesB, 1.0, w=["onesB"])
        P.op("dve", "memset", self.onesF, 1.0, w=["onesF"])
        m = A.mark()
        stg = A.alloc("stg", [128, 128], F32)
        for i, nm in enumerate(["norm_mix", "norm_x", "norm_ffn", "norm_mem"]):
            P.dma("sp", stg[i * 32:(i + 1) * 32, :], d[nm].rearrange("l (c p) -> (l c) p", p=128), w=[("stg", i)])
        pb, pk = self.bank()
        P.tr(pb[:, 0:128], stg, self.identF, r=[("stg", i) for i in range(4)] + ["identF"], w=[pk])
        P.copy("dve", self.colsA, pb[:, 0:128], r=[pk], w=["colsA"])
        stg2 = A.alloc("stg2", [128, 128], F32)
        P.op("dve", "memset", stg2, 0.0, w=["stg2"])
        P.dma("sp", stg2[0:4, :], d["x_q_norm"], r=[], w=["stg2"])
        P.dma("sp", stg2[4:8, :], d["x_k_norm"], r=["stg2"], w=["stg2b"])
        pb, pk = self.bank()
        P.tr(pb[:, 0:128], stg2, self.identF, r=["stg2", "stg2b", "identF"], w=[pk])
        P.copy("dve", self.colsB, pb[:, 0:128], r=[pk], w=["colsB"])
        stg3 = A.alloc("stg3", [128, 128], F32)
        P.op("dve", "memset", stg3, 0.0, w=["stg3z"])
        k3 = []
        def ld3(row, c0, src):
            k = ("stg3", len(k3))
            k3.append(k)
            P.dma("sp", stg3[row:row + 1, c0:c0 + src.shape[1]], src, r=["stg3z"], w=[k])
        for o in range(2):
            for half in range(2):
                ld3(o * 8 + 0, half * 64, d["od_c_q_norm"][o:o + 1, :])
                ld3(o * 8 + 1, half * 64, d["od_c_k_norm"][o:o + 1, :])
            ld3(o * 8 + 2, 0, d["od_c_sub_norm"][o:o + 1, :])
            ld3(o * 8 + 3, 0, d["od_q_lat_norm"][o:o + 1, 0:128])
            ld3(o * 8 + 4, 0, d["od_q_lat_norm"][o:o + 1, 128:256])
            ld3(o * 8 + 5, 0, d["od_kv_lat_norm"][o:o + 1, :])
            ld3(o * 8 + 6, 0, d["od_d_q_norm"][o:o + 1, :])
            ld3(o * 8 + 7, 0, d["od_d_k_norm"][o:o + 1, :])
        pb, pk = self.bank()
        P.tr(pb[:, 0:128], stg3, self.identF, r=k3 + ["identF"], w=[pk])
        P.copy("dve", self.colsC, pb[:, 0:128], r=[pk], w=["colsC"])
        P.dma("sp", self.maskSL, d["c_maskSL"], w=["maskSL"])
        P.dma("sp", self.maskIU, d["c_maskIU"], w=["maskIU"])
        P.dma("sp", self.lsel, d["c_lsel"], w=["lsel"])
        for e in range(2):
            st4 = A.alloc("stg4", [128, 128], F32)
            P.op("dve", "memset", st4, 0.0, w=[("st4z", e)])
            k4 = []
            def ld4(r0, src):
                k = ("stg4", e, len(k4))
                k4.append(k)
                P.dma("sp", st4[r0:r0 + src.shape[0], :], src, r=[("st4z", e)], w=[k])
            ld4(0, d["ev_conv_qkv"][e].rearrange("j (c p) -> (j c) p", p=128))
            ld4(48, d["ev_conv_b_w"][e].rearrange("j (c p) -> (j c) p", p=128))
            ld4(64, d["ev_conv_b_b"][e:e + 1, :].rearrange("o (c p) -> (o c) p", p=128))
            ld4(68, d["ev_gate_a_b"][e:e + 1, :].rearrange("o (c p) -> (o c) p", p=128))
            ld4(72, d["ev_gate_x_b"][e:e + 1, :].rearrange("o (c p) -> (o c) p", p=128))
            ld4(76, d["ev_lru_l"][e:e + 1, :].rearrange("o (c p) -> (o c) p", p=128))
            ld4(80, d["ev_o_norm"][e:e + 1, :])
            pb, pk = self.bank()
            P.tr(pb[:, 0:128], st4, self.identF, r=k4 + ["identF"], w=[pk])
            P.copy("dve", self.colsD[:, e * 128:(e + 1) * 128], pb[:, 0:128], r=[pk], w=[("colsD", e)])
        cst = A.alloc("cst", [128, 128], F32)
        P.dma("sp", cst, d["c_blockones"], w=["cst"])
        P.copy("dve", self.blockB, cst, r=["cst"], w=["blockB"])
        cst2 = A.alloc("cst2", [128, 128], F32)
        P.dma("sp", cst2, d["c_rotT"], w=["cst2"])
        P.copy("dve", self.rotTB, cst2, r=["cst2"], w=["rotTB"])
        P.dma("sp", self.freq, d["c_freq"], w=["freq"])
        pk_i = A.alloc("pk_i", [NT, 128], I32)
        pk_f = self.pkf
        P.dma("sp", pk_i, d["positions"].rearrange("o (t p) -> (o t) p", p=128), w=["pk_i"])
        P.copy("dve", pk_f, pk_i, r=["pk_i"], w=["pk_f"])
        pb, pk = self.bank()
        P.tr(pb[:, 0:NT], pk_f, self.identF[0:NT, 0:NT], r=["pk_f", "identF"], w=[pk])
        P.copy("dve", self.posk, pb[:, 0:NT], r=[pk], w=["posk"])
        P.ts("dve", self.negposk, self.posk, -1.0, None, ALU.mult, r=["posk"], w=["negposk"])
        xin = [A.alloc("xin", [128, D], F32) for _ in range(2)]
        for tt in range(NT):
            xb = xin[tt % 2]
            xk = ("xin", tt % 2)
            P.dma("sp", xb, d["x"][tt * 128:(tt + 1) * 128, :], w=[xk])
            for hb in range(2):
                pb, pk = self.bank()
                for q in range(4):
                    c = hb * 4 + q
                    P.tr(pb[:, q * 128:(q + 1) * 128], xb[:, c * 128:(c + 1) * 128], self.identF,
                         r=[xk, "identF"], w=[pk])
                eng = "dve" if hb == 0 else "act"
                P.copy(eng, self.xT[:, hb * 4:(hb + 1) * 4, tt * 128:(tt + 1) * 128],
                       pb.rearrange("p (a b) -> p a b", a=4),
                       r=[pk], w=[("xT", c, tt // 4) for c in range(hb * 4, hb * 4 + 4)])
        mm_ = [A.alloc("memin", [128, D], F32) for _ in range(2)]
        msq = A.alloc("msq", [128, D], F32)
        mss = A.alloc("mss", [128, 2], F32)
        for mt in range(2):
            P.dma("sp", mm_[mt], d["mem"][mt * 128:(mt + 1) * 128, :], w=[("memin", mt)])
            P.act(msq, mm_[mt], AF.Square, r=[("memin", mt)], w=["msq"], accum_out=mss[:, mt:mt + 1])
            P.act(mss[:, mt:mt + 1], mss[:, mt:mt + 1], AF.Sqrt, r=["msq"], w=[("mss", mt)], scale=1.0 / D, bias=EPS)
            P.op("dve", "reciprocal", mss[:, mt:mt + 1], mss[:, mt:mt + 1], r=[("mss", mt)], w=[("mss", mt)])
            P.ts("dve", mm_[mt], mm_[mt], mss[:, mt:mt + 1], None, ALU.mult, r=[("memin", mt), ("mss", mt)],
                 w=[("memin", mt)])
            for hb in range(2):
                pb, pk = self.bank()
                for q in range(4):
                    c = hb * 4 + q
                    P.tr(pb[:, q * 128:(q + 1) * 128], mm_[mt][:, c * 128:(c + 1) * 128], self.identF,
                         r=[("memin", mt), "identF"], w=[pk])
                P.copy("dve", self.memTn[:, hb * 4:(hb + 1) * 4, mt * 128:(mt + 1) * 128],
                       pb.rearrange("p (a b) -> p a b", a=4), r=[pk], w=["memTn"])
        P.fence()
        A.reset(m)

    def norm_block(self, tb, which, l, hT_out, hkey):
        P = self.P
        sl = slice(tb * TB, (tb + 1) * TB)
        sq, sqk = self.rot("sq")
        P.act(sq, self.xT[:, :, sl], AF.Square, r=[("xT", c, tb) for c in range(DC)], w=[sqk])
        ss, ssk = self.bank()
        for c in range(DC):
            P.mm(ss, self.onesB, sq[:, c, :], start=(c == 0), stop=(c == DC - 1), r=[sqk, "onesB"], w=[ssk])
        rs, rsk = self.rot("rs")
        P.act(rs, ss, AF.Ln, r=[ssk], w=[rsk], scale=1.0 / D, bias=EPS)
        P.act(rs, rs, AF.Exp, r=[rsk], w=[rsk], scale=-0.5)
        for c in range(DC):
            P.stt("dve", hT_out[:, c, :], self.xT[:, c, sl], self.gcol(which, l, c), rs, ALU.mult, ALU.mult,
                  r=[("xT", c, tb), rsk, "colsA"], w=[(hkey, c)])

    def pnorm(self, src, srck, npart, ones_l, inv_n, gain, out, outk):
        P = self.P
        srcks = srck if isinstance(srck, list) else [srck]
        sq, sqk = self.rot("sq1")
        P.act(sq[0:npart, :], src, AF.Square, r=srcks, w=[sqk])
        ss, ssk = self.bank()
        P.mm(ss[0:npart, :], ones_l, sq[0:npart, :], r=[sqk, "onesB"], w=[ssk])
        rs, rsk = self.rot("rs")
        P.act(rs[0:npart, :], ss[0:npart, :], AF.Ln, r=[ssk], w=[rsk], scale=inv_n, bias=EPS)
        P.act(rs[0:npart, :], rs[0:npart, :], AF.Exp, r=[rsk], w=[rsk], scale=-0.5)
        if gain is None:
            P.tt("dve", out, src, rs[0:npart, :], ALU.mult, r=srcks + [rsk], w=[outk])
        elif isinstance(gain, float):
            P.stt("dve", out, src, gain, rs[0:npart, :], ALU.mult, ALU.mult, r=srcks + [rsk], w=[outk])
        else:
            P.stt("dve", out, src, gain, rs[0:npart, :], ALU.mult, ALU.mult, r=srcks + [rsk], w=[outk])

    def pnorm_pipe(self, blocks):
        P = self.P
        prev = None

        def part_b(stt_):
            (src, srck, npart, ones_l, inv_n, gain, out, outk), sq, sqk = stt_
            srcks = srck if isinstance(srck, list) else [srck]
            ss, ssk = self.bank()
            P.mm(ss[0:npart, :], ones_l, sq[0:npart, :], r=[sqk], w=[ssk])
            rs, rsk = self.rot("rs")
            P.act(rs[0:npart, :], ss[0:npart, :], AF.Ln, r=[ssk], w=[rsk], scale=inv_n, bias=EPS)
            P.act(rs[0:npart, :], rs[0:npart, :], AF.Exp, r=[rsk], w=[rsk], scale=-0.5)
            if gain is None:
                P.tt("dve", out, src, rs[0:npart, :], ALU.mult, r=srcks + [rsk], w=[outk])
            else:
                P.stt("dve", out, src, gain, rs[0:npart, :], ALU.mult, ALU.mult, r=srcks + [rsk], w=[outk])

        for b in blocks:
            src, srck, npart = b[0], b[1], b[2]
            srcks = srck if isinstance(srck, list) else [srck]
            sq, sqk = self.rot("sq1")
            P.act(sq[0:npart, :], src, AF.Square, r=srcks, w=[sqk])
            if prev is not None:
                part_b(prev)
            prev = (b, sq, sqk)
        if prev is not None:
            part_b(prev)

    def run_chains(self, makers, K):
        pending = list(makers)
        active = []
        free = list(range(K))
        while pending or active:
            while pending and free:
                j = free.pop(0)
                active.append((pending.pop(0)(j), j))
            nxt = []
            for g, j in active:
                try:
                    next(g)
                    nxt.append((g, j))
                except StopIteration:
                    free.append(j)
            active = nxt

    def recip_act(self, out, outk, src, srck):
        P = self.P
        P.act(out, src, AF.Ln, r=[srck], w=[outk])
        P.act(out, out, AF.Exp, r=[outk], w=[outk], scale=-1.0)

    def mbank(self):
        i = self.misc_banks[self.mi % len(self.misc_banks)]
        self.mi += 1
        return self.ps[i], ("ps", i)

    def sbank(self):
        i = self.score_banks[self.si % len(self.score_banks)]
        self.si += 1
        return self.ps[i], ("ps", i)

    def ffn(self, l):
        P, d, A = self.P, self.d, self.A
        m = A.mark()
        self.rots = {}
        hT = A.alloc("ffn_hT", [128, DC, 1024], BF16)
        act = A.alloc("ffn_act", [128, FC, 1024], BF16)
        self.mkrot("win", 2, [128, DC, 1024], BF16)
        self.mkrot("wout", 2, [128, FC, 128], BF16)
        self.mkrot("sq", 1, [128, DC, TB], BF16)
        self.mkrot("rs", 2, [128, TB], F32)
        self.mkrot("sg", 2, [128, TB], F32)
        w_in_d = d["ffn_w_in"][l].rearrange("(c p) n -> p c n", p=128)
        w_out_d = d["ffn_w_out"][l].rearrange("(f p) n -> p f n", p=128)
        SLW = 512
        nsl = (FF + SLW - 1) // SLW
        for half in range(2):
            if half == 0:
                for j in range(2):
                    self.norm_block(j, 2, l, hT[:, :, j * TB:(j + 1) * TB], ("ffn_hT", j))
            for s in range(nsl):
                c0 = s * SLW
                ncol = min(SLW, FF - c0)
                wb, wk = self.rot("win")
                P.dma("pool", wb[:, :, 0:ncol], w_in_d[:, :, c0:c0 + ncol], w=[(wk, "g")])
                P.dma("pool", wb[:, :, SLW:SLW + ncol], w_in_d[:, :, FF + c0:FF + c0 + ncol], w=[(wk, "u")])
                for fi in range(ncol // 128):
                    f = (c0 // 128) + fi
                    for j in range(2):
                        gps, gk = self.bank()
                        ups, uk = self.bank()
                        for c in range(DC):
                            P.mm(gps, wb[:, c, fi * 128:(fi + 1) * 128], hT[:, c, j * TB:(j + 1) * TB],
                                 start=(c == 0), stop=(c == DC - 1), r=[(wk, "g"), (("ffn_hT", j), c)], w=[gk])
                        for c in range(DC):
                            P.mm(ups, wb[:, c, SLW + fi * 128:SLW + (fi + 1) * 128], hT[:, c, j * TB:(j + 1) * TB],
                                 start=(c == 0), stop=(c == DC - 1), r=[(wk, "u"), (("ffn_hT", j), c)], w=[uk])
                        sg, sgk = self.rot("sg")
                        P.act(sg, gps, AF.Silu, r=[gk], w=[sgk])
                        P.tt("dve", act[:, f, j * TB:(j + 1) * TB], sg, ups, ALU.mult, r=[sgk, uk], w=[("act", f, j)])
            if half == 0:
                for j in range(2):
                    self.norm_block(2 + j, 2, l, hT[:, :, j * TB:(j + 1) * TB], ("ffn_hT", j))
            for dc in range(DC):
                wo, wok = self.rot("wout")
                P.dma("pool", wo, w_out_d[:, :, dc * 128:(dc + 1) * 128], w=[wok])
                for j in range(2):
                    tb = half * 2 + j
                    sl = slice(tb * TB, (tb + 1) * TB)
                    yps, yk = self.bank()
                    for f in range(FC):
                        P.mm(yps, wo[:, f, :], act[:, f, j * TB:(j + 1) * TB], start=(f == 0), stop=(f == FC - 1),
                             r=[wok, ("act", f, j)], w=[yk])
                    P.tt("dve", self.xT[:, dc, sl], self.xT[:, dc, sl], yps, ALU.add, r=[("xT", dc, tb), yk],
                         w=[("xT", dc, tb)])
        P.fence()
        A.reset(m)

    def out_proj_block(self, wo, wok, oT, ok, nk, tb):
        P = self.P
        sl = slice(tb * TB, (tb + 1) * TB)
        for dc in range(DC):
            yps, yk = self.bank()
            for h in range(nk):
                P.mm(yps, wo[:, h, dc * 128:(dc + 1) * 128], oT[:, h, :], start=(h == 0), stop=(h == nk - 1),
                     r=[wok, (ok, h)], w=[yk])
            P.tt("dve", self.xT[:, dc, sl], self.xT[:, dc, sl], yps, ALU.add, r=[("xT", dc, tb), yk],
                 w=[("xT", dc, tb)])

    def xattn(self, l):
        P, d, A = self.P, self.d, self.A
        m = A.mark()
        self.rots = {}
        wq = A.alloc("x_wq", [128, DC, 512], BF16)
        wo = A.alloc("x_wo", [128, 4, D], BF16)
        wkv = A.alloc("x_wkv", [128, DC, D], BF16)
        mh = A.alloc("x_mh", [128, DC, NMEM], BF16)
        hT = A.alloc("x_hT", [128, DC, T], BF16)
        qall = A.alloc("x_q", [128, 4, T], BF16)
        oall = A.alloc("x_o", [128, 4, T], BF16)
        self.mkrot("sq", 1, [128, DC, TB], BF16)
        self.mkrot("sq1", 3, [128, TB], BF16)
        self.mkrot("rs", 3, [128, TB], F32)
        self.mkrot("pT", 6, [128, TB], BF16)
        P.dma("pool", wq, d["x_wq"][l].rearrange("(c p) n -> p c n", p=128), w=["x_wq"])
        for hh in range(2):
            P.dma("pool", wkv[:, :, hh * 512:(hh + 1) * 512],
                  d["x_wkv"][l].rearrange("(c p) n -> p c n", p=128)[:, :, hh * 512:(hh + 1) * 512], w=[("x_wkv", hh)])
        P.dma("pool", wo, d["x_wo"][l].rearrange("(h p) n -> p h n", p=128), w=["x_wo"])
        self.rot_banks = list(range(8))
        self.bi = 0
        for tb in range(NTB):
            self.norm_block(tb, 1, l, hT[:, :, tb * TB:(tb + 1) * TB], ("x_hT", tb))
        prev = None

        def q_b(stt_):
            qp, qk, sq, sqk, h, tb = stt_
            ss, ssk = self.bank()
            P.mm(ss, self.onesB, sq, r=[sqk], w=[ssk])
            rs, rsk = self.rot("rs")
            P.act(rs, ss, AF.Ln, r=[ssk], w=[rsk], scale=1.0 / 128, bias=EPS)
            P.act(rs, rs, AF.Exp, r=[rsk], w=[rsk], scale=-0.5)
            P.stt("dve", qall[:, h, tb * TB:(tb + 1) * TB], qp, self.colsB[:, l:l + 1], rs, ALU.mult, ALU.mult,
                  r=[qk, rsk], w=[("x_q", h, tb)])

        for tb in range(NTB):
            sl = slice(tb * TB, (tb + 1) * TB)
            for h in range(4):
                qp, qk = self.bank()
                for c in range(DC):
                    P.mm(qp, wq[:, c, h * 128:(h + 1) * 128], hT[:, c, sl], start=(c == 0), stop=(c == DC - 1),
                         r=["x_wq", (("x_hT", tb), c)], w=[qk])
                sq, sqk = self.rot("sq1")
                P.act(sq, qp, AF.Square, r=[qk], w=[sqk])
                if prev is not None:
                    q_b(prev)
                prev = (qp, qk, sq, sqk, h, tb)
        for c in range(DC):
            P.ts("dve", mh[:, c, :], self.memTn[:, c, :], self.gcol(3, l, c), None, ALU.mult, w=[("x_mh", c)])
        q_b(prev)
        for h in range(4):
            kp, kk = self.bank()
            for c in range(DC):
                P.mm(kp[:, 0:NMEM], wkv[:, c, h * 128:(h + 1) * 128], mh[:, c, :], start=(c == 0), stop=(c == DC - 1),
                     r=[("x_wkv", 0), ("x_mh", c)], w=[kk])
            sq, sqk = self.rot("sq1")
            P.act(sq[:, 0:NMEM], kp[:, 0:NMEM], AF.Square, r=[kk], w=[sqk])
            ss, ssk = self.bank()
            P.mm(ss[:, 0:NMEM], self.onesB, sq[:, 0:NMEM], r=[sqk], w=[ssk])
            rs, rsk = self.rot("rs")
            P.act(rs[:, 0:NMEM], ss[:, 0:NMEM], AF.Ln, r=[ssk], w=[rsk], scale=1.0 / 128, bias=EPS)
            P.act(rs[:, 0:NMEM], rs[:, 0:NMEM], AF.Exp, r=[rsk], w=[rsk], scale=-0.5)
            P.stt("dve", self.kTx[:, h, :], kp[:, 0:NMEM], self.colsB[:, 4 + l:5 + l], rs[:, 0:NMEM], ALU.mult, ALU.mult,
                  r=[kk, rsk], w=[("kTx", h)])
        for mt in range(2):
            vp, vk = self.bank()
            for c in range(DC):
                P.mm(vp, mh[:, c, mt * 128:(mt + 1) * 128], wkv[:, c, 512:1024], start=(c == 0), stop=(c == DC - 1),
                     r=[("x_wkv", 1), ("x_mh", c)], w=[vk])
            P.copy("act", self.vx[:, mt, :], vp, r=[vk], w=[("vx", mt)])
        self.score_banks = [4, 5, 6, 7]
        self.si = 0
        items = [(tb, h, mt) for tb in range(NTB) for h in range(4) for mt in range(2)]
        st = {}

        def stageA(it):
            tb, h, mt = it
            sp_, spk = self.sbank()
            P.mm(sp_, self.kTx[:, h, mt * 128:(mt + 1) * 128], qall[:, h, tb * TB:(tb + 1) * TB],
                 r=[("kTx", h), ("x_q", h, tb)], w=[spk])
            st[it] = (sp_, spk)

        def stageB(it):
            tb, h, mt = it
            g = tb * 4 + h
            sp_, spk = st.pop(it)
            pT, pTk = self.rot("pT")
            P.act(pT, sp_, AF.Exp, r=[spk], w=[pTk], scale=128 ** -0.5, bias=-SM_SHIFT)
            ob, obk = self.ps[g % 2], ("ps", g % 2)
            lb, lbk = self.ps[2 + g % 2], ("ps", 2 + g % 2)
            P.mm(ob, self.vx[:, mt, h * 128:(h + 1) * 128], pT, start=(mt == 0), stop=(mt == 1),
                 r=[("vx", mt), pTk], w=[obk])
            P.mm(lb, self.onesB, pT, start=(mt == 0), stop=(mt == 1), r=[pTk], w=[lbk])
            if mt == 1:
                rs, rsk = self.rot("rs")
                self.recip_act(rs, rsk, lb, lbk)
                P.tt("dve", oall[:, h, tb * TB:(tb + 1) * TB], ob, rs, ALU.mult, r=[obk, rsk], w=[(("x_o", tb), h)])

        LOOKX = 3
        for i in range(min(LOOKX, len(items))):
            stageA(items[i])
        for i in range(len(items)):
            if i + LOOKX < len(items):
                stageA(items[i + LOOKX])
            stageB(items[i])
        self.rot_banks = list(range(8))
        self.bi = 0
        for tb in range(NTB):
            self.out_proj_block(wo, "x_wo", oall[:, :, tb * TB:(tb + 1) * TB], ("x_o", tb), 4, tb)
        self.score_banks = [2, 3, 4, 5]
        self.si = 0
        P.fence()
        A.reset(m)

    def store(self):
        P, d, A = self.P, self.d, self.A
        m = A.mark()
        yo = [A.alloc("yout", [128, D], F32) for _ in range(2)]
        keys = []
        for tt in range(NT):
            yb = yo[tt % 2]
            for hb in range(2):
                pb, pk = self.bank()
                for q in range(4):
                    c = hb * 4 + q
                    P.tr(pb[:, q * 128:(q + 1) * 128], self.xT[:, c, tt * 128:(tt + 1) * 128], self.identF,
                         r=[("xT", c, tt // 4), "identF"], w=[pk])
                eng = "dve" if hb == 0 else "act"
                P.copy(eng, yb[:, hb * 512:(hb + 1) * 512], pb, r=[pk], w=[("yout", tt % 2, hb)])
            P.dma("sp", d["y"][tt * 128:(tt + 1) * 128, :], yb, r=[("yout", tt % 2, 0), ("yout", tt % 2, 1)],
                  w=[("y", tt)])
            keys.append(("y", tt))
        P.finish("sp", keys)
        A.reset(m)

    def even_mixer(self, e, l):
        P, d, A = self.P, self.d, self.A
        m0 = A.mark()
        self.rots = {}
        self.rot_banks = list(range(8))
        cD = lambda j: self.colsD[:, e * 128 + j:e * 128 + j + 1]
        w_in_d = d["ev_w_in"][e].rearrange("(c p) n -> p c n", p=128)
        hT = A.alloc("e_hT", [128, DC, T], BF16)
        m1 = A.mark()
        self.mkrot("sq", 1, [128, DC, TB], BF16)
        self.mkrot("rs", 2, [128, TB], F32)
        for tb in range(NTB):
            self.norm_block(tb, 0, l, hT[:, :, tb * TB:(tb + 1) * TB], ("e_hT", tb))
        P.fence()
        A.reset(m1)
        self.rots = {}

        def proj(sbw, sk, tb, dst_ps, ppk):
            sl = slice(tb * TB, (tb + 1) * TB)
            for c in range(DC):
                P.mm(dst_ps, sbw[:, c, :], hT[:, c, sl], start=(c == 0), stop=(c == DC - 1),
                     r=[sk, (("e_hT", tb), c)], w=[ppk])

        if not self.cfg.get("skip_e1"):
            self._even_e1(e, l, hT, w_in_d, cD, proj, m1)
        if not self.cfg.get("skip_e2"):
            self._even_e2(e, l, hT, w_in_d, cD, proj, m1)
        P.fence()
        A.reset(m0)

    def _even_e1(self, e, l, hT, w_in_d, cD, proj, m1):
        P, d, A = self.P, self.d, self.A
        ybT = A.alloc("ybT", [128, 4, T], BF16)
        wo = A.alloc("e_wo", [128, 4, D], BF16)
        xraw = A.alloc("xraw", [128, 3 + T], F32)
        xc = A.alloc("xc", [128, T], F32)
        av = A.alloc("av", [128, T], F32)
        hs = A.alloc("hs", [128, T], F32)
        rfull = A.alloc("rfull", [128, T], F32)
        ifull = A.alloc("ifull", [128, T], F32)
        xcb = A.alloc("xcb", [128, T], BF16)
        gts = [A.alloc("gate", [128, 128], BF16) for _ in range(8)]
        c1 = A.alloc("c1", [128, 4], F32)
        self.mkrot("slab", 2, [128, DC, 256], BF16)
        self.mkrot("gg", 2, [128, TB], F32)
        slabs = []
        for cc in range(2):
            sb, sk = self.rot("slab")
            P.dma("pool", sb[:, :, 0:128], w_in_d[:, :, 2056 + cc * 128:2056 + (cc + 1) * 128], w=[(sk, 0)])
            P.dma("pool", sb[:, :, 128:256], w_in_d[:, :, 2568 + cc * 128:2568 + (cc + 1) * 128], w=[(sk, 1)])
            slabs.append((sb, sk))
        lcols = self.colsD[:, e * 128 + 76:e * 128 + 80]
        P.act(c1, lcols, AF.Exp, w=["c1"], scale=-1.0)
        P.act(c1, c1, AF.Ln, r=["c1"], w=["c1"], bias=1.0)
        P.ts("dve", c1, c1, -8.0, None, ALU.mult, r=["c1"], w=["c1"])
        P.op("dve", "memset", xraw[:, 0:3], 0.0, w=["xraw_pad"])
        for gi, g in enumerate(gts):
            P.op("dve", "memset", g, 0.0, w=[("gz", gi)])
        for cc in range(4):
            for which, nm in enumerate(["ev_gate_a_w", "ev_gate_x_w"]):
                gi = which * 4 + cc
                P.dma("pool", gts[gi][0:64, 0:64], d[nm][e, 2 * cc], r=[("gz", gi)], w=[("gate", gi, 0)])
                P.dma("pool", gts[gi][64:128, 64:128], d[nm][e, 2 * cc + 1], r=[("gz", gi)], w=[("gate", gi, 1)])
        P.dma("pool", wo, d["ev_w_out"][e].rearrange("(h p) n -> p h n", p=128)[:, 4:8, :], w=["e_wo"])
        allT = list(range(NTB))
        for cc in range(4):
            if cc < 2:
                sb, sk = slabs[cc]
            else:
                sb, sk = self.rot("slab")
                P.dma("pool", sb[:, :, 0:128], w_in_d[:, :, 2056 + cc * 128:2056 + (cc + 1) * 128], w=[(sk, 0)])
                P.dma("pool", sb[:, :, 128:256], w_in_d[:, :, 2568 + cc * 128:2568 + (cc + 1) * 128], w=[(sk, 1)])
            for tb in range(NTB):
                pp, ppk = self.bank()
                proj(sb[:, :, 0:128], (sk, 0), tb, pp, ppk)
                P.copy("act", xraw[:, 3 + tb * TB:3 + (tb + 1) * TB], pp, r=[ppk], w=[("xraw", tb)])
            xrk = [("xraw", tb) for tb in range(NTB)] + ["xraw_pad"]
            P.ts("dve", xc, xraw[:, 3:3 + T], cD(48 + 3 * 4 + cc), cD(64 + cc), ALU.mult, ALU.add, r=xrk, w=["xc"])
            for j in range(3):
                P.stt("dve", xc, xraw[:, j:j + T], cD(48 + j * 4 + cc), xc, ALU.mult, ALU.add, r=xrk + ["xc"], w=["xc"])
            P.copy("act", xcb, xc, r=["xc"], w=["xcb"])
            for tb in range(NTB):
                sl = slice(tb * TB, (tb + 1) * TB)
                rp, rpk = self.bank()
                P.mm(rp, gts[cc], xcb[:, sl], r=["xcb", ("gate", cc, 0), ("gate", cc, 1)], w=[rpk])
                ip, ipk = self.bank()
                P.mm(ip, gts[4 + cc], xcb[:, sl], r=["xcb", ("gate", 4 + cc, 0), ("gate", 4 + cc, 1)], w=[ipk])
                P.act(rfull[:, sl], rp, AF.Sigmoid, r=[rpk], w=[("rfull", tb)], bias=cD(68 + cc))
                P.act(ifull[:, sl], ip, AF.Sigmoid, r=[ipk], w=[("ifull", tb)], bias=cD(72 + cc))
            rk_all = [("rfull", tb) for tb in allT]
            ik_all = [("ifull", tb) for tb in allT]
            P.act(av, rfull, AF.Exp, r=rk_all + ["c1"], w=["av"], scale=c1[:, cc:cc + 1])
            P.tt("dve", rfull, av, av, ALU.mult, r=["av"], w=rk_all)
            P.act(rfull, rfull, AF.Sqrt, r=rk_all, w=rk_all, scale=-1.0, bias=1.0)
            P.tt("dve", ifull, ifull, xc, ALU.mult, r=ik_all + ["xc"], w=ik_all)
            P.tt("dve", xc, rfull, ifull, ALU.mult, r=rk_all + ik_all, w=["xc"])
            P.op("dve", "tensor_tensor_scan", hs, av, xc, 0.0, ALU.mult, ALU.add, r=["av", "xc"], w=["hs"])
            for tb in range(NTB):
                sl = slice(tb * TB, (tb + 1) * TB)
                gp, gpk = self.bank()
                proj(sb[:, :, 128:256], (sk, 1), tb, gp, gpk)
                gg, ggk = self.rot("gg")
                P.act(gg, gp, AF.Gelu_apprx_tanh, r=[gpk], w=[ggk])
                P.tt("dve", ybT[:, cc, sl], gg, hs[:, sl], ALU.mult, r=[ggk, "hs"], w=[(("ybT", tb), cc)])
        for tb in range(NTB):
            self.out_proj_block(wo, "e_wo", ybT[:, :, tb * TB:(tb + 1) * TB], ("ybT", tb), 4, tb)
        P.fence()
        A.reset(m1)

    def _even_e2(self, e, l, hT, w_in_d, cD, proj, m1):
        P, d, A = self.P, self.d, self.A
        self.rots = {}
        yaT = A.alloc("yaT", [128, 4, T], BF16)
        woa = A.alloc("e_woa", [128, 4, D], BF16)
        P.dma("pool", woa, d["ev_w_out"][e].rearrange("(h p) n -> p h n", p=128)[:, 0:4, :], w=["e_woa"])
        scn = ["beta", "g", "cum", "cl", "kbe", "kdec", "negb", "tmp"]
        sc = {nm: A.alloc("sc_" + nm, [128, NT, 4], F32) for nm in scn}
        scf = {nm: sc[nm].rearrange("p t h -> p (t h)") for nm in scn}
        lastc = A.alloc("lastc", [128, 4, 32], F32)
        bd = A.alloc("bdrow", [1, 8], F32)
        bdb = A.alloc("bdb", [128, 8], F32)
        slab8 = A.alloc("slab8", [128, DC, 8], BF16)
        P.dma("sp", bd[0:1, 0:4], d["ev_dt_bias"][e:e + 1, :], w=[("bd", 0)])
        P.dma("sp", bd[0:1, 4:8], d["ev_a_log"][e:e + 1, :], w=[("bd", 1)])
        pb, pk = self.bank()
        P.mm(pb[:, 0:8], self.onesF[0:1, :], bd, r=[("bd", 0), ("bd", 1)], w=[pk])
        P.copy("dve", bdb, pb[:, 0:8], r=[pk], w=["bdb"])
        P.act(bdb[:, 4:8], bdb[:, 4:8], AF.Exp, r=["bdb"], w=["bdb2"])
        P.ts("dve", bdb[:, 4:8], bdb[:, 4:8], -1.0, None, ALU.mult, r=["bdb2"], w=["bdb2"])
        P.dma("pool", slab8, w_in_d[:, :, 2048:2056], w=["slab8"])
        for tt in range(NT):
            pp, ppk = self.bank()
            for c in range(DC):
                P.mm(pp[:, 0:8], hT[:, c, tt * 128:(tt + 1) * 128], slab8[:, c, :], start=(c == 0), stop=(c == DC - 1),
                     r=["slab8", (("e_hT", tt // 4), c)], w=[ppk])
            P.act(sc["beta"][:, tt, :], pp[:, 0:4], AF.Sigmoid, r=[ppk], w=[("sc_beta", tt)])
            P.tt("dve", sc["tmp"][:, tt, :], pp[:, 4:8], bdb[:, 0:4], ALU.add, r=[ppk, "bdb"], w=[("sc_tmp", tt)])
        alltmp = [("sc_tmp", tt) for tt in range(NT)]
        P.act(scf["tmp"], scf["tmp"], AF.Exp, r=alltmp, w=["sc_tmp2"])
        P.act(scf["tmp"], scf["tmp"], AF.Ln, r=["sc_tmp2"], w=["sc_tmp2"], bias=1.0)
        for tt in range(NT):
            P.tt("dve", sc["g"][:, tt, :], sc["tmp"][:, tt, :], bdb[:, 4:8], ALU.mult, r=["sc_tmp2", "bdb2"], w=[("sc_g", tt)])
        pb, pk = self.bank()
        P.mm(pb[:, 0:64], self.maskIU, scf["g"], r=[("sc_g", tt) for tt in range(NT)], w=[pk])
        P.copy("dve", scf["cum"], pb[:, 0:64], r=[pk], w=["sc_cum"])
        pb, pk = self.bank()
        P.mm(pb[:, 0:64], self.lsel, scf["cum"], r=["sc_cum"], w=[pk])
        P.copy("dve", scf["cl"], pb[:, 0:64], r=[pk], w=["sc_cl"])
        allb = [("sc_beta", tt) for tt in range(NT)]
        P.act(scf["kbe"], scf["cum"], AF.Exp, r=["sc_cum"], w=["sc_kbe"])
        P.tt("dve", scf["kbe"], scf["kbe"], scf["beta"], ALU.mult, r=["sc_kbe"] + allb, w=["sc_kbe"])
        P.tt("dve", scf["kdec"], scf["cl"], scf["cum"], ALU.subtract, r=["sc_cl", "sc_cum"], w=["sc_kdec"])
        P.act(scf["kdec"], scf["kdec"], AF.Exp, r=["sc_kdec"], w=["sc_kdec"])
        P.ts("dve", scf["negb"], scf["beta"], -1.0, None, ALU.mult, r=allb, w=["sc_negb"])
        P.fence()
        slabq = [A.alloc("slabq", [128, DC, 128], BF16) for _ in range(1)]

        def load_head_slabs(hh):
            cols = [hh * 128]
            for i_, c_ in enumerate(cols):
                P.dma("pool", slabq[i_], w_in_d[:, :, c_:c_ + 128], w=[("slabq", i_)], nofence=True)

        load_head_slabs(0)
        mh = A.mark()
        for h in range(4):
            A.reset(mh)
            self.rots = {}
            self.rot_banks = list(range(8))
            qT = A.alloc("qT", [128, T], BF16)
            kT = A.alloc("kT", [128, T], BF16)
            vT = A.alloc("vT", [128, T], BF16)
            zs = A.alloc("zs", [128, T], BF16)
            mA = A.mark()
            raws = [A.alloc("raw", [128, 3 + T], F32) for _ in range(3)]
            cvs = [A.alloc("cv", [128, T], F32) for _ in range(2)]
            self.mkrot("sq1", 2, [128, TB], BF16)
            self.mkrot("rs", 1, [128, TB], F32)
            self.mkrot("slabv", 2, [128, DC, 128], BF16)
            kinds = [(h * 128, qT, "qT"), (512 + h * 128, kT, "kT"), (1024 + h * 128, vT, "vT")]
            for kind, (col0, dst, dn) in enumerate(kinds):
                raw = raws[kind]
                P.op("dve", "memset", raw[:, 0:3], 0.0, w=[("raw_pad", kind)])
                if kind == 0:
                    sb, sk = slabq[0], ("slabq", 0)
                else:
                    sb, sk = self.rot("slabv")
                    P.dma("pool", sb, w_in_d[:, :, col0:col0 + 128], w=[sk])
                for tb in range(NTB):
                    pp, ppk = self.bank()
                    proj(sb, sk, tb, pp, ppk)
                    P.copy("act", raw[:, 3 + tb * TB:3 + (tb + 1) * TB], pp, r=[ppk], w=[("raw", kind, tb)])
            sb, sk = self.rot("slabv")
            P.dma("pool", sb, w_in_d[:, :, 1536 + h * 128:1536 + (h + 1) * 128], w=[sk])
            zps = []
            for tb in range(NTB):
                pp, ppk = self.bank()
                proj(sb, sk, tb, pp, ppk)
                zps.append((pp, ppk))

            def conv(kind, cv, cvk, eng):
                raw = raws[kind]
                cc = kind * 4 + h
                rk_ = [("raw", kind, tb) for tb in range(NTB)] + [("raw_pad", kind)]
                P.ts(eng, cv, raw[:, 3:3 + T], cD(3 * 12 + cc), None, ALU.mult, r=rk_, w=[cvk])
                for j in range(3):
                    P.stt(eng, cv, raw[:, j:j + T], cD(j * 12 + cc), cv, ALU.mult, ALU.add, r=rk_ + [cvk], w=[cvk])

            conv(0, cvs[0], "cv0", "dve")
            conv(1, cvs[1], "cv1", "dve")
            silq = raws[0][:, 3:3 + T]
            silk = raws[1][:, 3:3 + T]
            P.act(silq, cvs[0], AF.Silu, r=["cv0"], w=["silq"] + [("raw", 0, tb) for tb in range(NTB)])
            P.act(silk, cvs[1], AF.Silu, r=["cv1"], w=["silk"] + [("raw", 1, tb) for tb in range(NTB)])
            conv(2, cvs[0], "cv0", "dve")
            for tb in range(NTB):
                pp, ppk = zps[tb]
                P.act(zs[:, tb * TB:(tb + 1) * TB], pp, AF.Silu, r=[ppk], w=[("zs", tb)])
            P.act(vT, cvs[0], AF.Silu, r=["cv0"], w=["vT"])
            blocks = []
            for sil_, silkey, dst, dn, gsc in ((silq, "silq", qT, "qT", 128 ** -0.5), (silk, "silk", kT, "kT", None)):
                for tb in range(NTB):
                    sl = slice(tb * TB, (tb + 1) * TB)
                    blocks.append((sil_[:, sl], silkey, 128, self.onesB, 1.0, gsc, dst[:, sl], (dn, tb)))
            self.pnorm_pipe(blocks)
            P.fence()
            A.reset(mA)
            self.rots = {}
            if h + 1 < 4:
                load_head_slabs(h + 1)
            kdec = A.alloc("ktm_dec", [128, NT, 128], BF16)
            utm = A.alloc("u_tm", [128, NT, 128], BF16)
            wT = A.alloc("wT", [128, T], BF16)
            qdecT = A.alloc("qdecT", [128, T], BF16)
            qkT = A.alloc("qkT", [128, NT, 128], BF16)
            S = A.alloc("S", [128, 128], F32)
            Sb = A.alloc("Sb", [128, 128], BF16)
            KCH = 5
            slots = []
            for j in range(KCH):
                sd = {}
                for nm in ("dg", "t1"):
                    sd[nm] = A.alloc("sl_" + nm, [128, 128], F32)
                sd["t2"] = sd["dg"]
                for nm in ("Dt", "Dm", "E", "p0", "p1", "pT0", "pT1", "kbe", "vtb", "AT0", "AT1"):
                    sd[nm] = A.alloc("sl_" + nm, [128, 128], BF16)
                slots.append(sd)
            self.mkrot("vn", 2, [128, 128], BF16)
            self.mkrot("ot", 2, [128, TB], F32)
            self.mkrot("sq1", 1, [128, TB], BF16)
            self.mkrot("rs", 1, [128, TB], F32)
            iF, iB = self.identF, self.identB
            self.rot_banks = [1, 2, 3, 4, 5, 6, 7]
            self.bi = 0

            def tile_chain(tt, j, h=h):
                sd = slots[j]
                K_ = lambda nm: ("sl", j, nm)
                tsl = slice(tt * 128, (tt + 1) * 128)
                tb = tt // 4
                cumc = sc["cum"][:, tt, h:h + 1]
                kp, kpk = self.bank()
                P.mm(kp[:, 0:128], kT[:, tsl], iB, r=[("kT", tb)], w=[kpk])
                P.ts("dve", sd["kbe"], kp[:, 0:128], sc["kbe"][:, tt, h:h + 1], None, ALU.mult, r=[kpk], w=[K_("kbe")])
                P.act(kdec[:, tt, :], kp[:, 0:128], AF.Copy, r=[kpk], w=[("kdec", tt)], scale=sc["kdec"][:, tt, h:h + 1])
                vp, vpk = self.bank()
                P.mm(vp[:, 0:128], vT[:, tsl], iB, r=["vT"], w=[vpk])
                P.act(sd["vtb"], vp[:, 0:128], AF.Copy, r=[vpk], w=[K_("vtb")], scale=sc["beta"][:, tt, h:h + 1])
                P.ts("pool", sd["dg"], iF, cumc, None, ALU.mult, w=[K_("dg")])
                yield
                bc, bck = self.bank()
                P.mm(bc[:, 0:128], self.onesF, sd["dg"], r=[K_("dg")], w=[bck])
                P.ts("dve", sd["t1"], bc[:, 0:128], cumc, 0.0, ALU.subtract, ALU.min, r=[bck], w=[K_("t1")])
                P.act(sd["Dt"], sd["t1"], AF.Exp, r=[K_("t1")], w=[K_("Dt")])
                P.ts("dve", sd["t2"], bc[:, 0:128], cumc, 0.0, ALU.subtract, ALU.max, r=[bck], w=[K_("t2"), K_("dg")])
                P.act(sd["Dm"], sd["t2"], AF.Exp, r=[K_("t2")], w=[K_("Dm")], scale=-1.0)
                P.act(sd["E"], bc[:, 0:128], AF.Exp, r=[bck], w=[K_("E")])
                P.act(lastc[:, h, 2 * tt:2 * tt + 2], bc[:, 63:128:64], AF.Exp, r=[bck], w=[("lastc", tt)])
                P.tt("pool", qdecT[:, tsl], qT[:, tsl], sd["E"], ALU.mult, r=[("qT", tb), K_("E")], w=[("qdecT", tt)])
                P.tt("pool", sd["Dm"], sd["Dm"], self.maskSL, ALU.mult, r=[K_("Dm")], w=[K_("Dm")])
                P.tt("pool", sd["Dt"], sd["Dt"], self.maskIU, ALU.mult, r=[K_("Dt")], w=[K_("Dt")])
                yield
                KK, KKk = self.bank()
                P.mm(KK[:, 0:128], kT[:, tsl], kT[:, tsl], r=[("kT", tb)], w=[KKk])
                QK, QKk = self.bank()
                P.mm(QK[:, 0:128], kT[:, tsl], qT[:, tsl], r=[("kT", tb), ("qT", tb)], w=[QKk])
                p, pk_ = sd["p0"], K_("p0")
                P.stt("dve", p, KK[:, 0:128], sc["negb"][:, tt, h:h + 1], sd["Dm"], ALU.mult, ALU.mult,
                      r=[KKk, K_("Dm")], w=[pk_])
                P.tt("dve", qkT[:, tt, :], QK[:, 0:128], sd["Dt"], ALU.mult, r=[QKk, K_("Dt")], w=[("qkT", tt)])
                yield
                tp, tpk = self.bank()
                P.mm(tp[:, 0:128], p, iB, r=[pk_], w=[tpk])
                pT, pTk = sd["pT0"], K_("pT0")
                P.copy("act", pT, tp[:, 0:128], r=[tpk], w=[pTk])
                AT, ATk = sd["AT0"], K_("AT0")
                P.tt("dve", AT, tp[:, 0:128], iF, ALU.add, r=[tpk], w=[ATk])
                yield
                def a_update(pcur, pcurk, AT, ATk, s_):
                    an, ank = self.bank()
                    P.mm(an[:, 0:128], iB, AT, start=True, stop=False, r=[ATk], w=[ank])
                    P.mm(an[:, 0:128], pcur, AT, start=False, stop=True, r=[pcurk, ATk], w=[ank])
                    nx_ = "1" if (s_ % 2 == 0) else "0"
                    ATn, ATnk = sd["AT" + nx_], K_("AT" + nx_)
                    P.copy("dve" if s_ % 2 else "act", ATn, an[:, 0:128], r=[ank], w=[ATnk])
                    return ATn, ATnk

                for s_ in range(5):
                    nx = "1" if (s_ % 2 == 0) else "0"
                    p2, p2k = self.bank()
                    P.mm(p2[:, 0:128], pT, p, r=[pTk, pk_], w=[p2k])
                    pn, pnk = sd["p" + nx], K_("p" + nx)
                    P.copy("act", pn, p2[:, 0:128], r=[p2k], w=[pnk])
                    if s_ < 4:
                        p2T, p2Tk = self.bank()
                        P.mm(p2T[:, 0:128], p, pT, r=[pTk, pk_], w=[p2Tk])
                        pTn, pTnk = sd["pT" + nx], K_("pT" + nx)
                        P.copy("act" if s_ % 2 else "dve", pTn, p2T[:, 0:128], r=[p2Tk], w=[pTnk])
                    if s_ > 0:
                        AT, ATk = a_update(p, pk_, AT, ATk, s_ - 1)
                    yield
                    if s_ < 4:
                        p, pk_, pT, pTk = pn, pnk, pTn, pTnk
                    else:
                        p, pk_ = pn, pnk
                AT, ATk = a_update(p, pk_, AT, ATk, 4)
                yield
                wp, wpk = self.bank()
                P.mm(wp[:, 0:128], sd["kbe"], AT, r=[K_("kbe"), ATk], w=[wpk])
                P.copy("act", wT[:, tsl], wp[:, 0:128], r=[wpk], w=[("wT", tt)])
                up, upk = self.bank()
                P.mm(up[:, 0:128], AT, sd["vtb"], r=[K_("vtb"), ATk], w=[upk])
                P.copy("dve", utm[:, tt, :], up[:, 0:128], r=[upk], w=[("utm", tt)])

            def scan_chain(h=h):
                P.op("dve", "memset", S, 0.0, w=["S"])
                P.op("dve", "memset", Sb, 0.0, w=["Sb"])
                ob, obk = self.ps[0], ("ps", 0)
                for ck in range(32):
                    tt, half = ck // 2, ck % 2
                    yield ("need", tt)
                    hp = slice(half * 64, half * 64 + 64)
                    csl = slice(ck * 64, (ck + 1) * 64)
                    col = (ck % 8) * 64
                    a_ps, ak = self.bank()
                    P.mm(a_ps[hp, 0:128], wT[:, csl], Sb, r=[("wT", tt), "Sb"], w=[ak])
                    P.mm(ob[:, col:col + 64], Sb, qdecT[:, csl], start=True, stop=False, r=["Sb", ("qdecT", tt)], w=[obk])
                    vn, vnk = self.rot("vn")
                    P.tt("dve", vn[hp, :], utm[hp, tt, :], a_ps[hp, 0:128], ALU.subtract, r=[("utm", tt), ak], w=[vnk])
                    yield None
                    s_ps, spk = self.bank()
                    P.mm(s_ps[:, 0:128], kdec[hp, tt, :], vn[hp, :], r=[("kdec", tt), vnk], w=[spk])
                    P.mm(ob[:, col:col + 64], vn[hp, :], qkT[hp, tt, half * 64:half * 64 + 64], start=False, stop=True,
                         r=[vnk, ("qkT", tt)], w=[obk])
                    P.stt("dve", Sb, S, lastc[:, h, ck:ck + 1], s_ps[:, 0:128], ALU.mult, ALU.add,
                          r=[spk, "S", ("lastc", tt)], w=["Sb"])
                    P.stt("dve", S, S, lastc[:, h, ck:ck + 1], s_ps[:, 0:128], ALU.mult, ALU.add,
                          r=[spk, "S", ("lastc", tt)], w=["S"])
                    if ck % 8 == 7:
                        tb = ck // 8
                        sl = slice(tb * TB, (tb + 1) * TB)
                        ot, otk = self.rot("ot")
                        P.copy("act", ot, ob, r=[obk], w=[otk])
                        on, onk = self.rot("ot")
                        self.pnorm(ot, otk, 128, self.onesB, 1.0 / 128, cD(80), on, onk)
                        P.tt("dve", yaT[:, h, sl], on, zs[:, sl], ALU.mult, r=[onk, ("zs", tb)], w=[(("yaT", tb), h)])
                    yield None

            pending = list(range(NT))
            active = []
            free_slots = list(range(KCH))
            done_tiles = set()
            scan = scan_chain()
            scan_wait = next(scan)
            scan_done = False
            while pending or active or not scan_done:
                while pending and free_slots:
                    tt = pending.pop(0)
                    j = free_slots.pop(0)
                    active.append((tile_chain(tt, j), tt, j))
                def scan_step():
                    nonlocal scan_wait, scan_done
                    if not scan_done:
                        if scan_wait is None or scan_wait[1] in done_tiles:
                            try:
                                scan_wait = next(scan)
                            except StopIteration:
                                scan_done = True

                nxt = []
                for gi_, (g, tt, j) in enumerate(active):
                    try:
                        next(g)
                        nxt.append((g, tt, j))
                    except StopIteration:
                        done_tiles.add(tt)
                        free_slots.append(j)
                    scan_step()
                active = nxt
                if not active:
                    scan_step()
            self.rot_banks = list(range(8))
            self.bi = 0
            P.fence()
        for tb in range(NTB):
            self.out_proj_block(woa, "e_woa", yaT[:, :, tb * TB:(tb + 1) * TB], ("yaT", tb), 4, tb)
        P.fence()
        A.reset(m1)

    def build_posfb(self, qb):
        P = self.P
        pf, pfk = self.rot("posfb")
        pb, pk = self.bank()
        for j in range(4):
            tt = qb * 4 + j
            tm, tmk = self.rot("pkm")
            P.ts("dve", tm, self.pkf, self.identF[0:NT, tt:tt + 1], None, ALU.mult, w=[tmk])
            P.mm(pb[:, j * 128:(j + 1) * 128], self.onesF[0:NT, :], tm, r=[tmk], w=[pk])
        P.copy("act", pf, pb, r=[pk], w=[pfk])
        return pf, pfk

    def rope_tables(self, tb):
        P = self.P
        pf, pfk = self.build_posfb(tb)
        R = slice(64, 96)
        f0 = self.freq[R, 0:1]
        out = {}
        for nm, shift in (("sin", 0.0), ("cos", math.pi / 2)):
            ang, ak = self.rot("ang")
            P.ts("dve", ang[R, :], pf[R, :], f0, shift, ALU.mult, ALU.add, r=[pfk], w=[ak])
            ki, kik = self.rot("angi")
            P.ts("dve", ki[R, :], ang[R, :], 1.0 / (2 * math.pi), None, ALU.mult, r=[ak], w=[kik])
            kf, kfk = self.rot("ang")
            P.copy("dve", kf[R, :], ki[R, :], r=[kik], w=[kfk])
            P.stt("dve", ang[R, :], kf[R, :], -2 * math.pi, ang[R, :], ALU.mult, ALU.add, r=[kfk, ak], w=[ak])
            P.ts("dve", ang[R, :], ang[R, :], math.pi, -math.pi, ALU.min, ALU.max, r=[ak], w=[ak])
            tab, tk = self.rot(nm)
            P.act(tab[R, :], ang[R, :], AF.Sin, r=[ak], w=[tk])
            out[nm] = (tab, tk)
        return out

    def odd_mixer(self, o, l):
        P, d, A = self.P, self.d, self.A
        m0 = A.mark()
        self.rots = {}
        cC = lambda j: self.colsC[:, o * 8 + j:o * 8 + j + 1]
        SC_C = 64 ** -0.5
        SC_D = 96 ** -0.5
        slopes = [2.0 ** (-8.0 * (i + 1) / 4) for i in range(4)]
        lam_init = 0.8 - 0.6 * math.exp(-0.3 * l)
        qlatT = A.alloc("qlatT", [128, 2, T], BF16)
        kvlatT = A.alloc("kvlatT", [128, T], BF16)
        kropeT = A.alloc("kropeT", [128, T], BF16)
        neglam = A.alloc("neglam", [128, 2], F32)
        gsub = A.alloc("gsub", [128, 1], F32)
        lamt = A.alloc("lamt", [1, 4, 64], F32)
        lamp = A.alloc("lamp", [1, 2, 64], F32)
        ls = A.alloc("lams", [1, 8], F32)
        m1 = A.mark()
        qcT = A.alloc("qcT", [128, 4, T], BF16)
        kcT = A.alloc("kcT", [128, 4, T], BF16)
        vc = A.alloc("vc", [128, NT, 512], BF16)
        m2 = A.mark()
        for i, nm in enumerate(["od_lam_q1", "od_lam_k1", "od_lam_q2", "od_lam_k2"]):
            P.dma("sp", lamt[0:1, i, :], d[nm][o:o + 1, :], w=[("lamt", i)])
        P.tt("dve", lamp[0:1, 0, :], lamt[0:1, 0, :], lamt[0:1, 1, :], ALU.mult, r=[("lamt", 0), ("lamt", 1)], w=["lp0"])
        P.tt("dve", lamp[0:1, 1, :], lamt[0:1, 2, :], lamt[0:1, 3, :], ALU.mult, r=[("lamt", 2), ("lamt", 3)], w=["lp1"])
        P.op("dve", "memset", ls, 0.0, w=["ls"])
        P.op("dve", "reduce_sum", ls[0:1, 0:1], lamp[0:1, 0, :], AX.X, r=["lp0", "ls"], w=["ls0"])
        P.op("dve", "reduce_sum", ls[0:1, 1:2], lamp[0:1, 1, :], AX.X, r=["lp1", "ls"], w=["ls1"])
        P.act(ls[0:1, 0:2], ls[0:1, 0:2], AF.Exp, r=["ls0", "ls1"], w=["lse"])
        P.tt("dve", ls[0:1, 2:3], ls[0:1, 1:2], ls[0:1, 0:1], ALU.subtract, r=["lse"], w=["ls2"])
        P.ts("dve", ls[0:1, 4:5], ls[0:1, 2:3], -lam_init, None, ALU.add, r=["ls2"], w=["ls3"])
        pb, pk = self.bank()
        P.mm(pb[:, 0:2], self.onesF[0:1, :], ls[0:1, 4:6], r=["ls3"], w=[pk])
        P.copy("dve", neglam, pb[:, 0:2], r=[pk], w=["neglam"])
        P.ts("dve", gsub, cC(2), 1.0 - lam_init, None, ALU.mult, w=["gsub"])
        hT = A.alloc("o_hT", [128, DC, T], BF16)
        self.mkrot("slab", 1, [128, DC, 512], BF16)
        self.mkrot("sq", 1, [128, DC, TB], BF16)
        self.rots["slab"][0].append(self.rots["sq"][0][0])
        self.mkrot("sq1", 2, [128, TB], BF16)
        self.mkrot("rs", 2, [128, TB], F32)
        w_in_d = d["od_w_in"][o].rearrange("(c p) n -> p c n", p=128)
        for tb in range(NTB):
            self.norm_block(tb, 0, l, hT[:, :, tb * TB:(tb + 1) * TB], ("o_hT", tb))

        def load_slab(c0, n):
            sb, sk = self.rot("slab")
            wk = [sk, ("sq", 0)] if sk == ("slab", 1) else [sk]
            P.dma("pool", sb[:, :, 0:n], w_in_d[:, :, c0:c0 + n], w=wk)
            return sb, sk

        def pn_a(src, srck, ones_l, inv_n, gain, out, outk):
            sq, sqk = self.rot("sq1")
            P.act(sq, src, AF.Square, r=[srck], w=[sqk])
            return (src, srck, ones_l, inv_n, gain, out, outk, sq, sqk)

        def pn_b(stt_):
            src, srck, ones_l, inv_n, gain, out, outk, sq, sqk = stt_
            ss, ssk = self.bank()
            P.mm(ss, ones_l, sq, r=[sqk], w=[ssk])
            rs, rsk = self.rot("rs")
            P.act(rs, ss, AF.Ln, r=[ssk], w=[rsk], scale=inv_n, bias=EPS)
            P.act(rs, rs, AF.Exp, r=[rsk], w=[rsk], scale=-0.5)
            P.stt("dve", out, src, gain, rs, ALU.mult, ALU.mult, r=[srck, rsk], w=[outk])

        slab_q = load_slab(0, 512)
        slab_k = load_slab(512, 512)
        prev = None
        for dst, dname, gj, (sb, sk) in ((qcT, "qcT", 0, slab_q), (kcT, "kcT", 1, slab_k)):
            for ch in range(4):
                for tb in range(NTB):
                    sl = slice(tb * TB, (tb + 1) * TB)
                    pp, ppk = self.bank()
                    for c in range(DC):
                        P.mm(pp, sb[:, c, ch * 128:(ch + 1) * 128], hT[:, c, sl], start=(c == 0), stop=(c == DC - 1),
                             r=[sk, (("o_hT", tb), c)], w=[ppk])
                    cur = pn_a(pp, ppk, self.blockB, 1.0 / 64, cC(gj), dst[:, ch, sl], (dname, ch, tb))
                    if prev is not None:
                        pn_b(prev)
                    prev = cur
        sb, sk = load_slab(1024, 512)
        pn_b(prev)
        for tt in range(NT):
            pp, ppk = self.bank()
            for c in range(DC):
                P.mm(pp, hT[:, c, tt * 128:(tt + 1) * 128], sb[:, c, :], start=(c == 0), stop=(c == DC - 1),
                     r=[sk, (("o_hT", tt // 4), c)], w=[ppk])
            P.copy("act", vc[:, tt, :], pp, r=[ppk], w=[("vc", tt)])
        sb, sk = load_slab(1536, 416)
        for tb in range(NTB):
            sl = slice(tb * TB, (tb + 1) * TB)
            qq = [self.bank(), self.bank()]
            for ci, (pp, ppk) in enumerate(qq):
                for c in range(DC):
                    P.mm(pp, sb[:, c, ci * 128:(ci + 1) * 128], hT[:, c, sl], start=(c == 0), stop=(c == DC - 1),
                         r=[sk, (("o_hT", tb), c)], w=[ppk])
            ss, ssk = self.bank()
            for ci, (pp, ppk) in enumerate(qq):
                sq, sqk = self.rot("sq1")
                P.act(sq, pp, AF.Square, r=[ppk], w=[sqk])
                P.mm(ss, self.onesB, sq, start=(ci == 0), stop=(ci == 1), r=[sqk], w=[ssk])
            rs, rsk = self.rot("rs")
            P.act(rs, ss, AF.Ln, r=[ssk], w=[rsk], scale=1.0 / 256, bias=EPS)
            P.act(rs, rs, AF.Exp, r=[rsk], w=[rsk], scale=-0.5)
            for ci, (pp, ppk) in enumerate(qq):
                P.stt("dve", qlatT[:, ci, sl], pp, cC(3 + ci), rs, ALU.mult, ALU.mult, r=[ppk, rsk],
                      w=[("qlatT", ci, tb)])
            pp, ppk = self.bank()
            for c in range(DC):
                P.mm(pp, sb[:, c, 256:384], hT[:, c, sl], start=(c == 0), stop=(c == DC - 1),
                     r=[sk, (("o_hT", tb), c)], w=[ppk])
            self.pnorm(pp, ppk, 128, self.onesB, 1.0 / 128, cC(5), kvlatT[:, sl], ("kvlatT", tb))
            pp, ppk = self.bank()
            for c in range(DC):
                P.mm(pp[64:96, :], sb[:, c, 384:416], hT[:, c, sl], start=(c == 0), stop=(c == DC - 1),
                     r=[sk, (("o_hT", tb), c)], w=[ppk])
            P.copy("act", kropeT[64:96, sl], pp[64:96, :], r=[ppk], w=[("kropeT", tb)])
        P.fence()
        A.reset(m2)
        self.rots = {}
        wo = A.alloc("o_wo", [128, 4, D], BF16)
        P.dma("pool", wo, d["od_w_out"][o].rearrange("(h p) n -> p h n", p=128)[:, 0:4, :], w=["o_wo"])
        self.mkrot("posfb", 2, [128, TB], F32)
        self.mkrot("pkm", 2, [NT, 128], F32)
        dcache = [A.alloc("dcache", [128, TB], F16) for _ in range(NT)]
        self.mkrot("tS", 2, [128, TB], F32)
        self.mkrot("pT", 4, [128, TB], BF16)
        self.mkrot("yT", 2, [128, 4, TB], BF16)
        self.mkrot("rs", 3, [128, TB], F32)
        self.mkrot("sq1", 2, [128, TB], BF16)
        self.mkrot("ya", 1, [128, TB], F32)
        self.mkrot("yb", 1, [128, TB], F32)
        self.score_banks = [4, 5, 6, 7]
        self.si = 0
        self.rot_banks = [6, 7]
        self.bi = 0
        items = [(qb, h, kt) for qb in range(NTB) for h in range(4) for kt in range(4 * qb + 4)]
        st = {}
        deferred = []

        def defer(n, fn):
            deferred.append([n, fn])

        def tick(flush=False):
            while deferred and (flush or deferred[0][0] <= 0):
                deferred.pop(0)[1]()
            for dd_ in deferred:
                dd_[0] -= 1

        def stageA(it):
            qb, h, kt = it
            if h == 0 and kt == 0:
                st[("pf", qb)] = self.build_posfb(qb)
                st[("yT", qb)] = self.rot("yT")
            pf, pfk = st[("pf", qb)]
            j = kt - 4 * qb
            c0 = max(j, 0) * 128
            dist, dkk = dcache[kt], ("dcache", kt)
            if h == 0:
                P.act(dist[:, c0:], pf[:, c0:], AF.Abs, r=[pfk], w=[dkk], bias=self.negposk[:, kt:kt + 1])
            sps = []
            for mi in range(2):
                hs = slice(mi * 64, (mi + 1) * 64)
                sp_, spk = self.sbank()
                P.mm(sp_[:, c0:], kcT[hs, h, kt * 128:(kt + 1) * 128], qcT[hs, h, qb * TB + c0:(qb + 1) * TB],
                     r=[("kcT", h, kt // 4), ("qcT", h, qb)], w=[spk])
                sps.append((sp_, spk))
            st[it] = (dist, dkk, sps)

        def stageB(it):
            qb, h, kt = it
            yT, yk = st[("yT", qb)]
            nkt = 4 * qb + 4
            j = kt - 4 * qb
            c0 = max(j, 0) * 128
            dist, dkk, sps = st.pop(it)
            cur_banks = [spk[1] for _, spk in sps]
            lss = None
            pts = []
            for mi in range(2):
                sp_, spk = sps[mi]
                tS, tSk = self.rot("tS")
                P.stt("dve", tS[:, c0:], dist[:, c0:], -slopes[h] / SC_C, sp_[:, c0:], ALU.mult, ALU.add,
                      r=[dkk, spk], w=[tSk])
                pT, pTk = self.rot("pT")
                P.act(pT[:, c0:], tS[:, c0:], AF.Exp, r=[tSk], w=[pTk], scale=SC_C, bias=-SM_SHIFT)
                pts.append((pT, pTk))
            for mi in range(2):
                pT, pTk = pts[mi]
                if j >= 0:
                    P.op("pool", "memset", pT[64:128, c0:c0 + 64], 0.0, w=[pTk])
                ab = mi
                Ob, Ok = self.ps[ab], ("ps", ab)
                P.mm(Ob[:, c0:], vc[:, kt, h * 128:(h + 1) * 128], pT[:, c0:], start=(kt == 0),
                     stop=(kt == nkt - 1), r=[("vc", kt), pTk], w=[Ok])
                lb_, lbk_ = self.ps[2 + mi], ("ps", 2 + mi)
                P.mm(lb_[:, c0:], self.onesB, pT[:, c0:], start=(kt == 0), stop=(kt == nkt - 1), r=[pTk], w=[lbk_])
            self.rot_banks = cur_banks
            self.bi = 0
            tick()
            if kt == nkt - 1:
                ya, yak = self.rot("ya")
                yb, ybk = self.rot("yb")
                sq, sqk = self.rot("sq1")

                def step1(lss=lss, ya=ya, yak=yak, yb=yb, ybk=ybk, sq=sq, sqk=sqk, h=h):
                    for mi, (y_, y_k) in enumerate(((ya, yak), (yb, ybk))):
                        ab = mi
                        r0, r0k = self.rot("rs")
                        self.recip_act(r0, r0k, self.ps[2 + mi], ("ps", 2 + mi))
                        P.tt("dve", y_, self.ps[ab], r0, ALU.mult, r=[("ps", ab), r0k], w=[y_k])
                    P.stt("dve", ya, yb, neglam[:, 0:1], ya, ALU.mult, ALU.add, r=[ybk, yak], w=[yak])
                    P.act(sq, ya, AF.Square, r=[yak], w=[sqk])

                def step2(h=h, ya=ya, yak=yak, sq=sq, sqk=sqk, yT=yT, yk=yk):
                    ss, ssk = self.bank()
                    P.mm(ss, self.onesB, sq, r=[sqk], w=[ssk])
                    rs, rsk = self.rot("rs")
                    P.act(rs, ss, AF.Ln, r=[ssk], w=[rsk], scale=1.0 / 128, bias=EPS)
                    P.act(rs, rs, AF.Exp, r=[rsk], w=[rsk], scale=-0.5)
                    P.stt("dve", yT[:, h, :], ya, gsub, rs, ALU.mult, ALU.mult, r=[yak, rsk], w=[(yk, h)])

                step1()
                defer(1, step2)
                if h == 3:
                    defer(3, lambda qb=qb, yT=yT, yk=yk: self.out_proj_block(wo, "o_wo", yT, yk, 4, qb))

        LOOK2 = 1
        for i in range(min(LOOK2, len(items))):
            stageA(items[i])
        for i in range(len(items)):
            if i + LOOK2 < len(items):
                stageA(items[i + LOOK2])
            stageB(items[i])
        tick(flush=True)
        self.rot_banks = list(range(8))
        self.bi = 0
        self.score_banks = [2, 3, 4, 5]
        self.si = 0
        P.fence()
        A.reset(m1)
        self.rots = {}
        qdT = A.alloc("qdT", [128, 4, T], BF16)
        kdT = A.alloc("kdT", [128, 4, T], BF16)
        vd = A.alloc("vd", [128, NT, 512], BF16)
        wuq = A.alloc("wuq", [128, 2, 384], BF16)
        wukv = A.alloc("wukv", [128, 768], BF16)
        P.dma("pool", wuq, d["od_w_uq"][o].rearrange("(c p) n -> p c n", p=128), w=["wuq"])
        P.dma("pool", wukv, d["od_w_ukv"][o], w=["wukv"])
        m3 = A.mark()
        self.mkrot("posfb", 2, [128, TB], F32)
        self.mkrot("pkm", 2, [NT, 128], F32)
        self.mkrot("ang", 3, [128, TB], F32)
        self.mkrot("angi", 1, [128, TB], I32)
        self.mkrot("sin", 2, [128, TB], F32)
        self.mkrot("cos", 2, [128, TB], F32)
        KO3 = 3
        oslots = []
        for j in range(KO3):
            sd = {"sq": A.alloc("o3_sq", [128, TB], BF16)}
            for nm in ("rs", "kraw", "rtmp", "rtmp2"):
                sd[nm] = A.alloc("o3_" + nm, [128, TB], F32)
            oslots.append(sd)
        self.rot_banks = list(range(8))
        self.bi = 0
        ones96 = self.onesB[0:96, 0:96]

        def qk_chain(tb, h, isk, cosb, cosk, sinb, sink, j):
            sd = oslots[j]
            K_ = lambda nm: ("o3s", j, nm)
            sl = slice(tb * TB, (tb + 1) * TB)
            if not isk:
                dst, dkey, gcol = qdT[:, h, sl], ("qdT", h, tb), cC(6)[0:96, :]
                qp, qpk = self.bank()
                for c in range(2):
                    P.mm(qp[0:96, :], wuq[:, c, h * 96:(h + 1) * 96], qlatT[:, c, sl], start=(c == 0), stop=(c == 1),
                         r=["wuq", ("qlatT", c, tb)], w=[qpk])
                src, srck = qp[0:96, :], [qpk]
            else:
                dst, dkey, gcol = kdT[:, h, sl], ("kdT", h, tb), cC(7)[0:96, :]
                kp, kpk = self.bank()
                P.mm(kp[0:64, :], wukv[:, h * 192:h * 192 + 64], kvlatT[:, sl], r=["wukv", ("kvlatT", tb)], w=[kpk])
                kr = sd["kraw"]
                P.copy("act", kr[0:64, :], kp[0:64, :], r=[kpk], w=[K_("kraw0")])
                P.copy("act", kr[64:96, :], kropeT[64:96, sl], r=[("kropeT", tb)], w=[K_("kraw1")])
                src, srck = kr[0:96, :], [K_("kraw0"), K_("kraw1")]
            P.act(sd["sq"][0:96, :], src, AF.Square, r=srck, w=[K_("sq")])
            yield
            ss, ssk = self.bank()
            P.mm(ss[0:96, :], ones96, sd["sq"][0:96, :], r=[K_("sq")], w=[ssk])
            rs = sd["rs"]
            P.act(rs[0:96, :], ss[0:96, :], AF.Ln, r=[ssk], w=[K_("rs")], scale=1.0 / 96, bias=EPS)
            P.act(rs[0:96, :], rs[0:96, :], AF.Exp, r=[K_("rs")], w=[K_("rs")], scale=-0.5)
            P.stt("dve", dst[0:96, :], src, gcol, rs[0:96, :], ALU.mult, ALU.mult, r=srck + [K_("rs")], w=[dkey])
            yield
            rp, rpk = self.bank()
            P.mm(rp[0:96, :], self.rotTB[0:96, 0:96], dst[0:96, :], r=[dkey], w=[rpk])
            t1, t2 = sd["rtmp"], sd["rtmp2"]
            P.tt("dve", t1[64:96, :], dst[64:96, :], cosb[64:96, :], ALU.mult, r=[dkey, cosk], w=[K_("t1")])
            P.tt("dve", t2[64:96, :], rp[64:96, :], sinb[64:96, :], ALU.mult, r=[rpk, sink], w=[K_("t2")])
            yield
            P.tt("dve", dst[64:96, :], t1[64:96, :], t2[64:96, :], ALU.add, r=[K_("t1"), K_("t2")], w=[dkey])

        for tb in range(NTB):
            tabs = self.rope_tables(tb)
            cosb, cosk = tabs["cos"]
            sinb, sink = tabs["sin"]
            makers = []
            for h in range(4):
                for isk in (False, True):
                    makers.append(lambda j, tb=tb, h=h, isk=isk, cosb=cosb, cosk=cosk, sinb=sinb, sink=sink:
                                  qk_chain(tb, h, isk, cosb, cosk, sinb, sink, j))
            self.run_chains(makers, KO3)
        wv = wukv.rearrange("p (h e) -> p h e", h=4)[:, :, 64:192]
        for tt in range(NT):
            pp, ppk = self.bank()
            P.mm(pp.rearrange("p (h e) -> p h e", h=4), kvlatT[:, tt * 128:(tt + 1) * 128], wv,
                 r=["wukv", ("kvlatT", tt // 4)], w=[ppk])
            P.copy("act", vd[:, tt, :], pp, r=[ppk], w=[("vd", tt)])
        P.fence()
        A.reset(m3)
        self.rots = {}
        wo2 = A.alloc("o_wo2", [128, 4, D], BF16)
        P.dma("pool", wo2, d["od_w_out"][o].rearrange("(h p) n -> p h n", p=128)[:, 4:8, :], w=["o_wo2"])
        self.mkrot("rs", 3, [128, TB], F32)
        self.mkrot("pT", 6, [128, TB], BF16)
        self.mkrot("yT", 2, [128, 4, TB], BF16)
        self.rot_banks = [6, 7]
        self.bi = 0
        if self.cfg.get("pe_l", True):
            self.score_banks = [4, 5, 6, 7]
            self.si = 0
        self.mkrot("lsum", 2, [128, TB], F32)
        for qb in range(NTB):
            yT, yk = self.rot("yT")
            items = [(h, kt) for h in range(4) for kt in range(4 * qb + 4)]
            st = {}

            def stageA(it, qb=qb):
                h, kt = it
                c0 = max(kt - 4 * qb, 0) * 128
                sp_, spk = self.sbank()
                P.mm(sp_[:, c0:], kdT[0:96, h, kt * 128:(kt + 1) * 128], qdT[0:96, h, qb * TB + c0:(qb + 1) * TB],
                     r=[("kdT", h, kt // 4), ("qdT", h, qb)], w=[spk])
                st[it] = (sp_, spk)

            def stageB(it, qb=qb, yT=yT, yk=yk):
                h, kt = it
                nkt = 4 * qb + 4
                j = kt - 4 * qb
                c0 = max(j, 0) * 128
                sp_, spk = st.pop(it)
                if kt == 0:
                    st[("ls", h)] = self.rot("lsum")
                ls, lsk = st[("ls", h)]
                pT, pTk = self.rot("pT")
                P.act(pT[:, c0:], sp_[:, c0:], AF.Exp, r=[spk], w=[pTk], scale=SC_D, bias=-SM_SHIFT)
                if j >= 0:
                    P.op("dve", "memset", pT[64:128, c0:c0 + 64], 0.0, w=[pTk])
                Ob, Ok = self.ps[h % 2], ("ps", h % 2)
                P.mm(Ob[:, c0:], vd[:, kt, h * 128:(h + 1) * 128], pT[:, c0:], start=(kt == 0), stop=(kt == nkt - 1),
                     r=[("vd", kt), pTk], w=[Ok])
                if self.cfg.get("pe_l", True):
                    lb_, lbk_ = self.ps[2 + h % 2], ("ps", 2 + h % 2)
                    P.mm(lb_[:, c0:], self.onesB, pT[:, c0:], start=(kt == 0), stop=(kt == nkt - 1), r=[pTk], w=[lbk_])
                    if kt == nkt - 1:
                        r0, r0k = self.rot("rs")
                        self.recip_act(r0, r0k, lb_, lbk_)
                        P.tt("dve", yT[:, h, :], Ob, r0, ALU.mult, r=[Ok, r0k], w=[(yk, h)])
                        del st[("ls", h)]
                else:
                    le = "dve" if kt % 3 else "pool"
                    if kt == 0:
                        P.copy(le, ls, pT, r=[pTk], w=[lsk])
                    else:
                        P.tt(le, ls[:, c0:], ls[:, c0:], pT[:, c0:], ALU.add, r=[pTk, lsk], w=[lsk])
                    if kt == nkt - 1:
                        lp, lpk = self.bank()
                        P.mm(lp, self.onesF, ls, r=[lsk], w=[lpk])
                        r0, r0k = self.rot("rs")
                        self.recip_act(r0, r0k, lp, lpk)
                        P.tt("dve", yT[:, h, :], Ob, r0, ALU.mult, r=[Ok, r0k], w=[(yk, h)])
                        del st[("ls", h)]

            LOOK = 3
            for i in range(min(LOOK, len(items))):
                stageA(items[i])
            for i in range(len(items)):
                if i + LOOK < len(items):
                    stageA(items[i + LOOK])
                stageB(items[i])
            self.out_proj_block(wo2, "o_wo2", yT, yk, 4, qb)
        self.rot_banks = list(range(8))
        self.bi = 0
        P.fence()
        A.reset(m0)

    def build(self):
        cfg = self.cfg
        self.setup()
        for l in range(cfg.get("layers", DEPTH)):
            if cfg.get("mixer", True):
                if l % 2 == 0:
                    if not cfg.get("skip_even"):
                        self.even_mixer(l // 2, l)
                elif not cfg.get("skip_odd"):
                    self.odd_mixer(l // 2, l)
            if cfg.get("xattn", True):
                self.xattn(l)
            if cfg.get("ffn", True):
                self.ffn(l)
        self.store()
        self.P.emit()


def build_nc(cfg=None):
    nc = bass.Bass("TRN2", target_bir_lowering=False)
    b = Builder(nc, cfg or {})
    b.build()
    return nc, b


def make_in_maps(inputs, n):
    consts = host_consts()
    maps = []
    for i in range(n):
        mp = {
            "x": np.ascontiguousarray(inputs["x"][i]),
            "mem": np.ascontiguousarray(inputs["mem"][i]),
            "positions": np.ascontiguousarray(inputs["positions"][i:i + 1]),
        }
        for name, _ in WEIGHTS:
            mp[name] = np.ascontiguousarray(inputs[name])
        mp.update(consts)
        maps.append(mp)
    return maps


def kernel(**inputs):
    inputs = {k: np.asarray(v) for k, v in inputs.items()}
    n = inputs["x"].shape[0]
    nc, _ = build_nc({})
    in_maps = make_in_maps(inputs, n)
    res = run_bass_kernel_spmd(nc, in_maps, core_ids=list(range(n)))
    return np.stack([np.asarray(r["y"]) for r in res.results], axis=0).astype(np.float32)
```
